# Optimizing a Trainium2 kernel written in Bass

```python
import jax, jax.numpy as jnp
from jax import lax
import numpy as np

D_MODEL = 2048
BATCH = 4
SEQ = 4096
DEPTH = 2

GRID_W = 64
CTX_LEN = 256
D_MIX = D_MODEL
SGU_WIDTH = D_MIX // 2
SGU_GROUPS = 8
SGU_GROUP_DIM = SGU_WIDTH // SGU_GROUPS
CHUNK = 128
ATTN_WIDTH = D_MIX - SGU_WIDTH
HEAD_DIM = 128
N_Q_HEADS = ATTN_WIDTH // HEAD_DIM
N_KV_HEADS = 2
KV_WIDTH = N_KV_HEADS * HEAD_DIM
Q_BLOCK = 128
ROPE_THETA = 10000.0
EPS = 1e-6
SPLIT_POINTS = (
    SGU_WIDTH,
    2 * SGU_WIDTH,
    3 * SGU_WIDTH,
    3 * SGU_WIDTH + ATTN_WIDTH,
    3 * SGU_WIDTH + ATTN_WIDTH + KV_WIDTH,
    3 * SGU_WIDTH + ATTN_WIDTH + 2 * KV_WIDTH,
)
KV_START = SPLIT_POINTS[3]
KV_END = SPLIT_POINTS[5]
D_IN = SPLIT_POINTS[5] + ATTN_WIDTH

kernel_name = "hybrid_sgu_gqa_prefix_dit_block"


def rms_norm(x, w):
    xf = x.astype(jnp.float32)
    y = xf * lax.rsqrt(jnp.mean(xf * xf, axis=-1, keepdims=True) + EPS)
    return (y * w.astype(jnp.float32)).astype(x.dtype)


def modulation(cond, w_mod, b_mod):
    m = jax.nn.silu(cond) @ w_mod + b_mod
    m = m.reshape(-1, 1, 3 * D_MODEL)
    return jnp.split(m, 3, axis=-1)


def axial_rope_tables(n_tokens, dtype):
    rows = n_tokens // GRID_W
    row_id = jnp.broadcast_to(jnp.arange(rows)[:, None], (rows, GRID_W)).reshape(-1)
    col_id = jnp.broadcast_to(jnp.arange(GRID_W)[None, :], (rows, GRID_W)).reshape(-1)
    axis_dim = HEAD_DIM // 2
    inv_freq = ROPE_THETA ** (-jnp.arange(0, axis_dim, 2, dtype=jnp.float32) / axis_dim)
    ang_r = row_id.astype(jnp.float32)[:, None] * inv_freq[None, :]
    ang_c = col_id.astype(jnp.float32)[:, None] * inv_freq[None, :]
    ang = jnp.concatenate([ang_r, ang_r, ang_c, ang_c], axis=-1)
    return jnp.cos(ang).astype(dtype), jnp.sin(ang).astype(dtype)


def _rotate_half(t):
    t1, t2 = jnp.split(t, 2, axis=-1)
    return jnp.concatenate([-t2, t1], axis=-1)


def apply_axial_rope(x, cos, sin):
    x_r, x_c = jnp.split(x, 2, axis=-1)
    rot = jnp.concatenate([_rotate_half(x_r), _rotate_half(x_c)], axis=-1)
    return x * cos + rot * sin


def split_heads(t, n_heads):
    b, n, _ = t.shape
    return t.reshape(b, n, n_heads, HEAD_DIM).transpose(0, 2, 1, 3)


def merge_heads(t):
    b, h, n, d = t.shape
    return t.transpose(0, 2, 1, 3).reshape(b, n, h * d)


def kv_heads(k, v, k_norm_w):
    return rms_norm(split_heads(k, N_KV_HEADS), k_norm_w), split_heads(v, N_KV_HEADS)


def blocked_gqa(q, k, v):
    b, hq, nq, d = q.shape
    rep = hq // N_KV_HEADS
    qb = q.reshape(b, N_KV_HEADS, rep, nq // Q_BLOCK, Q_BLOCK, d)
    qb = jnp.moveaxis(qb, 3, 0)
    scale = HEAD_DIM ** -0.5

    def one_block(q_blk):
        s = jnp.einsum('bgrqd,bgkd->bgrqk', q_blk, k, preferred_element_type=jnp.float32) * scale
        p = jax.nn.softmax(s, axis=-1).astype(v.dtype)
        return jnp.einsum('bgrqk,bgkd->bgrqd', p, v)

    o = lax.map(one_block, qb)
    return jnp.moveaxis(o, 0, 3).reshape(b, hq, nq, d)


def chunk_sgu(u, v, w_sgu, b_sgu, v_norm_w):
    b, n, _ = u.shape
    u = jax.nn.gelu(u)
    v = jax.nn.gelu(v).reshape(b, n // CHUNK, CHUNK, SGU_GROUPS, SGU_GROUP_DIM)
    v = rms_norm(v, v_norm_w)
    s = jnp.einsum('gpq,bcqgd->bcpgd', w_sgu, v) + b_sgu.T[:, :, None]
    return u * s.reshape(b, n, SGU_WIDTH)


def setup_inputs(seed: int = 0) -> dict:
    key = jax.random.key(seed)
    ks = jax.random.split(key, 16)
    f32 = jnp.float32
    nrm = lambda k, shape: jax.random.normal(k, shape, f32)
    return {
        "x": nrm(ks[0], (BATCH, SEQ, D_MODEL)),
        "c": nrm(ks[1], (BATCH, D_MODEL)),
        "ctx": nrm(ks[2], (BATCH, CTX_LEN, D_MODEL)),
        "c_ctx": nrm(ks[3], (D_MODEL,)),
        "norm_w": 1.0 + 0.02 * nrm(ks[4], (DEPTH, D_MODEL)),
        "w_mod": 0.5 * D_MODEL ** -0.5 * nrm(ks[5], (DEPTH, D_MODEL, 3 * D_MODEL)),
        "b_mod": 0.01 * nrm(ks[6], (DEPTH, 3 * D_MODEL)),
        "w_in": D_MODEL ** -0.5 * nrm(ks[7], (DEPTH, D_MODEL, D_IN)),
        "w_sgu": CHUNK ** -0.5 * nrm(ks[8], (DEPTH, SGU_GROUPS, CHUNK, CHUNK)),
        "b_sgu": 1.0 + 0.02 * nrm(ks[9], (DEPTH, SGU_GROUPS, CHUNK)),
        "v_norm_w": 1.0 + 0.02 * nrm(ks[10], (DEPTH, SGU_GROUPS, SGU_GROUP_DIM)),
        "q_norm_w": 1.0 + 0.02 * nrm(ks[11], (DEPTH, HEAD_DIM)),
        "k_norm_w": 1.0 + 0.02 * nrm(ks[12], (DEPTH, HEAD_DIM)),
        "w_out": D_MIX ** -0.5 * nrm(ks[13], (DEPTH, D_MIX, D_MODEL)),
    }


def reference(x, c, ctx, c_ctx, norm_w, w_mod, b_mod, w_in, w_sgu, b_sgu, v_norm_w,
              q_norm_w, k_norm_w, w_out):
    n_lat = x.shape[1]
    cos, sin = axial_rope_tables(n_lat, x.dtype)
    xc = ctx
    for layer in range(DEPTH):
        last = layer == DEPTH - 1
        shift, scale, gate = modulation(c, w_mod[layer], b_mod[layer])
        shift_c, scale_c, gate_c = modulation(c_ctx, w_mod[layer], b_mod[layer])
        h = rms_norm(x, norm_w[layer]) * (1.0 + scale) + shift
        hc = rms_norm(xc, norm_w[layer]) * (1.0 + scale_c) + shift_c

        if last:
            proj_kv_c = hc @ w_in[layer][:, KV_START:KV_END]
            k_c, v_c = jnp.split(proj_kv_c, 2, axis=-1)
            kc, vc = kv_heads(k_c, v_c, k_norm_w[layer])
        else:
            proj_c = hc @ w_in[layer]
            u_c, v_c_sgu, za_c, q_c, k_c, v_c, zb_c = jnp.split(proj_c, SPLIT_POINTS, axis=-1)
            kc, vc = kv_heads(k_c, v_c, k_norm_w[layer])
            qc = rms_norm(split_heads(q_c, N_Q_HEADS), q_norm_w[layer])
            attn_c = blocked_gqa(qc, kc, vc)
            sgu_c = chunk_sgu(u_c, v_c_sgu, w_sgu[layer], b_sgu[layer], v_norm_w[layer])
            y_c = jnp.concatenate([sgu_c * jax.nn.silu(za_c),
                                   merge_heads(attn_c) * jax.nn.silu(zb_c)], axis=-1) @ w_out[layer]

        proj = h @ w_in[layer]
        u, v_sgu, za, q, k, v, zb = jnp.split(proj, SPLIT_POINTS, axis=-1)
        q = apply_axial_rope(rms_norm(split_heads(q, N_Q_HEADS), q_norm_w[layer]), cos, sin)
        k, v = kv_heads(k, v, k_norm_w[layer])
        k = apply_axial_rope(k, cos, sin)
        k_all = jnp.concatenate([kc, k], axis=2)
        v_all = jnp.concatenate([vc, v], axis=2)
        attn = blocked_gqa(q, k_all, v_all)
        sgu = chunk_sgu(u, v_sgu, w_sgu[layer], b_sgu[layer], v_norm_w[layer])
        y = jnp.concatenate([sgu * jax.nn.silu(za),
                             merge_heads(attn) * jax.nn.silu(zb)], axis=-1) @ w_out[layer]
        x = x + gate * y
        if not last:
            xc = xc + gate_c * y_c
    return x
```

```python
import numpy as np
from contextlib import ExitStack
import ml_dtypes
import concourse.bass as bass
import concourse.mybir as mybir
from concourse.bass_utils import run_bass_kernel_spmd

F32 = mybir.dt.float32
BF16 = mybir.dt.bfloat16
AF = mybir.ActivationFunctionType
ALU = mybir.AluOpType
AX = mybir.AxisListType

D = 2048
NKC = 16
DIN = 5632
NCB = 11
SEQ = 4096
CTX = 256
GRID_W = 64
EPS = 1e-6
TG = 4
NT_ALL = 34
DEBUG = False
MODE = "fused"


class Tok:
    __slots__ = ("w", "r")

    def __init__(self):
        self.w = None
        self.r = {}


class Sched:
    EPOCH = 20000

    def __init__(self, nc, es):
        self.nc = nc
        self.es = es
        self.names = ["pe", "act", "dve", "pool", "sp"]
        self.q = {k: [] for k in self.names}
        self.cnt = {k: 0 for k in self.names}
        self.esems = {k: [] for k in self.names}
        self.dsems = {}
        self.dcnt = {}
        self.waited = {k: {} for k in self.names}

    def _newsem(self, name):
        return self.es.enter_context(self.nc.semaphore(name))

    def _eng_event(self, eng):
        c = self.cnt[eng]
        ep, v = divmod(c, self.EPOCH)
        while len(self.esems[eng]) <= ep:
            self.esems[eng].append(self._newsem(f"e_{eng}_{len(self.esems[eng])}"))
        self.cnt[eng] = c + 1
        return (self.esems[eng][ep], v + 1, eng)

    def _collect(self, eng, reads, writes):
        evs = []
        for t in reads:
            if t.w is not None:
                evs.append(t.w)
        for t in writes:
            if t.w is not None:
                evs.append(t.w)
            evs.extend(t.r.values())
        waits = {}
        for (sem, val, e) in evs:
            if e is not None and e == eng and eng == "pe":
                continue
            key = id(sem)
            if self.waited[eng].get(key, 0) >= val:
                continue
            if key not in waits or waits[key][1] < val:
                waits[key] = (sem, val)
        for key, (sem, val) in waits.items():
            self.waited[eng][key] = val
        return list(waits.values())

    def _mark(self, ev, reads, writes):
        k = id(ev[0])
        for t in reads:
            old = t.r.get(k)
            if old is None or old[1] < ev[1]:
                t.r[k] = ev
        for t in writes:
            t.w = ev
            t.r = {}

    def op(self, eng, fn, reads=(), writes=()):
        waits = self._collect(eng, reads, writes)
        ev = self._eng_event(eng)
        self.q[eng].append((waits, fn, ev[0], 1))
        self._mark(ev, reads, writes)

    def dma(self, eng, fn, key, reads=(), writes=(), n=1):
        waits = self._collect(eng, reads, writes)
        if key not in self.dsems:
            self.dsems[key] = self._newsem(f"d_{key}")
            self.dcnt[key] = 0
        self.dcnt[key] += 16 * n
        ev = (self.dsems[key], self.dcnt[key], None)
        self.q[eng].append((waits, fn, self.dsems[key], 16))
        self._mark(ev, reads, writes)

    def final_wait(self, eng, toks):
        waits = self._collect(eng, toks, toks)
        self.q[eng].append((waits, None, None, 0))

    def emit(self, block):
        def mk(name):
            def body(e):
                for (waits, fn, sem, inc) in self.q[name]:
                    for (s, v) in waits:
                        e.wait_ge(s, v)
                    if fn is None:
                        continue
                    r = fn(e)
                    if isinstance(r, (list, tuple)):
                        for ins in r:
                            ins.then_inc(sem, inc)
                    else:
                        r.then_inc(sem, inc)
            return body
        block.tensor(mk("pe"))
        block.scalar(mk("act"))
        block.vector(mk("dve"))
        block.gpsimd(mk("pool"))
        block.sync(mk("sp"))


class Ring:
    def __init__(self, items):
        self.items = items
        self.i = 0

    def next(self):
        it = self.items[self.i % len(self.items)]
        self.i += 1
        return it


def is_ctx(t):
    return t % 17 == 16


def lat_index(t):
    return (t // 17) * 16 + (t % 17)


def build(layers, final, p1_tiles, p2_tiles_per_layer, n_out_tiles):
    nc = bass.Bass("TRN2", target_bir_lowering=False)
    NL = len(layers)

    def din(name, shape, dt=F32):
        return nc.dram_tensor(name, shape, dt, kind="ExternalInput").ap()

    xa = din("xa", [NT_ALL * 128, D])
    rope = din("rope", [32 * 128, 256])
    c2 = din("c2", [32, 128])
    identb_in = din("identb", [128, 128], BF16)
    identf_in = din("identf", [128, 128])
    w_mod = din("w_mod", [2, D, 3 * D])
    b_mod = din("b_mod", [2, 48, 128])
    norm_w = din("norm_w", [2, 16, 128])
    w_in = din("w_in", [2, D, DIN])
    w_out = din("w_out", [2, D, D])
    w_sgu = din("w_sgu", [2, 8, 128, 128])
    b_sgu = din("b_sgu", [2, 8, 128])
    v_norm_w = din("v_norm_w", [2, 1024])
    q_norm_w = din("q_norm_w", [2, 128])
    k_norm_w = din("k_norm_w", [2, 128])
    if final:
        out = nc.dram_tensor("out", [16 * 128, D], F32, kind="ExternalOutput").ap()
    else:
        out = nc.dram_tensor("xnext", [n_out_tiles * 128, D], F32, kind="ExternalOutput").ap()
    xs = nc.dram_tensor("xs", [NT_ALL * 128, D], F32).ap() if NL > 1 else None
    wbi = [nc.dram_tensor(f"wbi{l}", [NCB, 128, NKC * 512], BF16).ap() for l in layers]
    wbo = [nc.dram_tensor(f"wbo{l}", [4, 128, NKC * 512], BF16).ap() for l in layers]

    with ExitStack() as es:
        def sb(name, shape, dt):
            return es.enter_context(nc.sbuf_tensor(name, shape, dt))

        S = Sched(nc, es)
        identb = sb("identb_sb", [128, 128], BF16); t_identb = Tok()
        identf = sb("identf_sb", [128, 128], F32); t_identf = Tok()
        onesf = sb("onesf", [128, 128], F32); t_onesf = Tok()
        KT = sb("KT", [128, 2, NT_ALL * 128], BF16)
        VA = sb("VA", [128, NT_ALL, 2, 130], BF16)
        t_K = [Tok() for _ in range(NT_ALL)]
        t_V = [Tok() for _ in range(NT_ALL)]
        t_Vones = Tok()
        wbuf = [sb(f"wbuf{i}", [128, NKC * 512], BF16) for i in range(2)]
        t_wbuf = [Tok() for _ in range(2)]
        wring = Ring(list(zip(wbuf, t_wbuf, ["wbuf0", "wbuf1"])))
        xbuf = [sb(f"xbuf{i}", [128, D], F32) for i in range(2)]
        xring = Ring([(xbuf[i], Tok(), f"xbuf{i}") for i in range(2)])
        xn = [sb(f"xn{i}", [128, D], BF16) for i in range(2)]
        xnring = Ring([(xn[i], Tok()) for i in range(2)])
        hT = [sb(f"hT{i}", [128, NKC, 128], BF16) for i in range(TG)]
        t_hT = [Tok() for _ in range(TG)]
        h1ring = Ring([(hT[i], t_hT[i]) for i in range(2)])
        s_sb = [sb(f"s_sb{i}", [128, 512], F32) for i in range(TG)]
        t_s = [Tok() for _ in range(TG)]
        gated = [sb(f"gated{i}", [128, D], BF16) for i in range(TG)]
        t_gated = [Tok() for _ in range(TG)]
        qT = [sb(f"qT{i}", [128, 8, 128], BF16) for i in range(TG)]
        t_qT = [Tok() for _ in range(TG)]
        szb = [sb(f"szb{i}", [128, 1024], BF16) for i in range(TG)]
        t_szb = [Tok() for _ in range(TG)]
        ropeT = [sb(f"ropeT{i}", [128, 256], F32) for i in range(TG)]
        t_rope = [Tok() for _ in range(TG)]
        r1ring = Ring([(ropeT[i], t_rope[i], f"ropeT{i}") for i in range(2)])
        tmps = [sb(f"tmp{i}", [128, 512], F32) for i in range(6)]
        tring = Ring([(tmps[i], Tok()) for i in range(6)])
        vnb = [sb(f"vnb{i}", [128, 512], BF16) for i in range(2)]
        vnring = Ring([(vnb[i], Tok()) for i in range(2)])
        qbf = [sb(f"qbf{i}", [128, 512], BF16) for i in range(2)]
        qbring = Ring([(qbf[i], Tok()) for i in range(2)])
        PT = [sb(f"PT{i}", [128, 512], BF16) for i in range(4)]
        ptring = Ring([(PT[i], Tok()) for i in range(4)])
        xp = [sb(f"xp{i}", [128, 512], F32) for i in range(4)]
        xpring = Ring([(xp[i], Tok(), f"xp{i}") for i in range(4)])
        gate_b = sb("gate_b", [128, D], F32)
        t_gate = Tok()
        small = sb("small", [128, 64], F32)
        smring = Ring([(small[:, 8 * i:8 * i + 8], Tok()) for i in range(8)])
        cT = sb("cT", [128, 32], F32); t_cT = Tok()
        modT = [sb(f"modT{i}", [128, 48, 2], F32) for i in range(NL)]
        t_mod = [Tok() for _ in range(NL)]
        gT = [sb(f"gT{i}", [128, NKC, 2], F32) for i in range(NL)]
        t_gT = [Tok() for _ in range(NL)]
        nwT = sb("nwT", [128, 16], F32); t_nwT = Tok()
        bmT = sb("bmT", [128, 48], F32); t_bmT = Tok()
        rows = sb("rows", [48, 128], F32); t_rows = Tok()
        wsg_f = sb("wsg_f", [128, 8, 128], F32); t_wsgf = Tok()
        wsg_b = sb("wsg_b", [128, 8, 128], BF16); t_wsgb = Tok()
        wsguT = sb("wsguT", [128, 8, 128], BF16); t_wsguT = Tok()
        bsguT = sb("bsguT", [128, 8], F32); t_bsguT = Tok()
        vnw_b = sb("vnw_b", [128, 1024], F32); t_vnw = Tok()
        qnw_b = sb("qnw_b", [128, 128], F32); t_qnw = Tok()
        knw_b = sb("knw_b", [128, 128], F32); t_knw = Tok()
        negC = sb("negC", [128, 2], F32); t_negC = Tok()
        diag = [sb(f"diag{i}", [128, 128], F32) for i in range(2)]
        dring = Ring([(diag[i], Tok()) for i in range(2)])
        bank = [es.enter_context(nc.psum_tensor(f"bank{i}", [128, 512], F32)) for i in range(8)]
        t_bank = [Tok() for _ in range(8)]
        aring = Ring([(bank[i], t_bank[i]) for i in range(3)])
        Sb, t_Sb = bank[3], t_bank[3]
        Tb = [(bank[4], t_bank[4]), (bank[5], t_bank[5])]
        Ob = [(bank[6], t_bank[6]), (bank[7], t_bank[7])]

        block = es.enter_context(nc.Block())
        dbg_toks = []

        def dbg(name, ap, tok):
            if not DEBUG:
                return
            d = nc.dram_tensor("dbg_" + name, list(ap.shape), ap.dtype, kind="ExternalOutput").ap()
            tk = Tok()
            S.dma("sp", lambda e: e.dma_start(out=d, in_=ap), "dbg_" + name, reads=[tok], writes=[tk])
            dbg_toks.append(tk)

        S.dma("sp", lambda e: e.dma_start(out=identb[:], in_=identb_in), "c_identb", writes=[t_identb])
        S.dma("sp", lambda e: e.dma_start(out=identf[:], in_=identf_in), "c_identf", writes=[t_identf])
        S.op("dve", lambda e: e.memset(onesf[:], 1.0), writes=[t_onesf])
        S.op("dve", lambda e: e.memset(VA[:, :, :, 128:130], 1.0), writes=[t_Vones])

        t_wci = [Tok() for _ in range(NL)]
        t_wco = [Tok() for _ in range(NL)]
        for li, l in enumerate(layers):
            def cast_in(e, l=l, li=li):
                res = []
                for kc in range(NKC):
                    for c0 in (0, 6):
                        ncb = 6 if c0 == 0 else 5
                        dst = wbi[li][c0:c0 + ncb, :, kc * 512:(kc + 1) * 512]
                        src = w_in[l, kc * 128:(kc + 1) * 128, c0 * 512:(c0 + ncb) * 512].rearrange("p (cb c) -> cb p c", c=512)
                        res.append(e.dma_start(out=dst, in_=src))
                return res
            S.dma("pool", cast_in, f"cast_in{li}", writes=[t_wci[li]], n=2 * NKC)

            def cast_out(e, l=l, li=li):
                res = []
                for kc in range(NKC):
                    dst = wbo[li][:, :, kc * 512:(kc + 1) * 512]
                    src = w_out[l, kc * 128:(kc + 1) * 128, :].rearrange("p (cb c) -> cb p c", c=512)
                    res.append(e.dma_start(out=dst, in_=src))
                return res
            S.dma("pool", cast_out, f"cast_out{li}", writes=[t_wco[li]], n=NKC)

        def small_T(src_ap, n, dst, t_dst, extra_reads=()):
            S.dma("sp", lambda e: e.dma_start(out=rows[0:n, :], in_=src_ap), "rows", writes=[t_rows])
            S.op("pe", lambda e: e.transpose(out=Sb[:, 0:n], in_=rows[0:n, :], identity=identf[0:n, 0:n]),
                 reads=[t_rows, t_identf], writes=[t_Sb])
            S.op("dve", lambda e: e.tensor_copy(out=dst, in_=Sb[:, 0:n]), reads=[t_Sb], writes=[t_dst])

        S.dma("sp", lambda e: e.dma_start(out=rows[0:32, :], in_=c2), "rows", writes=[t_rows])
        S.op("act", lambda e: e.activation(out=rows[0:32, :], in_=rows[0:32, :], func=AF.Silu), reads=[t_rows], writes=[t_rows])
        S.op("pe", lambda e: e.transpose(out=Sb[:, 0:32], in_=rows[0:32, :], identity=identf[0:32, 0:32]),
             reads=[t_rows, t_identf], writes=[t_Sb])
        S.op("dve", lambda e: e.tensor_copy(out=cT[:], in_=Sb[:, 0:32]), reads=[t_Sb], writes=[t_cT])
        cTv = cT[:].rearrange("p (r k) -> p r k", r=2)

        for li, l in enumerate(layers):
            small_T(b_mod[l], 48, bmT[:], t_bmT)
            small_T(norm_w[l], 16, nwT[:], t_nwT)
            for jb in range(24):
                wb, t_wb, wkey = wring.next()
                wv = wb[:].bitcast(F32).rearrange("p (k c) -> p k c", c=256)
                S.dma("sp", lambda e, wv=wv, l=l, jb=jb: e.dma_start(
                    out=wv, in_=w_mod[l, :, jb * 256:(jb + 1) * 256].rearrange("(k p) c -> p k c", p=128)),
                    wkey, writes=[t_wb])

                def mm_mod(e, wv=wv, jb=jb):
                    r = None
                    for jj in range(2):
                        j = jb * 2 + jj
                        for kc in range(NKC):
                            r = e.matmul(Sb[:, 2 * j:2 * j + 2], lhsT=wv[:, kc, jj * 128:(jj + 1) * 128], rhs=cTv[:, :, kc],
                                         start=(kc == 0), stop=(kc == NKC - 1))
                    return r
                S.op("pe", mm_mod, reads=[t_wb, t_cT], writes=[t_Sb])
            modv = modT[li]
            S.op("dve", lambda e, modv=modv: e.tensor_tensor(
                out=modv[:], in0=Sb[:, 0:96].rearrange("p (j r) -> p j r", r=2),
                in1=bmT[:].unsqueeze(2).broadcast_to([128, 48, 2]), op=ALU.add),
                reads=[t_Sb, t_bmT], writes=[t_mod[li]])
            S.op("dve", lambda e, modv=modv, li=li: e.tensor_scalar(
                out=gT[li][:], in0=modv[:, 16:32, :], scalar1=1.0, scalar2=None, op0=ALU.add),
                reads=[t_mod[li]], writes=[t_gT[li]])
            S.op("dve", lambda e, li=li: e.tensor_tensor(
                out=gT[li][:], in0=gT[li][:], in1=nwT[:].unsqueeze(2).broadcast_to([128, 16, 2]), op=ALU.mult),
                reads=[t_nwT], writes=[t_gT[li]])

        for li in range(NL):
            dbg(f"modT{li}", modT[li][:], t_mod[li])
            dbg(f"gT{li}", gT[li][:], t_gT[li])
        dbg("cT", cT[:], t_cT)

        def rstd_small(ss_ap, t_ss, n, inv_n):
            sd, t_sd = smring.next()
            S.op("act", lambda e: e.activation(out=sd[:, 0:n], in_=ss_ap, func=AF.Sqrt, bias=EPS, scale=inv_n),
                 reads=[t_ss], writes=[t_sd])
            rs, t_rs = smring.next()
            S.op("dve", lambda e: e.reciprocal(out=rs[:, 0:n], in_=sd[:, 0:n]), reads=[t_sd], writes=[t_rs])
            return rs[:, 0:n], t_rs

        def make_hT(src_ap, t_src, t, li, dst, t_dst):
            r = 1 if is_ctx(t) else 0
            xb, t_xb, xkey = xring.next()
            S.dma("sp", lambda e: e.dma_start(out=xb[:], in_=src_ap[t * 128:(t + 1) * 128, :]), xkey, reads=[t_src[t]], writes=[t_xb])
            xnb, t_xn = xnring.next()
            ss, t_ss = smring.next()
            S.op("act", lambda e: e.activation(out=xnb[:], in_=xb[:], func=AF.Square, accum_out=ss[:, 0:1]),
                 reads=[t_xb], writes=[t_xn, t_ss])
            rs, t_rs = rstd_small(ss[:, 0:1], t_ss, 1, 1.0 / D)
            S.op("dve", lambda e: e.tensor_scalar(out=xnb[:], in0=xb[:], scalar1=rs[:, 0:1], scalar2=None, op0=ALU.mult),
                 reads=[t_xb, t_rs], writes=[t_xn])
            for half in range(2):
                tb, t_tb = Tb[half]
                tbv = tb[:].bitcast(BF16).rearrange("p (k c) -> p k c", c=128)

                def tr(e, half=half, tbv=tbv):
                    rr = None
                    for k in range(8):
                        kc = half * 8 + k
                        rr = e.transpose(out=tbv[:, k, :], in_=xnb[:, kc * 128:(kc + 1) * 128], identity=identb[:])
                    return rr
                S.op("pe", tr, reads=[t_xn, t_identb], writes=[t_tb])

                def ev(e, half=half, tbv=tbv):
                    rr = None
                    for k in range(8):
                        kc = half * 8 + k
                        rr = e.tensor_scalar(out=dst[:, kc, :], in0=tbv[:, k, :], scalar1=gT[li][:, kc, r:r + 1],
                                             scalar2=modT[li][:, kc, r:r + 1], op0=ALU.mult, op1=ALU.add)
                    return rr
                S.op("dve", ev, reads=[t_tb, t_gT[li], t_mod[li]], writes=[t_dst])

        def proj(lhs, t_lhs, wb, t_wb):
            ab, t_ab = aring.next()
            wv = wb[:].rearrange("p (k c) -> p k c", c=512)

            def mm(e):
                rr = None
                for kc in range(NKC):
                    rr = e.matmul(ab[:], lhsT=lhs[:, kc, :], rhs=wv[:, kc, :], start=(kc == 0), stop=(kc == NKC - 1))
                return rr
            S.op("pe", mm, reads=[t_lhs, t_wb], writes=[t_ab])
            return ab, t_ab

        def head_rstd(src_ap, t_src, nh):
            sq, t_sq = tring.next()
            S.op("act", lambda e: e.activation(out=sq[:, 0:nh * 128], in_=src_ap, func=AF.Square), reads=[t_src], writes=[t_sq])
            ss, t_ss = smring.next()
            S.op("dve", lambda e: e.tensor_reduce(out=ss[:, 0:nh], in_=sq[:, 0:nh * 128].rearrange("p (h d) -> p h d", d=128),
                                                  axis=AX.X, op=ALU.add), reads=[t_sq], writes=[t_ss])
            return rstd_small(ss[:, 0:nh], t_ss, nh, 1.0 / 128)

        def apply_rope(src, t_src, nh, rt, t_rt, dst, t_dst):
            n = nh * 128
            t1, t_t1 = tring.next()
            cosb = rt[:, 0:128].unsqueeze(1).broadcast_to([128, nh, 128])
            S.op("dve", lambda e: e.tensor_tensor(out=t1[:, 0:n].rearrange("p (h d) -> p h d", d=128),
                                                  in0=src[:, 0:n].rearrange("p (h d) -> p h d", d=128), in1=cosb, op=ALU.mult),
                 reads=[t_src, t_rt], writes=[t_t1])
            rot, t_rot = tring.next()
            sv = src[:, 0:n].rearrange("p (h b t d) -> p h b t d", b=2, t=2, d=32)
            rv = rot[:, 0:n].rearrange("p (h b t d) -> p h b t d", b=2, t=2, d=32)
            t1v = t1[:, 0:n].rearrange("p (h b t d) -> p h b t d", b=2, t=2, d=32)
            dv = dst[:, 0:n].rearrange("p (h b t d) -> p h b t d", b=2, t=2, d=32)
            sinv = rt[:, 128:256].rearrange("p (b t d) -> p b t d", b=2, t=2)

            def rotf(e):
                e.tensor_tensor(out=rv[:, :, :, 0, :], in0=sv[:, :, :, 1, :],
                                in1=sinv[:, :, 0, :].unsqueeze(1).broadcast_to([128, nh, 2, 32]), op=ALU.mult)
                return e.tensor_tensor(out=rv[:, :, :, 1, :], in0=sv[:, :, :, 0, :],
                                       in1=sinv[:, :, 1, :].unsqueeze(1).broadcast_to([128, nh, 2, 32]), op=ALU.mult)
            S.op("dve", rotf, reads=[t_src, t_rt], writes=[t_rot])

            def fin(e):
                e.tensor_tensor(out=dv[:, :, :, 0, :], in0=t1v[:, :, :, 0, :], in1=rv[:, :, :, 0, :], op=ALU.subtract)
                return e.tensor_tensor(out=dv[:, :, :, 1, :], in0=t1v[:, :, :, 1, :], in1=rv[:, :, :, 1, :], op=ALU.add)
            S.op("dve", fin, reads=[t_t1, t_rot], writes=[t_dst])

        out_toks = []

        def run_layer(li, l, src_ap, dst_ap, t_srcx, p2_tiles, last_in_prog):

            S.dma("sp", lambda e, l=l: e.dma_start(out=wsg_f[:], in_=w_sgu[l].rearrange("g p q -> p g q")), "wsgf", writes=[t_wsgf])
            S.op("dve", lambda e: e.tensor_copy(out=wsg_b[:], in_=wsg_f[:]), reads=[t_wsgf], writes=[t_wsgb])
            tb, t_tb = Tb[0]
            tbv0 = tb[:].bitcast(BF16).rearrange("p (k c) -> p k c", c=128)

            def trw(e, tbv0=tbv0):
                rr = None
                for g in range(8):
                    rr = e.transpose(out=tbv0[:, g, :], in_=wsg_b[:, g, :], identity=identb[:])
                return rr
            S.op("pe", trw, reads=[t_wsgb, t_identb], writes=[t_tb])
            S.op("dve", lambda e, tbv0=tbv0: e.tensor_copy(out=wsguT[:], in_=tbv0[:]), reads=[t_tb], writes=[t_wsguT])
            small_T(b_sgu[l], 8, bsguT[:], t_bsguT)
            S.dma("sp", lambda e, l=l: e.dma_start(out=vnw_b[:], in_=v_norm_w[l].partition_broadcast(128)), "vnw", writes=[t_vnw])
            S.dma("sp", lambda e, l=l: e.dma_start(out=qnw_b[:], in_=q_norm_w[l].partition_broadcast(128)), "qnw", writes=[t_qnw])
            S.dma("sp", lambda e, l=l: e.dma_start(out=knw_b[:], in_=k_norm_w[l].partition_broadcast(128)), "knw", writes=[t_knw])
            mq, t_mq = smring.next()
            S.op("dve", lambda e, mq=mq: e.tensor_reduce(out=mq[:, 0:1], in_=qnw_b[:], axis=AX.X, op=ALU.max, apply_absolute_value=True),
                 reads=[t_qnw], writes=[t_mq])
            S.op("dve", lambda e, mq=mq: e.tensor_reduce(out=mq[:, 1:2], in_=knw_b[:], axis=AX.X, op=ALU.max, apply_absolute_value=True),
                 reads=[t_knw], writes=[t_mq])
            S.op("dve", lambda e, mq=mq: e.tensor_tensor(out=negC[:, 0:1], in0=mq[:, 0:1], in1=mq[:, 1:2], op=ALU.mult),
                 reads=[t_mq], writes=[t_negC])
            S.op("dve", lambda e: e.tensor_scalar(out=negC[:, 0:1], in0=negC[:, 0:1], scalar1=-float(np.sqrt(128.0)), scalar2=None, op0=ALU.mult),
                 writes=[t_negC])
            def build_gate(r, li=li):
                for q4 in range(4):
                    for k in range(4):
                        kc = q4 * 4 + k
                        dg, t_dg = dring.next()
                        S.op("dve", lambda e, dg=dg, kc=kc, r=r: e.tensor_scalar(
                            out=dg[:], in0=identf[:], scalar1=modT[li][:, 32 + kc, r:r + 1], scalar2=None, op0=ALU.mult),
                            reads=[t_identf, t_mod[li]], writes=[t_dg])
                        S.op("pe", lambda e, dg=dg, k=k: e.matmul(Sb[:, k * 128:(k + 1) * 128], lhsT=onesf[:], rhs=dg[:], start=True, stop=True),
                             reads=[t_dg, t_onesf], writes=[t_Sb])
                    S.op("dve", lambda e, q4=q4: e.tensor_copy(out=gate_b[:, q4 * 512:(q4 + 1) * 512], in_=Sb[:]),
                         reads=[t_Sb], writes=[t_gate])
            build_gate(0)

            wkv, t_wkv, wkey = wring.next()
            S.dma("sp", lambda e, wkv=wkv: e.dma_start(out=wkv[:], in_=wbi[li][8]), wkey, reads=[t_wci[li]], writes=[t_wkv])
            for t in p1_tiles:
                hb, t_hb = h1ring.next()
                make_hT(src_ap, t_srcx, t, li, hb, t_hb)
                if t == p1_tiles[0] and li == 0:
                    dbg("hT_p1", hb[:], t_hb)
                ab, t_ab = proj(hb, t_hb, wkv, t_wkv)
                rk, t_rk = head_rstd(ab[:, 0:256], t_ab, 2)
                kn, t_kn = tring.next()

                def knf(e, ab=ab, rk=rk, kn=kn):
                    rr = None
                    for h in range(2):
                        rr = e.scalar_tensor_tensor(out=kn[:, h * 128:(h + 1) * 128], in0=ab[:, h * 128:(h + 1) * 128],
                                                    scalar=rk[:, h:h + 1], in1=knw_b[:], op0=ALU.mult, op1=ALU.mult)
                    return rr
                S.op("dve", knf, reads=[t_ab, t_rk, t_knw], writes=[t_kn])
                kb, t_kb = qbring.next()
                if is_ctx(t):
                    S.op("dve", lambda e, kb=kb, kn=kn: e.tensor_copy(out=kb[:, 0:256], in_=kn[:, 0:256]), reads=[t_kn], writes=[t_kb])
                else:
                    rt, t_rt, rkey = r1ring.next()
                    lt = lat_index(t)
                    S.dma("sp", lambda e, rt=rt, lt=lt: e.dma_start(out=rt[:], in_=rope[lt * 128:(lt + 1) * 128, :]), rkey, writes=[t_rt])
                    apply_rope(kn, t_kn, 2, rt, t_rt, kb, t_kb)
                tb, t_tb = Tb[0]
                tbv = tb[:].bitcast(BF16).rearrange("p (k c) -> p k c", c=128)

                def trk(e, kb=kb, tbv=tbv):
                    e.transpose(out=tbv[:, 0, :], in_=kb[:, 0:128], identity=identb[:])
                    return e.transpose(out=tbv[:, 1, :], in_=kb[:, 128:256], identity=identb[:])
                S.op("pe", trk, reads=[t_kb, t_identb], writes=[t_tb])
                S.op("act", lambda e, tbv=tbv, t=t: e.copy(out=KT[:, :, t * 128:(t + 1) * 128], in_=tbv[:, 0:2, :]),
                     reads=[t_tb], writes=[t_K[t]])
                S.op("act", lambda e, ab=ab, t=t: e.copy(out=VA[:, t, :, 0:128], in_=ab[:, 256:512].rearrange("p (h d) -> p h d", d=128)),
                     reads=[t_ab, t_Vones], writes=[t_V[t]])

            if li == 0:
                t0_ = p1_tiles[0]
                dbg("KT0", KT[:, :, t0_ * 128:(t0_ + 1) * 128], t_K[t0_])
                dbg("VA0", VA[:, t0_, :, :], t_V[t0_])
                dbg("negC", negC[:], t_negC)
                dbg("gate_b", gate_b[:], t_gate)
                dbg("wsguT", wsguT[:], t_wsguT)
            lat_tiles = [t for t in p2_tiles if not is_ctx(t)]
            ctx_tiles = [t for t in p2_tiles if is_ctx(t)]
            groups = [lat_tiles[i:i + TG] for i in range(0, len(lat_tiles), TG)]
            if ctx_tiles:
                groups.append(ctx_tiles)
            t_xs_next = [Tok() for _ in range(NT_ALL)]
            for grp in groups:
                ng = len(grp)
                rflag = 1 if is_ctx(grp[0]) else 0
                if rflag:
                    build_gate(1)
                ktiles = [t for t in p1_tiles if is_ctx(t)] if rflag else list(p1_tiles)
                for i, t in enumerate(grp):
                    make_hT(src_ap, t_srcx, t, li, hT[i], t_hT[i])
                    if not rflag:
                        lt = lat_index(t)
                        S.dma("sp", lambda e, i=i, lt=lt: e.dma_start(out=ropeT[i][:], in_=rope[lt * 128:(lt + 1) * 128, :]),
                              f"ropeT{i}", writes=[t_rope[i]])
                order = [(2, "v", 0), (0, "u", 0), (4, "za", 0), (3, "v", 1), (1, "u", 1), (5, "za", 1),
                         (6, "q", 0), (7, "q", 1), (9, "zb", 0), (10, "zb", 1)]
                for (cb, kind, hh) in order:
                    wb, t_wb, wkey = wring.next()
                    S.dma("sp", lambda e, wb=wb, cb=cb: e.dma_start(out=wb[:], in_=wbi[li][cb]), wkey, reads=[t_wci[li]], writes=[t_wb])
                    for i, t in enumerate(grp):
                        ab, t_ab = proj(hT[i], t_hT[i], wb, t_wb)
                        if kind == "v":
                            gv, t_gv = tring.next()
                            S.op("act", lambda e, gv=gv, ab=ab: e.activation(out=gv[:], in_=ab[:], func=AF.Gelu_apprx_tanh),
                                 reads=[t_ab], writes=[t_gv])
                            rv, t_rv = head_rstd(gv[:], t_gv, 4)
                            v1, t_v1 = tring.next()
                            S.op("dve", lambda e, v1=v1, gv=gv, rv=rv: e.tensor_tensor(
                                out=v1[:].rearrange("p (g d) -> p g d", d=128), in0=gv[:].rearrange("p (g d) -> p g d", d=128),
                                in1=rv.unsqueeze(2).broadcast_to([128, 4, 128]), op=ALU.mult), reads=[t_gv, t_rv], writes=[t_v1])
                            vb, t_vb = vnring.next()
                            S.op("dve", lambda e, vb=vb, v1=v1, hh=hh: e.tensor_tensor(
                                out=vb[:], in0=v1[:], in1=vnw_b[:, hh * 512:(hh + 1) * 512], op=ALU.mult),
                                reads=[t_v1, t_vnw], writes=[t_vb])

                            def sgu(e, vb=vb, hh=hh):
                                rr = None
                                for g in range(4):
                                    rr = e.matmul(Sb[:, g * 128:(g + 1) * 128], lhsT=wsguT[:, 4 * hh + g, :],
                                                  rhs=vb[:, g * 128:(g + 1) * 128], start=True, stop=True)
                                return rr
                            S.op("pe", sgu, reads=[t_vb, t_wsguT], writes=[t_Sb])
                            S.op("dve", lambda e, i=i, hh=hh: e.tensor_tensor(
                                out=s_sb[i][:].rearrange("p (g d) -> p g d", d=128), in0=Sb[:].rearrange("p (g d) -> p g d", d=128),
                                in1=bsguT[:, 4 * hh:4 * hh + 4].unsqueeze(2).broadcast_to([128, 4, 128]), op=ALU.add),
                                reads=[t_Sb, t_bsguT], writes=[t_s[i]])
                        elif kind == "u":
                            gu, t_gu = tring.next()
                            S.op("act", lambda e, gu=gu, ab=ab: e.activation(out=gu[:], in_=ab[:], func=AF.Gelu_apprx_tanh),
                                 reads=[t_ab], writes=[t_gu])
                            S.op("dve", lambda e, gu=gu, i=i: e.tensor_tensor(out=s_sb[i][:], in0=gu[:], in1=s_sb[i][:], op=ALU.mult),
                                 reads=[t_gu], writes=[t_s[i]])
                        elif kind == "za":
                            sz, t_sz = tring.next()
                            S.op("act", lambda e, sz=sz, ab=ab: e.activation(out=sz[:], in_=ab[:], func=AF.Silu), reads=[t_ab], writes=[t_sz])
                            S.op("dve", lambda e, sz=sz, i=i, hh=hh: e.tensor_tensor(
                                out=gated[i][:, hh * 512:(hh + 1) * 512], in0=sz[:], in1=s_sb[i][:], op=ALU.mult),
                                reads=[t_sz, t_s[i]], writes=[t_gated[i]])
                        elif kind == "q":
                            rq, t_rq = head_rstd(ab[:], t_ab, 4)
                            q1, t_q1 = tring.next()
                            S.op("dve", lambda e, q1=q1, ab=ab, rq=rq: e.tensor_tensor(
                                out=q1[:].rearrange("p (g d) -> p g d", d=128), in0=ab[:].rearrange("p (g d) -> p g d", d=128),
                                in1=rq.unsqueeze(2).broadcast_to([128, 4, 128]), op=ALU.mult), reads=[t_ab, t_rq], writes=[t_q1])
                            S.op("dve", lambda e, q1=q1: e.tensor_tensor(
                                out=q1[:].rearrange("p (g d) -> p g d", d=128), in0=q1[:].rearrange("p (g d) -> p g d", d=128),
                                in1=qnw_b[:].unsqueeze(1).broadcast_to([128, 4, 128]), op=ALU.mult), reads=[t_qnw], writes=[t_q1])
                            qb, t_qb = qbring.next()
                            if rflag:
                                S.op("dve", lambda e, qb=qb, q1=q1: e.tensor_copy(out=qb[:], in_=q1[:]), reads=[t_q1], writes=[t_qb])
                            else:
                                apply_rope(q1, t_q1, 4, ropeT[i], t_rope[i], qb, t_qb)
                            tb, t_tb = Tb[(i + hh) % 2]
                            tbv = tb[:].bitcast(BF16).rearrange("p (k c) -> p k c", c=128)

                            def trq(e, qb=qb, tbv=tbv):
                                rr = None
                                for h in range(4):
                                    rr = e.transpose(out=tbv[:, h, :], in_=qb[:, h * 128:(h + 1) * 128], identity=identb[:])
                                return rr
                            S.op("pe", trq, reads=[t_qb, t_identb], writes=[t_tb])
                            S.op("act", lambda e, tbv=tbv, i=i, hh=hh: e.copy(out=qT[i][:, 4 * hh:4 * hh + 4, :], in_=tbv[:, 0:4, :]),
                                 reads=[t_tb], writes=[t_qT[i]])
                        else:
                            S.op("act", lambda e, ab=ab, i=i, hh=hh: e.activation(out=szb[i][:, hh * 512:(hh + 1) * 512], in_=ab[:], func=AF.Silu),
                                 reads=[t_ab], writes=[t_szb[i]])

                if li == 0 and grp is groups[0]:
                    dbg("gatedA", gated[0][:, 0:1024], t_gated[0])
                    dbg("qT", qT[0][:], t_qT[0])
                    dbg("szb", szb[0][:], t_szb[0])
                inv_sqrt = float(128.0 ** -0.5)
                for i, t in enumerate(grp):
                    for g in range(2):
                        (o0, t_o0), (o1, t_o1) = Ob
                        for ki, kt in enumerate(ktiles):
                            sbk, t_sbk = aring.next()
                            S.op("pe", lambda e, sbk=sbk, g=g, kt=kt, i=i: e.matmul(
                                sbk[:], lhsT=KT[:, g, kt * 128:(kt + 1) * 128], rhs=qT[i][:, 4 * g:4 * g + 4, :], start=True, stop=True),
                                reads=[t_K[kt], t_qT[i]], writes=[t_sbk])
                            pt, t_pt = ptring.next()
                            S.op("act", lambda e, pt=pt, sbk=sbk: e.activation(out=pt[:], in_=sbk[:], func=AF.Exp, bias=negC[:, 0:1], scale=inv_sqrt),
                                 reads=[t_sbk, t_negC], writes=[t_pt])

                            def pv(e, pt=pt, kt=kt, g=g, ki=ki, o0=o0, o1=o1, nk=len(ktiles)):
                                rr = None
                                for hq in range(4):
                                    if hq < 3:
                                        oap = o0[:, hq * 129:hq * 129 + 129]
                                        st = (ki == 0 and hq == 0)
                                    else:
                                        oap = o1[:, 0:129]
                                        st = (ki == 0)
                                    rr = e.matmul(oap, lhsT=pt[:, hq * 128:(hq + 1) * 128], rhs=VA[:, kt, g, 0:129],
                                                  start=st, stop=(ki == nk - 1), skip_group_check=True)
                                return rr
                            S.op("pe", pv, reads=[t_pt, t_V[kt]], writes=[t_o0, t_o1])
                        rd, t_rd = smring.next()

                        def rden(e, rd=rd, o0=o0, o1=o1):
                            e.reciprocal(out=rd[:, 0:3], in_=o0[:, 0:387].rearrange("p (h c) -> p h c", c=129)[:, :, 128])
                            return e.reciprocal(out=rd[:, 3:4], in_=o1[:, 128:129])
                        S.op("dve", rden, reads=[t_o0, t_o1], writes=[t_rd])

                        def onorm(e, rd=rd, o0=o0, o1=o1, i=i, g=g):
                            rr = None
                            for hq in range(4):
                                h = 4 * g + hq
                                oap = o0[:, hq * 129:hq * 129 + 128] if hq < 3 else o1[:, 0:128]
                                rr = e.scalar_tensor_tensor(out=gated[i][:, 1024 + h * 128:1024 + (h + 1) * 128], in0=oap,
                                                            scalar=rd[:, hq:hq + 1], in1=szb[i][:, h * 128:(h + 1) * 128],
                                                            op0=ALU.mult, op1=ALU.mult)
                            return rr
                        S.op("dve", onorm, reads=[t_o0, t_o1, t_rd, t_szb[i]], writes=[t_gated[i]])

                if li == 0 and grp is groups[0]:
                    dbg("gated", gated[0][:], t_gated[0])
                for i, t in enumerate(grp):
                    for half in range(2):
                        tb, t_tb = Tb[half]
                        tbv = tb[:].bitcast(BF16).rearrange("p (k c) -> p k c", c=128)

                        def trg(e, half=half, tbv=tbv, i=i):
                            rr = None
                            for k in range(8):
                                kc = half * 8 + k
                                rr = e.transpose(out=tbv[:, k, :], in_=gated[i][:, kc * 128:(kc + 1) * 128], identity=identb[:])
                            return rr
                        S.op("pe", trg, reads=[t_gated[i], t_identb], writes=[t_tb])
                        S.op("act", lambda e, half=half, tbv=tbv, i=i: e.copy(out=hT[i][:, half * 8:(half + 1) * 8, :], in_=tbv[:]),
                             reads=[t_tb], writes=[t_hT[i]])

                for cb in range(4):
                    wb, t_wb, wkey = wring.next()
                    S.dma("sp", lambda e, wb=wb, cb=cb: e.dma_start(out=wb[:], in_=wbo[li][cb]), wkey, reads=[t_wco[li]], writes=[t_wb])
                    for i, t in enumerate(grp):
                        xpb, t_xp, xkey = xpring.next()
                        S.dma("sp", lambda e, xpb=xpb, t=t, cb=cb: e.dma_start(
                            out=xpb[:], in_=src_ap[t * 128:(t + 1) * 128, cb * 512:(cb + 1) * 512]), xkey, reads=[t_srcx[t]], writes=[t_xp])
                        ab, t_ab = proj(hT[i], t_hT[i], wb, t_wb)
                        yg, t_yg = tring.next()
                        S.op("dve", lambda e, yg=yg, ab=ab, cb=cb: e.tensor_tensor(
                            out=yg[:], in0=ab[:], in1=gate_b[:, cb * 512:(cb + 1) * 512], op=ALU.mult),
                            reads=[t_ab, t_gate], writes=[t_yg])
                        S.op("pool", lambda e, yg=yg, xpb=xpb: e.tensor_tensor(out=xpb[:], in0=xpb[:], in1=yg[:], op=ALU.add),
                             reads=[t_yg], writes=[t_xp])
                        if last_in_prog:
                            if final:
                                drow = t
                            else:
                                drow = t
                        else:
                            drow = t
                        S.dma("pool", lambda e, xpb=xpb, drow=drow, cb=cb: e.dma_start(
                            out=dst_ap[drow * 128:(drow + 1) * 128, cb * 512:(cb + 1) * 512], in_=xpb[:]), xkey,
                            reads=[t_xp], writes=[t_xs_next[t]])
                        if last_in_prog:
                            out_toks.append(t_xp)
            return t_xs_next

        t_xs_all = [Tok() for _ in range(NT_ALL)]
        for li_, l_ in enumerate(layers):
            last_ = (li_ == NL - 1)
            t_xs_all = run_layer(li_, l_, xa if li_ == 0 else xs, out if last_ else xs, t_xs_all,
                                 p2_tiles_per_layer[li_], last_)

        S.final_wait("pool", list({id(t): t for t in out_toks}.values()) + dbg_toks)
        S.emit(block)
    return nc


def _rope_tables(pos):
    rows = (pos // GRID_W).astype(np.float32)
    cols = (pos % GRID_W).astype(np.float32)
    inv_freq = (np.float32(10000.0) ** (-np.arange(0, 64, 2, dtype=np.float32) / np.float32(64))).astype(np.float32)
    ang_r = rows[:, None] * inv_freq[None, :]
    ang_c = cols[:, None] * inv_freq[None, :]
    ang = np.concatenate([ang_r, ang_r, ang_c, ang_c], axis=-1).astype(np.float32)
    return np.concatenate([np.cos(ang), np.sin(ang)], axis=-1).astype(np.float32)


_PROG_CACHE = {}


def _get_prog(key, *args):
    if key not in _PROG_CACHE:
        _PROG_CACHE[key] = build(*args)
    return _PROG_CACHE[key]


def _common_inputs(c, c_ctx, norm_w, w_mod, b_mod, w_in, w_sgu, b_sgu, v_norm_w, q_norm_w, k_norm_w, w_out):
    f = lambda a: np.ascontiguousarray(np.asarray(a, dtype=np.float32))
    shared = {
        "identb": np.eye(128, dtype=np.float32).astype(ml_dtypes.bfloat16),
        "identf": np.eye(128, dtype=np.float32),
        "w_mod": f(w_mod), "b_mod": f(b_mod).reshape(2, 48, 128), "norm_w": f(norm_w).reshape(2, 16, 128),
        "w_in": f(w_in), "w_out": f(w_out), "w_sgu": f(w_sgu), "b_sgu": f(b_sgu),
        "v_norm_w": f(v_norm_w).reshape(2, 1024), "q_norm_w": f(q_norm_w), "k_norm_w": f(k_norm_w),
    }
    return shared


def kernel(x, c, ctx, c_ctx, norm_w, w_mod, b_mod, w_in, w_sgu, b_sgu, v_norm_w, q_norm_w, k_norm_w, w_out):
    x = np.asarray(x, dtype=np.float32)
    ctx = np.asarray(ctx, dtype=np.float32)
    c = np.asarray(c, dtype=np.float32)
    c_ctx = np.asarray(c_ctx, dtype=np.float32)
    shared = _common_inputs(c, c_ctx, norm_w, w_mod, b_mod, w_in, w_sgu, b_sgu, v_norm_w, q_norm_w, k_norm_w, w_out)
    H = SEQ // 2
    CH = CTX // 2

    def core_maps(xfull, cfull):
        maps = []
        for core in range(8):
            b, hf = divmod(core, 2)
            o, p = hf, 1 - hf
            xa = np.concatenate([xfull[b, o * H:(o + 1) * H], cfull[b, o * CH:(o + 1) * CH],
                                 xfull[b, p * H:(p + 1) * H], cfull[b, p * CH:(p + 1) * CH]], axis=0)
            pos = np.concatenate([np.arange(o * H, (o + 1) * H), np.arange(p * H, (p + 1) * H)])
            m = dict(shared)
            m["xa"] = np.ascontiguousarray(xa)
            m["rope"] = _rope_tables(pos)
            m["c2"] = np.ascontiguousarray(np.stack([c[b], c_ctx], 0).reshape(32, 128))
            maps.append(m)
        return maps

    all_tiles = list(range(NT_ALL))
    own_lat = list(range(16))
    if MODE == "fused":
        nc = _get_prog("fused", [0, 1], True, all_tiles, [all_tiles, own_lat], 16)
        res = run_bass_kernel_spmd(nc, core_maps(x, ctx), core_ids=list(range(8)))
        outs = [r["out"] for r in res.results]
    else:
        ncA = _get_prog("L0", [0], False, all_tiles, [list(range(17))], 17)
        resA = run_bass_kernel_spmd(ncA, core_maps(x, ctx), core_ids=list(range(8)))
        x1 = np.empty_like(x)
        ctx1 = np.empty_like(ctx)
        for core in range(8):
            b, hf = divmod(core, 2)
            xn_ = resA.results[core]["xnext"]
            x1[b, hf * H:(hf + 1) * H] = xn_[0:H]
            ctx1[b, hf * CH:(hf + 1) * CH] = xn_[H:H + CH]
        ncB = _get_prog("L1", [1], True, all_tiles, [own_lat], 16)
        resB = run_bass_kernel_spmd(ncB, core_maps(x1, ctx1), core_ids=list(range(8)))
        outs = [r["out"] for r in resB.results]
    y = np.empty((4, SEQ, D), dtype=np.float32)
    for core in range(8):
        b, hf = divmod(core, 2)
        y[b, hf * H:(hf + 1) * H] = outs[core]
    return y
```

```python
import numpy as np
from contextlib import ExitStack
import ml_dtypes
import concourse.bass as bass
import concourse.mybir as mybir
from concourse.bass_utils import run_bass_kernel_spmd

F32 = mybir.dt.float32
BF16 = mybir.dt.bfloat16
AF = mybir.ActivationFunctionType
ALU = mybir.AluOpType
AX = mybir.AxisListType

D = 2048
NKC = 16
DIN = 5632
NCB = 11
SEQ = 4096
CTX = 256
GRID_W = 64
EPS = 1e-6
TG = 4
NT_ALL = 34
DEBUG = False
MODE = "fused"


class Tok:
    __slots__ = ("w", "r")

    def __init__(self):
        self.w = None
        self.r = {}


class Sched:
    EPOCH = 20000

    def __init__(self, nc, es):
        self.nc = nc
        self.es = es
        self.names = ["pe", "act", "dve", "pool", "sp"]
        self.q = {k: [] for k in self.names}
        self.cnt = {k: 0 for k in self.names}
        self.esems = {k: [] for k in self.names}
        self.dsems = {}
        self.dcnt = {}
        self.waited = {k: {} for k in self.names}

    def _newsem(self, name):
        return self.es.enter_context(self.nc.semaphore(name))

    def _eng_event(self, eng):
        c = self.cnt[eng]
        ep, v = divmod(c, self.EPOCH)
        while len(self.esems[eng]) <= ep:
            self.esems[eng].append(self._newsem(f"e_{eng}_{len(self.esems[eng])}"))
        self.cnt[eng] = c + 1
        return (self.esems[eng][ep], v + 1, eng)

    def _collect(self, eng, reads, writes):
        evs = []
        for t in reads:
            if t.w is not None:
                evs.append(t.w)
        for t in writes:
            if t.w is not None:
                evs.append(t.w)
            evs.extend(t.r.values())
        waits = {}
        for (sem, val, e) in evs:
            if e is not None and e == eng and eng == "pe":
                continue
            key = id(sem)
            if self.waited[eng].get(key, 0) >= val:
                continue
            if key not in waits or waits[key][1] < val:
                waits[key] = (sem, val)
        for key, (sem, val) in waits.items():
            self.waited[eng][key] = val
        return list(waits.values())

    def _mark(self, ev, reads, writes):
        k = id(ev[0])
        for t in reads:
            old = t.r.get(k)
            if old is None or old[1] < ev[1]:
                t.r[k] = ev
        for t in writes:
            t.w = ev
            t.r = {}

    def op(self, eng, fn, reads=(), writes=()):
        waits = self._collect(eng, reads, writes)
        ev = self._eng_event(eng)
        self.q[eng].append((waits, fn, ev[0], 1))
        self._mark(ev, reads, writes)

    def dma(self, eng, fn, key, reads=(), writes=(), n=1):
        waits = self._collect(eng, reads, writes)
        if key not in self.dsems:
            self.dsems[key] = self._newsem(f"d_{key}")
            self.dcnt[key] = 0
        self.dcnt[key] += 16 * n
        ev = (self.dsems[key], self.dcnt[key], None)
        self.q[eng].append((waits, fn, self.dsems[key], 16))
        self._mark(ev, reads, writes)

    def final_wait(self, eng, toks):
        waits = self._collect(eng, toks, toks)
        self.q[eng].append((waits, None, None, 0))

    def emit(self, block):
        def mk(name):
            def body(e):
                for (waits, fn, sem, inc) in self.q[name]:
                    for (s, v) in waits:
                        e.wait_ge(s, v)
                    if fn is None:
                        continue
                    r = fn(e)
                    if isinstance(r, (list, tuple)):
                        for ins in r:
                            ins.then_inc(sem, inc)
                    else:
                        r.then_inc(sem, inc)
            return body
        block.tensor(mk("pe"))
        block.scalar(mk("act"))
        block.vector(mk("dve"))
        block.gpsimd(mk("pool"))
        block.sync(mk("sp"))


class Ring:
    def __init__(self, items):
        self.items = items
        self.i = 0

    def next(self):
        it = self.items[self.i % len(self.items)]
        self.i += 1
        return it


def is_ctx(t):
    return t % 17 == 16


def lat_index(t):
    return (t // 17) * 16 + (t % 17)


def build(layers, final, p1_tiles, p2_tiles_per_layer, n_out_tiles):
    nc = bass.Bass("TRN2", target_bir_lowering=False)
    NL = len(layers)

    def din(name, shape, dt=F32):
        return nc.dram_tensor(name, shape, dt, kind="ExternalInput").ap()

    xa = din("xa", [NT_ALL * 128, D])
    rope = din("rope", [32 * 128, 256])
    c2 = din("c2", [32, 128])
    identb_in = din("identb", [128, 128], BF16)
    identf_in = din("identf", [128, 128])
    w_mod = din("w_mod", [2, D, 3 * D])
    b_mod = din("b_mod", [2, 48, 128])
    norm_w = din("norm_w", [2, 16, 128])
    w_in = din("w_in", [2, D, DIN])
    w_out = din("w_out", [2, D, D])
    w_sgu = din("w_sgu", [2, 8, 128, 128])
    b_sgu = din("b_sgu", [2, 8, 128])
    v_norm_w = din("v_norm_w", [2, 1024])
    q_norm_w = din("q_norm_w", [2, 128])
    k_norm_w = din("k_norm_w", [2, 128])
    if final:
        out = nc.dram_tensor("out", [16 * 128, D], F32, kind="ExternalOutput").ap()
    else:
        out = nc.dram_tensor("xnext", [n_out_tiles * 128, D], F32, kind="ExternalOutput").ap()
    xs = nc.dram_tensor("xs", [NT_ALL * 128, D], F32).ap() if NL > 1 else None
    wbi = [nc.dram_tensor(f"wbi{l}", [NCB, 128, NKC * 512], BF16).ap() for l in layers]
    wbo = [nc.dram_tensor(f"wbo{l}", [4, 128, NKC * 512], BF16).ap() for l in layers]

    with ExitStack() as es:
        def sb(name, shape, dt):
            return es.enter_context(nc.sbuf_tensor(name, shape, dt))

        S = Sched(nc, es)
        identb = sb("identb_sb", [128, 128], BF16); t_identb = Tok()
        identf = sb("identf_sb", [128, 128], F32); t_identf = Tok()
        onesf = sb("onesf", [128, 128], F32); t_onesf = Tok()
        KT = sb("KT", [128, 2, NT_ALL * 128], BF16)
        VA = sb("VA", [128, NT_ALL, 2, 130], BF16)
        t_K = [Tok() for _ in range(NT_ALL)]
        t_V = [Tok() for _ in range(NT_ALL)]
        t_Vones = Tok()
        wbuf = [sb(f"wbuf{i}", [128, NKC * 512], BF16) for i in range(2)]
        t_wbuf = [Tok() for _ in range(2)]
        wring = Ring(list(zip(wbuf, t_wbuf, ["wbuf0", "wbuf1"])))
        xbuf = [sb(f"xbuf{i}", [128, D], F32) for i in range(2)]
        xring = Ring([(xbuf[i], Tok(), f"xbuf{i}") for i in range(2)])
        xn = [sb(f"xn{i}", [128, D], BF16) for i in range(2)]
        xnring = Ring([(xn[i], Tok()) for i in range(2)])
        hT = [sb(f"hT{i}", [128, NKC, 128], BF16) for i in range(TG)]
        t_hT = [Tok() for _ in range(TG)]
        h1ring = Ring([(hT[i], t_hT[i]) for i in range(2)])
        s_sb = [sb(f"s_sb{i}", [128, 512], F32) for i in range(TG)]
        t_s = [Tok() for _ in range(TG)]
        gated = [sb(f"gated{i}", [128, D], BF16) for i in range(TG)]
        t_gated = [Tok() for _ in range(TG)]
        qT = [sb(f"qT{i}", [128, 8, 128], BF16) for i in range(TG)]
        t_qT = [Tok() for _ in range(TG)]
        szb = [sb(f"szb{i}", [128, 1024], BF16) for i in range(TG)]
        t_szb = [Tok() for _ in range(TG)]
        ropeT = [sb(f"ropeT{i}", [128, 256], F32) for i in range(TG)]
        t_rope = [Tok() for _ in range(TG)]
        r1ring = Ring([(ropeT[i], t_rope[i], f"ropeT{i}") for i in range(2)])
        tmps = [sb(f"tmp{i}", [128, 512], F32) for i in range(6)]
        tring = Ring([(tmps[i], Tok()) for i in range(6)])
        vnb = [sb(f"vnb{i}", [128, 512], BF16) for i in range(2)]
        vnring = Ring([(vnb[i], Tok()) for i in range(2)])
        qbf = [sb(f"qbf{i}", [128, 512], BF16) for i in range(2)]
        qbring = Ring([(qbf[i], Tok()) for i in range(2)])
        PT = [sb(f"PT{i}", [128, 512], BF16) for i in range(4)]
        ptring = Ring([(PT[i], Tok()) for i in range(4)])
        xp = [sb(f"xp{i}", [128, 512], F32) for i in range(4)]
        xpring = Ring([(xp[i], Tok(), f"xp{i}") for i in range(4)])
        gate_b = sb("gate_b", [128, D], F32)
        t_gate = Tok()
        small = sb("small", [128, 64], F32)
        smring = Ring([(small[:, 8 * i:8 * i + 8], Tok()) for i in range(8)])
        cT = sb("cT", [128, 32], F32); t_cT = Tok()
        modT = [sb(f"modT{i}", [128, 48, 2], F32) for i in range(NL)]
        t_mod = [Tok() for _ in range(NL)]
        gT = [sb(f"gT{i}", [128, NKC, 2], F32) for i in range(NL)]
        t_gT = [Tok() for _ in range(NL)]
        nwT = sb("nwT", [128, 16], F32); t_nwT = Tok()
        bmT = sb("bmT", [128, 48], F32); t_bmT = Tok()
        rows = sb("rows", [48, 128], F32); t_rows = Tok()
        wsg_f = sb("wsg_f", [128, 8, 128], F32); t_wsgf = Tok()
        wsg_b = sb("wsg_b", [128, 8, 128], BF16); t_wsgb = Tok()
        wsguT = sb("wsguT", [128, 8, 128], BF16); t_wsguT = Tok()
        bsguT = sb("bsguT", [128, 8], F32); t_bsguT = Tok()
        vnw_b = sb("vnw_b", [128, 1024], F32); t_vnw = Tok()
        qnw_b = sb("qnw_b", [128, 128], F32); t_qnw = Tok()
        knw_b = sb("knw_b", [128, 128], F32); t_knw = Tok()
        negC = sb("negC", [128, 2], F32); t_negC = Tok()
        diag = [sb(f"diag{i}", [128, 128], F32) for i in range(2)]
        dring = Ring([(diag[i], Tok()) for i in range(2)])
        bank = [es.enter_context(nc.psum_tensor(f"bank{i}", [128, 512], F32)) for i in range(8)]
        t_bank = [Tok() for _ in range(8)]
        aring = Ring([(bank[i], t_bank[i]) for i in range(3)])
        Sb, t_Sb = bank[3], t_bank[3]
        Tb = [(bank[4], t_bank[4]), (bank[5], t_bank[5])]
        Ob = [(bank[6], t_bank[6]), (bank[7], t_bank[7])]

        block = es.enter_context(nc.Block())
        dbg_toks = []

        def dbg(name, ap, tok):
            if not DEBUG:
                return
            d = nc.dram_tensor("dbg_" + name, list(ap.shape), ap.dtype, kind="ExternalOutput").ap()
            tk = Tok()
            S.dma("sp", lambda e: e.dma_start(out=d, in_=ap), "dbg_" + name, reads=[tok], writes=[tk])
            dbg_toks.append(tk)

        S.dma("sp", lambda e: e.dma_start(out=identb[:], in_=identb_in), "c_identb", writes=[t_identb])
        S.dma("sp", lambda e: e.dma_start(out=identf[:], in_=identf_in), "c_identf", writes=[t_identf])
        S.op("dve", lambda e: e.memset(onesf[:], 1.0), writes=[t_onesf])
        S.op("dve", lambda e: e.memset(VA[:, :, :, 128:130], 1.0), writes=[t_Vones])

        t_wci = [Tok() for _ in range(NL)]
        t_wco = [Tok() for _ in range(NL)]
        for li, l in enumerate(layers):
            def cast_in(e, l=l, li=li):
                res = []
                for kc in range(NKC):
                    for c0 in (0, 6):
                        ncb = 6 if c0 == 0 else 5
                        dst = wbi[li][c0:c0 + ncb, :, kc * 512:(kc + 1) * 512]
                        src = w_in[l, kc * 128:(kc + 1) * 128, c0 * 512:(c0 + ncb) * 512].rearrange("p (cb c) -> cb p c", c=512)
                        res.append(e.dma_start(out=dst, in_=src))
                return res
            S.dma("pool", cast_in, f"cast_in{li}", writes=[t_wci[li]], n=2 * NKC)

            def cast_out(e, l=l, li=li):
                res = []
                for kc in range(NKC):
                    dst = wbo[li][:, :, kc * 512:(kc + 1) * 512]
                    src = w_out[l, kc * 128:(kc + 1) * 128, :].rearrange("p (cb c) -> cb p c", c=512)
                    res.append(e.dma_start(out=dst, in_=src))
                return res
            S.dma("pool", cast_out, f"cast_out{li}", writes=[t_wco[li]], n=NKC)

        def small_T(src_ap, n, dst, t_dst, extra_reads=()):
            S.dma("sp", lambda e: e.dma_start(out=rows[0:n, :], in_=src_ap), "rows", writes=[t_rows])
            S.op("pe", lambda e: e.transpose(out=Sb[:, 0:n], in_=rows[0:n, :], identity=identf[0:n, 0:n]),
                 reads=[t_rows, t_identf], writes=[t_Sb])
            S.op("dve", lambda e: e.tensor_copy(out=dst, in_=Sb[:, 0:n]), reads=[t_Sb], writes=[t_dst])

        S.dma("sp", lambda e: e.dma_start(out=rows[0:32, :], in_=c2), "rows", writes=[t_rows])
        S.op("act", lambda e: e.activation(out=rows[0:32, :], in_=rows[0:32, :], func=AF.Silu), reads=[t_rows], writes=[t_rows])
        S.op("pe", lambda e: e.transpose(out=Sb[:, 0:32], in_=rows[0:32, :], identity=identf[0:32, 0:32]),
             reads=[t_rows, t_identf], writes=[t_Sb])
        S.op("dve", lambda e: e.tensor_copy(out=cT[:], in_=Sb[:, 0:32]), reads=[t_Sb], writes=[t_cT])
        cTv = cT[:].rearrange("p (r k) -> p r k", r=2)

        for li, l in enumerate(layers):
            small_T(b_mod[l], 48, bmT[:], t_bmT)
            small_T(norm_w[l], 16, nwT[:], t_nwT)
            for jb in range(24):
                wb, t_wb, wkey = wring.next()
                wv = wb[:].bitcast(F32).rearrange("p (k c) -> p k c", c=256)
                S.dma("sp", lambda e, wv=wv, l=l, jb=jb: e.dma_start(
                    out=wv, in_=w_mod[l, :, jb * 256:(jb + 1) * 256].rearrange("(k p) c -> p k c", p=128)),
                    wkey, writes=[t_wb])

                def mm_mod(e, wv=wv, jb=jb):
                    r = None
                    for jj in range(2):
                        j = jb * 2 + jj
                        for kc in range(NKC):
                            r = e.matmul(Sb[:, 2 * j:2 * j + 2], lhsT=wv[:, kc, jj * 128:(jj + 1) * 128], rhs=cTv[:, :, kc],
                                         start=(kc == 0), stop=(kc == NKC - 1))
                    return r
                S.op("pe", mm_mod, reads=[t_wb, t_cT], writes=[t_Sb])
            modv = modT[li]
            S.op("dve", lambda e, modv=modv: e.tensor_tensor(
                out=modv[:], in0=Sb[:, 0:96].rearrange("p (j r) -> p j r", r=2),
                in1=bmT[:].unsqueeze(2).broadcast_to([128, 48, 2]), op=ALU.add),
                reads=[t_Sb, t_bmT], writes=[t_mod[li]])
            S.op("dve", lambda e, modv=modv, li=li: e.tensor_scalar(
                out=gT[li][:], in0=modv[:, 16:32, :], scalar1=1.0, scalar2=None, op0=ALU.add),
                reads=[t_mod[li]], writes=[t_gT[li]])
            S.op("dve", lambda e, li=li: e.tensor_tensor(
                out=gT[li][:], in0=gT[li][:], in1=nwT[:].unsqueeze(2).broadcast_to([128, 16, 2]), op=ALU.mult),
                reads=[t_nwT], writes=[t_gT[li]])

        for li in range(NL):
            dbg(f"modT{li}", modT[li][:], t_mod[li])
            dbg(f"gT{li}", gT[li][:], t_gT[li])
        dbg("cT", cT[:], t_cT)

        def rstd_small(ss_ap, t_ss, n, inv_n):
            sd, t_sd = smring.next()
            S.op("act", lambda e: e.activation(out=sd[:, 0:n], in_=ss_ap, func=AF.Sqrt, bias=EPS, scale=inv_n),
                 reads=[t_ss], writes=[t_sd])
            rs, t_rs = smring.next()
            S.op("dve", lambda e: e.reciprocal(out=rs[:, 0:n], in_=sd[:, 0:n]), reads=[t_sd], writes=[t_rs])
            return rs[:, 0:n], t_rs

        def make_hT(src_ap, t_src, t, li, dst, t_dst):
            r = 1 if is_ctx(t) else 0
            xb, t_xb, xkey = xring.next()
            S.dma("sp", lambda e: e.dma_start(out=xb[:], in_=src_ap[t * 128:(t + 1) * 128, :]), xkey, reads=[t_src[t]], writes=[t_xb])
            xnb, t_xn = xnring.next()
            ss, t_ss = smring.next()
            S.op("act", lambda e: e.activation(out=xnb[:], in_=xb[:], func=AF.Square, accum_out=ss[:, 0:1]),
                 reads=[t_xb], writes=[t_xn, t_ss])
            rs, t_rs = rstd_small(ss[:, 0:1], t_ss, 1, 1.0 / D)
            S.op("pool", lambda e: e.tensor_scalar(out=xnb[:], in0=xb[:], scalar1=rs[:, 0:1], scalar2=None, op0=ALU.mult),
                 reads=[t_xb, t_rs], writes=[t_xn])
            for half in range(2):
                tb, t_tb = Tb[half]
                tbv = tb[:].bitcast(BF16).rearrange("p (k c) -> p k c", c=128)

                def tr(e, half=half, tbv=tbv):
                    rr = None
                    for k in range(8):
                        kc = half * 8 + k
                        rr = e.transpose(out=tbv[:, k, :], in_=xnb[:, kc * 128:(kc + 1) * 128], identity=identb[:])
                    return rr
                S.op("pe", tr, reads=[t_xn, t_identb], writes=[t_tb])

                def ev(e, half=half, tbv=tbv):
                    rr = None
                    for k in range(8):
                        kc = half * 8 + k
                        rr = e.tensor_scalar(out=dst[:, kc, :], in0=tbv[:, k, :], scalar1=gT[li][:, kc, r:r + 1],
                                             scalar2=modT[li][:, kc, r:r + 1], op0=ALU.mult, op1=ALU.add)
                    return rr
                def ev_act(e, half=half, tbv=tbv):
                    rr = None
                    for k in range(8):
                        kc = half * 8 + k
                        rr = e.activation(out=dst[:, kc, :], in_=tbv[:, k, :], func=AF.Identity,
                                          scale=gT[li][:, kc, r:r + 1], bias=modT[li][:, kc, r:r + 1])
                    return rr
                if half == 0:
                    S.op("dve", ev, reads=[t_tb, t_gT[li], t_mod[li]], writes=[t_dst])
                else:
                    S.op("act", ev_act, reads=[t_tb, t_gT[li], t_mod[li]], writes=[t_dst])

        def proj(lhs, t_lhs, wb, t_wb):
            ab, t_ab = aring.next()
            wv = wb[:].rearrange("p (k c) -> p k c", c=512)

            def mm(e):
                rr = None
                for kc in range(NKC):
                    rr = e.matmul(ab[:], lhsT=lhs[:, kc, :], rhs=wv[:, kc, :], start=(kc == 0), stop=(kc == NKC - 1))
                return rr
            S.op("pe", mm, reads=[t_lhs, t_wb], writes=[t_ab])
            return ab, t_ab

        def head_rstd(src_ap, t_src, nh):
            sq, t_sq = tring.next()
            S.op("act", lambda e: e.activation(out=sq[:, 0:nh * 128], in_=src_ap, func=AF.Square), reads=[t_src], writes=[t_sq])
            ss, t_ss = smring.next()
            S.op("dve", lambda e: e.tensor_reduce(out=ss[:, 0:nh], in_=sq[:, 0:nh * 128].rearrange("p (h d) -> p h d", d=128),
                                                  axis=AX.X, op=ALU.add), reads=[t_sq], writes=[t_ss])
            return rstd_small(ss[:, 0:nh], t_ss, nh, 1.0 / 128)

        def apply_rope(src, t_src, nh, rt, t_rt, dst, t_dst):
            n = nh * 128
            t1, t_t1 = tring.next()
            cosb = rt[:, 0:128].unsqueeze(1).broadcast_to([128, nh, 128])
            S.op("dve", lambda e: e.tensor_tensor(out=t1[:, 0:n].rearrange("p (h d) -> p h d", d=128),
                                                  in0=src[:, 0:n].rearrange("p (h d) -> p h d", d=128), in1=cosb, op=ALU.mult),
                 reads=[t_src, t_rt], writes=[t_t1])
            rot, t_rot = tring.next()
            sv = src[:, 0:n].rearrange("p (h b t d) -> p h b t d", b=2, t=2, d=32)
            rv = rot[:, 0:n].rearrange("p (h b t d) -> p h b t d", b=2, t=2, d=32)
            t1v = t1[:, 0:n].rearrange("p (h b t d) -> p h b t d", b=2, t=2, d=32)
            dv = dst[:, 0:n].rearrange("p (h b t d) -> p h b t d", b=2, t=2, d=32)
            sinv = rt[:, 128:256].rearrange("p (b t d) -> p b t d", b=2, t=2)

            def rotf(e):
                e.tensor_tensor(out=rv[:, :, :, 0, :], in0=sv[:, :, :, 1, :],
                                in1=sinv[:, :, 0, :].unsqueeze(1).broadcast_to([128, nh, 2, 32]), op=ALU.mult)
                return e.tensor_tensor(out=rv[:, :, :, 1, :], in0=sv[:, :, :, 0, :],
                                       in1=sinv[:, :, 1, :].unsqueeze(1).broadcast_to([128, nh, 2, 32]), op=ALU.mult)
            S.op("dve", rotf, reads=[t_src, t_rt], writes=[t_rot])

            def fin(e):
                e.tensor_tensor(out=dv[:, :, :, 0, :], in0=t1v[:, :, :, 0, :], in1=rv[:, :, :, 0, :], op=ALU.subtract)
                return e.tensor_tensor(out=dv[:, :, :, 1, :], in0=t1v[:, :, :, 1, :], in1=rv[:, :, :, 1, :], op=ALU.add)
            S.op("dve", fin, reads=[t_t1, t_rot], writes=[t_dst])

        out_toks = []

        def run_layer(li, l, src_ap, dst_ap, t_srcx, p2_tiles, last_in_prog):

            S.dma("sp", lambda e, l=l: e.dma_start(out=wsg_f[:], in_=w_sgu[l].rearrange("g p q -> p g q")), "wsgf", writes=[t_wsgf])
            S.op("dve", lambda e: e.tensor_copy(out=wsg_b[:], in_=wsg_f[:]), reads=[t_wsgf], writes=[t_wsgb])
            tb, t_tb = Tb[0]
            tbv0 = tb[:].bitcast(BF16).rearrange("p (k c) -> p k c", c=128)

            def trw(e, tbv0=tbv0):
                rr = None
                for g in range(8):
                    rr = e.transpose(out=tbv0[:, g, :], in_=wsg_b[:, g, :], identity=identb[:])
                return rr
            S.op("pe", trw, reads=[t_wsgb, t_identb], writes=[t_tb])
            S.op("dve", lambda e, tbv0=tbv0: e.tensor_copy(out=wsguT[:], in_=tbv0[:]), reads=[t_tb], writes=[t_wsguT])
            small_T(b_sgu[l], 8, bsguT[:], t_bsguT)
            S.dma("sp", lambda e, l=l: e.dma_start(out=vnw_b[:], in_=v_norm_w[l].partition_broadcast(128)), "vnw", writes=[t_vnw])
            S.dma("sp", lambda e, l=l: e.dma_start(out=qnw_b[:], in_=q_norm_w[l].partition_broadcast(128)), "qnw", writes=[t_qnw])
            S.dma("sp", lambda e, l=l: e.dma_start(out=knw_b[:], in_=k_norm_w[l].partition_broadcast(128)), "knw", writes=[t_knw])
            mq, t_mq = smring.next()
            S.op("dve", lambda e, mq=mq: e.tensor_reduce(out=mq[:, 0:1], in_=qnw_b[:], axis=AX.X, op=ALU.max, apply_absolute_value=True),
                 reads=[t_qnw], writes=[t_mq])
            S.op("dve", lambda e, mq=mq: e.tensor_reduce(out=mq[:, 1:2], in_=knw_b[:], axis=AX.X, op=ALU.max, apply_absolute_value=True),
                 reads=[t_knw], writes=[t_mq])
            S.op("dve", lambda e, mq=mq: e.tensor_tensor(out=negC[:, 0:1], in0=mq[:, 0:1], in1=mq[:, 1:2], op=ALU.mult),
                 reads=[t_mq], writes=[t_negC])
            S.op("dve", lambda e: e.tensor_scalar(out=negC[:, 0:1], in0=negC[:, 0:1], scalar1=-float(np.sqrt(128.0)), scalar2=None, op0=ALU.mult),
                 writes=[t_negC])
            def build_gate(r, li=li):
                for q4 in range(4):
                    for k in range(4):
                        kc = q4 * 4 + k
                        dg, t_dg = dring.next()
                        S.op("dve", lambda e, dg=dg, kc=kc, r=r: e.tensor_scalar(
                            out=dg[:], in0=identf[:], scalar1=modT[li][:, 32 + kc, r:r + 1], scalar2=None, op0=ALU.mult),
                            reads=[t_identf, t_mod[li]], writes=[t_dg])
                        S.op("pe", lambda e, dg=dg, k=k: e.matmul(Sb[:, k * 128:(k + 1) * 128], lhsT=onesf[:], rhs=dg[:], start=True, stop=True),
                             reads=[t_dg, t_onesf], writes=[t_Sb])
                    S.op("dve", lambda e, q4=q4: e.tensor_copy(out=gate_b[:, q4 * 512:(q4 + 1) * 512], in_=Sb[:]),
                         reads=[t_Sb], writes=[t_gate])
            build_gate(0)

            wkv, t_wkv, wkey = wring.next()
            S.dma("sp", lambda e, wkv=wkv: e.dma_start(out=wkv[:], in_=wbi[li][8]), wkey, reads=[t_wci[li]], writes=[t_wkv])
            for t in p1_tiles:
                hb, t_hb = h1ring.next()
                make_hT(src_ap, t_srcx, t, li, hb, t_hb)
                if t == p1_tiles[0] and li == 0:
                    dbg("hT_p1", hb[:], t_hb)
                ab, t_ab = proj(hb, t_hb, wkv, t_wkv)
                rk, t_rk = head_rstd(ab[:, 0:256], t_ab, 2)
                kn, t_kn = tring.next()

                def knf(e, ab=ab, rk=rk, kn=kn):
                    rr = None
                    for h in range(2):
                        rr = e.scalar_tensor_tensor(out=kn[:, h * 128:(h + 1) * 128], in0=ab[:, h * 128:(h + 1) * 128],
                                                    scalar=rk[:, h:h + 1], in1=knw_b[:], op0=ALU.mult, op1=ALU.mult)
                    return rr
                S.op("dve", knf, reads=[t_ab, t_rk, t_knw], writes=[t_kn])
                kb, t_kb = qbring.next()
                if is_ctx(t):
                    S.op("dve", lambda e, kb=kb, kn=kn: e.tensor_copy(out=kb[:, 0:256], in_=kn[:, 0:256]), reads=[t_kn], writes=[t_kb])
                else:
                    rt, t_rt, rkey = r1ring.next()
                    lt = lat_index(t)
                    S.dma("sp", lambda e, rt=rt, lt=lt: e.dma_start(out=rt[:], in_=rope[lt * 128:(lt + 1) * 128, :]), rkey, writes=[t_rt])
                    apply_rope(kn, t_kn, 2, rt, t_rt, kb, t_kb)
                tb, t_tb = Tb[0]
                tbv = tb[:].bitcast(BF16).rearrange("p (k c) -> p k c", c=128)

                def trk(e, kb=kb, tbv=tbv):
                    e.transpose(out=tbv[:, 0, :], in_=kb[:, 0:128], identity=identb[:])
                    return e.transpose(out=tbv[:, 1, :], in_=kb[:, 128:256], identity=identb[:])
                S.op("pe", trk, reads=[t_kb, t_identb], writes=[t_tb])
                S.op("act", lambda e, tbv=tbv, t=t: e.copy(out=KT[:, :, t * 128:(t + 1) * 128], in_=tbv[:, 0:2, :]),
                     reads=[t_tb], writes=[t_K[t]])
                S.op("act", lambda e, ab=ab, t=t: e.copy(out=VA[:, t, :, 0:128], in_=ab[:, 256:512].rearrange("p (h d) -> p h d", d=128)),
                     reads=[t_ab, t_Vones], writes=[t_V[t]])

            if li == 0:
                t0_ = p1_tiles[0]
                dbg("KT0", KT[:, :, t0_ * 128:(t0_ + 1) * 128], t_K[t0_])
                dbg("VA0", VA[:, t0_, :, :], t_V[t0_])
                dbg("negC", negC[:], t_negC)
                dbg("gate_b", gate_b[:], t_gate)
                dbg("wsguT", wsguT[:], t_wsguT)
            lat_tiles = [t for t in p2_tiles if not is_ctx(t)]
            ctx_tiles = [t for t in p2_tiles if is_ctx(t)]
            groups = [lat_tiles[i:i + TG] for i in range(0, len(lat_tiles), TG)]
            if ctx_tiles:
                groups.append(ctx_tiles)
            t_xs_next = [Tok() for _ in range(NT_ALL)]
            for grp in groups:
                ng = len(grp)
                rflag = 1 if is_ctx(grp[0]) else 0
                if rflag:
                    build_gate(1)
                ktiles = [t for t in p1_tiles if is_ctx(t)] if rflag else list(p1_tiles)
                for i, t in enumerate(grp):
                    make_hT(src_ap, t_srcx, t, li, hT[i], t_hT[i])
                    if not rflag:
                        lt = lat_index(t)
                        S.dma("sp", lambda e, i=i, lt=lt: e.dma_start(out=ropeT[i][:], in_=rope[lt * 128:(lt + 1) * 128, :]),
                              f"ropeT{i}", writes=[t_rope[i]])
                order = [(2, "v", 0), (0, "u", 0), (4, "za", 0), (3, "v", 1), (1, "u", 1), (5, "za", 1),
                         (6, "q", 0), (7, "q", 1), (9, "zb", 0), (10, "zb", 1)]
                pending_backs = []
                for (cb, kind, hh) in order:
                    wb, t_wb, wkey = wring.next()
                    S.dma("sp", lambda e, wb=wb, cb=cb: e.dma_start(out=wb[:], in_=wbi[li][cb]), wkey, reads=[t_wci[li]], writes=[t_wb])
                    for i, t in enumerate(grp):
                        ab, t_ab = proj(hT[i], t_hT[i], wb, t_wb)
                        if pending_backs:
                            pending_backs.pop(0)()
                        back = None
                        if kind == "v":
                            gv, t_gv = tring.next()
                            S.op("act", lambda e, gv=gv, ab=ab: e.activation(out=gv[:], in_=ab[:], func=AF.Gelu_apprx_tanh),
                                 reads=[t_ab], writes=[t_gv])
                            rv, t_rv = head_rstd(gv[:], t_gv, 4)
                            v1, t_v1 = tring.next()
                            S.op("dve", lambda e, v1=v1, gv=gv, rv=rv: e.tensor_tensor(
                                out=v1[:].rearrange("p (g d) -> p g d", d=128), in0=gv[:].rearrange("p (g d) -> p g d", d=128),
                                in1=rv.unsqueeze(2).broadcast_to([128, 4, 128]), op=ALU.mult), reads=[t_gv, t_rv], writes=[t_v1])
                            vb, t_vb = vnring.next()
                            S.op("dve", lambda e, vb=vb, v1=v1, hh=hh: e.tensor_tensor(
                                out=vb[:], in0=v1[:], in1=vnw_b[:, hh * 512:(hh + 1) * 512], op=ALU.mult),
                                reads=[t_v1, t_vnw], writes=[t_vb])

                            def back(vb=vb, t_vb=t_vb, hh=hh, i=i):
                                def sgu(e):
                                    rr = None
                                    for g in range(4):
                                        rr = e.matmul(Sb[:, g * 128:(g + 1) * 128], lhsT=wsguT[:, 4 * hh + g, :],
                                                      rhs=vb[:, g * 128:(g + 1) * 128], start=True, stop=True)
                                    return rr
                                S.op("pe", sgu, reads=[t_vb, t_wsguT], writes=[t_Sb])
                                S.op("dve", lambda e: e.tensor_tensor(
                                    out=s_sb[i][:].rearrange("p (g d) -> p g d", d=128), in0=Sb[:].rearrange("p (g d) -> p g d", d=128),
                                    in1=bsguT[:, 4 * hh:4 * hh + 4].unsqueeze(2).broadcast_to([128, 4, 128]), op=ALU.add),
                                    reads=[t_Sb, t_bsguT], writes=[t_s[i]])
                        elif kind == "u":
                            gu, t_gu = tring.next()
                            S.op("act", lambda e, gu=gu, ab=ab: e.activation(out=gu[:], in_=ab[:], func=AF.Gelu_apprx_tanh),
                                 reads=[t_ab], writes=[t_gu])
                            S.op("dve", lambda e, gu=gu, i=i: e.tensor_tensor(out=s_sb[i][:], in0=gu[:], in1=s_sb[i][:], op=ALU.mult),
                                 reads=[t_gu], writes=[t_s[i]])
                        elif kind == "za":
                            sz, t_sz = tring.next()
                            S.op("act", lambda e, sz=sz, ab=ab: e.activation(out=sz[:], in_=ab[:], func=AF.Silu), reads=[t_ab], writes=[t_sz])
                            S.op("dve", lambda e, sz=sz, i=i, hh=hh: e.tensor_tensor(
                                out=gated[i][:, hh * 512:(hh + 1) * 512], in0=sz[:], in1=s_sb[i][:], op=ALU.mult),
                                reads=[t_sz, t_s[i]], writes=[t_gated[i]])
                        elif kind == "q":
                            rq, t_rq = head_rstd(ab[:], t_ab, 4)
                            q1, t_q1 = tring.next()
                            S.op("dve", lambda e, q1=q1, ab=ab, rq=rq: e.tensor_tensor(
                                out=q1[:].rearrange("p (g d) -> p g d", d=128), in0=ab[:].rearrange("p (g d) -> p g d", d=128),
                                in1=rq.unsqueeze(2).broadcast_to([128, 4, 128]), op=ALU.mult), reads=[t_ab, t_rq], writes=[t_q1])
                            S.op("dve", lambda e, q1=q1: e.tensor_tensor(
                                out=q1[:].rearrange("p (g d) -> p g d", d=128), in0=q1[:].rearrange("p (g d) -> p g d", d=128),
                                in1=qnw_b[:].unsqueeze(1).broadcast_to([128, 4, 128]), op=ALU.mult), reads=[t_qnw], writes=[t_q1])
                            qb, t_qb = qbring.next()
                            if rflag:
                                S.op("dve", lambda e, qb=qb, q1=q1: e.tensor_copy(out=qb[:], in_=q1[:]), reads=[t_q1], writes=[t_qb])
                            else:
                                apply_rope(q1, t_q1, 4, ropeT[i], t_rope[i], qb, t_qb)

                            def back(qb=qb, t_qb=t_qb, hh=hh, i=i):
                                tb, t_tb = Tb[(i + hh) % 2]
                                tbv = tb[:].bitcast(BF16).rearrange("p (k c) -> p k c", c=128)

                                def trq(e):
                                    rr = None
                                    for h in range(4):
                                        rr = e.transpose(out=tbv[:, h, :], in_=qb[:, h * 128:(h + 1) * 128], identity=identb[:])
                                    return rr
                                S.op("pe", trq, reads=[t_qb, t_identb], writes=[t_tb])
                                S.op("act", lambda e: e.copy(out=qT[i][:, 4 * hh:4 * hh + 4, :], in_=tbv[:, 0:4, :]),
                                     reads=[t_tb], writes=[t_qT[i]])
                        else:
                            S.op("act", lambda e, ab=ab, i=i, hh=hh: e.activation(out=szb[i][:, hh * 512:(hh + 1) * 512], in_=ab[:], func=AF.Silu),
                                 reads=[t_ab], writes=[t_szb[i]])
                        if back is not None:
                            pending_backs.append(back)
                while pending_backs:
                    pending_backs.pop(0)()

                if li == 0 and grp is groups[0]:
                    dbg("gatedA", gated[0][:, 0:1024], t_gated[0])
                    dbg("qT", qT[0][:], t_qT[0])
                    dbg("szb", szb[0][:], t_szb[0])
                inv_sqrt = float(128.0 ** -0.5)
                nk = len(ktiles)
                units = [(i, g, ki, kt) for i in range(ng) for g in range(2) for ki, kt in enumerate(ktiles)]
                LAG = 2
                (o0, t_o0), (o1, t_o1) = Ob

                def attn_front(u):
                    i, g, ki, kt = u
                    sbk, t_sbk = aring.next()
                    S.op("pe", lambda e: e.matmul(
                        sbk[:], lhsT=KT[:, g, kt * 128:(kt + 1) * 128], rhs=qT[i][:, 4 * g:4 * g + 4, :], start=True, stop=True),
                        reads=[t_K[kt], t_qT[i]], writes=[t_sbk])
                    pt, t_pt = ptring.next()
                    S.op("act", lambda e: e.activation(out=pt[:], in_=sbk[:], func=AF.Exp, bias=negC[:, 0:1], scale=inv_sqrt),
                         reads=[t_sbk, t_negC], writes=[t_pt])
                    return pt, t_pt

                def attn_back(u, pt, t_pt):
                    i, g, ki, kt = u

                    def pv(e):
                        rr = None
                        for hq in range(4):
                            if hq < 3:
                                oap = o0[:, hq * 129:hq * 129 + 129]
                                st = (ki == 0 and hq == 0)
                            else:
                                oap = o1[:, 0:129]
                                st = (ki == 0)
                            rr = e.matmul(oap, lhsT=pt[:, hq * 128:(hq + 1) * 128], rhs=VA[:, kt, g, 0:129],
                                          start=st, stop=(ki == nk - 1), skip_group_check=True)
                        return rr
                    S.op("pe", pv, reads=[t_pt, t_V[kt]], writes=[t_o0, t_o1])
                    if ki != nk - 1:
                        return
                    rd, t_rd = smring.next()

                    def rden(e):
                        e.reciprocal(out=rd[:, 0:3], in_=o0[:, 0:387].rearrange("p (h c) -> p h c", c=129)[:, :, 128])
                        return e.reciprocal(out=rd[:, 3:4], in_=o1[:, 128:129])
                    S.op("dve", rden, reads=[t_o0, t_o1], writes=[t_rd])

                    def onorm(e):
                        rr = None
                        for hq in range(4):
                            h = 4 * g + hq
                            oap = o0[:, hq * 129:hq * 129 + 128] if hq < 3 else o1[:, 0:128]
                            rr = e.scalar_tensor_tensor(out=gated[i][:, 1024 + h * 128:1024 + (h + 1) * 128], in0=oap,
                                                        scalar=rd[:, hq:hq + 1], in1=szb[i][:, h * 128:(h + 1) * 128],
                                                        op0=ALU.mult, op1=ALU.mult)
                        return rr
                    S.op("dve", onorm, reads=[t_o0, t_o1, t_rd, t_szb[i]], writes=[t_gated[i]])

                pend = []
                for idx in range(len(units) + LAG):
                    if idx < len(units):
                        pend.append(attn_front(units[idx]))
                    if idx >= LAG:
                        attn_back(units[idx - LAG], *pend[idx - LAG])

                if li == 0 and grp is groups[0]:
                    dbg("gated", gated[0][:], t_gated[0])
                for i, t in enumerate(grp):
                    for half in range(2):
                        tb, t_tb = Tb[half]
                        tbv = tb[:].bitcast(BF16).rearrange("p (k c) -> p k c", c=128)

                        def trg(e, half=half, tbv=tbv, i=i):
                            rr = None
                            for k in range(8):
                                kc = half * 8 + k
                                rr = e.transpose(out=tbv[:, k, :], in_=gated[i][:, kc * 128:(kc + 1) * 128], identity=identb[:])
                            return rr
                        S.op("pe", trg, reads=[t_gated[i], t_identb], writes=[t_tb])
                        S.op("act", lambda e, half=half, tbv=tbv, i=i: e.copy(out=hT[i][:, half * 8:(half + 1) * 8, :], in_=tbv[:]),
                             reads=[t_tb], writes=[t_hT[i]])

                for cb in range(4):
                    wb, t_wb, wkey = wring.next()
                    S.dma("sp", lambda e, wb=wb, cb=cb: e.dma_start(out=wb[:], in_=wbo[li][cb]), wkey, reads=[t_wco[li]], writes=[t_wb])
                    for i, t in enumerate(grp):
                        xpb, t_xp, xkey = xpring.next()
                        S.dma("sp", lambda e, xpb=xpb, t=t, cb=cb: e.dma_start(
                            out=xpb[:], in_=src_ap[t * 128:(t + 1) * 128, cb * 512:(cb + 1) * 512]), xkey, reads=[t_srcx[t]], writes=[t_xp])
                        ab, t_ab = proj(hT[i], t_hT[i], wb, t_wb)
                        yg, t_yg = tring.next()
                        S.op("dve", lambda e, yg=yg, ab=ab, cb=cb: e.tensor_tensor(
                            out=yg[:], in0=ab[:], in1=gate_b[:, cb * 512:(cb + 1) * 512], op=ALU.mult),
                            reads=[t_ab, t_gate], writes=[t_yg])
                        S.op("pool", lambda e, yg=yg, xpb=xpb: e.tensor_tensor(out=xpb[:], in0=xpb[:], in1=yg[:], op=ALU.add),
                             reads=[t_yg], writes=[t_xp])
                        if last_in_prog:
                            if final:
                                drow = t
                            else:
                                drow = t
                        else:
                            drow = t
                        S.dma("pool", lambda e, xpb=xpb, drow=drow, cb=cb: e.dma_start(
                            out=dst_ap[drow * 128:(drow + 1) * 128, cb * 512:(cb + 1) * 512], in_=xpb[:]), xkey,
                            reads=[t_xp], writes=[t_xs_next[t]])
                        if last_in_prog:
                            out_toks.append(t_xp)
            return t_xs_next

        t_xs_all = [Tok() for _ in range(NT_ALL)]
        for li_, l_ in enumerate(layers):
            last_ = (li_ == NL - 1)
            t_xs_all = run_layer(li_, l_, xa if li_ == 0 else xs, out if last_ else xs, t_xs_all,
                                 p2_tiles_per_layer[li_], last_)

        S.final_wait("pool", list({id(t): t for t in out_toks}.values()) + dbg_toks)
        S.emit(block)
    return nc


def _rope_tables(pos):
    rows = (pos // GRID_W).astype(np.float32)
    cols = (pos % GRID_W).astype(np.float32)
    inv_freq = (np.float32(10000.0) ** (-np.arange(0, 64, 2, dtype=np.float32) / np.float32(64))).astype(np.float32)
    ang_r = rows[:, None] * inv_freq[None, :]
    ang_c = cols[:, None] * inv_freq[None, :]
    ang = np.concatenate([ang_r, ang_r, ang_c, ang_c], axis=-1).astype(np.float32)
    return np.concatenate([np.cos(ang), np.sin(ang)], axis=-1).astype(np.float32)


_PROG_CACHE = {}


def _get_prog(key, *args):
    if key not in _PROG_CACHE:
        _PROG_CACHE[key] = build(*args)
    return _PROG_CACHE[key]


def _common_inputs(c, c_ctx, norm_w, w_mod, b_mod, w_in, w_sgu, b_sgu, v_norm_w, q_norm_w, k_norm_w, w_out):
    f = lambda a: np.ascontiguousarray(np.asarray(a, dtype=np.float32))
    shared = {
        "identb": np.eye(128, dtype=np.float32).astype(ml_dtypes.bfloat16),
        "identf": np.eye(128, dtype=np.float32),
        "w_mod": f(w_mod), "b_mod": f(b_mod).reshape(2, 48, 128), "norm_w": f(norm_w).reshape(2, 16, 128),
        "w_in": f(w_in), "w_out": f(w_out), "w_sgu": f(w_sgu), "b_sgu": f(b_sgu),
        "v_norm_w": f(v_norm_w).reshape(2, 1024), "q_norm_w": f(q_norm_w), "k_norm_w": f(k_norm_w),
    }
    return shared


def kernel(x, c, ctx, c_ctx, norm_w, w_mod, b_mod, w_in, w_sgu, b_sgu, v_norm_w, q_norm_w, k_norm_w, w_out):
    x = np.asarray(x, dtype=np.float32)
    ctx = np.asarray(ctx, dtype=np.float32)
    c = np.asarray(c, dtype=np.float32)
    c_ctx = np.asarray(c_ctx, dtype=np.float32)
    shared = _common_inputs(c, c_ctx, norm_w, w_mod, b_mod, w_in, w_sgu, b_sgu, v_norm_w, q_norm_w, k_norm_w, w_out)
    H = SEQ // 2
    CH = CTX // 2

    def core_maps(xfull, cfull):
        maps = []
        for core in range(8):
            b, hf = divmod(core, 2)
            o, p = hf, 1 - hf
            xa = np.concatenate([xfull[b, o * H:(o + 1) * H], cfull[b, o * CH:(o + 1) * CH],
                                 xfull[b, p * H:(p + 1) * H], cfull[b, p * CH:(p + 1) * CH]], axis=0)
            pos = np.concatenate([np.arange(o * H, (o + 1) * H), np.arange(p * H, (p + 1) * H)])
            m = dict(shared)
            m["xa"] = np.ascontiguousarray(xa)
            m["rope"] = _rope_tables(pos)
            m["c2"] = np.ascontiguousarray(np.stack([c[b], c_ctx], 0).reshape(32, 128))
            maps.append(m)
        return maps

    all_tiles = list(range(NT_ALL))
    own_lat = list(range(16))
    if MODE == "fused":
        nc = _get_prog("fused", [0, 1], True, all_tiles, [all_tiles, own_lat], 16)
        res = run_bass_kernel_spmd(nc, core_maps(x, ctx), core_ids=list(range(8)))
        outs = [r["out"] for r in res.results]
    else:
        ncA = _get_prog("L0", [0], False, all_tiles, [list(range(17))], 17)
        resA = run_bass_kernel_spmd(ncA, core_maps(x, ctx), core_ids=list(range(8)))
        x1 = np.empty_like(x)
        ctx1 = np.empty_like(ctx)
        for core in range(8):
            b, hf = divmod(core, 2)
            xn_ = resA.results[core]["xnext"]
            x1[b, hf * H:(hf + 1) * H] = xn_[0:H]
            ctx1[b, hf * CH:(hf + 1) * CH] = xn_[H:H + CH]
        ncB = _get_prog("L1", [1], True, all_tiles, [own_lat], 16)
        resB = run_bass_kernel_spmd(ncB, core_maps(x1, ctx1), core_ids=list(range(8)))
        outs = [r["out"] for r in resB.results]
    y = np.empty((4, SEQ, D), dtype=np.float32)
    for core in range(8):
        b, hf = divmod(core, 2)
        y[b, hf * H:(hf + 1) * H] = outs[core]
    return y
```

```python
import numpy as np
from contextlib import ExitStack
import ml_dtypes
import concourse.bass as bass
import concourse.mybir as mybir
from concourse.bass_utils import run_bass_kernel_spmd

F32 = mybir.dt.float32
BF16 = mybir.dt.bfloat16
AF = mybir.ActivationFunctionType
ALU = mybir.AluOpType
AX = mybir.AxisListType

D = 2048
NKC = 16
DIN = 5632
NCB = 11
SEQ = 4096
CTX = 256
GRID_W = 64
EPS = 1e-6
TG = 4
NT_ALL = 34
DEBUG = False
MODE = "fused"


class Tok:
    __slots__ = ("w", "r")

    def __init__(self):
        self.w = None
        self.r = {}


class Sched:
    EPOCH = 20000

    def __init__(self, nc, es):
        self.nc = nc
        self.es = es
        self.names = ["pe", "act", "dve", "pool", "sp"]
        self.q = {k: [] for k in self.names}
        self.cnt = {k: 0 for k in self.names}
        self.esems = {k: [] for k in self.names}
        self.dsems = {}
        self.dcnt = {}
        self.waited = {k: {} for k in self.names}

    def _newsem(self, name):
        return self.es.enter_context(self.nc.semaphore(name))

    def _eng_event(self, eng):
        c = self.cnt[eng]
        ep, v = divmod(c, self.EPOCH)
        while len(self.esems[eng]) <= ep:
            self.esems[eng].append(self._newsem(f"e_{eng}_{len(self.esems[eng])}"))
        self.cnt[eng] = c + 1
        return (self.esems[eng][ep], v + 1, eng)

    def _collect(self, eng, reads, writes):
        evs = []
        for t in reads:
            if t.w is not None:
                evs.append(t.w)
        for t in writes:
            if t.w is not None:
                evs.append(t.w)
            evs.extend(t.r.values())
        waits = {}
        for (sem, val, e) in evs:
            if e is not None and e == eng and eng == "pe":
                continue
            key = id(sem)
            if self.waited[eng].get(key, 0) >= val:
                continue
            if key not in waits or waits[key][1] < val:
                waits[key] = (sem, val)
        for key, (sem, val) in waits.items():
            self.waited[eng][key] = val
        return list(waits.values())

    def _mark(self, ev, reads, writes):
        k = id(ev[0])
        for t in reads:
            old = t.r.get(k)
            if old is None or old[1] < ev[1]:
                t.r[k] = ev
        for t in writes:
            t.w = ev
            t.r = {}

    def op(self, eng, fn, reads=(), writes=()):
        waits = self._collect(eng, reads, writes)
        ev = self._eng_event(eng)
        self.q[eng].append((waits, fn, ev[0], 1))
        self._mark(ev, reads, writes)

    def dma(self, eng, fn, key, reads=(), writes=(), n=1):
        waits = self._collect(eng, reads, writes)
        if key not in self.dsems:
            self.dsems[key] = self._newsem(f"d_{key}")
            self.dcnt[key] = 0
        self.dcnt[key] += 16 * n
        ev = (self.dsems[key], self.dcnt[key], None)
        self.q[eng].append((waits, fn, self.dsems[key], 16))
        self._mark(ev, reads, writes)

    def final_wait(self, eng, toks):
        waits = self._collect(eng, toks, toks)
        self.q[eng].append((waits, None, None, 0))

    def emit(self, block):
        def mk(name):
            def body(e):
                for (waits, fn, sem, inc) in self.q[name]:
                    for (s, v) in waits:
                        e.wait_ge(s, v)
                    if fn is None:
                        continue
                    r = fn(e)
                    if isinstance(r, (list, tuple)):
                        for ins in r:
                            ins.then_inc(sem, inc)
                    else:
                        r.then_inc(sem, inc)
            return body
        block.tensor(mk("pe"))
        block.scalar(mk("act"))
        block.vector(mk("dve"))
        block.gpsimd(mk("pool"))
        block.sync(mk("sp"))


class Ring:
    def __init__(self, items):
        self.items = items
        self.i = 0

    def next(self):
        it = self.items[self.i % len(self.items)]
        self.i += 1
        return it


def is_ctx(t):
    return t % 17 == 16


def lat_index(t):
    return (t // 17) * 16 + (t % 17)


def build(layers, final, p1_tiles, p2_tiles_per_layer, n_out_tiles):
    nc = bass.Bass("TRN2", target_bir_lowering=False)
    NL = len(layers)

    def din(name, shape, dt=F32):
        return nc.dram_tensor(name, shape, dt, kind="ExternalInput").ap()

    xa = din("xa", [NT_ALL * 128, D])
    rope = din("rope", [32 * 128, 256])
    c2 = din("c2", [32, 128])
    identb_in = din("identb", [128, 128], BF16)
    identf_in = din("identf", [128, 128])
    w_mod = din("w_mod", [2, D, 3 * D])
    b_mod = din("b_mod", [2, 48, 128])
    norm_w = din("norm_w", [2, 16, 128])
    w_in = din("w_in", [2, D, DIN])
    w_out = din("w_out", [2, D, D])
    w_sgu = din("w_sgu", [2, 8, 128, 128])
    b_sgu = din("b_sgu", [2, 8, 128])
    v_norm_w = din("v_norm_w", [2, 1024])
    q_norm_w = din("q_norm_w", [2, 128])
    k_norm_w = din("k_norm_w", [2, 128])
    if final:
        out = nc.dram_tensor("out", [16 * 128, D], F32, kind="ExternalOutput").ap()
    else:
        out = nc.dram_tensor("xnext", [n_out_tiles * 128, D], F32, kind="ExternalOutput").ap()
    xs = nc.dram_tensor("xs", [NT_ALL * 128, D], F32).ap() if NL > 1 else None
    wbi = [nc.dram_tensor(f"wbi{l}", [NCB, 128, NKC * 512], BF16).ap() for l in layers]
    wbo = [nc.dram_tensor(f"wbo{l}", [4, 128, NKC * 512], BF16).ap() for l in layers]

    with ExitStack() as es:
        def sb(name, shape, dt):
            return es.enter_context(nc.sbuf_tensor(name, shape, dt))

        S = Sched(nc, es)
        identb = sb("identb_sb", [128, 128], BF16); t_identb = Tok()
        identf = sb("identf_sb", [128, 128], F32); t_identf = Tok()
        onesf = sb("onesf", [128, 128], F32); t_onesf = Tok()
        KT = sb("KT", [128, 2, NT_ALL * 128], BF16)
        VA = sb("VA", [128, NT_ALL, 2, 130], BF16)
        t_K = [Tok() for _ in range(NT_ALL)]
        t_V = [Tok() for _ in range(NT_ALL)]
        t_Vones = Tok()
        wbuf = [sb(f"wbuf{i}", [128, NKC * 512], BF16) for i in range(2)]
        t_wbuf = [Tok() for _ in range(2)]
        wring = Ring(list(zip(wbuf, t_wbuf, ["wbuf0", "wbuf1"])))
        xbuf = [sb(f"xbuf{i}", [128, D], F32) for i in range(2)]
        xring = Ring([(xbuf[i], Tok(), f"xbuf{i}") for i in range(2)])
        xn = [sb(f"xn{i}", [128, D], BF16) for i in range(2)]
        xnring = Ring([(xn[i], Tok()) for i in range(2)])
        hT = [sb(f"hT{i}", [128, NKC, 128], BF16) for i in range(TG)]
        t_hT = [Tok() for _ in range(TG)]
        h1ring = Ring([(hT[i], t_hT[i]) for i in range(2)])
        s_sb = [sb(f"s_sb{i}", [128, 512], F32) for i in range(TG)]
        t_s = [Tok() for _ in range(TG)]
        gated = [sb(f"gated{i}", [128, D], BF16) for i in range(TG)]
        t_gated = [Tok() for _ in range(TG)]
        qT = [sb(f"qT{i}", [128, 8, 128], BF16) for i in range(TG)]
        t_qT = [Tok() for _ in range(TG)]
        szb = [sb(f"szb{i}", [128, 1024], BF16) for i in range(TG)]
        t_szb = [Tok() for _ in range(TG)]
        ropeT = [sb(f"ropeT{i}", [128, 256], F32) for i in range(TG)]
        t_rope = [Tok() for _ in range(TG)]
        r1ring = Ring([(ropeT[i], t_rope[i], f"ropeT{i}") for i in range(2)])
        tmps = [sb(f"tmp{i}", [128, 512], F32) for i in range(6)]
        tring = Ring([(tmps[i], Tok()) for i in range(6)])
        vnb = [sb(f"vnb{i}", [128, 512], BF16) for i in range(2)]
        vnring = Ring([(vnb[i], Tok()) for i in range(2)])
        qbf = [sb(f"qbf{i}", [128, 512], BF16) for i in range(2)]
        qbring = Ring([(qbf[i], Tok()) for i in range(2)])
        PT = [sb(f"PT{i}", [128, 512], BF16) for i in range(4)]
        ptring = Ring([(PT[i], Tok()) for i in range(4)])
        xp = [sb(f"xp{i}", [128, 512], F32) for i in range(4)]
        xpring = Ring([(xp[i], Tok(), f"xp{i}") for i in range(4)])
        gate_b = sb("gate_b", [128, D], F32)
        t_gate = Tok()
        small = sb("small", [128, 64], F32)
        smring = Ring([(small[:, 8 * i:8 * i + 8], Tok()) for i in range(8)])
        cT = sb("cT", [128, 32], F32); t_cT = Tok()
        modT = [sb(f"modT{i}", [128, 48, 2], F32) for i in range(NL)]
        t_mod = [Tok() for _ in range(NL)]
        gT = [sb(f"gT{i}", [128, NKC, 2], F32) for i in range(NL)]
        t_gT = [Tok() for _ in range(NL)]
        nwT = sb("nwT", [128, 16], F32); t_nwT = Tok()
        bmT = sb("bmT", [128, 48], F32); t_bmT = Tok()
        rows = sb("rows", [48, 128], F32); t_rows = Tok()
        wsg_f = sb("wsg_f", [128, 8, 128], F32); t_wsgf = Tok()
        wsg_b = sb("wsg_b", [128, 8, 128], BF16); t_wsgb = Tok()
        wsguT = sb("wsguT", [128, 8, 128], BF16); t_wsguT = Tok()
        bsguT = sb("bsguT", [128, 8], F32); t_bsguT = Tok()
        vnw_b = sb("vnw_b", [128, 1024], F32); t_vnw = Tok()
        qnw_b = sb("qnw_b", [128, 128], F32); t_qnw = Tok()
        knw_b = sb("knw_b", [128, 128], F32); t_knw = Tok()
        negC = sb("negC", [128, 2], F32); t_negC = Tok()
        diag = [sb(f"diag{i}", [128, 128], F32) for i in range(2)]
        dring = Ring([(diag[i], Tok()) for i in range(2)])
        bank = [es.enter_context(nc.psum_tensor(f"bank{i}", [128, 512], F32)) for i in range(8)]
        t_bank = [Tok() for _ in range(8)]
        aring = Ring([(bank[i], t_bank[i]) for i in range(3)])
        Sb, t_Sb = bank[3], t_bank[3]
        Tb = [(bank[4], t_bank[4]), (bank[5], t_bank[5])]
        Ob = [(bank[6], t_bank[6]), (bank[7], t_bank[7])]

        block = es.enter_context(nc.Block())
        dbg_toks = []

        def dbg(name, ap, tok):
            if not DEBUG:
                return
            d = nc.dram_tensor("dbg_" + name, list(ap.shape), ap.dtype, kind="ExternalOutput").ap()
            tk = Tok()
            S.dma("sp", lambda e: e.dma_start(out=d, in_=ap), "dbg_" + name, reads=[tok], writes=[tk])
            dbg_toks.append(tk)

        S.dma("sp", lambda e: e.dma_start(out=identb[:], in_=identb_in), "c_identb", writes=[t_identb])
        S.dma("sp", lambda e: e.dma_start(out=identf[:], in_=identf_in), "c_identf", writes=[t_identf])
        S.op("dve", lambda e: e.memset(onesf[:], 1.0), writes=[t_onesf])
        S.op("dve", lambda e: e.memset(VA[:, :, :, 128:130], 1.0), writes=[t_Vones])

        t_wci = [Tok() for _ in range(NL)]
        t_wco = [Tok() for _ in range(NL)]
        for li, l in enumerate(layers):
            def cast_in(e, l=l, li=li):
                res = []
                for kc in range(NKC):
                    for c0 in (0, 6):
                        ncb = 6 if c0 == 0 else 5
                        dst = wbi[li][c0:c0 + ncb, :, kc * 512:(kc + 1) * 512]
                        src = w_in[l, kc * 128:(kc + 1) * 128, c0 * 512:(c0 + ncb) * 512].rearrange("p (cb c) -> cb p c", c=512)
                        res.append(e.dma_start(out=dst, in_=src))
                return res
            S.dma("pool", cast_in, f"cast_in{li}", writes=[t_wci[li]], n=2 * NKC)

            def cast_out(e, l=l, li=li):
                res = []
                for kc in range(NKC):
                    dst = wbo[li][:, :, kc * 512:(kc + 1) * 512]
                    src = w_out[l, kc * 128:(kc + 1) * 128, :].rearrange("p (cb c) -> cb p c", c=512)
                    res.append(e.dma_start(out=dst, in_=src))
                return res
            S.dma("pool", cast_out, f"cast_out{li}", writes=[t_wco[li]], n=NKC)

        def small_T(src_ap, n, dst, t_dst, extra_reads=()):
            S.dma("sp", lambda e: e.dma_start(out=rows[0:n, :], in_=src_ap), "rows", writes=[t_rows])
            S.op("pe", lambda e: e.transpose(out=Sb[:, 0:n], in_=rows[0:n, :], identity=identf[0:n, 0:n]),
                 reads=[t_rows, t_identf], writes=[t_Sb])
            S.op("dve", lambda e: e.tensor_copy(out=dst, in_=Sb[:, 0:n]), reads=[t_Sb], writes=[t_dst])

        S.dma("sp", lambda e: e.dma_start(out=rows[0:32, :], in_=c2), "rows", writes=[t_rows])
        S.op("act", lambda e: e.activation(out=rows[0:32, :], in_=rows[0:32, :], func=AF.Silu), reads=[t_rows], writes=[t_rows])
        S.op("pe", lambda e: e.transpose(out=Sb[:, 0:32], in_=rows[0:32, :], identity=identf[0:32, 0:32]),
             reads=[t_rows, t_identf], writes=[t_Sb])
        S.op("dve", lambda e: e.tensor_copy(out=cT[:], in_=Sb[:, 0:32]), reads=[t_Sb], writes=[t_cT])
        cTv = cT[:].rearrange("p (r k) -> p r k", r=2)

        for li, l in enumerate(layers):
            small_T(b_mod[l], 48, bmT[:], t_bmT)
            small_T(norm_w[l], 16, nwT[:], t_nwT)
            for jb in range(24):
                wb, t_wb, wkey = wring.next()
                wv = wb[:].bitcast(F32).rearrange("p (k c) -> p k c", c=256)
                S.dma("sp", lambda e, wv=wv, l=l, jb=jb: e.dma_start(
                    out=wv, in_=w_mod[l, :, jb * 256:(jb + 1) * 256].rearrange("(k p) c -> p k c", p=128)),
                    wkey, writes=[t_wb])

                def mm_mod(e, wv=wv, jb=jb):
                    r = None
                    for jj in range(2):
                        j = jb * 2 + jj
                        for kc in range(NKC):
                            r = e.matmul(Sb[:, 2 * j:2 * j + 2], lhsT=wv[:, kc, jj * 128:(jj + 1) * 128], rhs=cTv[:, :, kc],
                                         start=(kc == 0), stop=(kc == NKC - 1))
                    return r
                S.op("pe", mm_mod, reads=[t_wb, t_cT], writes=[t_Sb])
            modv = modT[li]
            S.op("dve", lambda e, modv=modv: e.tensor_tensor(
                out=modv[:], in0=Sb[:, 0:96].rearrange("p (j r) -> p j r", r=2),
                in1=bmT[:].unsqueeze(2).broadcast_to([128, 48, 2]), op=ALU.add),
                reads=[t_Sb, t_bmT], writes=[t_mod[li]])
            S.op("dve", lambda e, modv=modv, li=li: e.tensor_scalar(
                out=gT[li][:], in0=modv[:, 16:32, :], scalar1=1.0, scalar2=None, op0=ALU.add),
                reads=[t_mod[li]], writes=[t_gT[li]])
            S.op("dve", lambda e, li=li: e.tensor_tensor(
                out=gT[li][:], in0=gT[li][:], in1=nwT[:].unsqueeze(2).broadcast_to([128, 16, 2]), op=ALU.mult),
                reads=[t_nwT], writes=[t_gT[li]])

        for li in range(NL):
            dbg(f"modT{li}", modT[li][:], t_mod[li])
            dbg(f"gT{li}", gT[li][:], t_gT[li])
        dbg("cT", cT[:], t_cT)

        def rstd_small(ss_ap, t_ss, n, inv_n):
            sd, t_sd = smring.next()
            S.op("act", lambda e: e.activation(out=sd[:, 0:n], in_=ss_ap, func=AF.Sqrt, bias=EPS, scale=inv_n),
                 reads=[t_ss], writes=[t_sd])
            rs, t_rs = smring.next()
            S.op("dve", lambda e: e.reciprocal(out=rs[:, 0:n], in_=sd[:, 0:n]), reads=[t_sd], writes=[t_rs])
            return rs[:, 0:n], t_rs

        def make_hT(src_ap, t_src, t, li, dst, t_dst):
            r = 1 if is_ctx(t) else 0
            xb, t_xb, xkey = xring.next()
            S.dma("sp", lambda e: e.dma_start(out=xb[:], in_=src_ap[t * 128:(t + 1) * 128, :]), xkey, reads=[t_src[t]], writes=[t_xb])
            xnb, t_xn = xnring.next()
            ss, t_ss = smring.next()
            S.op("act", lambda e: e.activation(out=xnb[:], in_=xb[:], func=AF.Square, accum_out=ss[:, 0:1]),
                 reads=[t_xb], writes=[t_xn, t_ss])
            rs, t_rs = rstd_small(ss[:, 0:1], t_ss, 1, 1.0 / D)
            S.op("act", lambda e: e.activation(out=xnb[:], in_=xb[:], func=AF.Copy, scale=rs[:, 0:1]),
                 reads=[t_xb, t_rs], writes=[t_xn])
            for half in range(2):
                tb, t_tb = Tb[half]
                tbv = tb[:].bitcast(BF16).rearrange("p (k c) -> p k c", c=128)

                def tr(e, half=half, tbv=tbv):
                    rr = None
                    for k in range(8):
                        kc = half * 8 + k
                        rr = e.transpose(out=tbv[:, k, :], in_=xnb[:, kc * 128:(kc + 1) * 128], identity=identb[:])
                    return rr
                S.op("pe", tr, reads=[t_xn, t_identb], writes=[t_tb])

                def ev(e, half=half, tbv=tbv):
                    rr = None
                    for k in range(8):
                        kc = half * 8 + k
                        rr = e.tensor_scalar(out=dst[:, kc, :], in0=tbv[:, k, :], scalar1=gT[li][:, kc, r:r + 1],
                                             scalar2=modT[li][:, kc, r:r + 1], op0=ALU.mult, op1=ALU.add)
                    return rr
                def ev_act(e, half=half, tbv=tbv):
                    rr = None
                    for k in range(8):
                        kc = half * 8 + k
                        rr = e.activation(out=dst[:, kc, :], in_=tbv[:, k, :], func=AF.Identity,
                                          scale=gT[li][:, kc, r:r + 1], bias=modT[li][:, kc, r:r + 1])
                    return rr
                if half == 0:
                    S.op("dve", ev, reads=[t_tb, t_gT[li], t_mod[li]], writes=[t_dst])
                else:
                    S.op("act", ev_act, reads=[t_tb, t_gT[li], t_mod[li]], writes=[t_dst])

        def proj(lhs, t_lhs, wb, t_wb):
            ab, t_ab = aring.next()
            wv = wb[:].rearrange("p (k c) -> p k c", c=512)

            def mm(e):
                rr = None
                for kc in range(NKC):
                    rr = e.matmul(ab[:], lhsT=lhs[:, kc, :], rhs=wv[:, kc, :], start=(kc == 0), stop=(kc == NKC - 1))
                return rr
            S.op("pe", mm, reads=[t_lhs, t_wb], writes=[t_ab])
            return ab, t_ab

        def head_rstd(src_ap, t_src, nh):
            sq, t_sq = tring.next()
            S.op("act", lambda e: e.activation(out=sq[:, 0:nh * 128], in_=src_ap, func=AF.Square), reads=[t_src], writes=[t_sq])
            ss, t_ss = smring.next()
            S.op("dve", lambda e: e.tensor_reduce(out=ss[:, 0:nh], in_=sq[:, 0:nh * 128].rearrange("p (h d) -> p h d", d=128),
                                                  axis=AX.X, op=ALU.add), reads=[t_sq], writes=[t_ss])
            return rstd_small(ss[:, 0:nh], t_ss, nh, 1.0 / 128)

        def apply_rope(src, t_src, nh, rt, t_rt, dst, t_dst):
            n = nh * 128
            t1, t_t1 = tring.next()
            cosb = rt[:, 0:128].unsqueeze(1).broadcast_to([128, nh, 128])
            S.op("dve", lambda e: e.tensor_tensor(out=t1[:, 0:n].rearrange("p (h d) -> p h d", d=128),
                                                  in0=src[:, 0:n].rearrange("p (h d) -> p h d", d=128), in1=cosb, op=ALU.mult),
                 reads=[t_src, t_rt], writes=[t_t1])
            rot, t_rot = tring.next()
            sv = src[:, 0:n].rearrange("p (h b t d) -> p h b t d", b=2, t=2, d=32)
            rv = rot[:, 0:n].rearrange("p (h b t d) -> p h b t d", b=2, t=2, d=32)
            t1v = t1[:, 0:n].rearrange("p (h b t d) -> p h b t d", b=2, t=2, d=32)
            dv = dst[:, 0:n].rearrange("p (h b t d) -> p h b t d", b=2, t=2, d=32)
            sinv = rt[:, 128:256].rearrange("p (b t d) -> p b t d", b=2, t=2)

            def rotf(e):
                e.tensor_tensor(out=rv[:, :, :, 0, :], in0=sv[:, :, :, 1, :],
                                in1=sinv[:, :, 0, :].unsqueeze(1).broadcast_to([128, nh, 2, 32]), op=ALU.mult)
                return e.tensor_tensor(out=rv[:, :, :, 1, :], in0=sv[:, :, :, 0, :],
                                       in1=sinv[:, :, 1, :].unsqueeze(1).broadcast_to([128, nh, 2, 32]), op=ALU.mult)
            S.op("dve", rotf, reads=[t_src, t_rt], writes=[t_rot])

            def fin(e):
                e.tensor_tensor(out=dv[:, :, :, 0, :], in0=t1v[:, :, :, 0, :], in1=rv[:, :, :, 0, :], op=ALU.subtract)
                return e.tensor_tensor(out=dv[:, :, :, 1, :], in0=t1v[:, :, :, 1, :], in1=rv[:, :, :, 1, :], op=ALU.add)
            S.op("dve", fin, reads=[t_t1, t_rot], writes=[t_dst])

        out_toks = []

        def run_layer(li, l, src_ap, dst_ap, t_srcx, p2_tiles, last_in_prog):

            S.dma("sp", lambda e, l=l: e.dma_start(out=wsg_f[:], in_=w_sgu[l].rearrange("g p q -> p g q")), "wsgf", writes=[t_wsgf])
            S.op("dve", lambda e: e.tensor_copy(out=wsg_b[:], in_=wsg_f[:]), reads=[t_wsgf], writes=[t_wsgb])
            tb, t_tb = Tb[0]
            tbv0 = tb[:].bitcast(BF16).rearrange("p (k c) -> p k c", c=128)

            def trw(e, tbv0=tbv0):
                rr = None
                for g in range(8):
                    rr = e.transpose(out=tbv0[:, g, :], in_=wsg_b[:, g, :], identity=identb[:])
                return rr
            S.op("pe", trw, reads=[t_wsgb, t_identb], writes=[t_tb])
            S.op("dve", lambda e, tbv0=tbv0: e.tensor_copy(out=wsguT[:], in_=tbv0[:]), reads=[t_tb], writes=[t_wsguT])
            small_T(b_sgu[l], 8, bsguT[:], t_bsguT)
            S.dma("sp", lambda e, l=l: e.dma_start(out=vnw_b[:], in_=v_norm_w[l].partition_broadcast(128)), "vnw", writes=[t_vnw])
            S.dma("sp", lambda e, l=l: e.dma_start(out=qnw_b[:], in_=q_norm_w[l].partition_broadcast(128)), "qnw", writes=[t_qnw])
            S.dma("sp", lambda e, l=l: e.dma_start(out=knw_b[:], in_=k_norm_w[l].partition_broadcast(128)), "knw", writes=[t_knw])
            mq, t_mq = smring.next()
            S.op("dve", lambda e, mq=mq: e.tensor_reduce(out=mq[:, 0:1], in_=qnw_b[:], axis=AX.X, op=ALU.max, apply_absolute_value=True),
                 reads=[t_qnw], writes=[t_mq])
            S.op("dve", lambda e, mq=mq: e.tensor_reduce(out=mq[:, 1:2], in_=knw_b[:], axis=AX.X, op=ALU.max, apply_absolute_value=True),
                 reads=[t_knw], writes=[t_mq])
            S.op("dve", lambda e, mq=mq: e.tensor_tensor(out=negC[:, 0:1], in0=mq[:, 0:1], in1=mq[:, 1:2], op=ALU.mult),
                 reads=[t_mq], writes=[t_negC])
            S.op("dve", lambda e: e.tensor_scalar(out=negC[:, 0:1], in0=negC[:, 0:1], scalar1=-float(np.sqrt(128.0)), scalar2=None, op0=ALU.mult),
                 writes=[t_negC])
            def build_gate(r, li=li):
                for q4 in range(4):
                    for k in range(4):
                        kc = q4 * 4 + k
                        dg, t_dg = dring.next()
                        S.op("dve", lambda e, dg=dg, kc=kc, r=r: e.tensor_scalar(
                            out=dg[:], in0=identf[:], scalar1=modT[li][:, 32 + kc, r:r + 1], scalar2=None, op0=ALU.mult),
                            reads=[t_identf, t_mod[li]], writes=[t_dg])
                        S.op("pe", lambda e, dg=dg, k=k: e.matmul(Sb[:, k * 128:(k + 1) * 128], lhsT=onesf[:], rhs=dg[:], start=True, stop=True),
                             reads=[t_dg, t_onesf], writes=[t_Sb])
                    S.op("dve", lambda e, q4=q4: e.tensor_copy(out=gate_b[:, q4 * 512:(q4 + 1) * 512], in_=Sb[:]),
                         reads=[t_Sb], writes=[t_gate])
            build_gate(0)

            wkv, t_wkv, wkey = wring.next()
            S.dma("sp", lambda e, wkv=wkv: e.dma_start(out=wkv[:], in_=wbi[li][8]), wkey, reads=[t_wci[li]], writes=[t_wkv])
            for t in p1_tiles:
                hb, t_hb = h1ring.next()
                make_hT(src_ap, t_srcx, t, li, hb, t_hb)
                if t == p1_tiles[0] and li == 0:
                    dbg("hT_p1", hb[:], t_hb)
                ab, t_ab = proj(hb, t_hb, wkv, t_wkv)
                rk, t_rk = head_rstd(ab[:, 0:256], t_ab, 2)
                kn, t_kn = tring.next()

                def knf(e, ab=ab, rk=rk, kn=kn):
                    rr = None
                    for h in range(2):
                        rr = e.scalar_tensor_tensor(out=kn[:, h * 128:(h + 1) * 128], in0=ab[:, h * 128:(h + 1) * 128],
                                                    scalar=rk[:, h:h + 1], in1=knw_b[:], op0=ALU.mult, op1=ALU.mult)
                    return rr
                S.op("dve", knf, reads=[t_ab, t_rk, t_knw], writes=[t_kn])
                kb, t_kb = qbring.next()
                if is_ctx(t):
                    S.op("dve", lambda e, kb=kb, kn=kn: e.tensor_copy(out=kb[:, 0:256], in_=kn[:, 0:256]), reads=[t_kn], writes=[t_kb])
                else:
                    rt, t_rt, rkey = r1ring.next()
                    lt = lat_index(t)
                    S.dma("sp", lambda e, rt=rt, lt=lt: e.dma_start(out=rt[:], in_=rope[lt * 128:(lt + 1) * 128, :]), rkey, writes=[t_rt])
                    apply_rope(kn, t_kn, 2, rt, t_rt, kb, t_kb)
                tb, t_tb = Tb[0]
                tbv = tb[:].bitcast(BF16).rearrange("p (k c) -> p k c", c=128)

                def trk(e, kb=kb, tbv=tbv):
                    e.transpose(out=tbv[:, 0, :], in_=kb[:, 0:128], identity=identb[:])
                    return e.transpose(out=tbv[:, 1, :], in_=kb[:, 128:256], identity=identb[:])
                S.op("pe", trk, reads=[t_kb, t_identb], writes=[t_tb])
                S.op("act", lambda e, tbv=tbv, t=t: e.copy(out=KT[:, :, t * 128:(t + 1) * 128], in_=tbv[:, 0:2, :]),
                     reads=[t_tb], writes=[t_K[t]])
                S.op("act", lambda e, ab=ab, t=t: e.copy(out=VA[:, t, :, 0:128], in_=ab[:, 256:512].rearrange("p (h d) -> p h d", d=128)),
                     reads=[t_ab, t_Vones], writes=[t_V[t]])

            if li == 0:
                t0_ = p1_tiles[0]
                dbg("KT0", KT[:, :, t0_ * 128:(t0_ + 1) * 128], t_K[t0_])
                dbg("VA0", VA[:, t0_, :, :], t_V[t0_])
                dbg("negC", negC[:], t_negC)
                dbg("gate_b", gate_b[:], t_gate)
                dbg("wsguT", wsguT[:], t_wsguT)
            lat_tiles = [t for t in p2_tiles if not is_ctx(t)]
            ctx_tiles = [t for t in p2_tiles if is_ctx(t)]
            groups = [lat_tiles[i:i + TG] for i in range(0, len(lat_tiles), TG)]
            if ctx_tiles:
                groups.append(ctx_tiles)
            t_xs_next = [Tok() for _ in range(NT_ALL)]
            for grp in groups:
                ng = len(grp)
                rflag = 1 if is_ctx(grp[0]) else 0
                if rflag:
                    build_gate(1)
                ktiles = [t for t in p1_tiles if is_ctx(t)] if rflag else list(p1_tiles)
                for i, t in enumerate(grp):
                    make_hT(src_ap, t_srcx, t, li, hT[i], t_hT[i])
                    if not rflag:
                        lt = lat_index(t)
                        S.dma("sp", lambda e, i=i, lt=lt: e.dma_start(out=ropeT[i][:], in_=rope[lt * 128:(lt + 1) * 128, :]),
                              f"ropeT{i}", writes=[t_rope[i]])
                order = [(2, "v", 0), (0, "u", 0), (4, "za", 0), (3, "v", 1), (1, "u", 1), (5, "za", 1),
                         (6, "q", 0), (7, "q", 1), (9, "zb", 0), (10, "zb", 1)]
                pending_backs = []
                for (cb, kind, hh) in order:
                    wb, t_wb, wkey = wring.next()
                    S.dma("sp", lambda e, wb=wb, cb=cb: e.dma_start(out=wb[:], in_=wbi[li][cb]), wkey, reads=[t_wci[li]], writes=[t_wb])
                    for i, t in enumerate(grp):
                        ab, t_ab = proj(hT[i], t_hT[i], wb, t_wb)
                        if pending_backs:
                            pending_backs.pop(0)()
                        back = None
                        if kind == "v":
                            gv, t_gv = tring.next()
                            S.op("act", lambda e, gv=gv, ab=ab: e.activation(out=gv[:], in_=ab[:], func=AF.Gelu_apprx_tanh),
                                 reads=[t_ab], writes=[t_gv])
                            rv, t_rv = head_rstd(gv[:], t_gv, 4)
                            v1, t_v1 = tring.next()
                            S.op("dve", lambda e, v1=v1, gv=gv, rv=rv: e.tensor_tensor(
                                out=v1[:].rearrange("p (g d) -> p g d", d=128), in0=gv[:].rearrange("p (g d) -> p g d", d=128),
                                in1=rv.unsqueeze(2).broadcast_to([128, 4, 128]), op=ALU.mult), reads=[t_gv, t_rv], writes=[t_v1])
                            vb, t_vb = vnring.next()
                            S.op("dve", lambda e, vb=vb, v1=v1, hh=hh: e.tensor_tensor(
                                out=vb[:], in0=v1[:], in1=vnw_b[:, hh * 512:(hh + 1) * 512], op=ALU.mult),
                                reads=[t_v1, t_vnw], writes=[t_vb])

                            def back(vb=vb, t_vb=t_vb, hh=hh, i=i):
                                def sgu(e):
                                    rr = None
                                    for g in range(4):
                                        rr = e.matmul(Sb[:, g * 128:(g + 1) * 128], lhsT=wsguT[:, 4 * hh + g, :],
                                                      rhs=vb[:, g * 128:(g + 1) * 128], start=True, stop=True)
                                    return rr
                                S.op("pe", sgu, reads=[t_vb, t_wsguT], writes=[t_Sb])
                                S.op("dve", lambda e: e.tensor_tensor(
                                    out=s_sb[i][:].rearrange("p (g d) -> p g d", d=128), in0=Sb[:].rearrange("p (g d) -> p g d", d=128),
                                    in1=bsguT[:, 4 * hh:4 * hh + 4].unsqueeze(2).broadcast_to([128, 4, 128]), op=ALU.add),
                                    reads=[t_Sb, t_bsguT], writes=[t_s[i]])
                        elif kind == "u":
                            gu, t_gu = tring.next()
                            S.op("act", lambda e, gu=gu, ab=ab: e.activation(out=gu[:], in_=ab[:], func=AF.Gelu_apprx_tanh),
                                 reads=[t_ab], writes=[t_gu])
                            S.op("dve", lambda e, gu=gu, i=i: e.tensor_tensor(out=s_sb[i][:], in0=gu[:], in1=s_sb[i][:], op=ALU.mult),
                                 reads=[t_gu], writes=[t_s[i]])
                        elif kind == "za":
                            sz, t_sz = tring.next()
                            S.op("act", lambda e, sz=sz, ab=ab: e.activation(out=sz[:], in_=ab[:], func=AF.Silu), reads=[t_ab], writes=[t_sz])
                            S.op("dve", lambda e, sz=sz, i=i, hh=hh: e.tensor_tensor(
                                out=gated[i][:, hh * 512:(hh + 1) * 512], in0=sz[:], in1=s_sb[i][:], op=ALU.mult),
                                reads=[t_sz, t_s[i]], writes=[t_gated[i]])
                        elif kind == "q":
                            rq, t_rq = head_rstd(ab[:], t_ab, 4)
                            q1, t_q1 = tring.next()
                            S.op("dve", lambda e, q1=q1, ab=ab, rq=rq: e.tensor_tensor(
                                out=q1[:].rearrange("p (g d) -> p g d", d=128), in0=ab[:].rearrange("p (g d) -> p g d", d=128),
                                in1=rq.unsqueeze(2).broadcast_to([128, 4, 128]), op=ALU.mult), reads=[t_ab, t_rq], writes=[t_q1])
                            S.op("dve", lambda e, q1=q1: e.tensor_tensor(
                                out=q1[:].rearrange("p (g d) -> p g d", d=128), in0=q1[:].rearrange("p (g d) -> p g d", d=128),
                                in1=qnw_b[:].unsqueeze(1).broadcast_to([128, 4, 128]), op=ALU.mult), reads=[t_qnw], writes=[t_q1])
                            qb, t_qb = qbring.next()
                            if rflag:
                                S.op("dve", lambda e, qb=qb, q1=q1: e.tensor_copy(out=qb[:], in_=q1[:]), reads=[t_q1], writes=[t_qb])
                            else:
                                apply_rope(q1, t_q1, 4, ropeT[i], t_rope[i], qb, t_qb)

                            def back(qb=qb, t_qb=t_qb, hh=hh, i=i):
                                tb, t_tb = Tb[(i + hh) % 2]
                                tbv = tb[:].bitcast(BF16).rearrange("p (k c) -> p k c", c=128)

                                def trq(e):
                                    rr = None
                                    for h in range(4):
                                        rr = e.transpose(out=tbv[:, h, :], in_=qb[:, h * 128:(h + 1) * 128], identity=identb[:])
                                    return rr
                                S.op("pe", trq, reads=[t_qb, t_identb], writes=[t_tb])
                                S.op("act", lambda e: e.copy(out=qT[i][:, 4 * hh:4 * hh + 4, :], in_=tbv[:, 0:4, :]),
                                     reads=[t_tb], writes=[t_qT[i]])
                        else:
                            S.op("act", lambda e, ab=ab, i=i, hh=hh: e.activation(out=szb[i][:, hh * 512:(hh + 1) * 512], in_=ab[:], func=AF.Silu),
                                 reads=[t_ab], writes=[t_szb[i]])
                        if back is not None:
                            pending_backs.append(back)
                while pending_backs:
                    pending_backs.pop(0)()

                if li == 0 and grp is groups[0]:
                    dbg("gatedA", gated[0][:, 0:1024], t_gated[0])
                    dbg("qT", qT[0][:], t_qT[0])
                    dbg("szb", szb[0][:], t_szb[0])
                inv_sqrt = float(128.0 ** -0.5)
                nk = len(ktiles)
                units = [(i, g, ki, kt) for i in range(ng) for g in range(2) for ki, kt in enumerate(ktiles)]
                LAG = 2
                (o0, t_o0), (o1, t_o1) = Ob

                def attn_front(u):
                    i, g, ki, kt = u
                    sbk, t_sbk = aring.next()
                    S.op("pe", lambda e: e.matmul(
                        sbk[:], lhsT=KT[:, g, kt * 128:(kt + 1) * 128], rhs=qT[i][:, 4 * g:4 * g + 4, :], start=True, stop=True),
                        reads=[t_K[kt], t_qT[i]], writes=[t_sbk])
                    pt, t_pt = ptring.next()
                    S.op("act", lambda e: e.activation(out=pt[:], in_=sbk[:], func=AF.Exp, bias=negC[:, 0:1], scale=inv_sqrt),
                         reads=[t_sbk, t_negC], writes=[t_pt])
                    return pt, t_pt

                def attn_back(u, pt, t_pt):
                    i, g, ki, kt = u

                    def pv(e):
                        rr = None
                        for hq in range(4):
                            if hq < 3:
                                oap = o0[:, hq * 129:hq * 129 + 129]
                                st = (ki == 0 and hq == 0)
                            else:
                                oap = o1[:, 0:129]
                                st = (ki == 0)
                            rr = e.matmul(oap, lhsT=pt[:, hq * 128:(hq + 1) * 128], rhs=VA[:, kt, g, 0:129],
                                          start=st, stop=(ki == nk - 1), skip_group_check=True)
                        return rr
                    S.op("pe", pv, reads=[t_pt, t_V[kt]], writes=[t_o0, t_o1])
                    if ki != nk - 1:
                        return
                    rd, t_rd = smring.next()

                    def rden(e):
                        e.reciprocal(out=rd[:, 0:3], in_=o0[:, 0:387].rearrange("p (h c) -> p h c", c=129)[:, :, 128])
                        return e.reciprocal(out=rd[:, 3:4], in_=o1[:, 128:129])
                    S.op("dve", rden, reads=[t_o0, t_o1], writes=[t_rd])

                    def onorm(e):
                        rr = None
                        for hq in range(4):
                            h = 4 * g + hq
                            oap = o0[:, hq * 129:hq * 129 + 128] if hq < 3 else o1[:, 0:128]
                            rr = e.scalar_tensor_tensor(out=gated[i][:, 1024 + h * 128:1024 + (h + 1) * 128], in0=oap,
                                                        scalar=rd[:, hq:hq + 1], in1=szb[i][:, h * 128:(h + 1) * 128],
                                                        op0=ALU.mult, op1=ALU.mult)
                        return rr
                    S.op("dve", onorm, reads=[t_o0, t_o1, t_rd, t_szb[i]], writes=[t_gated[i]])

                pend = []
                for idx in range(len(units) + LAG):
                    if idx < len(units):
                        pend.append(attn_front(units[idx]))
                    if idx >= LAG:
                        attn_back(units[idx - LAG], *pend[idx - LAG])

                if li == 0 and grp is groups[0]:
                    dbg("gated", gated[0][:], t_gated[0])
                for i, t in enumerate(grp):
                    for half in range(2):
                        tb, t_tb = Tb[half]
                        tbv = tb[:].bitcast(BF16).rearrange("p (k c) -> p k c", c=128)

                        def trg(e, half=half, tbv=tbv, i=i):
                            rr = None
                            for k in range(8):
                                kc = half * 8 + k
                                rr = e.transpose(out=tbv[:, k, :], in_=gated[i][:, kc * 128:(kc + 1) * 128], identity=identb[:])
                            return rr
                        S.op("pe", trg, reads=[t_gated[i], t_identb], writes=[t_tb])
                        S.op("act", lambda e, half=half, tbv=tbv, i=i: e.copy(out=hT[i][:, half * 8:(half + 1) * 8, :], in_=tbv[:]),
                             reads=[t_tb], writes=[t_hT[i]])

                for cb in range(4):
                    wb, t_wb, wkey = wring.next()
                    S.dma("sp", lambda e, wb=wb, cb=cb: e.dma_start(out=wb[:], in_=wbo[li][cb]), wkey, reads=[t_wco[li]], writes=[t_wb])
                    for i, t in enumerate(grp):
                        xpb, t_xp, xkey = xpring.next()
                        S.dma("sp", lambda e, xpb=xpb, t=t, cb=cb: e.dma_start(
                            out=xpb[:], in_=src_ap[t * 128:(t + 1) * 128, cb * 512:(cb + 1) * 512]), xkey, reads=[t_srcx[t]], writes=[t_xp])
                        ab, t_ab = proj(hT[i], t_hT[i], wb, t_wb)
                        yg, t_yg = tring.next()
                        S.op("dve", lambda e, yg=yg, ab=ab, cb=cb: e.tensor_tensor(
                            out=yg[:], in0=ab[:], in1=gate_b[:, cb * 512:(cb + 1) * 512], op=ALU.mult),
                            reads=[t_ab, t_gate], writes=[t_yg])
                        S.op("pool", lambda e, yg=yg, xpb=xpb: e.tensor_tensor(out=xpb[:], in0=xpb[:], in1=yg[:], op=ALU.add),
                             reads=[t_yg], writes=[t_xp])
                        if last_in_prog:
                            if final:
                                drow = t
                            else:
                                drow = t
                        else:
                            drow = t
                        S.dma("pool", lambda e, xpb=xpb, drow=drow, cb=cb: e.dma_start(
                            out=dst_ap[drow * 128:(drow + 1) * 128, cb * 512:(cb + 1) * 512], in_=xpb[:]), xkey,
                            reads=[t_xp], writes=[t_xs_next[t]])
                        if last_in_prog:
                            out_toks.append(t_xp)
            return t_xs_next

        t_xs_all = [Tok() for _ in range(NT_ALL)]
        for li_, l_ in enumerate(layers):
            last_ = (li_ == NL - 1)
            t_xs_all = run_layer(li_, l_, xa if li_ == 0 else xs, out if last_ else xs, t_xs_all,
                                 p2_tiles_per_layer[li_], last_)

        S.final_wait("pool", list({id(t): t for t in out_toks}.values()) + dbg_toks)
        S.emit(block)
    return nc


def _rope_tables(pos):
    rows = (pos // GRID_W).astype(np.float32)
    cols = (pos % GRID_W).astype(np.float32)
    inv_freq = (np.float32(10000.0) ** (-np.arange(0, 64, 2, dtype=np.float32) / np.float32(64))).astype(np.float32)
    ang_r = rows[:, None] * inv_freq[None, :]
    ang_c = cols[:, None] * inv_freq[None, :]
    ang = np.concatenate([ang_r, ang_r, ang_c, ang_c], axis=-1).astype(np.float32)
    return np.concatenate([np.cos(ang), np.sin(ang)], axis=-1).astype(np.float32)


_PROG_CACHE = {}


def _get_prog(key, *args):
    if key not in _PROG_CACHE:
        _PROG_CACHE[key] = build(*args)
    return _PROG_CACHE[key]


def _common_inputs(c, c_ctx, norm_w, w_mod, b_mod, w_in, w_sgu, b_sgu, v_norm_w, q_norm_w, k_norm_w, w_out):
    f = lambda a: np.ascontiguousarray(np.asarray(a, dtype=np.float32))
    shared = {
        "identb": np.eye(128, dtype=np.float32).astype(ml_dtypes.bfloat16),
        "identf": np.eye(128, dtype=np.float32),
        "w_mod": f(w_mod), "b_mod": f(b_mod).reshape(2, 48, 128), "norm_w": f(norm_w).reshape(2, 16, 128),
        "w_in": f(w_in), "w_out": f(w_out), "w_sgu": f(w_sgu), "b_sgu": f(b_sgu),
        "v_norm_w": f(v_norm_w).reshape(2, 1024), "q_norm_w": f(q_norm_w), "k_norm_w": f(k_norm_w),
    }
    return shared


def kernel(x, c, ctx, c_ctx, norm_w, w_mod, b_mod, w_in, w_sgu, b_sgu, v_norm_w, q_norm_w, k_norm_w, w_out):
    x = np.asarray(x, dtype=np.float32)
    ctx = np.asarray(ctx, dtype=np.float32)
    c = np.asarray(c, dtype=np.float32)
    c_ctx = np.asarray(c_ctx, dtype=np.float32)
    shared = _common_inputs(c, c_ctx, norm_w, w_mod, b_mod, w_in, w_sgu, b_sgu, v_norm_w, q_norm_w, k_norm_w, w_out)
    H = SEQ // 2
    CH = CTX // 2

    def core_maps(xfull, cfull):
        maps = []
        for core in range(8):
            b, hf = divmod(core, 2)
            o, p = hf, 1 - hf
            xa = np.concatenate([xfull[b, o * H:(o + 1) * H], cfull[b, o * CH:(o + 1) * CH],
                                 xfull[b, p * H:(p + 1) * H], cfull[b, p * CH:(p + 1) * CH]], axis=0)
            pos = np.concatenate([np.arange(o * H, (o + 1) * H), np.arange(p * H, (p + 1) * H)])
            m = dict(shared)
            m["xa"] = np.ascontiguousarray(xa)
            m["rope"] = _rope_tables(pos)
            m["c2"] = np.ascontiguousarray(np.stack([c[b], c_ctx], 0).reshape(32, 128))
            maps.append(m)
        return maps

    all_tiles = list(range(NT_ALL))
    own_lat = list(range(16))
    if MODE == "fused":
        nc = _get_prog("fused", [0, 1], True, all_tiles, [all_tiles, own_lat], 16)
        res = run_bass_kernel_spmd(nc, core_maps(x, ctx), core_ids=list(range(8)))
        outs = [r["out"] for r in res.results]
    else:
        ncA = _get_prog("L0", [0], False, all_tiles, [list(range(17))], 17)
        resA = run_bass_kernel_spmd(ncA, core_maps(x, ctx), core_ids=list(range(8)))
        x1 = np.empty_like(x)
        ctx1 = np.empty_like(ctx)
        for core in range(8):
            b, hf = divmod(core, 2)
            xn_ = resA.results[core]["xnext"]
            x1[b, hf * H:(hf + 1) * H] = xn_[0:H]
            ctx1[b, hf * CH:(hf + 1) * CH] = xn_[H:H + CH]
        ncB = _get_prog("L1", [1], True, all_tiles, [own_lat], 16)
        resB = run_bass_kernel_spmd(ncB, core_maps(x1, ctx1), core_ids=list(range(8)))
        outs = [r["out"] for r in resB.results]
    y = np.empty((4, SEQ, D), dtype=np.float32)
    for core in range(8):
        b, hf = divmod(core, 2)
        y[b, hf * H:(hf + 1) * H] = outs[core]
    return y
```

```python
import numpy as np
from contextlib import ExitStack
import ml_dtypes
import concourse.bass as bass
import concourse.mybir as mybir
from concourse.bass_utils import run_bass_kernel_spmd

F32 = mybir.dt.float32
BF16 = mybir.dt.bfloat16
AF = mybir.ActivationFunctionType
ALU = mybir.AluOpType
AX = mybir.AxisListType

D = 2048
NKC = 16
DIN = 5632
NCB = 11
SEQ = 4096
CTX = 256
GRID_W = 64
EPS = 1e-6
TG = 4
NT_ALL = 34
DEBUG = False
MODE = "fused"


class Tok:
    __slots__ = ("w", "r")

    def __init__(self):
        self.w = None
        self.r = {}


class Sched:
    EPOCH = 20000

    def __init__(self, nc, es):
        self.nc = nc
        self.es = es
        self.names = ["pe", "act", "dve", "pool", "sp"]
        self.q = {k: [] for k in self.names}
        self.cnt = {k: 0 for k in self.names}
        self.esems = {k: [] for k in self.names}
        self.dsems = {}
        self.dcnt = {}
        self.waited = {k: {} for k in self.names}

    def _newsem(self, name):
        return self.es.enter_context(self.nc.semaphore(name))

    def _eng_event(self, eng):
        c = self.cnt[eng]
        ep, v = divmod(c, self.EPOCH)
        while len(self.esems[eng]) <= ep:
            self.esems[eng].append(self._newsem(f"e_{eng}_{len(self.esems[eng])}"))
        self.cnt[eng] = c + 1
        return (self.esems[eng][ep], v + 1, eng)

    def _collect(self, eng, reads, writes):
        evs = []
        for t in reads:
            if t.w is not None:
                evs.append(t.w)
        for t in writes:
            if t.w is not None:
                evs.append(t.w)
            evs.extend(t.r.values())
        waits = {}
        for (sem, val, e) in evs:
            if e is not None and e == eng and eng == "pe":
                continue
            key = id(sem)
            if self.waited[eng].get(key, 0) >= val:
                continue
            if key not in waits or waits[key][1] < val:
                waits[key] = (sem, val)
        for key, (sem, val) in waits.items():
            self.waited[eng][key] = val
        return list(waits.values())

    def _mark(self, ev, reads, writes):
        k = id(ev[0])
        for t in reads:
            old = t.r.get(k)
            if old is None or old[1] < ev[1]:
                t.r[k] = ev
        for t in writes:
            t.w = ev
            t.r = {}

    def op(self, eng, fn, reads=(), writes=()):
        waits = self._collect(eng, reads, writes)
        ev = self._eng_event(eng)
        self.q[eng].append((waits, fn, ev[0], 1))
        self._mark(ev, reads, writes)

    def dma(self, eng, fn, key, reads=(), writes=(), n=1, inc=16):
        waits = self._collect(eng, reads, writes)
        if key not in self.dsems:
            self.dsems[key] = self._newsem(f"d_{key}")
            self.dcnt[key] = 0
        self.dcnt[key] += inc * n
        ev = (self.dsems[key], self.dcnt[key], None)
        self.q[eng].append((waits, fn, self.dsems[key], inc))
        self._mark(ev, reads, writes)

    def final_wait(self, eng, toks):
        waits = self._collect(eng, toks, toks)
        self.q[eng].append((waits, None, None, 0))

    def emit(self, block):
        def mk(name):
            def body(e):
                for (waits, fn, sem, inc) in self.q[name]:
                    for (s, v) in waits:
                        e.wait_ge(s, v)
                    if fn is None:
                        continue
                    r = fn(e)
                    if isinstance(r, (list, tuple)):
                        for ins in r:
                            ins.then_inc(sem, inc)
                    else:
                        r.then_inc(sem, inc)
            return body
        block.tensor(mk("pe"))
        block.scalar(mk("act"))
        block.vector(mk("dve"))
        block.gpsimd(mk("pool"))
        block.sync(mk("sp"))


class Ring:
    def __init__(self, items):
        self.items = items
        self.i = 0

    def next(self):
        it = self.items[self.i % len(self.items)]
        self.i += 1
        return it


def is_ctx(t):
    return t % 17 == 16


def lat_index(t):
    return (t // 17) * 16 + (t % 17)


PAIRS = [[0, 1], [2, 3], [4, 5], [6, 7]]
KVW = 2 * 17 * 128 + 17 * 260


def build(layers, final, p1_tiles, p2_tiles_per_layer, n_out_tiles, cc=False):
    nc = bass.Bass("TRN2", target_bir_lowering=False)
    NL = len(layers)

    def din(name, shape, dt=F32):
        return nc.dram_tensor(name, shape, dt, kind="ExternalInput").ap()

    NX = 17 if cc else NT_ALL
    xa = din("xa", [NX * 128, D])
    rope = din("rope", [(16 if cc else 32) * 128, 256])
    c2 = din("c2", [32, 128])
    identb_in = din("identb", [128, 128], BF16)
    identf_in = din("identf", [128, 128])
    w_mod = din("w_mod", [2, D, 3 * D])
    b_mod = din("b_mod", [2, 48, 128])
    norm_w = din("norm_w", [2, 16, 128])
    w_in = din("w_in", [2, D, DIN])
    w_out = din("w_out", [2, D, D])
    w_sgu = din("w_sgu", [2, 8, 128, 128])
    b_sgu = din("b_sgu", [2, 8, 128])
    v_norm_w = din("v_norm_w", [2, 1024])
    q_norm_w = din("q_norm_w", [2, 128])
    k_norm_w = din("k_norm_w", [2, 128])
    if final:
        out = nc.dram_tensor("out", [16 * 128, D], F32, kind="ExternalOutput").ap()
    else:
        out = nc.dram_tensor("xnext", [n_out_tiles * 128, D], F32, kind="ExternalOutput").ap()
    xs = nc.dram_tensor("xs", [NX * 128, D], F32).ap() if NL > 1 else None
    if cc:
        kv_send = [nc.dram_tensor(f"kv_send{i}", [128, KVW], BF16).ap() for i in range(NL)]
        kv_recv = [nc.dram_tensor(f"kv_recv{i}", [256, KVW], BF16).ap() for i in range(NL)]
    wbi = [nc.dram_tensor(f"wbi{l}", [NCB, 128, NKC * 512], BF16).ap() for l in layers]
    wbo = [nc.dram_tensor(f"wbo{l}", [4, 128, NKC * 512], BF16).ap() for l in layers]

    with ExitStack() as es:
        def sb(name, shape, dt):
            return es.enter_context(nc.sbuf_tensor(name, shape, dt))

        S = Sched(nc, es)
        identb = sb("identb_sb", [128, 128], BF16); t_identb = Tok()
        identf = sb("identf_sb", [128, 128], F32); t_identf = Tok()
        onesf = sb("onesf", [128, 128], F32); t_onesf = Tok()
        KT = sb("KT", [128, 2, NT_ALL * 128], BF16)
        VA = sb("VA", [128, NT_ALL, 2, 130], BF16)
        if cc:
            tkb = [Tok(), Tok()]
            tvb = [Tok(), Tok()]
            t_K = [tkb[kt // 17] for kt in range(NT_ALL)]
            t_V = [tvb[kt // 17] for kt in range(NT_ALL)]
            kst = [sb(f"kst{i}", [128, 2, 128], BF16) for i in range(2)]
            kstring = Ring([(kst[i], Tok(), f"kst{i}") for i in range(2)])
            vst = [sb(f"vst{i}", [128, 2, 130], BF16) for i in range(2)]
            vstring = Ring([(vst[i], Tok(), f"vst{i}") for i in range(2)])
        else:
            t_K = [Tok() for _ in range(NT_ALL)]
            t_V = [Tok() for _ in range(NT_ALL)]
        t_Vones = Tok()
        wbuf = [sb(f"wbuf{i}", [128, NKC * 512], BF16) for i in range(2)]
        t_wbuf = [Tok() for _ in range(2)]
        wring = Ring(list(zip(wbuf, t_wbuf, ["wbuf0", "wbuf1"])))
        xbuf = [sb(f"xbuf{i}", [128, D], F32) for i in range(2)]
        xring = Ring([(xbuf[i], Tok(), f"xbuf{i}") for i in range(2)])
        xn = [sb(f"xn{i}", [128, D], BF16) for i in range(2)]
        xnring = Ring([(xn[i], Tok()) for i in range(2)])
        hT = [sb(f"hT{i}", [128, NKC, 128], BF16) for i in range(TG)]
        t_hT = [Tok() for _ in range(TG)]
        h1ring = Ring([(hT[i], t_hT[i]) for i in range(2)])
        s_sb = [sb(f"s_sb{i}", [128, 512], F32) for i in range(TG)]
        t_s = [Tok() for _ in range(TG)]
        gated = [sb(f"gated{i}", [128, D], BF16) for i in range(TG)]
        t_gated = [Tok() for _ in range(TG)]
        qT = [sb(f"qT{i}", [128, 8, 128], BF16) for i in range(TG)]
        t_qT = [Tok() for _ in range(TG)]
        szb = [sb(f"szb{i}", [128, 1024], BF16) for i in range(TG)]
        t_szb = [Tok() for _ in range(TG)]
        ropeT = [sb(f"ropeT{i}", [128, 256], F32) for i in range(TG)]
        t_rope = [Tok() for _ in range(TG)]
        r1ring = Ring([(ropeT[i], t_rope[i], f"ropeT{i}") for i in range(2)])
        tmps = [sb(f"tmp{i}", [128, 512], F32) for i in range(6)]
        tring = Ring([(tmps[i], Tok()) for i in range(6)])
        vnb = [sb(f"vnb{i}", [128, 512], BF16) for i in range(3)]
        vnring = Ring([(vnb[i], Tok()) for i in range(3)])
        qbf = [sb(f"qbf{i}", [128, 512], BF16) for i in range(3)]
        qbring = Ring([(qbf[i], Tok()) for i in range(3)])
        PT = [sb(f"PT{i}", [128, 512], BF16) for i in range(4)]
        ptring = Ring([(PT[i], Tok()) for i in range(4)])
        xp = [sb(f"xp{i}", [128, 512], F32) for i in range(4)]
        xpring = Ring([(xp[i], Tok(), f"xp{i}") for i in range(4)])
        gate_b = sb("gate_b", [128, D], F32)
        t_gate = Tok()
        small = sb("small", [128, 64], F32)
        smring = Ring([(small[:, 8 * i:8 * i + 8], Tok()) for i in range(8)])
        cT = sb("cT", [128, 32], F32); t_cT = Tok()
        modT = [sb(f"modT{i}", [128, 48, 2], F32) for i in range(NL)]
        t_mod = [Tok() for _ in range(NL)]
        gT = [sb(f"gT{i}", [128, NKC, 2], F32) for i in range(NL)]
        t_gT = [Tok() for _ in range(NL)]
        nwT = sb("nwT", [128, 16], F32); t_nwT = Tok()
        bmT = sb("bmT", [128, 48], F32); t_bmT = Tok()
        rows = sb("rows", [48, 128], F32); t_rows = Tok()
        wsg_f = sb("wsg_f", [128, 8, 128], F32); t_wsgf = Tok()
        wsg_b = sb("wsg_b", [128, 8, 128], BF16); t_wsgb = Tok()
        wsguT = sb("wsguT", [128, 8, 128], BF16); t_wsguT = Tok()
        bsguT = sb("bsguT", [128, 8], F32); t_bsguT = Tok()
        vnw_b = sb("vnw_b", [128, 1024], F32); t_vnw = Tok()
        qnw_b = sb("qnw_b", [128, 128], F32); t_qnw = Tok()
        knw_b = sb("knw_b", [128, 128], F32); t_knw = Tok()
        negC = sb("negC", [128, 2], F32); t_negC = Tok()
        diag = [sb(f"diag{i}", [128, 128], F32) for i in range(2)]
        dring = Ring([(diag[i], Tok()) for i in range(2)])
        bank = [es.enter_context(nc.psum_tensor(f"bank{i}", [128, 512], F32)) for i in range(8)]
        t_bank = [Tok() for _ in range(8)]
        aring = Ring([(bank[i], t_bank[i]) for i in range(3)])
        Sb, t_Sb = bank[3], t_bank[3]
        Tb = [(bank[4], t_bank[4]), (bank[5], t_bank[5])]
        Ob = [(bank[6], t_bank[6]), (bank[7], t_bank[7])]

        block = es.enter_context(nc.Block())
        dbg_toks = []

        def dbg(name, ap, tok):
            if not DEBUG:
                return
            d = nc.dram_tensor("dbg_" + name, list(ap.shape), ap.dtype, kind="ExternalOutput").ap()
            tk = Tok()
            S.dma("sp", lambda e: e.dma_start(out=d, in_=ap), "dbg_" + name, reads=[tok], writes=[tk])
            dbg_toks.append(tk)

        S.dma("sp", lambda e: e.dma_start(out=identb[:], in_=identb_in), "c_identb", writes=[t_identb])
        S.dma("sp", lambda e: e.dma_start(out=identf[:], in_=identf_in), "c_identf", writes=[t_identf])
        S.op("dve", lambda e: e.memset(onesf[:], 1.0), writes=[t_onesf])
        if cc:
            def ones_v(e):
                e.memset(vst[0][:, :, 128:130], 1.0)
                return e.memset(vst[1][:, :, 128:130], 1.0)
            S.op("dve", ones_v, writes=[t_Vones])
        else:
            S.op("dve", lambda e: e.memset(VA[:, :, :, 128:130], 1.0), writes=[t_Vones])

        t_wci = [Tok() for _ in range(NL)]
        t_wco = [Tok() for _ in range(NL)]
        def emit_casts(li):
            l = layers[li]

            def cast_in(e):
                res = []
                for kc in range(NKC):
                    for c0 in (0, 6):
                        ncb = 6 if c0 == 0 else 5
                        dst = wbi[li][c0:c0 + ncb, :, kc * 512:(kc + 1) * 512]
                        src = w_in[l, kc * 128:(kc + 1) * 128, c0 * 512:(c0 + ncb) * 512].rearrange("p (cb c) -> cb p c", c=512)
                        res.append(e.dma_start(out=dst, in_=src))
                return res
            S.dma("pool", cast_in, f"cast_in{li}", writes=[t_wci[li]], n=2 * NKC)

            def cast_out(e):
                res = []
                for kc in range(NKC):
                    dst = wbo[li][:, :, kc * 512:(kc + 1) * 512]
                    src = w_out[l, kc * 128:(kc + 1) * 128, :].rearrange("p (cb c) -> cb p c", c=512)
                    res.append(e.dma_start(out=dst, in_=src))
                return res
            S.dma("pool", cast_out, f"cast_out{li}", writes=[t_wco[li]], n=NKC)
        emit_casts(0)

        def small_T(src_ap, n, dst, t_dst, extra_reads=()):
            S.dma("sp", lambda e: e.dma_start(out=rows[0:n, :], in_=src_ap), "rows", writes=[t_rows])
            S.op("pe", lambda e: e.transpose(out=Sb[:, 0:n], in_=rows[0:n, :], identity=identf[0:n, 0:n]),
                 reads=[t_rows, t_identf], writes=[t_Sb])
            S.op("dve", lambda e: e.tensor_copy(out=dst, in_=Sb[:, 0:n]), reads=[t_Sb], writes=[t_dst])

        S.dma("sp", lambda e: e.dma_start(out=rows[0:32, :], in_=c2), "rows", writes=[t_rows])
        S.op("act", lambda e: e.activation(out=rows[0:32, :], in_=rows[0:32, :], func=AF.Silu), reads=[t_rows], writes=[t_rows])
        S.op("pe", lambda e: e.transpose(out=Sb[:, 0:32], in_=rows[0:32, :], identity=identf[0:32, 0:32]),
             reads=[t_rows, t_identf], writes=[t_Sb])
        S.op("dve", lambda e: e.tensor_copy(out=cT[:], in_=Sb[:, 0:32]), reads=[t_Sb], writes=[t_cT])
        cTv = cT[:].rearrange("p (r k) -> p r k", r=2)

        for li, l in enumerate(layers):
            small_T(b_mod[l], 48, bmT[:], t_bmT)
            small_T(norm_w[l], 16, nwT[:], t_nwT)
            for jb in range(24):
                wb, t_wb, wkey = wring.next()
                wv = wb[:].bitcast(F32).rearrange("p (k c) -> p k c", c=256)
                S.dma("sp", lambda e, wv=wv, l=l, jb=jb: e.dma_start(
                    out=wv, in_=w_mod[l, :, jb * 256:(jb + 1) * 256].rearrange("(k p) c -> p k c", p=128)),
                    wkey, writes=[t_wb])

                def mm_mod(e, wv=wv, jb=jb):
                    r = None
                    for jj in range(2):
                        j = jb * 2 + jj
                        for kc in range(NKC):
                            r = e.matmul(Sb[:, 2 * j:2 * j + 2], lhsT=wv[:, kc, jj * 128:(jj + 1) * 128], rhs=cTv[:, :, kc],
                                         start=(kc == 0), stop=(kc == NKC - 1))
                    return r
                S.op("pe", mm_mod, reads=[t_wb, t_cT], writes=[t_Sb])
            modv = modT[li]
            S.op("dve", lambda e, modv=modv: e.tensor_tensor(
                out=modv[:], in0=Sb[:, 0:96].rearrange("p (j r) -> p j r", r=2),
                in1=bmT[:].unsqueeze(2).broadcast_to([128, 48, 2]), op=ALU.add),
                reads=[t_Sb, t_bmT], writes=[t_mod[li]])
            S.op("dve", lambda e, modv=modv, li=li: e.tensor_scalar(
                out=gT[li][:], in0=modv[:, 16:32, :], scalar1=1.0, scalar2=None, op0=ALU.add),
                reads=[t_mod[li]], writes=[t_gT[li]])
            S.op("dve", lambda e, li=li: e.tensor_tensor(
                out=gT[li][:], in0=gT[li][:], in1=nwT[:].unsqueeze(2).broadcast_to([128, 16, 2]), op=ALU.mult),
                reads=[t_nwT], writes=[t_gT[li]])

        for li in range(NL):
            dbg(f"modT{li}", modT[li][:], t_mod[li])
            dbg(f"gT{li}", gT[li][:], t_gT[li])
        dbg("cT", cT[:], t_cT)

        def rstd_small(ss_ap, t_ss, n, inv_n):
            sd, t_sd = smring.next()
            S.op("act", lambda e: e.activation(out=sd[:, 0:n], in_=ss_ap, func=AF.Sqrt, bias=EPS, scale=inv_n),
                 reads=[t_ss], writes=[t_sd])
            rs, t_rs = smring.next()
            S.op("dve", lambda e: e.reciprocal(out=rs[:, 0:n], in_=sd[:, 0:n]), reads=[t_sd], writes=[t_rs])
            return rs[:, 0:n], t_rs

        def make_hT(src_ap, t_src, t, li, dst, t_dst):
            r = 1 if is_ctx(t) else 0
            xb, t_xb, xkey = xring.next()
            S.dma("sp", lambda e: e.dma_start(out=xb[:], in_=src_ap[t * 128:(t + 1) * 128, :]), xkey, reads=[t_src[t]], writes=[t_xb])
            xnb, t_xn = xnring.next()
            ss, t_ss = smring.next()
            S.op("act", lambda e: e.activation(out=xnb[:], in_=xb[:], func=AF.Square, accum_out=ss[:, 0:1]),
                 reads=[t_xb], writes=[t_xn, t_ss])
            rs, t_rs = rstd_small(ss[:, 0:1], t_ss, 1, 1.0 / D)
            S.op("act", lambda e: e.activation(out=xnb[:], in_=xb[:], func=AF.Copy, scale=rs[:, 0:1]),
                 reads=[t_xb, t_rs], writes=[t_xn])
            for half in range(2):
                tb, t_tb = Tb[half]
                tbv = tb[:].bitcast(BF16).rearrange("p (k c) -> p k c", c=128)

                def tr(e, half=half, tbv=tbv):
                    rr = None
                    for k in range(8):
                        kc = half * 8 + k
                        rr = e.transpose(out=tbv[:, k, :], in_=xnb[:, kc * 128:(kc + 1) * 128], identity=identb[:])
                    return rr
                S.op("pe", tr, reads=[t_xn, t_identb], writes=[t_tb])

                def ev(e, half=half, tbv=tbv):
                    rr = None
                    for k in range(8):
                        kc = half * 8 + k
                        rr = e.tensor_scalar(out=dst[:, kc, :], in0=tbv[:, k, :], scalar1=gT[li][:, kc, r:r + 1],
                                             scalar2=modT[li][:, kc, r:r + 1], op0=ALU.mult, op1=ALU.add)
                    return rr
                def ev_act(e, half=half, tbv=tbv):
                    rr = None
                    for k in range(8):
                        kc = half * 8 + k
                        rr = e.activation(out=dst[:, kc, :], in_=tbv[:, k, :], func=AF.Identity,
                                          scale=gT[li][:, kc, r:r + 1], bias=modT[li][:, kc, r:r + 1])
                    return rr
                if half == 0:
                    S.op("dve", ev, reads=[t_tb, t_gT[li], t_mod[li]], writes=[t_dst])
                else:
                    S.op("act", ev_act, reads=[t_tb, t_gT[li], t_mod[li]], writes=[t_dst])

        def proj(lhs, t_lhs, wb, t_wb):
            ab, t_ab = aring.next()
            wv = wb[:].rearrange("p (k c) -> p k c", c=512)

            def mm(e):
                rr = None
                for kc in range(NKC):
                    rr = e.matmul(ab[:], lhsT=lhs[:, kc, :], rhs=wv[:, kc, :], start=(kc == 0), stop=(kc == NKC - 1))
                return rr
            S.op("pe", mm, reads=[t_lhs, t_wb], writes=[t_ab])
            return ab, t_ab

        def head_rstd(src_ap, t_src, nh):
            sq, t_sq = tring.next()
            S.op("act", lambda e: e.activation(out=sq[:, 0:nh * 128], in_=src_ap, func=AF.Square), reads=[t_src], writes=[t_sq])
            ss, t_ss = smring.next()
            S.op("dve", lambda e: e.tensor_reduce(out=ss[:, 0:nh], in_=sq[:, 0:nh * 128].rearrange("p (h d) -> p h d", d=128),
                                                  axis=AX.X, op=ALU.add), reads=[t_sq], writes=[t_ss])
            return rstd_small(ss[:, 0:nh], t_ss, nh, 1.0 / 128)

        def apply_rope(src, t_src, nh, rt, t_rt, dst, t_dst):
            n = nh * 128
            t1, t_t1 = tring.next()
            cosb = rt[:, 0:128].unsqueeze(1).broadcast_to([128, nh, 128])
            S.op("dve", lambda e: e.tensor_tensor(out=t1[:, 0:n].rearrange("p (h d) -> p h d", d=128),
                                                  in0=src[:, 0:n].rearrange("p (h d) -> p h d", d=128), in1=cosb, op=ALU.mult),
                 reads=[t_src, t_rt], writes=[t_t1])
            rot, t_rot = tring.next()
            sv = src[:, 0:n].rearrange("p (h b t d) -> p h b t d", b=2, t=2, d=32)
            rv = rot[:, 0:n].rearrange("p (h b t d) -> p h b t d", b=2, t=2, d=32)
            t1v = t1[:, 0:n].rearrange("p (h b t d) -> p h b t d", b=2, t=2, d=32)
            dv = dst[:, 0:n].rearrange("p (h b t d) -> p h b t d", b=2, t=2, d=32)
            sinv = rt[:, 128:256].rearrange("p (b t d) -> p b t d", b=2, t=2)

            def rotf(e):
                e.tensor_tensor(out=rv[:, :, :, 0, :], in0=sv[:, :, :, 1, :],
                                in1=sinv[:, :, 0, :].unsqueeze(1).broadcast_to([128, nh, 2, 32]), op=ALU.mult)
                return e.tensor_tensor(out=rv[:, :, :, 1, :], in0=sv[:, :, :, 0, :],
                                       in1=sinv[:, :, 1, :].unsqueeze(1).broadcast_to([128, nh, 2, 32]), op=ALU.mult)
            S.op("dve", rotf, reads=[t_src, t_rt], writes=[t_rot])

            def fin(e):
                e.tensor_tensor(out=dv[:, :, :, 0, :], in0=t1v[:, :, :, 0, :], in1=rv[:, :, :, 0, :], op=ALU.subtract)
                return e.tensor_tensor(out=dv[:, :, :, 1, :], in0=t1v[:, :, :, 1, :], in1=rv[:, :, :, 1, :], op=ALU.add)
            S.op("dve", fin, reads=[t_t1, t_rot], writes=[t_dst])

        out_toks = []

        def run_layer(li, l, src_ap, dst_ap, t_srcx, p2_tiles, last_in_prog):

            S.dma("sp", lambda e, l=l: e.dma_start(out=wsg_f[:], in_=w_sgu[l].rearrange("g p q -> p g q")), "wsgf", writes=[t_wsgf])
            S.op("dve", lambda e: e.tensor_copy(out=wsg_b[:], in_=wsg_f[:]), reads=[t_wsgf], writes=[t_wsgb])
            tb, t_tb = Tb[0]
            tbv0 = tb[:].bitcast(BF16).rearrange("p (k c) -> p k c", c=128)

            def trw(e, tbv0=tbv0):
                rr = None
                for g in range(8):
                    rr = e.transpose(out=tbv0[:, g, :], in_=wsg_b[:, g, :], identity=identb[:])
                return rr
            S.op("pe", trw, reads=[t_wsgb, t_identb], writes=[t_tb])
            S.op("dve", lambda e, tbv0=tbv0: e.tensor_copy(out=wsguT[:], in_=tbv0[:]), reads=[t_tb], writes=[t_wsguT])
            small_T(b_sgu[l], 8, bsguT[:], t_bsguT)
            S.dma("sp", lambda e, l=l: e.dma_start(out=vnw_b[:], in_=v_norm_w[l].partition_broadcast(128)), "vnw", writes=[t_vnw])
            S.dma("sp", lambda e, l=l: e.dma_start(out=qnw_b[:], in_=q_norm_w[l].partition_broadcast(128)), "qnw", writes=[t_qnw])
            S.dma("sp", lambda e, l=l: e.dma_start(out=knw_b[:], in_=k_norm_w[l].partition_broadcast(128)), "knw", writes=[t_knw])
            mq, t_mq = smring.next()
            S.op("dve", lambda e, mq=mq: e.tensor_reduce(out=mq[:, 0:1], in_=qnw_b[:], axis=AX.X, op=ALU.max, apply_absolute_value=True),
                 reads=[t_qnw], writes=[t_mq])
            S.op("dve", lambda e, mq=mq: e.tensor_reduce(out=mq[:, 1:2], in_=knw_b[:], axis=AX.X, op=ALU.max, apply_absolute_value=True),
                 reads=[t_knw], writes=[t_mq])
            S.op("dve", lambda e, mq=mq: e.tensor_tensor(out=negC[:, 0:1], in0=mq[:, 0:1], in1=mq[:, 1:2], op=ALU.mult),
                 reads=[t_mq], writes=[t_negC])
            S.op("dve", lambda e: e.tensor_scalar(out=negC[:, 0:1], in0=negC[:, 0:1], scalar1=-float(np.sqrt(128.0)), scalar2=None, op0=ALU.mult),
                 writes=[t_negC])
            def build_gate(r, li=li):
                for q4 in range(4):
                    for k in range(4):
                        kc = q4 * 4 + k
                        dg, t_dg = dring.next()
                        S.op("dve", lambda e, dg=dg, kc=kc, r=r: e.tensor_scalar(
                            out=dg[:], in0=identf[:], scalar1=modT[li][:, 32 + kc, r:r + 1], scalar2=None, op0=ALU.mult),
                            reads=[t_identf, t_mod[li]], writes=[t_dg])
                        S.op("pe", lambda e, dg=dg, k=k: e.matmul(Sb[:, k * 128:(k + 1) * 128], lhsT=onesf[:], rhs=dg[:], start=True, stop=True),
                             reads=[t_dg, t_onesf], writes=[t_Sb])
                    S.op("dve", lambda e, q4=q4: e.tensor_copy(out=gate_b[:, q4 * 512:(q4 + 1) * 512], in_=Sb[:]),
                         reads=[t_Sb], writes=[t_gate])
            build_gate(0)

            wkv, t_wkv, wkey = wring.next()
            S.dma("sp", lambda e, wkv=wkv: e.dma_start(out=wkv[:], in_=wbi[li][8]), wkey, reads=[t_wci[li]], writes=[t_wkv])
            kv_toks = []
            hbs = {}
            for n_, t in enumerate(list(p1_tiles) + [None]):
                if t is not None:
                    hbs[t] = h1ring.next()
                    make_hT(src_ap, t_srcx, t, li, hbs[t][0], hbs[t][1])
                if n_ == 0:
                    continue
                t = p1_tiles[n_ - 1]
                hb, t_hb = hbs.pop(t)
                if t == p1_tiles[0] and li == 0:
                    dbg("hT_p1", hb[:], t_hb)
                ab, t_ab = proj(hb, t_hb, wkv, t_wkv)
                rk, t_rk = head_rstd(ab[:, 0:256], t_ab, 2)
                kn, t_kn = tring.next()

                def knf(e, ab=ab, rk=rk, kn=kn):
                    rr = None
                    for h in range(2):
                        rr = e.scalar_tensor_tensor(out=kn[:, h * 128:(h + 1) * 128], in0=ab[:, h * 128:(h + 1) * 128],
                                                    scalar=rk[:, h:h + 1], in1=knw_b[:], op0=ALU.mult, op1=ALU.mult)
                    return rr
                S.op("dve", knf, reads=[t_ab, t_rk, t_knw], writes=[t_kn])
                kb, t_kb = qbring.next()
                if is_ctx(t):
                    S.op("dve", lambda e, kb=kb, kn=kn: e.tensor_copy(out=kb[:, 0:256], in_=kn[:, 0:256]), reads=[t_kn], writes=[t_kb])
                else:
                    rt, t_rt, rkey = r1ring.next()
                    lt = lat_index(t)
                    S.dma("sp", lambda e, rt=rt, lt=lt: e.dma_start(out=rt[:], in_=rope[lt * 128:(lt + 1) * 128, :]), rkey, writes=[t_rt])
                    apply_rope(kn, t_kn, 2, rt, t_rt, kb, t_kb)
                tb, t_tb = Tb[0]
                tbv = tb[:].bitcast(BF16).rearrange("p (k c) -> p k c", c=128)

                def trk(e, kb=kb, tbv=tbv):
                    e.transpose(out=tbv[:, 0, :], in_=kb[:, 0:128], identity=identb[:])
                    return e.transpose(out=tbv[:, 1, :], in_=kb[:, 128:256], identity=identb[:])
                S.op("pe", trk, reads=[t_kb, t_identb], writes=[t_tb])
                if cc:
                    ks, t_ks, kkey = kstring.next()
                    S.op("act", lambda e, tbv=tbv, ks=ks: e.copy(out=ks[:], in_=tbv[:, 0:2, :]), reads=[t_tb], writes=[t_ks])
                    tk_ = Tok()
                    S.dma("sp", lambda e, ks=ks, t=t: e.dma_start(
                        out=kv_send[li][:, 0:4352].rearrange("p (h n) -> p h n", h=2)[:, :, t * 128:(t + 1) * 128], in_=ks[:]),
                        kkey, reads=[t_ks], writes=[tk_])
                    vs, t_vs, vkey = vstring.next()
                    S.op("act", lambda e, ab=ab, vs=vs: e.copy(out=vs[:, :, 0:128], in_=ab[:, 256:512].rearrange("p (h d) -> p h d", d=128)),
                         reads=[t_ab, t_Vones], writes=[t_vs])
                    tv_ = Tok()
                    S.dma("sp", lambda e, vs=vs, t=t: e.dma_start(
                        out=kv_send[li][:, 4352 + t * 260:4352 + (t + 1) * 260], in_=vs[:].rearrange("p g c -> p (g c)")),
                        vkey, reads=[t_vs], writes=[tv_])
                    kv_toks.extend([tk_, tv_])
                else:
                    S.op("act", lambda e, tbv=tbv, t=t: e.copy(out=KT[:, :, t * 128:(t + 1) * 128], in_=tbv[:, 0:2, :]),
                         reads=[t_tb], writes=[t_K[t]])
                    S.op("act", lambda e, ab=ab, t=t: e.copy(out=VA[:, t, :, 0:128], in_=ab[:, 256:512].rearrange("p (h d) -> p h d", d=128)),
                         reads=[t_ab, t_Vones], writes=[t_V[t]])
            if cc:
                t_recv = Tok()
                S.dma("pool", lambda e: e.collective_compute("AllGather", ALU.bypass, replica_groups=PAIRS,
                                                             ins=[kv_send[li]], outs=[kv_recv[li]]),
                      f"cc{li}", reads=kv_toks, writes=[t_recv], inc=1)
                for blk in range(2):
                    S.dma("sp", lambda e, blk=blk: e.dma_start(
                        out=KT[:, :, blk * 2176:(blk + 1) * 2176],
                        in_=kv_recv[li][blk * 128:(blk + 1) * 128, 0:4352].rearrange("p (h n) -> p h n", h=2)),
                        f"KTl{blk}", reads=[t_recv], writes=[tkb[blk]])
                    S.dma("sp", lambda e, blk=blk: e.dma_start(
                        out=VA[:, blk * 17:(blk + 1) * 17, :, :],
                        in_=kv_recv[li][blk * 128:(blk + 1) * 128, 4352:KVW].rearrange("p (j g c) -> p j g c", g=2, c=130)),
                        f"VAl{blk}", reads=[t_recv], writes=[tvb[blk]])

            if li == 0:
                t0_ = p1_tiles[0]
                dbg("KT0", KT[:, :, t0_ * 128:(t0_ + 1) * 128], t_K[t0_])
                dbg("VA0", VA[:, t0_, :, :], t_V[t0_])
                if cc:
                    dbg("KT33", KT[:, :, 33 * 128:34 * 128], t_K[33])
                dbg("negC", negC[:], t_negC)
                dbg("gate_b", gate_b[:], t_gate)
                dbg("wsguT", wsguT[:], t_wsguT)
            if li + 1 < NL:
                emit_casts(li + 1)
            lat_tiles = [t for t in p2_tiles if not is_ctx(t)]
            ctx_tiles = [t for t in p2_tiles if is_ctx(t)]
            groups = [lat_tiles[i:i + TG] for i in range(0, len(lat_tiles), TG)]
            if ctx_tiles:
                groups.append(ctx_tiles)
            t_xs_next = [Tok() for _ in range(NT_ALL)]
            for grp in groups:
                ng = len(grp)
                rflag = 1 if is_ctx(grp[0]) else 0
                if rflag:
                    build_gate(1)
                kall = list(range(NT_ALL)) if cc else list(p1_tiles)
                ktiles = [t for t in kall if is_ctx(t)] if rflag else kall
                for i, t in enumerate(grp):
                    make_hT(src_ap, t_srcx, t, li, hT[i], t_hT[i])
                    if not rflag:
                        lt = lat_index(t)
                        S.dma("sp", lambda e, i=i, lt=lt: e.dma_start(out=ropeT[i][:], in_=rope[lt * 128:(lt + 1) * 128, :]),
                              f"ropeT{i}", writes=[t_rope[i]])
                order = [(2, "v", 0), (0, "u", 0), (4, "za", 0), (3, "v", 1), (1, "u", 1), (5, "za", 1),
                         (6, "q", 0), (7, "q", 1), (9, "zb", 0), (10, "zb", 1)]
                pending_backs = []
                for (cb, kind, hh) in order:
                    wb, t_wb, wkey = wring.next()
                    S.dma("sp", lambda e, wb=wb, cb=cb: e.dma_start(out=wb[:], in_=wbi[li][cb]), wkey, reads=[t_wci[li]], writes=[t_wb])
                    for i, t in enumerate(grp):
                        ab, t_ab = proj(hT[i], t_hT[i], wb, t_wb)
                        if i == 0:
                            while pending_backs:
                                pending_backs.pop(0)()
                        elif len(pending_backs) >= 2:
                            pending_backs.pop(0)()
                        back = None
                        if kind == "v":
                            gv, t_gv = tring.next()
                            S.op("act", lambda e, gv=gv, ab=ab: e.activation(out=gv[:], in_=ab[:], func=AF.Gelu_apprx_tanh),
                                 reads=[t_ab], writes=[t_gv])
                            rv, t_rv = head_rstd(gv[:], t_gv, 4)
                            v1, t_v1 = tring.next()
                            S.op("dve", lambda e, v1=v1, gv=gv, rv=rv: e.tensor_tensor(
                                out=v1[:].rearrange("p (g d) -> p g d", d=128), in0=gv[:].rearrange("p (g d) -> p g d", d=128),
                                in1=rv.unsqueeze(2).broadcast_to([128, 4, 128]), op=ALU.mult), reads=[t_gv, t_rv], writes=[t_v1])
                            vb, t_vb = vnring.next()
                            S.op("dve", lambda e, vb=vb, v1=v1, hh=hh: e.tensor_tensor(
                                out=vb[:], in0=v1[:], in1=vnw_b[:, hh * 512:(hh + 1) * 512], op=ALU.mult),
                                reads=[t_v1, t_vnw], writes=[t_vb])

                            def back(vb=vb, t_vb=t_vb, hh=hh, i=i):
                                def sgu(e):
                                    rr = None
                                    for g in range(4):
                                        rr = e.matmul(Sb[:, g * 128:(g + 1) * 128], lhsT=wsguT[:, 4 * hh + g, :],
                                                      rhs=vb[:, g * 128:(g + 1) * 128], start=True, stop=True)
                                    return rr
                                S.op("pe", sgu, reads=[t_vb, t_wsguT], writes=[t_Sb])
                                S.op("dve", lambda e: e.tensor_tensor(
                                    out=s_sb[i][:].rearrange("p (g d) -> p g d", d=128), in0=Sb[:].rearrange("p (g d) -> p g d", d=128),
                                    in1=bsguT[:, 4 * hh:4 * hh + 4].unsqueeze(2).broadcast_to([128, 4, 128]), op=ALU.add),
                                    reads=[t_Sb, t_bsguT], writes=[t_s[i]])
                        elif kind == "u":
                            gu, t_gu = tring.next()
                            S.op("act", lambda e, gu=gu, ab=ab: e.activation(out=gu[:], in_=ab[:], func=AF.Gelu_apprx_tanh),
                                 reads=[t_ab], writes=[t_gu])
                            S.op("dve", lambda e, gu=gu, i=i: e.tensor_tensor(out=s_sb[i][:], in0=gu[:], in1=s_sb[i][:], op=ALU.mult),
                                 reads=[t_gu], writes=[t_s[i]])
                        elif kind == "za":
                            sz, t_sz = tring.next()
                            S.op("act", lambda e, sz=sz, ab=ab: e.activation(out=sz[:], in_=ab[:], func=AF.Silu), reads=[t_ab], writes=[t_sz])
                            S.op("dve", lambda e, sz=sz, i=i, hh=hh: e.tensor_tensor(
                                out=gated[i][:, hh * 512:(hh + 1) * 512], in0=sz[:], in1=s_sb[i][:], op=ALU.mult),
                                reads=[t_sz, t_s[i]], writes=[t_gated[i]])
                        elif kind == "q":
                            rq, t_rq = head_rstd(ab[:], t_ab, 4)
                            q1, t_q1 = tring.next()
                            S.op("dve", lambda e, q1=q1, ab=ab, rq=rq: e.tensor_tensor(
                                out=q1[:].rearrange("p (g d) -> p g d", d=128), in0=ab[:].rearrange("p (g d) -> p g d", d=128),
                                in1=rq.unsqueeze(2).broadcast_to([128, 4, 128]), op=ALU.mult), reads=[t_ab, t_rq], writes=[t_q1])
                            S.op("dve", lambda e, q1=q1: e.tensor_tensor(
                                out=q1[:].rearrange("p (g d) -> p g d", d=128), in0=q1[:].rearrange("p (g d) -> p g d", d=128),
                                in1=qnw_b[:].unsqueeze(1).broadcast_to([128, 4, 128]), op=ALU.mult), reads=[t_qnw], writes=[t_q1])
                            qb, t_qb = qbring.next()
                            if rflag:
                                S.op("dve", lambda e, qb=qb, q1=q1: e.tensor_copy(out=qb[:], in_=q1[:]), reads=[t_q1], writes=[t_qb])
                            else:
                                apply_rope(q1, t_q1, 4, ropeT[i], t_rope[i], qb, t_qb)

                            def back(qb=qb, t_qb=t_qb, hh=hh, i=i):
                                tb, t_tb = Tb[(i + hh) % 2]
                                tbv = tb[:].bitcast(BF16).rearrange("p (k c) -> p k c", c=128)

                                def trq(e):
                                    rr = None
                                    for h in range(4):
                                        rr = e.transpose(out=tbv[:, h, :], in_=qb[:, h * 128:(h + 1) * 128], identity=identb[:])
                                    return rr
                                S.op("pe", trq, reads=[t_qb, t_identb], writes=[t_tb])
                                S.op("act", lambda e: e.copy(out=qT[i][:, 4 * hh:4 * hh + 4, :], in_=tbv[:, 0:4, :]),
                                     reads=[t_tb], writes=[t_qT[i]])
                        else:
                            S.op("act", lambda e, ab=ab, i=i, hh=hh: e.activation(out=szb[i][:, hh * 512:(hh + 1) * 512], in_=ab[:], func=AF.Silu),
                                 reads=[t_ab], writes=[t_szb[i]])
                        if back is not None:
                            pending_backs.append(back)
                while pending_backs:
                    pending_backs.pop(0)()

                if li == 0 and grp is groups[0]:
                    dbg("gatedA", gated[0][:, 0:1024], t_gated[0])
                    dbg("qT", qT[0][:], t_qT[0])
                    dbg("szb", szb[0][:], t_szb[0])
                inv_sqrt = float(128.0 ** -0.5)
                nk = len(ktiles)
                units = [(i, g, ki, kt) for i in range(ng) for g in range(2) for ki, kt in enumerate(ktiles)]
                LAG = 2
                (o0, t_o0), (o1, t_o1) = Ob

                def attn_front(u):
                    i, g, ki, kt = u
                    sbk, t_sbk = aring.next()
                    S.op("pe", lambda e: e.matmul(
                        sbk[:], lhsT=KT[:, g, kt * 128:(kt + 1) * 128], rhs=qT[i][:, 4 * g:4 * g + 4, :], start=True, stop=True),
                        reads=[t_K[kt], t_qT[i]], writes=[t_sbk])
                    pt, t_pt = ptring.next()
                    S.op("act", lambda e: e.activation(out=pt[:], in_=sbk[:], func=AF.Exp, bias=negC[:, 0:1], scale=inv_sqrt),
                         reads=[t_sbk, t_negC], writes=[t_pt])
                    return pt, t_pt

                def attn_back(u, pt, t_pt):
                    i, g, ki, kt = u

                    def pv(e):
                        rr = None
                        for hq in range(4):
                            if hq < 3:
                                oap = o0[:, hq * 129:hq * 129 + 129]
                                st = (ki == 0 and hq == 0)
                            else:
                                oap = o1[:, 0:129]
                                st = (ki == 0)
                            rr = e.matmul(oap, lhsT=pt[:, hq * 128:(hq + 1) * 128], rhs=VA[:, kt, g, 0:129],
                                          start=st, stop=(ki == nk - 1), skip_group_check=True)
                        return rr
                    S.op("pe", pv, reads=[t_pt, t_V[kt]], writes=[t_o0, t_o1])
                    if ki != nk - 1:
                        return
                    rd, t_rd = smring.next()

                    def rden(e):
                        e.reciprocal(out=rd[:, 0:3], in_=o0[:, 0:387].rearrange("p (h c) -> p h c", c=129)[:, :, 128])
                        return e.reciprocal(out=rd[:, 3:4], in_=o1[:, 128:129])
                    S.op("dve", rden, reads=[t_o0, t_o1], writes=[t_rd])

                    def onorm(e):
                        rr = None
                        for hq in range(4):
                            h = 4 * g + hq
                            oap = o0[:, hq * 129:hq * 129 + 128] if hq < 3 else o1[:, 0:128]
                            rr = e.scalar_tensor_tensor(out=gated[i][:, 1024 + h * 128:1024 + (h + 1) * 128], in0=oap,
                                                        scalar=rd[:, hq:hq + 1], in1=szb[i][:, h * 128:(h + 1) * 128],
                                                        op0=ALU.mult, op1=ALU.mult)
                        return rr
                    S.op("dve", onorm, reads=[t_o0, t_o1, t_rd, t_szb[i]], writes=[t_gated[i]])

                pend = []
                for idx in range(len(units) + LAG):
                    if idx < len(units):
                        pend.append(attn_front(units[idx]))
                    if idx >= LAG:
                        attn_back(units[idx - LAG], *pend[idx - LAG])

                if li == 0 and grp is groups[0]:
                    dbg("gated", gated[0][:], t_gated[0])
                for i, t in enumerate(grp):
                    for half in range(2):
                        tb, t_tb = Tb[half]
                        tbv = tb[:].bitcast(BF16).rearrange("p (k c) -> p k c", c=128)

                        def trg(e, half=half, tbv=tbv, i=i):
                            rr = None
                            for k in range(8):
                                kc = half * 8 + k
                                rr = e.transpose(out=tbv[:, k, :], in_=gated[i][:, kc * 128:(kc + 1) * 128], identity=identb[:])
                            return rr
                        S.op("pe", trg, reads=[t_gated[i], t_identb], writes=[t_tb])
                        S.op("act", lambda e, half=half, tbv=tbv, i=i: e.copy(out=hT[i][:, half * 8:(half + 1) * 8, :], in_=tbv[:]),
                             reads=[t_tb], writes=[t_hT[i]])

                for cb in range(4):
                    wb, t_wb, wkey = wring.next()
                    S.dma("sp", lambda e, wb=wb, cb=cb: e.dma_start(out=wb[:], in_=wbo[li][cb]), wkey, reads=[t_wco[li]], writes=[t_wb])
                    for i, t in enumerate(grp):
                        xpb, t_xp, xkey = xpring.next()
                        S.dma("sp", lambda e, xpb=xpb, t=t, cb=cb: e.dma_start(
                            out=xpb[:], in_=src_ap[t * 128:(t + 1) * 128, cb * 512:(cb + 1) * 512]), xkey, reads=[t_srcx[t]], writes=[t_xp])
                        ab, t_ab = proj(hT[i], t_hT[i], wb, t_wb)
                        yg, t_yg = tring.next()
                        S.op("dve", lambda e, yg=yg, ab=ab, cb=cb: e.tensor_tensor(
                            out=yg[:], in0=ab[:], in1=gate_b[:, cb * 512:(cb + 1) * 512], op=ALU.mult),
                            reads=[t_ab, t_gate], writes=[t_yg])
                        S.op("pool", lambda e, yg=yg, xpb=xpb: e.tensor_tensor(out=xpb[:], in0=xpb[:], in1=yg[:], op=ALU.add),
                             reads=[t_yg], writes=[t_xp])
                        if last_in_prog:
                            if final:
                                drow = t
                            else:
                                drow = t
                        else:
                            drow = t
                        S.dma("pool", lambda e, xpb=xpb, drow=drow, cb=cb: e.dma_start(
                            out=dst_ap[drow * 128:(drow + 1) * 128, cb * 512:(cb + 1) * 512], in_=xpb[:]), xkey,
                            reads=[t_xp], writes=[t_xs_next[t]])
                        if last_in_prog:
                            out_toks.append(t_xp)
            return t_xs_next

        t_xs_all = [Tok() for _ in range(NT_ALL)]
        for li_, l_ in enumerate(layers):
            last_ = (li_ == NL - 1)
            t_xs_all = run_layer(li_, l_, xa if li_ == 0 else xs, out if last_ else xs, t_xs_all,
                                 p2_tiles_per_layer[li_], last_)

        S.final_wait("pool", list({id(t): t for t in out_toks}.values()) + dbg_toks)
        S.emit(block)
    return nc


def _rope_tables(pos):
    rows = (pos // GRID_W).astype(np.float32)
    cols = (pos % GRID_W).astype(np.float32)
    inv_freq = (np.float32(10000.0) ** (-np.arange(0, 64, 2, dtype=np.float32) / np.float32(64))).astype(np.float32)
    ang_r = rows[:, None] * inv_freq[None, :]
    ang_c = cols[:, None] * inv_freq[None, :]
    ang = np.concatenate([ang_r, ang_r, ang_c, ang_c], axis=-1).astype(np.float32)
    return np.concatenate([np.cos(ang), np.sin(ang)], axis=-1).astype(np.float32)


_PROG_CACHE = {}


def _get_prog(key, *args):
    if key not in _PROG_CACHE:
        _PROG_CACHE[key] = build(*args)
    return _PROG_CACHE[key]


def _common_inputs(c, c_ctx, norm_w, w_mod, b_mod, w_in, w_sgu, b_sgu, v_norm_w, q_norm_w, k_norm_w, w_out):
    f = lambda a: np.ascontiguousarray(np.asarray(a, dtype=np.float32))
    shared = {
        "identb": np.eye(128, dtype=np.float32).astype(ml_dtypes.bfloat16),
        "identf": np.eye(128, dtype=np.float32),
        "w_mod": f(w_mod), "b_mod": f(b_mod).reshape(2, 48, 128), "norm_w": f(norm_w).reshape(2, 16, 128),
        "w_in": f(w_in), "w_out": f(w_out), "w_sgu": f(w_sgu), "b_sgu": f(b_sgu),
        "v_norm_w": f(v_norm_w).reshape(2, 1024), "q_norm_w": f(q_norm_w), "k_norm_w": f(k_norm_w),
    }
    return shared


def kernel(x, c, ctx, c_ctx, norm_w, w_mod, b_mod, w_in, w_sgu, b_sgu, v_norm_w, q_norm_w, k_norm_w, w_out):
    x = np.asarray(x, dtype=np.float32)
    ctx = np.asarray(ctx, dtype=np.float32)
    c = np.asarray(c, dtype=np.float32)
    c_ctx = np.asarray(c_ctx, dtype=np.float32)
    shared = _common_inputs(c, c_ctx, norm_w, w_mod, b_mod, w_in, w_sgu, b_sgu, v_norm_w, q_norm_w, k_norm_w, w_out)
    H = SEQ // 2
    CH = CTX // 2

    def core_maps(xfull, cfull):
        maps = []
        for core in range(8):
            b, hf = divmod(core, 2)
            o, p = hf, 1 - hf
            xa = np.concatenate([xfull[b, o * H:(o + 1) * H], cfull[b, o * CH:(o + 1) * CH],
                                 xfull[b, p * H:(p + 1) * H], cfull[b, p * CH:(p + 1) * CH]], axis=0)
            pos = np.concatenate([np.arange(o * H, (o + 1) * H), np.arange(p * H, (p + 1) * H)])
            m = dict(shared)
            m["xa"] = np.ascontiguousarray(xa)
            m["rope"] = _rope_tables(pos)
            m["c2"] = np.ascontiguousarray(np.stack([c[b], c_ctx], 0).reshape(32, 128))
            maps.append(m)
        return maps

    all_tiles = list(range(NT_ALL))
    own_lat = list(range(16))
    if MODE == "cc":
        maps = []
        for core in range(8):
            b, hf = divmod(core, 2)
            m = dict(shared)
            m["xa"] = np.ascontiguousarray(np.concatenate([x[b, hf * H:(hf + 1) * H], ctx[b, hf * CH:(hf + 1) * CH]], axis=0))
            m["rope"] = _rope_tables(np.arange(hf * H, (hf + 1) * H))
            m["c2"] = np.ascontiguousarray(np.stack([c[b], c_ctx], 0).reshape(32, 128))
            maps.append(m)
        nc = _get_prog("cc", [0, 1], True, list(range(17)), [list(range(17)), own_lat], 16, True)
        res = run_bass_kernel_spmd(nc, maps, core_ids=list(range(8)))
        outs = [r["out"] for r in res.results]
    elif MODE == "fused":
        nc = _get_prog("fused", [0, 1], True, all_tiles, [all_tiles, own_lat], 16)
        res = run_bass_kernel_spmd(nc, core_maps(x, ctx), core_ids=list(range(8)))
        outs = [r["out"] for r in res.results]
    else:
        ncA = _get_prog("L0", [0], False, all_tiles, [list(range(17))], 17)
        resA = run_bass_kernel_spmd(ncA, core_maps(x, ctx), core_ids=list(range(8)))
        x1 = np.empty_like(x)
        ctx1 = np.empty_like(ctx)
        for core in range(8):
            b, hf = divmod(core, 2)
            xn_ = resA.results[core]["xnext"]
            x1[b, hf * H:(hf + 1) * H] = xn_[0:H]
            ctx1[b, hf * CH:(hf + 1) * CH] = xn_[H:H + CH]
        ncB = _get_prog("L1", [1], True, all_tiles, [own_lat], 16)
        resB = run_bass_kernel_spmd(ncB, core_maps(x1, ctx1), core_ids=list(range(8)))
        outs = [r["out"] for r in resB.results]
    y = np.empty((4, SEQ, D), dtype=np.float32)
    for core in range(8):
        b, hf = divmod(core, 2)
        y[b, hf * H:(hf + 1) * H] = outs[core]
    return y
```

```python
import numpy as np
from contextlib import ExitStack
import ml_dtypes
import concourse.bass as bass
import concourse.mybir as mybir
from concourse.bass_utils import run_bass_kernel_spmd

F32 = mybir.dt.float32
BF16 = mybir.dt.bfloat16
AF = mybir.ActivationFunctionType
ALU = mybir.AluOpType
AX = mybir.AxisListType

D = 2048
NKC = 16
DIN = 5632
NCB = 11
SEQ = 4096
CTX = 256
GRID_W = 64
EPS = 1e-6
TG = 4
NT_ALL = 34
DEBUG = False
MODE = "fused"


class Tok:
    __slots__ = ("w", "r")

    def __init__(self):
        self.w = None
        self.r = {}


class Sched:
    EPOCH = 20000

    def __init__(self, nc, es):
        self.nc = nc
        self.es = es
        self.names = ["pe", "act", "dve", "pool", "sp"]
        self.q = {k: [] for k in self.names}
        self.cnt = {k: 0 for k in self.names}
        self.esems = {k: [] for k in self.names}
        self.dsems = {}
        self.dcnt = {}
        self.waited = {k: {} for k in self.names}

    def _newsem(self, name):
        return self.es.enter_context(self.nc.semaphore(name))

    def _eng_event(self, eng):
        c = self.cnt[eng]
        ep, v = divmod(c, self.EPOCH)
        while len(self.esems[eng]) <= ep:
            self.esems[eng].append(self._newsem(f"e_{eng}_{len(self.esems[eng])}"))
        self.cnt[eng] = c + 1
        return (self.esems[eng][ep], v + 1, eng)

    def _collect(self, eng, reads, writes):
        evs = []
        for t in reads:
            if t.w is not None:
                evs.append(t.w)
        for t in writes:
            if t.w is not None:
                evs.append(t.w)
            evs.extend(t.r.values())
        waits = {}
        for (sem, val, e) in evs:
            if e is not None and e == eng and eng == "pe":
                continue
            key = id(sem)
            if self.waited[eng].get(key, 0) >= val:
                continue
            if key not in waits or waits[key][1] < val:
                waits[key] = (sem, val)
        for key, (sem, val) in waits.items():
            self.waited[eng][key] = val
        return list(waits.values())

    def _mark(self, ev, reads, writes):
        k = id(ev[0])
        for t in reads:
            old = t.r.get(k)
            if old is None or old[1] < ev[1]:
                t.r[k] = ev
        for t in writes:
            t.w = ev
            t.r = {}

    def op(self, eng, fn, reads=(), writes=()):
        waits = self._collect(eng, reads, writes)
        ev = self._eng_event(eng)
        self.q[eng].append((waits, fn, ev[0], 1))
        self._mark(ev, reads, writes)

    def dma(self, eng, fn, key, reads=(), writes=(), n=1, inc=16):
        waits = self._collect(eng, reads, writes)
        if key not in self.dsems:
            self.dsems[key] = self._newsem(f"d_{key}")
            self.dcnt[key] = 0
        self.dcnt[key] += inc * n
        ev = (self.dsems[key], self.dcnt[key], None)
        self.q[eng].append((waits, fn, self.dsems[key], inc))
        self._mark(ev, reads, writes)

    def final_wait(self, eng, toks):
        waits = self._collect(eng, toks, toks)
        self.q[eng].append((waits, None, None, 0))

    def emit(self, block):
        def mk(name):
            def body(e):
                for (waits, fn, sem, inc) in self.q[name]:
                    for (s, v) in waits:
                        e.wait_ge(s, v)
                    if fn is None:
                        continue
                    r = fn(e)
                    if isinstance(r, (list, tuple)):
                        for ins in r:
                            ins.then_inc(sem, inc)
                    else:
                        r.then_inc(sem, inc)
            return body
        block.tensor(mk("pe"))
        block.scalar(mk("act"))
        block.vector(mk("dve"))
        block.gpsimd(mk("pool"))
        block.sync(mk("sp"))


class Ring:
    def __init__(self, items):
        self.items = items
        self.i = 0

    def next(self):
        it = self.items[self.i % len(self.items)]
        self.i += 1
        return it


def is_ctx(t):
    return t % 17 == 16


def lat_index(t):
    return (t // 17) * 16 + (t % 17)


PAIRS = [[0, 1], [2, 3], [4, 5], [6, 7]]
KVW = 2 * 17 * 128 + 17 * 260


def build(layers, final, p1_tiles, p2_tiles_per_layer, n_out_tiles, cc=False):
    nc = bass.Bass("TRN2", target_bir_lowering=False)
    NL = len(layers)

    def din(name, shape, dt=F32):
        return nc.dram_tensor(name, shape, dt, kind="ExternalInput").ap()

    NX = 17 if cc else NT_ALL
    xa = din("xa", [NX * 128, D])
    rope = din("rope", [(16 if cc else 32) * 128, 256])
    c2 = din("c2", [32, 128])
    identb_in = din("identb", [128, 128], BF16)
    identf_in = din("identf", [128, 128])
    w_mod = din("w_mod", [2, D, 3 * D])
    b_mod = din("b_mod", [2, 48, 128])
    norm_w = din("norm_w", [2, 16, 128])
    w_in = din("w_in", [2, D, DIN])
    w_out = din("w_out", [2, D, D])
    w_sgu = din("w_sgu", [2, 8, 128, 128])
    b_sgu = din("b_sgu", [2, 8, 128])
    v_norm_w = din("v_norm_w", [2, 1024])
    q_norm_w = din("q_norm_w", [2, 128])
    k_norm_w = din("k_norm_w", [2, 128])
    if final:
        out = nc.dram_tensor("out", [16 * 128, D], F32, kind="ExternalOutput").ap()
    else:
        out = nc.dram_tensor("xnext", [n_out_tiles * 128, D], F32, kind="ExternalOutput").ap()
    xs = nc.dram_tensor("xs", [NX * 128, D], F32).ap() if NL > 1 else None
    if cc:
        kv_send = [nc.dram_tensor(f"kv_send{i}", [128, KVW], BF16).ap() for i in range(NL)]
        kv_recv = [nc.dram_tensor(f"kv_recv{i}", [256, KVW], BF16).ap() for i in range(NL)]
    wbi = [nc.dram_tensor(f"wbi{l}", [NCB, 128, NKC * 512], BF16).ap() for l in layers]
    wbo = [nc.dram_tensor(f"wbo{l}", [4, 128, NKC * 512], BF16).ap() for l in layers]

    with ExitStack() as es:
        def sb(name, shape, dt):
            return es.enter_context(nc.sbuf_tensor(name, shape, dt))

        S = Sched(nc, es)
        identb = sb("identb_sb", [128, 128], BF16); t_identb = Tok()
        identf = sb("identf_sb", [128, 128], F32); t_identf = Tok()
        onesf = sb("onesf", [128, 128], F32); t_onesf = Tok()
        KT = sb("KT", [128, 2, NT_ALL * 128], BF16)
        VA = sb("VA", [128, NT_ALL, 2, 130], BF16)
        if cc:
            tkb = [Tok(), Tok()]
            tvb = [Tok(), Tok()]
            t_K = [tkb[kt // 17] for kt in range(NT_ALL)]
            t_V = [tvb[kt // 17] for kt in range(NT_ALL)]
            kst = [sb(f"kst{i}", [128, 2, 128], BF16) for i in range(2)]
            kstring = Ring([(kst[i], Tok(), f"kst{i}") for i in range(2)])
            vst = [sb(f"vst{i}", [128, 2, 130], BF16) for i in range(2)]
            vstring = Ring([(vst[i], Tok(), f"vst{i}") for i in range(2)])
        else:
            t_K = [Tok() for _ in range(NT_ALL)]
            t_V = [Tok() for _ in range(NT_ALL)]
        t_Vones = Tok()
        wbuf = [sb(f"wbuf{i}", [128, NKC * 512], BF16) for i in range(2)]
        t_wbuf = [Tok() for _ in range(2)]
        wring = Ring(list(zip(wbuf, t_wbuf, ["wbuf0", "wbuf1"])))
        xbuf = [sb(f"xbuf{i}", [128, D], F32) for i in range(2)]
        xring = Ring([(xbuf[i], Tok(), f"xbuf{i}") for i in range(2)])
        xn = [sb(f"xn{i}", [128, D], BF16) for i in range(2)]
        xnring = Ring([(xn[i], Tok()) for i in range(2)])
        hT = [sb(f"hT{i}", [128, NKC, 128], BF16) for i in range(TG)]
        t_hT = [Tok() for _ in range(TG)]
        h1ring = Ring([(hT[i], t_hT[i]) for i in range(2)])
        s_sb = [sb(f"s_sb{i}", [128, 512], F32) for i in range(TG)]
        t_s = [Tok() for _ in range(TG)]
        gated = [sb(f"gated{i}", [128, D], BF16) for i in range(TG)]
        t_gated = [Tok() for _ in range(TG)]
        qT = [sb(f"qT{i}", [128, 8, 128], BF16) for i in range(TG)]
        t_qT = [Tok() for _ in range(TG)]
        szb = [sb(f"szb{i}", [128, 1024], BF16) for i in range(TG)]
        t_szb = [Tok() for _ in range(TG)]
        ropeT = [sb(f"ropeT{i}", [128, 256], F32) for i in range(TG)]
        t_rope = [Tok() for _ in range(TG)]
        r1ring = Ring([(ropeT[i], t_rope[i], f"ropeT{i}") for i in range(2)])
        tmps = [sb(f"tmp{i}", [128, 512], F32) for i in range(6)]
        tring = Ring([(tmps[i], Tok()) for i in range(6)])
        vnb = [sb(f"vnb{i}", [128, 512], BF16) for i in range(3)]
        vnring = Ring([(vnb[i], Tok()) for i in range(3)])
        qbf = [sb(f"qbf{i}", [128, 512], BF16) for i in range(3)]
        qbring = Ring([(qbf[i], Tok()) for i in range(3)])
        PT = [sb(f"PT{i}", [128, 512], BF16) for i in range(4)]
        ptring = Ring([(PT[i], Tok()) for i in range(4)])
        xp = [sb(f"xp{i}", [128, 512], F32) for i in range(4)]
        xpring = Ring([(xp[i], Tok(), f"xp{i}") for i in range(4)])
        gate_b = sb("gate_b", [128, D], F32)
        t_gate = Tok()
        small = sb("small", [128, 64], F32)
        smring = Ring([(small[:, 8 * i:8 * i + 8], Tok()) for i in range(8)])
        cT = sb("cT", [128, 32], F32); t_cT = Tok()
        modT = [sb(f"modT{i}", [128, 48, 2], F32) for i in range(NL)]
        t_mod = [Tok() for _ in range(NL)]
        gT = [sb(f"gT{i}", [128, NKC, 2], F32) for i in range(NL)]
        t_gT = [Tok() for _ in range(NL)]
        nwT = sb("nwT", [128, 16], F32); t_nwT = Tok()
        bmT = sb("bmT", [128, 48], F32); t_bmT = Tok()
        rows = sb("rows", [48, 128], F32); t_rows = Tok()
        wsg_f = sb("wsg_f", [128, 8, 128], F32); t_wsgf = Tok()
        wsg_b = sb("wsg_b", [128, 8, 128], BF16); t_wsgb = Tok()
        wsguT = sb("wsguT", [128, 8, 128], BF16); t_wsguT = Tok()
        bsguT = sb("bsguT", [128, 8], F32); t_bsguT = Tok()
        vnw_b = sb("vnw_b", [128, 1024], F32); t_vnw = Tok()
        qnw_b = sb("qnw_b", [128, 128], F32); t_qnw = Tok()
        knw_b = sb("knw_b", [128, 128], F32); t_knw = Tok()
        negC = sb("negC", [128, 2], F32); t_negC = Tok()
        diag = [sb(f"diag{i}", [128, 128], F32) for i in range(2)]
        dring = Ring([(diag[i], Tok()) for i in range(2)])
        bank = [es.enter_context(nc.psum_tensor(f"bank{i}", [128, 512], F32)) for i in range(8)]
        t_bank = [Tok() for _ in range(8)]
        aring = Ring([(bank[i], t_bank[i]) for i in range(3)])
        Sb, t_Sb = bank[3], t_bank[3]
        Tb = [(bank[4], t_bank[4]), (bank[5], t_bank[5])]
        Ob = [(bank[6], t_bank[6]), (bank[7], t_bank[7])]

        block = es.enter_context(nc.Block())
        dbg_toks = []

        def dbg(name, ap, tok):
            if not DEBUG:
                return
            d = nc.dram_tensor("dbg_" + name, list(ap.shape), ap.dtype, kind="ExternalOutput").ap()
            tk = Tok()
            S.dma("sp", lambda e: e.dma_start(out=d, in_=ap), "dbg_" + name, reads=[tok], writes=[tk])
            dbg_toks.append(tk)

        S.dma("sp", lambda e: e.dma_start(out=identb[:], in_=identb_in), "c_identb", writes=[t_identb])
        S.dma("sp", lambda e: e.dma_start(out=identf[:], in_=identf_in), "c_identf", writes=[t_identf])
        S.op("dve", lambda e: e.memset(onesf[:], 1.0), writes=[t_onesf])
        if cc:
            def ones_v(e):
                e.memset(vst[0][:, :, 128:130], 1.0)
                return e.memset(vst[1][:, :, 128:130], 1.0)
            S.op("dve", ones_v, writes=[t_Vones])
        else:
            S.op("dve", lambda e: e.memset(VA[:, :, :, 128:130], 1.0), writes=[t_Vones])

        t_wci = [Tok() for _ in range(NL)]
        t_wco = [Tok() for _ in range(NL)]
        def emit_casts(li):
            l = layers[li]

            def cast_in(e):
                res = []
                for kc in range(NKC):
                    for c0 in (0, 6):
                        ncb = 6 if c0 == 0 else 5
                        dst = wbi[li][c0:c0 + ncb, :, kc * 512:(kc + 1) * 512]
                        src = w_in[l, kc * 128:(kc + 1) * 128, c0 * 512:(c0 + ncb) * 512].rearrange("p (cb c) -> cb p c", c=512)
                        res.append(e.dma_start(out=dst, in_=src))
                return res
            S.dma("pool", cast_in, f"cast_in{li}", writes=[t_wci[li]], n=2 * NKC)

            def cast_out(e):
                res = []
                for kc in range(NKC):
                    dst = wbo[li][:, :, kc * 512:(kc + 1) * 512]
                    src = w_out[l, kc * 128:(kc + 1) * 128, :].rearrange("p (cb c) -> cb p c", c=512)
                    res.append(e.dma_start(out=dst, in_=src))
                return res
            S.dma("pool", cast_out, f"cast_out{li}", writes=[t_wco[li]], n=NKC)
        emit_casts(0)

        def small_T(src_ap, n, dst, t_dst, extra_reads=()):
            S.dma("sp", lambda e: e.dma_start(out=rows[0:n, :], in_=src_ap), "rows", writes=[t_rows])
            S.op("pe", lambda e: e.transpose(out=Sb[:, 0:n], in_=rows[0:n, :], identity=identf[0:n, 0:n]),
                 reads=[t_rows, t_identf], writes=[t_Sb])
            S.op("dve", lambda e: e.tensor_copy(out=dst, in_=Sb[:, 0:n]), reads=[t_Sb], writes=[t_dst])

        S.dma("sp", lambda e: e.dma_start(out=rows[0:32, :], in_=c2), "rows", writes=[t_rows])
        S.op("act", lambda e: e.activation(out=rows[0:32, :], in_=rows[0:32, :], func=AF.Silu), reads=[t_rows], writes=[t_rows])
        S.op("pe", lambda e: e.transpose(out=Sb[:, 0:32], in_=rows[0:32, :], identity=identf[0:32, 0:32]),
             reads=[t_rows, t_identf], writes=[t_Sb])
        S.op("dve", lambda e: e.tensor_copy(out=cT[:], in_=Sb[:, 0:32]), reads=[t_Sb], writes=[t_cT])
        cTv = cT[:].rearrange("p (r k) -> p r k", r=2)

        for li, l in enumerate(layers):
            small_T(b_mod[l], 48, bmT[:], t_bmT)
            small_T(norm_w[l], 16, nwT[:], t_nwT)
            for jb in range(24):
                wb, t_wb, wkey = wring.next()
                wv = wb[:].bitcast(F32).rearrange("p (k c) -> p k c", c=256)
                S.dma("sp", lambda e, wv=wv, l=l, jb=jb: e.dma_start(
                    out=wv, in_=w_mod[l, :, jb * 256:(jb + 1) * 256].rearrange("(k p) c -> p k c", p=128)),
                    wkey, writes=[t_wb])

                def mm_mod(e, wv=wv, jb=jb):
                    r = None
                    for jj in range(2):
                        j = jb * 2 + jj
                        for kc in range(NKC):
                            r = e.matmul(Sb[:, 2 * j:2 * j + 2], lhsT=wv[:, kc, jj * 128:(jj + 1) * 128], rhs=cTv[:, :, kc],
                                         start=(kc == 0), stop=(kc == NKC - 1))
                    return r
                S.op("pe", mm_mod, reads=[t_wb, t_cT], writes=[t_Sb])
            modv = modT[li]
            S.op("dve", lambda e, modv=modv: e.tensor_tensor(
                out=modv[:], in0=Sb[:, 0:96].rearrange("p (j r) -> p j r", r=2),
                in1=bmT[:].unsqueeze(2).broadcast_to([128, 48, 2]), op=ALU.add),
                reads=[t_Sb, t_bmT], writes=[t_mod[li]])
            S.op("dve", lambda e, modv=modv, li=li: e.tensor_scalar(
                out=gT[li][:], in0=modv[:, 16:32, :], scalar1=1.0, scalar2=None, op0=ALU.add),
                reads=[t_mod[li]], writes=[t_gT[li]])
            S.op("dve", lambda e, li=li: e.tensor_tensor(
                out=gT[li][:], in0=gT[li][:], in1=nwT[:].unsqueeze(2).broadcast_to([128, 16, 2]), op=ALU.mult),
                reads=[t_nwT], writes=[t_gT[li]])

        for li in range(NL):
            dbg(f"modT{li}", modT[li][:], t_mod[li])
            dbg(f"gT{li}", gT[li][:], t_gT[li])
        dbg("cT", cT[:], t_cT)

        def rstd_small(ss_ap, t_ss, n, inv_n):
            sd, t_sd = smring.next()
            S.op("act", lambda e: e.activation(out=sd[:, 0:n], in_=ss_ap, func=AF.Sqrt, bias=EPS, scale=inv_n),
                 reads=[t_ss], writes=[t_sd])
            rs, t_rs = smring.next()
            S.op("dve", lambda e: e.reciprocal(out=rs[:, 0:n], in_=sd[:, 0:n]), reads=[t_sd], writes=[t_rs])
            return rs[:, 0:n], t_rs

        def make_hT_a0(src_ap, t_src, t):
            xb, t_xb, xkey = xring.next()
            S.dma("sp", lambda e: e.dma_start(out=xb[:], in_=src_ap[t * 128:(t + 1) * 128, :]), xkey, reads=[t_src[t]], writes=[t_xb])
            return xb, t_xb

        def make_hT_a1(st0):
            xb, t_xb = st0
            xnb, t_xn = xnring.next()
            ss, t_ss = smring.next()
            S.op("act", lambda e: e.activation(out=xnb[:], in_=xb[:], func=AF.Square, accum_out=ss[:, 0:1]),
                 reads=[t_xb], writes=[t_xn, t_ss])
            rs, t_rs = rstd_small(ss[:, 0:1], t_ss, 1, 1.0 / D)
            S.op("act", lambda e: e.activation(out=xnb[:], in_=xb[:], func=AF.Copy, scale=rs[:, 0:1]),
                 reads=[t_xb, t_rs], writes=[t_xn])
            return xnb, t_xn

        def make_hT_a(src_ap, t_src, t):
            return make_hT_a1(make_hT_a0(src_ap, t_src, t))

        def make_hT_b(st, t, li, dst, t_dst, n_act=4):
            xnb, t_xn = st
            r = 1 if is_ctx(t) else 0
            for half in range(2):
                tb, t_tb = Tb[half]
                tbv = tb[:].bitcast(BF16).rearrange("p (k c) -> p k c", c=128)

                def tr(e, half=half, tbv=tbv):
                    rr = None
                    for k in range(8):
                        kc = half * 8 + k
                        rr = e.transpose(out=tbv[:, k, :], in_=xnb[:, kc * 128:(kc + 1) * 128], identity=identb[:])
                    return rr
                S.op("pe", tr, reads=[t_xn, t_identb], writes=[t_tb])
                na = n_act if half == 1 else 0

                def ev(e, half=half, tbv=tbv, na=na):
                    rr = None
                    for k in range(8 - na):
                        kc = half * 8 + k
                        rr = e.tensor_scalar(out=dst[:, kc, :], in0=tbv[:, k, :], scalar1=gT[li][:, kc, r:r + 1],
                                             scalar2=modT[li][:, kc, r:r + 1], op0=ALU.mult, op1=ALU.add)
                    return rr

                def ev_act(e, half=half, tbv=tbv, na=na):
                    rr = None
                    for k in range(8 - na, 8):
                        kc = half * 8 + k
                        rr = e.activation(out=dst[:, kc, :], in_=tbv[:, k, :], func=AF.Identity,
                                          scale=gT[li][:, kc, r:r + 1], bias=modT[li][:, kc, r:r + 1])
                    return rr
                S.op("dve", ev, reads=[t_tb, t_gT[li], t_mod[li]], writes=[t_dst])
                if na:
                    S.op("act", ev_act, reads=[t_tb, t_gT[li], t_mod[li]], writes=[t_dst])

        def make_hT(src_ap, t_src, t, li, dst, t_dst):
            make_hT_b(make_hT_a(src_ap, t_src, t), t, li, dst, t_dst)

        def proj(lhs, t_lhs, wb, t_wb):
            ab, t_ab = aring.next()
            wv = wb[:].rearrange("p (k c) -> p k c", c=512)

            def mm(e):
                rr = None
                for kc in range(NKC):
                    rr = e.matmul(ab[:], lhsT=lhs[:, kc, :], rhs=wv[:, kc, :], start=(kc == 0), stop=(kc == NKC - 1))
                return rr
            S.op("pe", mm, reads=[t_lhs, t_wb], writes=[t_ab])
            return ab, t_ab

        def head_rstd(src_ap, t_src, nh):
            sq, t_sq = tring.next()
            S.op("act", lambda e: e.activation(out=sq[:, 0:nh * 128], in_=src_ap, func=AF.Square), reads=[t_src], writes=[t_sq])
            ss, t_ss = smring.next()
            S.op("dve", lambda e: e.tensor_reduce(out=ss[:, 0:nh], in_=sq[:, 0:nh * 128].rearrange("p (h d) -> p h d", d=128),
                                                  axis=AX.X, op=ALU.add), reads=[t_sq], writes=[t_ss])
            return rstd_small(ss[:, 0:nh], t_ss, nh, 1.0 / 128)

        def apply_rope(src, t_src, nh, rt, t_rt, dst, t_dst):
            n = nh * 128
            t1, t_t1 = tring.next()
            cosb = rt[:, 0:128].unsqueeze(1).broadcast_to([128, nh, 128])
            S.op("dve", lambda e: e.tensor_tensor(out=t1[:, 0:n].rearrange("p (h d) -> p h d", d=128),
                                                  in0=src[:, 0:n].rearrange("p (h d) -> p h d", d=128), in1=cosb, op=ALU.mult),
                 reads=[t_src, t_rt], writes=[t_t1])
            rot, t_rot = tring.next()
            sv = src[:, 0:n].rearrange("p (h b t d) -> p h b t d", b=2, t=2, d=32)
            rv = rot[:, 0:n].rearrange("p (h b t d) -> p h b t d", b=2, t=2, d=32)
            t1v = t1[:, 0:n].rearrange("p (h b t d) -> p h b t d", b=2, t=2, d=32)
            dv = dst[:, 0:n].rearrange("p (h b t d) -> p h b t d", b=2, t=2, d=32)
            sinv = rt[:, 128:256].rearrange("p (b t d) -> p b t d", b=2, t=2)

            def rotf(e):
                e.tensor_tensor(out=rv[:, :, :, 0, :], in0=sv[:, :, :, 1, :],
                                in1=sinv[:, :, 0, :].unsqueeze(1).broadcast_to([128, nh, 2, 32]), op=ALU.mult)
                return e.tensor_tensor(out=rv[:, :, :, 1, :], in0=sv[:, :, :, 0, :],
                                       in1=sinv[:, :, 1, :].unsqueeze(1).broadcast_to([128, nh, 2, 32]), op=ALU.mult)
            S.op("dve", rotf, reads=[t_src, t_rt], writes=[t_rot])

            def fin(e):
                e.tensor_tensor(out=dv[:, :, :, 0, :], in0=t1v[:, :, :, 0, :], in1=rv[:, :, :, 0, :], op=ALU.subtract)
                return e.tensor_tensor(out=dv[:, :, :, 1, :], in0=t1v[:, :, :, 1, :], in1=rv[:, :, :, 1, :], op=ALU.add)
            S.op("dve", fin, reads=[t_t1, t_rot], writes=[t_dst])

        out_toks = []

        def run_layer(li, l, src_ap, dst_ap, t_srcx, p2_tiles, last_in_prog):

            S.dma("sp", lambda e, l=l: e.dma_start(out=wsg_f[:], in_=w_sgu[l].rearrange("g p q -> p g q")), "wsgf", writes=[t_wsgf])
            S.op("dve", lambda e: e.tensor_copy(out=wsg_b[:], in_=wsg_f[:]), reads=[t_wsgf], writes=[t_wsgb])
            tb, t_tb = Tb[0]
            tbv0 = tb[:].bitcast(BF16).rearrange("p (k c) -> p k c", c=128)

            def trw(e, tbv0=tbv0):
                rr = None
                for g in range(8):
                    rr = e.transpose(out=tbv0[:, g, :], in_=wsg_b[:, g, :], identity=identb[:])
                return rr
            S.op("pe", trw, reads=[t_wsgb, t_identb], writes=[t_tb])
            S.op("dve", lambda e, tbv0=tbv0: e.tensor_copy(out=wsguT[:], in_=tbv0[:]), reads=[t_tb], writes=[t_wsguT])
            small_T(b_sgu[l], 8, bsguT[:], t_bsguT)
            S.dma("sp", lambda e, l=l: e.dma_start(out=vnw_b[:], in_=v_norm_w[l].partition_broadcast(128)), "vnw", writes=[t_vnw])
            S.dma("sp", lambda e, l=l: e.dma_start(out=qnw_b[:], in_=q_norm_w[l].partition_broadcast(128)), "qnw", writes=[t_qnw])
            S.dma("sp", lambda e, l=l: e.dma_start(out=knw_b[:], in_=k_norm_w[l].partition_broadcast(128)), "knw", writes=[t_knw])
            mq, t_mq = smring.next()
            S.op("dve", lambda e, mq=mq: e.tensor_reduce(out=mq[:, 0:1], in_=qnw_b[:], axis=AX.X, op=ALU.max, apply_absolute_value=True),
                 reads=[t_qnw], writes=[t_mq])
            S.op("dve", lambda e, mq=mq: e.tensor_reduce(out=mq[:, 1:2], in_=knw_b[:], axis=AX.X, op=ALU.max, apply_absolute_value=True),
                 reads=[t_knw], writes=[t_mq])
            S.op("dve", lambda e, mq=mq: e.tensor_tensor(out=negC[:, 0:1], in0=mq[:, 0:1], in1=mq[:, 1:2], op=ALU.mult),
                 reads=[t_mq], writes=[t_negC])
            S.op("dve", lambda e: e.tensor_scalar(out=negC[:, 0:1], in0=negC[:, 0:1], scalar1=-float(np.sqrt(128.0)), scalar2=None, op0=ALU.mult),
                 writes=[t_negC])
            def build_gate(r, li=li):
                for q4 in range(4):
                    for k in range(4):
                        kc = q4 * 4 + k
                        dg, t_dg = dring.next()
                        S.op("dve", lambda e, dg=dg, kc=kc, r=r: e.tensor_scalar(
                            out=dg[:], in0=identf[:], scalar1=modT[li][:, 32 + kc, r:r + 1], scalar2=None, op0=ALU.mult),
                            reads=[t_identf, t_mod[li]], writes=[t_dg])
                        S.op("pe", lambda e, dg=dg, k=k: e.matmul(Sb[:, k * 128:(k + 1) * 128], lhsT=onesf[:], rhs=dg[:], start=True, stop=True),
                             reads=[t_dg, t_onesf], writes=[t_Sb])
                    S.op("dve", lambda e, q4=q4: e.tensor_copy(out=gate_b[:, q4 * 512:(q4 + 1) * 512], in_=Sb[:]),
                         reads=[t_Sb], writes=[t_gate])
            build_gate(0)

            wkv, t_wkv, wkey = wring.next()
            S.dma("sp", lambda e, wkv=wkv: e.dma_start(out=wkv[:], in_=wbi[li][8]), wkey, reads=[t_wci[li]], writes=[t_wkv])
            kv_toks = []
            P1 = list(p1_tiles)
            NP1 = len(P1)
            hbs = {}
            st0 = {}
            st1 = {}
            for j in range(min(2, NP1)):
                st0[j] = make_hT_a0(src_ap, t_srcx, P1[j])
            st1[0] = make_hT_a1(st0.pop(0))
            hbs[0] = h1ring.next()
            make_hT_b(st1.pop(0), P1[0], li, hbs[0][0], hbs[0][1])
            if NP1 > 1:
                st1[1] = make_hT_a1(st0.pop(1))
            if NP1 > 2:
                st0[2] = make_hT_a0(src_ap, t_srcx, P1[2])
            for n_, t in enumerate(P1):
                if n_ + 3 < NP1:
                    st0[n_ + 3] = make_hT_a0(src_ap, t_srcx, P1[n_ + 3])
                if n_ + 2 < NP1:
                    st1[n_ + 2] = make_hT_a1(st0.pop(n_ + 2))
                hb, t_hb = hbs.pop(n_)
                if n_ == 0 and li == 0:
                    dbg("hT_p1", hb[:], t_hb)
                ab, t_ab = proj(hb, t_hb, wkv, t_wkv)
                if n_ + 1 < NP1:
                    hbs[n_ + 1] = h1ring.next()
                    make_hT_b(st1.pop(n_ + 1), P1[n_ + 1], li, hbs[n_ + 1][0], hbs[n_ + 1][1])
                rk, t_rk = head_rstd(ab[:, 0:256], t_ab, 2)
                kn, t_kn = tring.next()

                def knf(e, ab=ab, rk=rk, kn=kn):
                    rr = None
                    for h in range(2):
                        rr = e.scalar_tensor_tensor(out=kn[:, h * 128:(h + 1) * 128], in0=ab[:, h * 128:(h + 1) * 128],
                                                    scalar=rk[:, h:h + 1], in1=knw_b[:], op0=ALU.mult, op1=ALU.mult)
                    return rr
                S.op("dve", knf, reads=[t_ab, t_rk, t_knw], writes=[t_kn])
                kb, t_kb = qbring.next()
                if is_ctx(t):
                    S.op("dve", lambda e, kb=kb, kn=kn: e.tensor_copy(out=kb[:, 0:256], in_=kn[:, 0:256]), reads=[t_kn], writes=[t_kb])
                else:
                    rt, t_rt, rkey = r1ring.next()
                    lt = lat_index(t)
                    S.dma("sp", lambda e, rt=rt, lt=lt: e.dma_start(out=rt[:], in_=rope[lt * 128:(lt + 1) * 128, :]), rkey, writes=[t_rt])
                    apply_rope(kn, t_kn, 2, rt, t_rt, kb, t_kb)
                tb, t_tb = Tb[0]
                tbv = tb[:].bitcast(BF16).rearrange("p (k c) -> p k c", c=128)

                def trk(e, kb=kb, tbv=tbv):
                    e.transpose(out=tbv[:, 0, :], in_=kb[:, 0:128], identity=identb[:])
                    return e.transpose(out=tbv[:, 1, :], in_=kb[:, 128:256], identity=identb[:])
                S.op("pe", trk, reads=[t_kb, t_identb], writes=[t_tb])
                if cc:
                    ks, t_ks, kkey = kstring.next()
                    S.op("act", lambda e, tbv=tbv, ks=ks: e.copy(out=ks[:], in_=tbv[:, 0:2, :]), reads=[t_tb], writes=[t_ks])
                    tk_ = Tok()
                    S.dma("sp", lambda e, ks=ks, t=t: e.dma_start(
                        out=kv_send[li][:, 0:4352].rearrange("p (h n) -> p h n", h=2)[:, :, t * 128:(t + 1) * 128], in_=ks[:]),
                        kkey, reads=[t_ks], writes=[tk_])
                    vs, t_vs, vkey = vstring.next()
                    S.op("act", lambda e, ab=ab, vs=vs: e.copy(out=vs[:, :, 0:128], in_=ab[:, 256:512].rearrange("p (h d) -> p h d", d=128)),
                         reads=[t_ab, t_Vones], writes=[t_vs])
                    tv_ = Tok()
                    S.dma("sp", lambda e, vs=vs, t=t: e.dma_start(
                        out=kv_send[li][:, 4352 + t * 260:4352 + (t + 1) * 260], in_=vs[:].rearrange("p g c -> p (g c)")),
                        vkey, reads=[t_vs], writes=[tv_])
                    kv_toks.extend([tk_, tv_])
                else:
                    S.op("act", lambda e, tbv=tbv, t=t: e.copy(out=KT[:, :, t * 128:(t + 1) * 128], in_=tbv[:, 0:2, :]),
                         reads=[t_tb], writes=[t_K[t]])
                    S.op("act", lambda e, ab=ab, t=t: e.copy(out=VA[:, t, :, 0:128], in_=ab[:, 256:512].rearrange("p (h d) -> p h d", d=128)),
                         reads=[t_ab, t_Vones], writes=[t_V[t]])
            if cc:
                t_recv = Tok()
                S.dma("pool", lambda e: e.collective_compute("AllGather", ALU.bypass, replica_groups=PAIRS,
                                                             ins=[kv_send[li]], outs=[kv_recv[li]]),
                      f"cc{li}", reads=kv_toks, writes=[t_recv], inc=1)
                for blk in range(2):
                    S.dma("sp", lambda e, blk=blk: e.dma_start(
                        out=KT[:, :, blk * 2176:(blk + 1) * 2176],
                        in_=kv_recv[li][blk * 128:(blk + 1) * 128, 0:4352].rearrange("p (h n) -> p h n", h=2)),
                        f"KTl{blk}", reads=[t_recv], writes=[tkb[blk]])
                    S.dma("sp", lambda e, blk=blk: e.dma_start(
                        out=VA[:, blk * 17:(blk + 1) * 17, :, :],
                        in_=kv_recv[li][blk * 128:(blk + 1) * 128, 4352:KVW].rearrange("p (j g c) -> p j g c", g=2, c=130)),
                        f"VAl{blk}", reads=[t_recv], writes=[tvb[blk]])

            if li == 0:
                t0_ = p1_tiles[0]
                dbg("KT0", KT[:, :, t0_ * 128:(t0_ + 1) * 128], t_K[t0_])
                dbg("VA0", VA[:, t0_, :, :], t_V[t0_])
                if cc:
                    dbg("KT33", KT[:, :, 33 * 128:34 * 128], t_K[33])
                dbg("negC", negC[:], t_negC)
                dbg("gate_b", gate_b[:], t_gate)
                dbg("wsguT", wsguT[:], t_wsguT)
            if li + 1 < NL:
                emit_casts(li + 1)
            lat_tiles = [t for t in p2_tiles if not is_ctx(t)]
            ctx_tiles = [t for t in p2_tiles if is_ctx(t)]
            groups = [lat_tiles[i:i + TG] for i in range(0, len(lat_tiles), TG)]
            if ctx_tiles:
                groups.append(ctx_tiles)
            t_xs_next = [Tok() for _ in range(NT_ALL)]
            for grp in groups:
                ng = len(grp)
                rflag = 1 if is_ctx(grp[0]) else 0
                if rflag:
                    build_gate(1)
                kall = list(range(NT_ALL)) if cc else list(p1_tiles)
                ktiles = [t for t in kall if is_ctx(t)] if rflag else kall
                for i, t in enumerate(grp):
                    make_hT(src_ap, t_srcx, t, li, hT[i], t_hT[i])
                    if not rflag:
                        lt = lat_index(t)
                        S.dma("sp", lambda e, i=i, lt=lt: e.dma_start(out=ropeT[i][:], in_=rope[lt * 128:(lt + 1) * 128, :]),
                              f"ropeT{i}", writes=[t_rope[i]])
                order = [(2, "v", 0), (0, "u", 0), (4, "za", 0), (3, "v", 1), (1, "u", 1), (5, "za", 1),
                         (6, "q", 0), (7, "q", 1), (9, "zb", 0), (10, "zb", 1)]
                pending_backs = []
                for (cb, kind, hh) in order:
                    wb, t_wb, wkey = wring.next()
                    S.dma("sp", lambda e, wb=wb, cb=cb: e.dma_start(out=wb[:], in_=wbi[li][cb]), wkey, reads=[t_wci[li]], writes=[t_wb])
                    for i, t in enumerate(grp):
                        ab, t_ab = proj(hT[i], t_hT[i], wb, t_wb)
                        if i == 0:
                            while pending_backs:
                                pending_backs.pop(0)()
                        elif len(pending_backs) >= 2:
                            pending_backs.pop(0)()
                        back = None
                        if kind == "v":
                            gv, t_gv = tring.next()
                            S.op("act", lambda e, gv=gv, ab=ab: e.activation(out=gv[:], in_=ab[:], func=AF.Gelu_apprx_tanh),
                                 reads=[t_ab], writes=[t_gv])
                            rv, t_rv = head_rstd(gv[:], t_gv, 4)
                            v1, t_v1 = tring.next()
                            S.op("dve", lambda e, v1=v1, gv=gv, rv=rv: e.tensor_tensor(
                                out=v1[:].rearrange("p (g d) -> p g d", d=128), in0=gv[:].rearrange("p (g d) -> p g d", d=128),
                                in1=rv.unsqueeze(2).broadcast_to([128, 4, 128]), op=ALU.mult), reads=[t_gv, t_rv], writes=[t_v1])
                            vb, t_vb = vnring.next()
                            S.op("dve", lambda e, vb=vb, v1=v1, hh=hh: e.tensor_tensor(
                                out=vb[:], in0=v1[:], in1=vnw_b[:, hh * 512:(hh + 1) * 512], op=ALU.mult),
                                reads=[t_v1, t_vnw], writes=[t_vb])

                            def back(vb=vb, t_vb=t_vb, hh=hh, i=i):
                                def sgu(e):
                                    rr = None
                                    for g in range(4):
                                        rr = e.matmul(Sb[:, g * 128:(g + 1) * 128], lhsT=wsguT[:, 4 * hh + g, :],
                                                      rhs=vb[:, g * 128:(g + 1) * 128], start=True, stop=True)
                                    return rr
                                S.op("pe", sgu, reads=[t_vb, t_wsguT], writes=[t_Sb])
                                S.op("dve", lambda e: e.tensor_tensor(
                                    out=s_sb[i][:].rearrange("p (g d) -> p g d", d=128), in0=Sb[:].rearrange("p (g d) -> p g d", d=128),
                                    in1=bsguT[:, 4 * hh:4 * hh + 4].unsqueeze(2).broadcast_to([128, 4, 128]), op=ALU.add),
                                    reads=[t_Sb, t_bsguT], writes=[t_s[i]])
                        elif kind == "u":
                            gu, t_gu = tring.next()
                            S.op("act", lambda e, gu=gu, ab=ab: e.activation(out=gu[:], in_=ab[:], func=AF.Gelu_apprx_tanh),
                                 reads=[t_ab], writes=[t_gu])
                            S.op("dve", lambda e, gu=gu, i=i: e.tensor_tensor(out=s_sb[i][:], in0=gu[:], in1=s_sb[i][:], op=ALU.mult),
                                 reads=[t_gu], writes=[t_s[i]])
                        elif kind == "za":
                            sz, t_sz = tring.next()
                            S.op("act", lambda e, sz=sz, ab=ab: e.activation(out=sz[:], in_=ab[:], func=AF.Silu), reads=[t_ab], writes=[t_sz])
                            S.op("dve", lambda e, sz=sz, i=i, hh=hh: e.tensor_tensor(
                                out=gated[i][:, hh * 512:(hh + 1) * 512], in0=sz[:], in1=s_sb[i][:], op=ALU.mult),
                                reads=[t_sz, t_s[i]], writes=[t_gated[i]])
                        elif kind == "q":
                            rq, t_rq = head_rstd(ab[:], t_ab, 4)
                            q1, t_q1 = tring.next()
                            S.op("dve", lambda e, q1=q1, ab=ab, rq=rq: e.tensor_tensor(
                                out=q1[:].rearrange("p (g d) -> p g d", d=128), in0=ab[:].rearrange("p (g d) -> p g d", d=128),
                                in1=rq.unsqueeze(2).broadcast_to([128, 4, 128]), op=ALU.mult), reads=[t_ab, t_rq], writes=[t_q1])
                            S.op("dve", lambda e, q1=q1: e.tensor_tensor(
                                out=q1[:].rearrange("p (g d) -> p g d", d=128), in0=q1[:].rearrange("p (g d) -> p g d", d=128),
                                in1=qnw_b[:].unsqueeze(1).broadcast_to([128, 4, 128]), op=ALU.mult), reads=[t_qnw], writes=[t_q1])
                            qb, t_qb = qbring.next()
                            if rflag:
                                S.op("dve", lambda e, qb=qb, q1=q1: e.tensor_copy(out=qb[:], in_=q1[:]), reads=[t_q1], writes=[t_qb])
                            else:
                                apply_rope(q1, t_q1, 4, ropeT[i], t_rope[i], qb, t_qb)

                            def back(qb=qb, t_qb=t_qb, hh=hh, i=i):
                                tb, t_tb = Tb[(i + hh) % 2]
                                tbv = tb[:].bitcast(BF16).rearrange("p (k c) -> p k c", c=128)

                                def trq(e):
                                    rr = None
                                    for h in range(4):
                                        rr = e.transpose(out=tbv[:, h, :], in_=qb[:, h * 128:(h + 1) * 128], identity=identb[:])
                                    return rr
                                S.op("pe", trq, reads=[t_qb, t_identb], writes=[t_tb])
                                S.op("act", lambda e: e.copy(out=qT[i][:, 4 * hh:4 * hh + 4, :], in_=tbv[:, 0:4, :]),
                                     reads=[t_tb], writes=[t_qT[i]])
                        else:
                            S.op("act", lambda e, ab=ab, i=i, hh=hh: e.activation(out=szb[i][:, hh * 512:(hh + 1) * 512], in_=ab[:], func=AF.Silu),
                                 reads=[t_ab], writes=[t_szb[i]])
                        if back is not None:
                            pending_backs.append(back)
                while pending_backs:
                    pending_backs.pop(0)()

                if li == 0 and grp is groups[0]:
                    dbg("gatedA", gated[0][:, 0:1024], t_gated[0])
                    dbg("qT", qT[0][:], t_qT[0])
                    dbg("szb", szb[0][:], t_szb[0])
                inv_sqrt = float(128.0 ** -0.5)
                nk = len(ktiles)
                units = [(i, g, ki, kt) for i in range(ng) for g in range(2) for ki, kt in enumerate(ktiles)]
                LAG = 2
                (o0, t_o0), (o1, t_o1) = Ob

                def attn_front(u):
                    i, g, ki, kt = u
                    sbk, t_sbk = aring.next()
                    S.op("pe", lambda e: e.matmul(
                        sbk[:], lhsT=KT[:, g, kt * 128:(kt + 1) * 128], rhs=qT[i][:, 4 * g:4 * g + 4, :], start=True, stop=True),
                        reads=[t_K[kt], t_qT[i]], writes=[t_sbk])
                    pt, t_pt = ptring.next()
                    S.op("act", lambda e: e.activation(out=pt[:], in_=sbk[:], func=AF.Exp, bias=negC[:, 0:1], scale=inv_sqrt),
                         reads=[t_sbk, t_negC], writes=[t_pt])
                    return pt, t_pt

                def attn_back(u, pt, t_pt):
                    i, g, ki, kt = u

                    def pv(e):
                        rr = None
                        for hq in range(4):
                            if hq < 3:
                                oap = o0[:, hq * 129:hq * 129 + 129]
                                st = (ki == 0 and hq == 0)
                            else:
                                oap = o1[:, 0:129]
                                st = (ki == 0)
                            rr = e.matmul(oap, lhsT=pt[:, hq * 128:(hq + 1) * 128], rhs=VA[:, kt, g, 0:129],
                                          start=st, stop=(ki == nk - 1), skip_group_check=True)
                        return rr
                    S.op("pe", pv, reads=[t_pt, t_V[kt]], writes=[t_o0, t_o1])
                    if ki != nk - 1:
                        return
                    rd, t_rd = smring.next()

                    def rden(e):
                        e.reciprocal(out=rd[:, 0:3], in_=o0[:, 0:387].rearrange("p (h c) -> p h c", c=129)[:, :, 128])
                        return e.reciprocal(out=rd[:, 3:4], in_=o1[:, 128:129])
                    S.op("dve", rden, reads=[t_o0, t_o1], writes=[t_rd])

                    def onorm(e):
                        rr = None
                        for hq in range(4):
                            h = 4 * g + hq
                            oap = o0[:, hq * 129:hq * 129 + 128] if hq < 3 else o1[:, 0:128]
                            rr = e.scalar_tensor_tensor(out=gated[i][:, 1024 + h * 128:1024 + (h + 1) * 128], in0=oap,
                                                        scalar=rd[:, hq:hq + 1], in1=szb[i][:, h * 128:(h + 1) * 128],
                                                        op0=ALU.mult, op1=ALU.mult)
                        return rr
                    S.op("dve", onorm, reads=[t_o0, t_o1, t_rd, t_szb[i]], writes=[t_gated[i]])

                pend = []
                for idx in range(len(units) + LAG):
                    if idx < len(units):
                        pend.append(attn_front(units[idx]))
                    if idx >= LAG:
                        attn_back(units[idx - LAG], *pend[idx - LAG])

                if li == 0 and grp is groups[0]:
                    dbg("gated", gated[0][:], t_gated[0])
                for i, t in enumerate(grp):
                    for half in range(2):
                        tb, t_tb = Tb[half]
                        tbv = tb[:].bitcast(BF16).rearrange("p (k c) -> p k c", c=128)

                        def trg(e, half=half, tbv=tbv, i=i):
                            rr = None
                            for k in range(8):
                                kc = half * 8 + k
                                rr = e.transpose(out=tbv[:, k, :], in_=gated[i][:, kc * 128:(kc + 1) * 128], identity=identb[:])
                            return rr
                        S.op("pe", trg, reads=[t_gated[i], t_identb], writes=[t_tb])
                        S.op("act", lambda e, half=half, tbv=tbv, i=i: e.copy(out=hT[i][:, half * 8:(half + 1) * 8, :], in_=tbv[:]),
                             reads=[t_tb], writes=[t_hT[i]])

                for cb in range(4):
                    wb, t_wb, wkey = wring.next()
                    S.dma("sp", lambda e, wb=wb, cb=cb: e.dma_start(out=wb[:], in_=wbo[li][cb]), wkey, reads=[t_wco[li]], writes=[t_wb])
                    for i, t in enumerate(grp):
                        xpb, t_xp, xkey = xpring.next()
                        S.dma("sp", lambda e, xpb=xpb, t=t, cb=cb: e.dma_start(
                            out=xpb[:], in_=src_ap[t * 128:(t + 1) * 128, cb * 512:(cb + 1) * 512]), xkey, reads=[t_srcx[t]], writes=[t_xp])
                        ab, t_ab = proj(hT[i], t_hT[i], wb, t_wb)
                        yg, t_yg = tring.next()
                        S.op("dve", lambda e, yg=yg, ab=ab, cb=cb: e.tensor_tensor(
                            out=yg[:], in0=ab[:], in1=gate_b[:, cb * 512:(cb + 1) * 512], op=ALU.mult),
                            reads=[t_ab, t_gate], writes=[t_yg])
                        S.op("pool", lambda e, yg=yg, xpb=xpb: e.tensor_tensor(out=xpb[:], in0=xpb[:], in1=yg[:], op=ALU.add),
                             reads=[t_yg], writes=[t_xp])
                        if last_in_prog:
                            if final:
                                drow = t
                            else:
                                drow = t
                        else:
                            drow = t
                        S.dma("pool", lambda e, xpb=xpb, drow=drow, cb=cb: e.dma_start(
                            out=dst_ap[drow * 128:(drow + 1) * 128, cb * 512:(cb + 1) * 512], in_=xpb[:]), xkey,
                            reads=[t_xp], writes=[t_xs_next[t]])
                        if last_in_prog:
                            out_toks.append(t_xp)
            return t_xs_next

        t_xs_all = [Tok() for _ in range(NT_ALL)]
        for li_, l_ in enumerate(layers):
            last_ = (li_ == NL - 1)
            t_xs_all = run_layer(li_, l_, xa if li_ == 0 else xs, out if last_ else xs, t_xs_all,
                                 p2_tiles_per_layer[li_], last_)

        S.final_wait("pool", list({id(t): t for t in out_toks}.values()) + dbg_toks)
        S.emit(block)
    return nc


def _rope_tables(pos):
    rows = (pos // GRID_W).astype(np.float32)
    cols = (pos % GRID_W).astype(np.float32)
    inv_freq = (np.float32(10000.0) ** (-np.arange(0, 64, 2, dtype=np.float32) / np.float32(64))).astype(np.float32)
    ang_r = rows[:, None] * inv_freq[None, :]
    ang_c = cols[:, None] * inv_freq[None, :]
    ang = np.concatenate([ang_r, ang_r, ang_c, ang_c], axis=-1).astype(np.float32)
    return np.concatenate([np.cos(ang), np.sin(ang)], axis=-1).astype(np.float32)


_PROG_CACHE = {}


def _get_prog(key, *args):
    if key not in _PROG_CACHE:
        _PROG_CACHE[key] = build(*args)
    return _PROG_CACHE[key]


def _common_inputs(c, c_ctx, norm_w, w_mod, b_mod, w_in, w_sgu, b_sgu, v_norm_w, q_norm_w, k_norm_w, w_out):
    f = lambda a: np.ascontiguousarray(np.asarray(a, dtype=np.float32))
    shared = {
        "identb": np.eye(128, dtype=np.float32).astype(ml_dtypes.bfloat16),
        "identf": np.eye(128, dtype=np.float32),
        "w_mod": f(w_mod), "b_mod": f(b_mod).reshape(2, 48, 128), "norm_w": f(norm_w).reshape(2, 16, 128),
        "w_in": f(w_in), "w_out": f(w_out), "w_sgu": f(w_sgu), "b_sgu": f(b_sgu),
        "v_norm_w": f(v_norm_w).reshape(2, 1024), "q_norm_w": f(q_norm_w), "k_norm_w": f(k_norm_w),
    }
    return shared


def kernel(x, c, ctx, c_ctx, norm_w, w_mod, b_mod, w_in, w_sgu, b_sgu, v_norm_w, q_norm_w, k_norm_w, w_out):
    x = np.asarray(x, dtype=np.float32)
    ctx = np.asarray(ctx, dtype=np.float32)
    c = np.asarray(c, dtype=np.float32)
    c_ctx = np.asarray(c_ctx, dtype=np.float32)
    shared = _common_inputs(c, c_ctx, norm_w, w_mod, b_mod, w_in, w_sgu, b_sgu, v_norm_w, q_norm_w, k_norm_w, w_out)
    H = SEQ // 2
    CH = CTX // 2

    def core_maps(xfull, cfull):
        maps = []
        for core in range(8):
            b, hf = divmod(core, 2)
            o, p = hf, 1 - hf
            xa = np.concatenate([xfull[b, o * H:(o + 1) * H], cfull[b, o * CH:(o + 1) * CH],
                                 xfull[b, p * H:(p + 1) * H], cfull[b, p * CH:(p + 1) * CH]], axis=0)
            pos = np.concatenate([np.arange(o * H, (o + 1) * H), np.arange(p * H, (p + 1) * H)])
            m = dict(shared)
            m["xa"] = np.ascontiguousarray(xa)
            m["rope"] = _rope_tables(pos)
            m["c2"] = np.ascontiguousarray(np.stack([c[b], c_ctx], 0).reshape(32, 128))
            maps.append(m)
        return maps

    all_tiles = list(range(NT_ALL))
    own_lat = list(range(16))
    if MODE == "cc":
        maps = []
        for core in range(8):
            b, hf = divmod(core, 2)
            m = dict(shared)
            m["xa"] = np.ascontiguousarray(np.concatenate([x[b, hf * H:(hf + 1) * H], ctx[b, hf * CH:(hf + 1) * CH]], axis=0))
            m["rope"] = _rope_tables(np.arange(hf * H, (hf + 1) * H))
            m["c2"] = np.ascontiguousarray(np.stack([c[b], c_ctx], 0).reshape(32, 128))
            maps.append(m)
        nc = _get_prog("cc", [0, 1], True, list(range(17)), [list(range(17)), own_lat], 16, True)
        res = run_bass_kernel_spmd(nc, maps, core_ids=list(range(8)))
        outs = [r["out"] for r in res.results]
    elif MODE == "fused":
        nc = _get_prog("fused", [0, 1], True, all_tiles, [all_tiles, own_lat], 16)
        res = run_bass_kernel_spmd(nc, core_maps(x, ctx), core_ids=list(range(8)))
        outs = [r["out"] for r in res.results]
    else:
        ncA = _get_prog("L0", [0], False, all_tiles, [list(range(17))], 17)
        resA = run_bass_kernel_spmd(ncA, core_maps(x, ctx), core_ids=list(range(8)))
        x1 = np.empty_like(x)
        ctx1 = np.empty_like(ctx)
        for core in range(8):
            b, hf = divmod(core, 2)
            xn_ = resA.results[core]["xnext"]
            x1[b, hf * H:(hf + 1) * H] = xn_[0:H]
            ctx1[b, hf * CH:(hf + 1) * CH] = xn_[H:H + CH]
        ncB = _get_prog("L1", [1], True, all_tiles, [own_lat], 16)
        resB = run_bass_kernel_spmd(ncB, core_maps(x1, ctx1), core_ids=list(range(8)))
        outs = [r["out"] for r in resB.results]
    y = np.empty((4, SEQ, D), dtype=np.float32)
    for core in range(8):
        b, hf = divmod(core, 2)
        y[b, hf * H:(hf + 1) * H] = outs[core]
    return y
```

```python
import numpy as np
from contextlib import ExitStack
import ml_dtypes
import concourse.bass as bass
import concourse.mybir as mybir
from concourse.bass_utils import run_bass_kernel_spmd

F32 = mybir.dt.float32
BF16 = mybir.dt.bfloat16
AF = mybir.ActivationFunctionType
ALU = mybir.AluOpType
AX = mybir.AxisListType

D = 2048
NKC = 16
DIN = 5632
NCB = 11
SEQ = 4096
CTX = 256
GRID_W = 64
EPS = 1e-6
TG = 4
NT_ALL = 34
DEBUG = False
MODE = "fused"


class Tok:
    __slots__ = ("w", "r")

    def __init__(self):
        self.w = None
        self.r = {}


class Sched:
    EPOCH = 20000

    def __init__(self, nc, es):
        self.nc = nc
        self.es = es
        self.names = ["pe", "act", "dve", "pool", "sp"]
        self.q = {k: [] for k in self.names}
        self.cnt = {k: 0 for k in self.names}
        self.esems = {k: [] for k in self.names}
        self.dsems = {}
        self.dcnt = {}
        self.waited = {k: {} for k in self.names}

    def _newsem(self, name):
        return self.es.enter_context(self.nc.semaphore(name))

    def _eng_event(self, eng):
        c = self.cnt[eng]
        ep, v = divmod(c, self.EPOCH)
        while len(self.esems[eng]) <= ep:
            self.esems[eng].append(self._newsem(f"e_{eng}_{len(self.esems[eng])}"))
        self.cnt[eng] = c + 1
        return (self.esems[eng][ep], v + 1, eng)

    def _collect(self, eng, reads, writes):
        evs = []
        for t in reads:
            if t.w is not None:
                evs.append(t.w)
        for t in writes:
            if t.w is not None:
                evs.append(t.w)
            evs.extend(t.r.values())
        waits = {}
        for (sem, val, e) in evs:
            if e is not None and e == eng and eng == "pe":
                continue
            key = id(sem)
            if self.waited[eng].get(key, 0) >= val:
                continue
            if key not in waits or waits[key][1] < val:
                waits[key] = (sem, val)
        for key, (sem, val) in waits.items():
            self.waited[eng][key] = val
        return list(waits.values())

    def _mark(self, ev, reads, writes):
        k = id(ev[0])
        for t in reads:
            old = t.r.get(k)
            if old is None or old[1] < ev[1]:
                t.r[k] = ev
        for t in writes:
            t.w = ev
            t.r = {}

    def op(self, eng, fn, reads=(), writes=()):
        waits = self._collect(eng, reads, writes)
        ev = self._eng_event(eng)
        self.q[eng].append((waits, fn, ev[0], 1))
        self._mark(ev, reads, writes)

    def dma(self, eng, fn, key, reads=(), writes=(), n=1, inc=16):
        waits = self._collect(eng, reads, writes)
        if key not in self.dsems:
            self.dsems[key] = self._newsem(f"d_{key}")
            self.dcnt[key] = 0
        self.dcnt[key] += inc * n
        ev = (self.dsems[key], self.dcnt[key], None)
        self.q[eng].append((waits, fn, self.dsems[key], inc))
        self._mark(ev, reads, writes)

    def final_wait(self, eng, toks):
        waits = self._collect(eng, toks, toks)
        self.q[eng].append((waits, None, None, 0))

    def emit(self, block):
        def mk(name):
            def body(e):
                for (waits, fn, sem, inc) in self.q[name]:
                    for (s, v) in waits:
                        e.wait_ge(s, v)
                    if fn is None:
                        continue
                    r = fn(e)
                    if isinstance(r, (list, tuple)):
                        for ins in r:
                            ins.then_inc(sem, inc)
                    else:
                        r.then_inc(sem, inc)
            return body
        block.tensor(mk("pe"))
        block.scalar(mk("act"))
        block.vector(mk("dve"))
        block.gpsimd(mk("pool"))
        block.sync(mk("sp"))


class Ring:
    def __init__(self, items):
        self.items = items
        self.i = 0

    def next(self):
        it = self.items[self.i % len(self.items)]
        self.i += 1
        return it


def is_ctx(t):
    return t % 17 == 16


def lat_index(t):
    return (t // 17) * 16 + (t % 17)


PAIRS = [[0, 1], [2, 3], [4, 5], [6, 7]]
KVW = 2 * 17 * 128 + 17 * 260


def build(layers, final, p1_tiles, p2_tiles_per_layer, n_out_tiles, cc=False):
    nc = bass.Bass("TRN2", target_bir_lowering=False)
    NL = len(layers)

    def din(name, shape, dt=F32):
        return nc.dram_tensor(name, shape, dt, kind="ExternalInput").ap()

    NX = 17 if cc else NT_ALL
    xa = din("xa", [NX * 128, D])
    rope = din("rope", [(16 if cc else 32) * 128, 256])
    c2 = din("c2", [32, 128])
    identb_in = din("identb", [128, 128], BF16)
    identf_in = din("identf", [128, 128])
    w_mod = din("w_mod", [2, D, 3 * D])
    b_mod = din("b_mod", [2, 48, 128])
    norm_w = din("norm_w", [2, 16, 128])
    w_in = din("w_in", [2, D, DIN])
    w_out = din("w_out", [2, D, D])
    w_sgu = din("w_sgu", [2, 8, 128, 128])
    b_sgu = din("b_sgu", [2, 8, 128])
    v_norm_w = din("v_norm_w", [2, 1024])
    q_norm_w = din("q_norm_w", [2, 128])
    k_norm_w = din("k_norm_w", [2, 128])
    if final:
        out = nc.dram_tensor("out", [16 * 128, D], F32, kind="ExternalOutput").ap()
    else:
        out = nc.dram_tensor("xnext", [n_out_tiles * 128, D], F32, kind="ExternalOutput").ap()
    xs = nc.dram_tensor("xs", [NX * 128, D], F32).ap() if NL > 1 else None
    if cc:
        kv_send = [nc.dram_tensor(f"kv_send{i}", [128, KVW], BF16).ap() for i in range(NL)]
        kv_recv = [nc.dram_tensor(f"kv_recv{i}", [256, KVW], BF16).ap() for i in range(NL)]
    wbi = [nc.dram_tensor(f"wbi{l}", [NCB, 128, NKC * 512], BF16).ap() for l in layers]
    wbo = [nc.dram_tensor(f"wbo{l}", [4, 128, NKC * 512], BF16).ap() for l in layers]

    with ExitStack() as es:
        def sb(name, shape, dt):
            return es.enter_context(nc.sbuf_tensor(name, shape, dt))

        S = Sched(nc, es)
        identb = sb("identb_sb", [128, 128], BF16); t_identb = Tok()
        identf = sb("identf_sb", [128, 128], F32); t_identf = Tok()
        onesf = sb("onesf", [128, 128], F32); t_onesf = Tok()
        KT = sb("KT", [128, 2, NT_ALL * 128], BF16)
        VA = sb("VA", [128, NT_ALL, 2, 130], BF16)
        if cc:
            tkb = [Tok(), Tok()]
            tvb = [Tok(), Tok()]
            t_K = [tkb[kt // 17] for kt in range(NT_ALL)]
            t_V = [tvb[kt // 17] for kt in range(NT_ALL)]
            kst = [sb(f"kst{i}", [128, 2, 128], BF16) for i in range(2)]
            kstring = Ring([(kst[i], Tok(), f"kst{i}") for i in range(2)])
            vst = [sb(f"vst{i}", [128, 2, 130], BF16) for i in range(2)]
            vstring = Ring([(vst[i], Tok(), f"vst{i}") for i in range(2)])
        else:
            t_K = [Tok() for _ in range(NT_ALL)]
            t_V = [Tok() for _ in range(NT_ALL)]
        t_Vones = Tok()
        wbuf = [sb(f"wbuf{i}", [128, NKC * 512], BF16) for i in range(2)]
        t_wbuf = [Tok() for _ in range(2)]
        wring = Ring(list(zip(wbuf, t_wbuf, ["wbuf0", "wbuf1"])))
        xbuf = [sb(f"xbuf{i}", [128, D], F32) for i in range(2)]
        xring = Ring([(xbuf[i], Tok(), f"xbuf{i}") for i in range(2)])
        xn = [sb(f"xn{i}", [128, D], BF16) for i in range(2)]
        xnring = Ring([(xn[i], Tok()) for i in range(2)])
        hT = [sb(f"hT{i}", [128, NKC, 128], BF16) for i in range(TG)]
        t_hT = [(Tok(), Tok()) for _ in range(TG)]
        h1ring = Ring([(hT[i], t_hT[i]) for i in range(2)])
        s_sb = [sb(f"s_sb{i}", [128, 512], F32) for i in range(TG)]
        t_s = [Tok() for _ in range(TG)]
        gated = [sb(f"gated{i}", [128, D], BF16) for i in range(TG)]
        t_gated = [Tok() for _ in range(TG)]
        qT = [sb(f"qT{i}", [128, 8, 128], BF16) for i in range(TG)]
        t_qT = [Tok() for _ in range(TG)]
        szb = [sb(f"szb{i}", [128, 1024], BF16) for i in range(TG)]
        t_szb = [Tok() for _ in range(TG)]
        ropeT = [sb(f"ropeT{i}", [128, 256], F32) for i in range(TG)]
        t_rope = [Tok() for _ in range(TG)]
        r1ring = Ring([(ropeT[i], t_rope[i], f"ropeT{i}") for i in range(2)])
        tmps = [sb(f"tmp{i}", [128, 512], F32) for i in range(6)]
        tring = Ring([(tmps[i], Tok()) for i in range(6)])
        vnb = [sb(f"vnb{i}", [128, 512], BF16) for i in range(3)]
        vnring = Ring([(vnb[i], Tok()) for i in range(3)])
        qbf = [sb(f"qbf{i}", [128, 512], BF16) for i in range(3)]
        qbring = Ring([(qbf[i], Tok()) for i in range(3)])
        PT = [sb(f"PT{i}", [128, 512], BF16) for i in range(4)]
        ptring = Ring([(PT[i], Tok()) for i in range(4)])
        xp = [sb(f"xp{i}", [128, 512], F32) for i in range(4)]
        xpring = Ring([(xp[i], Tok(), f"xp{i}") for i in range(4)])
        gate_b = sb("gate_b", [128, D], F32)
        t_gate = Tok()
        small = sb("small", [128, 64], F32)
        smring = Ring([(small[:, 8 * i:8 * i + 8], Tok()) for i in range(8)])
        cT = sb("cT", [128, 32], F32); t_cT = Tok()
        modT = [sb(f"modT{i}", [128, 48, 2], F32) for i in range(NL)]
        t_mod = [Tok() for _ in range(NL)]
        gT = [sb(f"gT{i}", [128, NKC, 2], F32) for i in range(NL)]
        t_gT = [Tok() for _ in range(NL)]
        nwT = sb("nwT", [128, 16], F32); t_nwT = Tok()
        bmT = sb("bmT", [128, 48], F32); t_bmT = Tok()
        rows = sb("rows", [48, 128], F32); t_rows = Tok()
        wsg_f = sb("wsg_f", [128, 8, 128], F32); t_wsgf = Tok()
        wsg_b = sb("wsg_b", [128, 8, 128], BF16); t_wsgb = Tok()
        wsguT = sb("wsguT", [128, 8, 128], BF16); t_wsguT = Tok()
        bsguT = sb("bsguT", [128, 8], F32); t_bsguT = Tok()
        vnw_b = sb("vnw_b", [128, 1024], F32); t_vnw = Tok()
        qnw_b = sb("qnw_b", [128, 128], F32); t_qnw = Tok()
        knw_b = sb("knw_b", [128, 128], F32); t_knw = Tok()
        negC = sb("negC", [128, 2], F32); t_negC = Tok()
        diag = [sb(f"diag{i}", [128, 128], F32) for i in range(2)]
        dring = Ring([(diag[i], Tok()) for i in range(2)])
        bank = [es.enter_context(nc.psum_tensor(f"bank{i}", [128, 512], F32)) for i in range(8)]
        t_bank = [Tok() for _ in range(8)]
        aring = Ring([(bank[i], t_bank[i]) for i in range(3)])
        Sb, t_Sb = bank[3], t_bank[3]
        Tb = [(bank[4], t_bank[4]), (bank[5], t_bank[5])]
        Ob = [(bank[6], t_bank[6]), (bank[7], t_bank[7])]

        block = es.enter_context(nc.Block())
        dbg_toks = []

        def dbg(name, ap, tok):
            if not DEBUG:
                return
            d = nc.dram_tensor("dbg_" + name, list(ap.shape), ap.dtype, kind="ExternalOutput").ap()
            tk = Tok()
            S.dma("sp", lambda e: e.dma_start(out=d, in_=ap), "dbg_" + name, reads=[tok], writes=[tk])
            dbg_toks.append(tk)

        S.dma("sp", lambda e: e.dma_start(out=identb[:], in_=identb_in), "c_identb", writes=[t_identb])
        S.dma("sp", lambda e: e.dma_start(out=identf[:], in_=identf_in), "c_identf", writes=[t_identf])
        S.op("dve", lambda e: e.memset(onesf[:], 1.0), writes=[t_onesf])
        if cc:
            def ones_v(e):
                e.memset(vst[0][:, :, 128:130], 1.0)
                return e.memset(vst[1][:, :, 128:130], 1.0)
            S.op("dve", ones_v, writes=[t_Vones])
        else:
            S.op("dve", lambda e: e.memset(VA[:, :, :, 128:130], 1.0), writes=[t_Vones])

        t_wci = [Tok() for _ in range(NL)]
        t_wco = [Tok() for _ in range(NL)]
        def emit_casts(li):
            l = layers[li]

            def cast_in(e):
                res = []
                for kc in range(NKC):
                    for c0 in (0, 6):
                        ncb = 6 if c0 == 0 else 5
                        dst = wbi[li][c0:c0 + ncb, :, kc * 512:(kc + 1) * 512]
                        src = w_in[l, kc * 128:(kc + 1) * 128, c0 * 512:(c0 + ncb) * 512].rearrange("p (cb c) -> cb p c", c=512)
                        res.append(e.dma_start(out=dst, in_=src))
                return res
            S.dma("pool", cast_in, f"cast_in{li}", writes=[t_wci[li]], n=2 * NKC)

            def cast_out(e):
                res = []
                for kc in range(NKC):
                    dst = wbo[li][:, :, kc * 512:(kc + 1) * 512]
                    src = w_out[l, kc * 128:(kc + 1) * 128, :].rearrange("p (cb c) -> cb p c", c=512)
                    res.append(e.dma_start(out=dst, in_=src))
                return res
            S.dma("pool", cast_out, f"cast_out{li}", writes=[t_wco[li]], n=NKC)
        emit_casts(0)

        def small_T(src_ap, n, dst, t_dst, extra_reads=()):
            S.dma("sp", lambda e: e.dma_start(out=rows[0:n, :], in_=src_ap), "rows", writes=[t_rows])
            S.op("pe", lambda e: e.transpose(out=Sb[:, 0:n], in_=rows[0:n, :], identity=identf[0:n, 0:n]),
                 reads=[t_rows, t_identf], writes=[t_Sb])
            S.op("dve", lambda e: e.tensor_copy(out=dst, in_=Sb[:, 0:n]), reads=[t_Sb], writes=[t_dst])

        S.dma("sp", lambda e: e.dma_start(out=rows[0:32, :], in_=c2), "rows", writes=[t_rows])
        S.op("act", lambda e: e.activation(out=rows[0:32, :], in_=rows[0:32, :], func=AF.Silu), reads=[t_rows], writes=[t_rows])
        S.op("pe", lambda e: e.transpose(out=Sb[:, 0:32], in_=rows[0:32, :], identity=identf[0:32, 0:32]),
             reads=[t_rows, t_identf], writes=[t_Sb])
        S.op("dve", lambda e: e.tensor_copy(out=cT[:], in_=Sb[:, 0:32]), reads=[t_Sb], writes=[t_cT])
        cTv = cT[:].rearrange("p (r k) -> p r k", r=2)

        for li, l in enumerate(layers):
            small_T(b_mod[l], 48, bmT[:], t_bmT)
            small_T(norm_w[l], 16, nwT[:], t_nwT)
            for jb in range(24):
                wb, t_wb, wkey = wring.next()
                wv = wb[:].bitcast(F32).rearrange("p (k c) -> p k c", c=256)
                S.dma("sp", lambda e, wv=wv, l=l, jb=jb: e.dma_start(
                    out=wv, in_=w_mod[l, :, jb * 256:(jb + 1) * 256].rearrange("(k p) c -> p k c", p=128)),
                    wkey, writes=[t_wb])

                def mm_mod(e, wv=wv, jb=jb):
                    r = None
                    for jj in range(2):
                        j = jb * 2 + jj
                        for kc in range(NKC):
                            r = e.matmul(Sb[:, 2 * j:2 * j + 2], lhsT=wv[:, kc, jj * 128:(jj + 1) * 128], rhs=cTv[:, :, kc],
                                         start=(kc == 0), stop=(kc == NKC - 1))
                    return r
                S.op("pe", mm_mod, reads=[t_wb, t_cT], writes=[t_Sb])
            modv = modT[li]
            S.op("dve", lambda e, modv=modv: e.tensor_tensor(
                out=modv[:], in0=Sb[:, 0:96].rearrange("p (j r) -> p j r", r=2),
                in1=bmT[:].unsqueeze(2).broadcast_to([128, 48, 2]), op=ALU.add),
                reads=[t_Sb, t_bmT], writes=[t_mod[li]])
            S.op("dve", lambda e, modv=modv, li=li: e.tensor_scalar(
                out=gT[li][:], in0=modv[:, 16:32, :], scalar1=1.0, scalar2=None, op0=ALU.add),
                reads=[t_mod[li]], writes=[t_gT[li]])
            S.op("dve", lambda e, li=li: e.tensor_tensor(
                out=gT[li][:], in0=gT[li][:], in1=nwT[:].unsqueeze(2).broadcast_to([128, 16, 2]), op=ALU.mult),
                reads=[t_nwT], writes=[t_gT[li]])

        for li in range(NL):
            dbg(f"modT{li}", modT[li][:], t_mod[li])
            dbg(f"gT{li}", gT[li][:], t_gT[li])
        dbg("cT", cT[:], t_cT)

        def rstd_small(ss_ap, t_ss, n, inv_n):
            sd, t_sd = smring.next()
            S.op("act", lambda e: e.activation(out=sd[:, 0:n], in_=ss_ap, func=AF.Sqrt, bias=EPS, scale=inv_n),
                 reads=[t_ss], writes=[t_sd])
            rs, t_rs = smring.next()
            S.op("dve", lambda e: e.reciprocal(out=rs[:, 0:n], in_=sd[:, 0:n]), reads=[t_sd], writes=[t_rs])
            return rs[:, 0:n], t_rs

        def make_hT_a0(src_ap, t_src, t):
            xb, t_xb, xkey = xring.next()
            S.dma("sp", lambda e: e.dma_start(out=xb[:], in_=src_ap[t * 128:(t + 1) * 128, :]), xkey, reads=[t_src[t]], writes=[t_xb])
            return xb, t_xb

        def make_hT_a1(st0):
            xb, t_xb = st0
            xnb, t_xn = xnring.next()
            ss, t_ss = smring.next()
            S.op("act", lambda e: e.activation(out=xnb[:], in_=xb[:], func=AF.Square, accum_out=ss[:, 0:1]),
                 reads=[t_xb], writes=[t_xn, t_ss])
            rs, t_rs = rstd_small(ss[:, 0:1], t_ss, 1, 1.0 / D)
            S.op("act", lambda e: e.activation(out=xnb[:], in_=xb[:], func=AF.Copy, scale=rs[:, 0:1]),
                 reads=[t_xb, t_rs], writes=[t_xn])
            return xnb, t_xn

        def make_hT_a(src_ap, t_src, t):
            return make_hT_a1(make_hT_a0(src_ap, t_src, t))

        def make_hT_b(st, t, li, dst, t_dst, n_act=4):
            xnb, t_xn = st
            r = 1 if is_ctx(t) else 0
            for half in range(2):
                tb, t_tb = Tb[half]
                tbv = tb[:].bitcast(BF16).rearrange("p (k c) -> p k c", c=128)

                def tr(e, half=half, tbv=tbv):
                    rr = None
                    for k in range(8):
                        kc = half * 8 + k
                        rr = e.transpose(out=tbv[:, k, :], in_=xnb[:, kc * 128:(kc + 1) * 128], identity=identb[:])
                    return rr
                S.op("pe", tr, reads=[t_xn, t_identb], writes=[t_tb])
                na = 8 if half == 1 else 0

                def ev(e, half=half, tbv=tbv, na=na):
                    rr = None
                    for k in range(8 - na):
                        kc = half * 8 + k
                        rr = e.tensor_scalar(out=dst[:, kc, :], in0=tbv[:, k, :], scalar1=gT[li][:, kc, r:r + 1],
                                             scalar2=modT[li][:, kc, r:r + 1], op0=ALU.mult, op1=ALU.add)
                    return rr

                def ev_act(e, half=half, tbv=tbv, na=na):
                    rr = None
                    for k in range(8 - na, 8):
                        kc = half * 8 + k
                        rr = e.activation(out=dst[:, kc, :], in_=tbv[:, k, :], func=AF.Identity,
                                          scale=gT[li][:, kc, r:r + 1], bias=modT[li][:, kc, r:r + 1])
                    return rr
                if na < 8:
                    S.op("dve", ev, reads=[t_tb, t_gT[li], t_mod[li]], writes=[t_dst[0]])
                if na:
                    S.op("act", ev_act, reads=[t_tb, t_gT[li], t_mod[li]], writes=[t_dst[1]])

        def make_hT(src_ap, t_src, t, li, dst, t_dst):
            make_hT_b(make_hT_a(src_ap, t_src, t), t, li, dst, t_dst)

        def proj(lhs, t_lhs, wb, t_wb):
            ab, t_ab = aring.next()
            wv = wb[:].rearrange("p (k c) -> p k c", c=512)

            def mm(e):
                rr = None
                for kc in range(NKC):
                    rr = e.matmul(ab[:], lhsT=lhs[:, kc, :], rhs=wv[:, kc, :], start=(kc == 0), stop=(kc == NKC - 1))
                return rr
            S.op("pe", mm, reads=[*t_lhs, t_wb], writes=[t_ab])
            return ab, t_ab

        def head_rstd(src_ap, t_src, nh):
            sq, t_sq = tring.next()
            S.op("act", lambda e: e.activation(out=sq[:, 0:nh * 128], in_=src_ap, func=AF.Square), reads=[t_src], writes=[t_sq])
            ss, t_ss = smring.next()
            S.op("dve", lambda e: e.tensor_reduce(out=ss[:, 0:nh], in_=sq[:, 0:nh * 128].rearrange("p (h d) -> p h d", d=128),
                                                  axis=AX.X, op=ALU.add), reads=[t_sq], writes=[t_ss])
            return rstd_small(ss[:, 0:nh], t_ss, nh, 1.0 / 128)

        def apply_rope(src, t_src, nh, rt, t_rt, dst, t_dst):
            n = nh * 128
            t1, t_t1 = tring.next()
            cosb = rt[:, 0:128].unsqueeze(1).broadcast_to([128, nh, 128])
            S.op("dve", lambda e: e.tensor_tensor(out=t1[:, 0:n].rearrange("p (h d) -> p h d", d=128),
                                                  in0=src[:, 0:n].rearrange("p (h d) -> p h d", d=128), in1=cosb, op=ALU.mult),
                 reads=[t_src, t_rt], writes=[t_t1])
            rot, t_rot = tring.next()
            sv = src[:, 0:n].rearrange("p (h b t d) -> p h b t d", b=2, t=2, d=32)
            rv = rot[:, 0:n].rearrange("p (h b t d) -> p h b t d", b=2, t=2, d=32)
            t1v = t1[:, 0:n].rearrange("p (h b t d) -> p h b t d", b=2, t=2, d=32)
            dv = dst[:, 0:n].rearrange("p (h b t d) -> p h b t d", b=2, t=2, d=32)
            sinv = rt[:, 128:256].rearrange("p (b t d) -> p b t d", b=2, t=2)

            def rotf(e):
                e.tensor_tensor(out=rv[:, :, :, 0, :], in0=sv[:, :, :, 1, :],
                                in1=sinv[:, :, 0, :].unsqueeze(1).broadcast_to([128, nh, 2, 32]), op=ALU.mult)
                return e.tensor_tensor(out=rv[:, :, :, 1, :], in0=sv[:, :, :, 0, :],
                                       in1=sinv[:, :, 1, :].unsqueeze(1).broadcast_to([128, nh, 2, 32]), op=ALU.mult)
            S.op("dve", rotf, reads=[t_src, t_rt], writes=[t_rot])

            def fin(e):
                e.tensor_tensor(out=dv[:, :, :, 0, :], in0=t1v[:, :, :, 0, :], in1=rv[:, :, :, 0, :], op=ALU.subtract)
                return e.tensor_tensor(out=dv[:, :, :, 1, :], in0=t1v[:, :, :, 1, :], in1=rv[:, :, :, 1, :], op=ALU.add)
            S.op("dve", fin, reads=[t_t1, t_rot], writes=[t_dst])

        out_toks = []

        def run_layer(li, l, src_ap, dst_ap, t_srcx, p2_tiles, last_in_prog):

            S.dma("sp", lambda e, l=l: e.dma_start(out=wsg_f[:], in_=w_sgu[l].rearrange("g p q -> p g q")), "wsgf", writes=[t_wsgf])
            S.op("dve", lambda e: e.tensor_copy(out=wsg_b[:], in_=wsg_f[:]), reads=[t_wsgf], writes=[t_wsgb])
            tb, t_tb = Tb[0]
            tbv0 = tb[:].bitcast(BF16).rearrange("p (k c) -> p k c", c=128)

            def trw(e, tbv0=tbv0):
                rr = None
                for g in range(8):
                    rr = e.transpose(out=tbv0[:, g, :], in_=wsg_b[:, g, :], identity=identb[:])
                return rr
            S.op("pe", trw, reads=[t_wsgb, t_identb], writes=[t_tb])
            S.op("dve", lambda e, tbv0=tbv0: e.tensor_copy(out=wsguT[:], in_=tbv0[:]), reads=[t_tb], writes=[t_wsguT])
            small_T(b_sgu[l], 8, bsguT[:], t_bsguT)
            S.dma("sp", lambda e, l=l: e.dma_start(out=vnw_b[:], in_=v_norm_w[l].partition_broadcast(128)), "vnw", writes=[t_vnw])
            S.dma("sp", lambda e, l=l: e.dma_start(out=qnw_b[:], in_=q_norm_w[l].partition_broadcast(128)), "qnw", writes=[t_qnw])
            S.dma("sp", lambda e, l=l: e.dma_start(out=knw_b[:], in_=k_norm_w[l].partition_broadcast(128)), "knw", writes=[t_knw])
            mq, t_mq = smring.next()
            S.op("dve", lambda e, mq=mq: e.tensor_reduce(out=mq[:, 0:1], in_=qnw_b[:], axis=AX.X, op=ALU.max, apply_absolute_value=True),
                 reads=[t_qnw], writes=[t_mq])
            S.op("dve", lambda e, mq=mq: e.tensor_reduce(out=mq[:, 1:2], in_=knw_b[:], axis=AX.X, op=ALU.max, apply_absolute_value=True),
                 reads=[t_knw], writes=[t_mq])
            S.op("dve", lambda e, mq=mq: e.tensor_tensor(out=negC[:, 0:1], in0=mq[:, 0:1], in1=mq[:, 1:2], op=ALU.mult),
                 reads=[t_mq], writes=[t_negC])
            S.op("dve", lambda e: e.tensor_scalar(out=negC[:, 0:1], in0=negC[:, 0:1], scalar1=-float(np.sqrt(128.0)), scalar2=None, op0=ALU.mult),
                 writes=[t_negC])
            def build_gate(r, li=li):
                for q4 in range(4):
                    for k in range(4):
                        kc = q4 * 4 + k
                        dg, t_dg = dring.next()
                        S.op("dve", lambda e, dg=dg, kc=kc, r=r: e.tensor_scalar(
                            out=dg[:], in0=identf[:], scalar1=modT[li][:, 32 + kc, r:r + 1], scalar2=None, op0=ALU.mult),
                            reads=[t_identf, t_mod[li]], writes=[t_dg])
                        S.op("pe", lambda e, dg=dg, k=k: e.matmul(Sb[:, k * 128:(k + 1) * 128], lhsT=onesf[:], rhs=dg[:], start=True, stop=True),
                             reads=[t_dg, t_onesf], writes=[t_Sb])
                    S.op("dve", lambda e, q4=q4: e.tensor_copy(out=gate_b[:, q4 * 512:(q4 + 1) * 512], in_=Sb[:]),
                         reads=[t_Sb], writes=[t_gate])
            build_gate(0)

            wkv, t_wkv, wkey = wring.next()
            S.dma("sp", lambda e, wkv=wkv: e.dma_start(out=wkv[:], in_=wbi[li][8]), wkey, reads=[t_wci[li]], writes=[t_wkv])
            kv_toks = []
            P1 = list(p1_tiles)
            NP1 = len(P1)
            hbs = {}
            st0 = {}
            st1 = {}
            p1_tails = []
            for j in range(min(2, NP1)):
                st0[j] = make_hT_a0(src_ap, t_srcx, P1[j])
            st1[0] = make_hT_a1(st0.pop(0))
            hbs[0] = h1ring.next()
            make_hT_b(st1.pop(0), P1[0], li, hbs[0][0], hbs[0][1])
            if NP1 > 1:
                st1[1] = make_hT_a1(st0.pop(1))
            if NP1 > 2:
                st0[2] = make_hT_a0(src_ap, t_srcx, P1[2])
            for n_, t in enumerate(P1):
                if n_ + 3 < NP1:
                    st0[n_ + 3] = make_hT_a0(src_ap, t_srcx, P1[n_ + 3])
                if n_ + 2 < NP1:
                    st1[n_ + 2] = make_hT_a1(st0.pop(n_ + 2))
                hb, t_hb = hbs.pop(n_)
                if n_ == 0 and li == 0:
                    dbg("hT_p1", hb[:], t_hb[1])
                ab, t_ab = proj(hb, t_hb, wkv, t_wkv)
                if n_ + 1 < NP1:
                    hbs[n_ + 1] = h1ring.next()
                    make_hT_b(st1.pop(n_ + 1), P1[n_ + 1], li, hbs[n_ + 1][0], hbs[n_ + 1][1])
                if p1_tails:
                    p1_tails.pop(0)()
                rk, t_rk = head_rstd(ab[:, 0:256], t_ab, 2)
                kn, t_kn = tring.next()

                def knf(e, ab=ab, rk=rk, kn=kn):
                    rr = None
                    for h in range(2):
                        rr = e.scalar_tensor_tensor(out=kn[:, h * 128:(h + 1) * 128], in0=ab[:, h * 128:(h + 1) * 128],
                                                    scalar=rk[:, h:h + 1], in1=knw_b[:], op0=ALU.mult, op1=ALU.mult)
                    return rr
                S.op("dve", knf, reads=[t_ab, t_rk, t_knw], writes=[t_kn])
                kb, t_kb = qbring.next()
                if is_ctx(t):
                    S.op("dve", lambda e, kb=kb, kn=kn: e.tensor_copy(out=kb[:, 0:256], in_=kn[:, 0:256]), reads=[t_kn], writes=[t_kb])
                else:
                    rt, t_rt, rkey = r1ring.next()
                    lt = lat_index(t)
                    S.dma("sp", lambda e, rt=rt, lt=lt: e.dma_start(out=rt[:], in_=rope[lt * 128:(lt + 1) * 128, :]), rkey, writes=[t_rt])
                    apply_rope(kn, t_kn, 2, rt, t_rt, kb, t_kb)
                def tail(kb=kb, t_kb=t_kb, ab=ab, t_ab=t_ab, t=t):
                    tb, t_tb = Tb[0]
                    tbv = tb[:].bitcast(BF16).rearrange("p (k c) -> p k c", c=128)

                    def trk(e, kb=kb, tbv=tbv):
                        e.transpose(out=tbv[:, 0, :], in_=kb[:, 0:128], identity=identb[:])
                        return e.transpose(out=tbv[:, 1, :], in_=kb[:, 128:256], identity=identb[:])
                    S.op("pe", trk, reads=[t_kb, t_identb], writes=[t_tb])
                    if cc:
                        ks, t_ks, kkey = kstring.next()
                        S.op("act", lambda e, tbv=tbv, ks=ks: e.copy(out=ks[:], in_=tbv[:, 0:2, :]), reads=[t_tb], writes=[t_ks])
                        tk_ = Tok()
                        S.dma("sp", lambda e, ks=ks, t=t: e.dma_start(
                            out=kv_send[li][:, 0:4352].rearrange("p (h n) -> p h n", h=2)[:, :, t * 128:(t + 1) * 128], in_=ks[:]),
                            kkey, reads=[t_ks], writes=[tk_])
                        vs, t_vs, vkey = vstring.next()
                        S.op("act", lambda e, ab=ab, vs=vs: e.copy(out=vs[:, :, 0:128], in_=ab[:, 256:512].rearrange("p (h d) -> p h d", d=128)),
                             reads=[t_ab, t_Vones], writes=[t_vs])
                        tv_ = Tok()
                        S.dma("sp", lambda e, vs=vs, t=t: e.dma_start(
                            out=kv_send[li][:, 4352 + t * 260:4352 + (t + 1) * 260], in_=vs[:].rearrange("p g c -> p (g c)")),
                            vkey, reads=[t_vs], writes=[tv_])
                        kv_toks.extend([tk_, tv_])
                    else:
                        S.op("act", lambda e, tbv=tbv, t=t: e.copy(out=KT[:, :, t * 128:(t + 1) * 128], in_=tbv[:, 0:2, :]),
                             reads=[t_tb], writes=[t_K[t]])
                        S.op("act", lambda e, ab=ab, t=t: e.copy(out=VA[:, t, :, 0:128], in_=ab[:, 256:512].rearrange("p (h d) -> p h d", d=128)),
                             reads=[t_ab, t_Vones], writes=[t_V[t]])
                p1_tails.append(tail)
            while p1_tails:
                p1_tails.pop(0)()
            if cc:
                t_recv = Tok()
                S.dma("pool", lambda e: e.collective_compute("AllGather", ALU.bypass, replica_groups=PAIRS,
                                                             ins=[kv_send[li]], outs=[kv_recv[li]]),
                      f"cc{li}", reads=kv_toks, writes=[t_recv], inc=1)
                for blk in range(2):
                    S.dma("sp", lambda e, blk=blk: e.dma_start(
                        out=KT[:, :, blk * 2176:(blk + 1) * 2176],
                        in_=kv_recv[li][blk * 128:(blk + 1) * 128, 0:4352].rearrange("p (h n) -> p h n", h=2)),
                        f"KTl{blk}", reads=[t_recv], writes=[tkb[blk]])
                    S.dma("sp", lambda e, blk=blk: e.dma_start(
                        out=VA[:, blk * 17:(blk + 1) * 17, :, :],
                        in_=kv_recv[li][blk * 128:(blk + 1) * 128, 4352:KVW].rearrange("p (j g c) -> p j g c", g=2, c=130)),
                        f"VAl{blk}", reads=[t_recv], writes=[tvb[blk]])

            if li == 0:
                t0_ = p1_tiles[0]
                dbg("KT0", KT[:, :, t0_ * 128:(t0_ + 1) * 128], t_K[t0_])
                dbg("VA0", VA[:, t0_, :, :], t_V[t0_])
                if cc:
                    dbg("KT33", KT[:, :, 33 * 128:34 * 128], t_K[33])
                dbg("negC", negC[:], t_negC)
                dbg("gate_b", gate_b[:], t_gate)
                dbg("wsguT", wsguT[:], t_wsguT)
            if li + 1 < NL:
                emit_casts(li + 1)
            lat_tiles = [t for t in p2_tiles if not is_ctx(t)]
            ctx_tiles = [t for t in p2_tiles if is_ctx(t)]
            groups = [lat_tiles[i:i + TG] for i in range(0, len(lat_tiles), TG)]
            if ctx_tiles:
                groups.append(ctx_tiles)
            t_xs_next = [Tok() for _ in range(NT_ALL)]
            for grp in groups:
                ng = len(grp)
                rflag = 1 if is_ctx(grp[0]) else 0
                if rflag:
                    build_gate(1)
                kall = list(range(NT_ALL)) if cc else list(p1_tiles)
                ktiles = [t for t in kall if is_ctx(t)] if rflag else kall
                for i, t in enumerate(grp):
                    make_hT(src_ap, t_srcx, t, li, hT[i], t_hT[i])
                    if not rflag:
                        lt = lat_index(t)
                        S.dma("sp", lambda e, i=i, lt=lt: e.dma_start(out=ropeT[i][:], in_=rope[lt * 128:(lt + 1) * 128, :]),
                              f"ropeT{i}", writes=[t_rope[i]])
                order = [(2, "v", 0), (0, "u", 0), (4, "za", 0), (3, "v", 1), (1, "u", 1), (5, "za", 1),
                         (6, "q", 0), (7, "q", 1), (9, "zb", 0), (10, "zb", 1)]
                pending_backs = []
                for (cb, kind, hh) in order:
                    wb, t_wb, wkey = wring.next()
                    S.dma("sp", lambda e, wb=wb, cb=cb: e.dma_start(out=wb[:], in_=wbi[li][cb]), wkey, reads=[t_wci[li]], writes=[t_wb])
                    for i, t in enumerate(grp):
                        ab, t_ab = proj(hT[i], t_hT[i], wb, t_wb)
                        if i == 0:
                            while pending_backs:
                                pending_backs.pop(0)()
                        elif len(pending_backs) >= 2:
                            pending_backs.pop(0)()
                        back = None
                        if kind == "v":
                            gv, t_gv = tring.next()
                            S.op("act", lambda e, gv=gv, ab=ab: e.activation(out=gv[:], in_=ab[:], func=AF.Gelu_apprx_tanh),
                                 reads=[t_ab], writes=[t_gv])
                            rv, t_rv = head_rstd(gv[:], t_gv, 4)
                            v1, t_v1 = tring.next()
                            S.op("dve", lambda e, v1=v1, gv=gv, rv=rv: e.tensor_tensor(
                                out=v1[:].rearrange("p (g d) -> p g d", d=128), in0=gv[:].rearrange("p (g d) -> p g d", d=128),
                                in1=rv.unsqueeze(2).broadcast_to([128, 4, 128]), op=ALU.mult), reads=[t_gv, t_rv], writes=[t_v1])
                            vb, t_vb = vnring.next()
                            S.op("dve", lambda e, vb=vb, v1=v1, hh=hh: e.tensor_tensor(
                                out=vb[:], in0=v1[:], in1=vnw_b[:, hh * 512:(hh + 1) * 512], op=ALU.mult),
                                reads=[t_v1, t_vnw], writes=[t_vb])

                            def back(vb=vb, t_vb=t_vb, hh=hh, i=i):
                                def sgu(e):
                                    rr = None
                                    for g in range(4):
                                        rr = e.matmul(Sb[:, g * 128:(g + 1) * 128], lhsT=wsguT[:, 4 * hh + g, :],
                                                      rhs=vb[:, g * 128:(g + 1) * 128], start=True, stop=True)
                                    return rr
                                S.op("pe", sgu, reads=[t_vb, t_wsguT], writes=[t_Sb])
                                S.op("dve", lambda e: e.tensor_tensor(
                                    out=s_sb[i][:].rearrange("p (g d) -> p g d", d=128), in0=Sb[:].rearrange("p (g d) -> p g d", d=128),
                                    in1=bsguT[:, 4 * hh:4 * hh + 4].unsqueeze(2).broadcast_to([128, 4, 128]), op=ALU.add),
                                    reads=[t_Sb, t_bsguT], writes=[t_s[i]])
                        elif kind == "u":
                            gu, t_gu = tring.next()
                            S.op("act", lambda e, gu=gu, ab=ab: e.activation(out=gu[:], in_=ab[:], func=AF.Gelu_apprx_tanh),
                                 reads=[t_ab], writes=[t_gu])
                            S.op("dve", lambda e, gu=gu, i=i: e.tensor_tensor(out=s_sb[i][:], in0=gu[:], in1=s_sb[i][:], op=ALU.mult),
                                 reads=[t_gu], writes=[t_s[i]])
                        elif kind == "za":
                            sz, t_sz = tring.next()
                            S.op("act", lambda e, sz=sz, ab=ab: e.activation(out=sz[:], in_=ab[:], func=AF.Silu), reads=[t_ab], writes=[t_sz])
                            S.op("dve", lambda e, sz=sz, i=i, hh=hh: e.tensor_tensor(
                                out=gated[i][:, hh * 512:(hh + 1) * 512], in0=sz[:], in1=s_sb[i][:], op=ALU.mult),
                                reads=[t_sz, t_s[i]], writes=[t_gated[i]])
                        elif kind == "q":
                            rq, t_rq = head_rstd(ab[:], t_ab, 4)
                            q1, t_q1 = tring.next()
                            S.op("dve", lambda e, q1=q1, ab=ab, rq=rq: e.tensor_tensor(
                                out=q1[:].rearrange("p (g d) -> p g d", d=128), in0=ab[:].rearrange("p (g d) -> p g d", d=128),
                                in1=rq.unsqueeze(2).broadcast_to([128, 4, 128]), op=ALU.mult), reads=[t_ab, t_rq], writes=[t_q1])
                            S.op("dve", lambda e, q1=q1: e.tensor_tensor(
                                out=q1[:].rearrange("p (g d) -> p g d", d=128), in0=q1[:].rearrange("p (g d) -> p g d", d=128),
                                in1=qnw_b[:].unsqueeze(1).broadcast_to([128, 4, 128]), op=ALU.mult), reads=[t_qnw], writes=[t_q1])
                            qb, t_qb = qbring.next()
                            if rflag:
                                S.op("dve", lambda e, qb=qb, q1=q1: e.tensor_copy(out=qb[:], in_=q1[:]), reads=[t_q1], writes=[t_qb])
                            else:
                                apply_rope(q1, t_q1, 4, ropeT[i], t_rope[i], qb, t_qb)

                            def back(qb=qb, t_qb=t_qb, hh=hh, i=i):
                                tb, t_tb = Tb[(i + hh) % 2]
                                tbv = tb[:].bitcast(BF16).rearrange("p (k c) -> p k c", c=128)

                                def trq(e):
                                    rr = None
                                    for h in range(4):
                                        rr = e.transpose(out=tbv[:, h, :], in_=qb[:, h * 128:(h + 1) * 128], identity=identb[:])
                                    return rr
                                S.op("pe", trq, reads=[t_qb, t_identb], writes=[t_tb])
                                S.op("act", lambda e: e.copy(out=qT[i][:, 4 * hh:4 * hh + 4, :], in_=tbv[:, 0:4, :]),
                                     reads=[t_tb], writes=[t_qT[i]])
                        else:
                            S.op("act", lambda e, ab=ab, i=i, hh=hh: e.activation(out=szb[i][:, hh * 512:(hh + 1) * 512], in_=ab[:], func=AF.Silu),
                                 reads=[t_ab], writes=[t_szb[i]])
                        if back is not None:
                            pending_backs.append(back)
                while pending_backs:
                    pending_backs.pop(0)()

                if li == 0 and grp is groups[0]:
                    dbg("gatedA", gated[0][:, 0:1024], t_gated[0])
                    dbg("qT", qT[0][:], t_qT[0])
                    dbg("szb", szb[0][:], t_szb[0])
                inv_sqrt = float(128.0 ** -0.5)
                nk = len(ktiles)
                units = [(i, g, ki, kt) for i in range(ng) for g in range(2) for ki, kt in enumerate(ktiles)]
                LAG = 2
                (o0, t_o0), (o1, t_o1) = Ob

                def attn_front(u):
                    i, g, ki, kt = u
                    sbk, t_sbk = aring.next()
                    S.op("pe", lambda e: e.matmul(
                        sbk[:], lhsT=KT[:, g, kt * 128:(kt + 1) * 128], rhs=qT[i][:, 4 * g:4 * g + 4, :], start=True, stop=True),
                        reads=[t_K[kt], t_qT[i]], writes=[t_sbk])
                    pt, t_pt = ptring.next()
                    S.op("act", lambda e: e.activation(out=pt[:], in_=sbk[:], func=AF.Exp, bias=negC[:, 0:1], scale=inv_sqrt),
                         reads=[t_sbk, t_negC], writes=[t_pt])
                    return pt, t_pt

                def attn_back(u, pt, t_pt):
                    i, g, ki, kt = u

                    def pv(e):
                        rr = None
                        for hq in range(4):
                            if hq < 3:
                                oap = o0[:, hq * 129:hq * 129 + 129]
                                st = (ki == 0 and hq == 0)
                            else:
                                oap = o1[:, 0:129]
                                st = (ki == 0)
                            rr = e.matmul(oap, lhsT=pt[:, hq * 128:(hq + 1) * 128], rhs=VA[:, kt, g, 0:129],
                                          start=st, stop=(ki == nk - 1), skip_group_check=True)
                        return rr
                    S.op("pe", pv, reads=[t_pt, t_V[kt]], writes=[t_o0, t_o1])
                    if ki != nk - 1:
                        return
                    rd, t_rd = smring.next()

                    def rden(e):
                        e.reciprocal(out=rd[:, 0:3], in_=o0[:, 0:387].rearrange("p (h c) -> p h c", c=129)[:, :, 128])
                        return e.reciprocal(out=rd[:, 3:4], in_=o1[:, 128:129])
                    S.op("dve", rden, reads=[t_o0, t_o1], writes=[t_rd])

                    def onorm(e):
                        rr = None
                        for hq in range(4):
                            h = 4 * g + hq
                            oap = o0[:, hq * 129:hq * 129 + 128] if hq < 3 else o1[:, 0:128]
                            rr = e.scalar_tensor_tensor(out=gated[i][:, 1024 + h * 128:1024 + (h + 1) * 128], in0=oap,
                                                        scalar=rd[:, hq:hq + 1], in1=szb[i][:, h * 128:(h + 1) * 128],
                                                        op0=ALU.mult, op1=ALU.mult)
                        return rr
                    S.op("dve", onorm, reads=[t_o0, t_o1, t_rd, t_szb[i]], writes=[t_gated[i]])

                pend = []
                for idx in range(len(units) + LAG):
                    if idx < len(units):
                        pend.append(attn_front(units[idx]))
                    if idx >= LAG:
                        attn_back(units[idx - LAG], *pend[idx - LAG])

                if li == 0 and grp is groups[0]:
                    dbg("gated", gated[0][:], t_gated[0])
                for i, t in enumerate(grp):
                    for half in range(2):
                        tb, t_tb = Tb[half]
                        tbv = tb[:].bitcast(BF16).rearrange("p (k c) -> p k c", c=128)

                        def trg(e, half=half, tbv=tbv, i=i):
                            rr = None
                            for k in range(8):
                                kc = half * 8 + k
                                rr = e.transpose(out=tbv[:, k, :], in_=gated[i][:, kc * 128:(kc + 1) * 128], identity=identb[:])
                            return rr
                        S.op("pe", trg, reads=[t_gated[i], t_identb], writes=[t_tb])
                        S.op("act", lambda e, half=half, tbv=tbv, i=i: e.copy(out=hT[i][:, half * 8:(half + 1) * 8, :], in_=tbv[:]),
                             reads=[t_tb], writes=[t_hT[i][0], t_hT[i][1]])

                for cb in range(4):
                    wb, t_wb, wkey = wring.next()
                    S.dma("sp", lambda e, wb=wb, cb=cb: e.dma_start(out=wb[:], in_=wbo[li][cb]), wkey, reads=[t_wco[li]], writes=[t_wb])
                    for i, t in enumerate(grp):
                        xpb, t_xp, xkey = xpring.next()
                        S.dma("sp", lambda e, xpb=xpb, t=t, cb=cb: e.dma_start(
                            out=xpb[:], in_=src_ap[t * 128:(t + 1) * 128, cb * 512:(cb + 1) * 512]), xkey, reads=[t_srcx[t]], writes=[t_xp])
                        ab, t_ab = proj(hT[i], t_hT[i], wb, t_wb)
                        yg, t_yg = tring.next()
                        S.op("dve", lambda e, yg=yg, ab=ab, cb=cb: e.tensor_tensor(
                            out=yg[:], in0=ab[:], in1=gate_b[:, cb * 512:(cb + 1) * 512], op=ALU.mult),
                            reads=[t_ab, t_gate], writes=[t_yg])
                        S.op("pool", lambda e, yg=yg, xpb=xpb: e.tensor_tensor(out=xpb[:], in0=xpb[:], in1=yg[:], op=ALU.add),
                             reads=[t_yg], writes=[t_xp])
                        if last_in_prog:
                            if final:
                                drow = t
                            else:
                                drow = t
                        else:
                            drow = t
                        S.dma("pool", lambda e, xpb=xpb, drow=drow, cb=cb: e.dma_start(
                            out=dst_ap[drow * 128:(drow + 1) * 128, cb * 512:(cb + 1) * 512], in_=xpb[:]), xkey,
                            reads=[t_xp], writes=[t_xs_next[t]])
                        if last_in_prog:
                            out_toks.append(t_xp)
            return t_xs_next

        t_xs_all = [Tok() for _ in range(NT_ALL)]
        for li_, l_ in enumerate(layers):
            last_ = (li_ == NL - 1)
            t_xs_all = run_layer(li_, l_, xa if li_ == 0 else xs, out if last_ else xs, t_xs_all,
                                 p2_tiles_per_layer[li_], last_)

        S.final_wait("pool", list({id(t): t for t in out_toks}.values()) + dbg_toks)
        S.emit(block)
    return nc


def _rope_tables(pos):
    rows = (pos // GRID_W).astype(np.float32)
    cols = (pos % GRID_W).astype(np.float32)
    inv_freq = (np.float32(10000.0) ** (-np.arange(0, 64, 2, dtype=np.float32) / np.float32(64))).astype(np.float32)
    ang_r = rows[:, None] * inv_freq[None, :]
    ang_c = cols[:, None] * inv_freq[None, :]
    ang = np.concatenate([ang_r, ang_r, ang_c, ang_c], axis=-1).astype(np.float32)
    return np.concatenate([np.cos(ang), np.sin(ang)], axis=-1).astype(np.float32)


_PROG_CACHE = {}


def _get_prog(key, *args):
    if key not in _PROG_CACHE:
        _PROG_CACHE[key] = build(*args)
    return _PROG_CACHE[key]


def _common_inputs(c, c_ctx, norm_w, w_mod, b_mod, w_in, w_sgu, b_sgu, v_norm_w, q_norm_w, k_norm_w, w_out):
    f = lambda a: np.ascontiguousarray(np.asarray(a, dtype=np.float32))
    shared = {
        "identb": np.eye(128, dtype=np.float32).astype(ml_dtypes.bfloat16),
        "identf": np.eye(128, dtype=np.float32),
        "w_mod": f(w_mod), "b_mod": f(b_mod).reshape(2, 48, 128), "norm_w": f(norm_w).reshape(2, 16, 128),
        "w_in": f(w_in), "w_out": f(w_out), "w_sgu": f(w_sgu), "b_sgu": f(b_sgu),
        "v_norm_w": f(v_norm_w).reshape(2, 1024), "q_norm_w": f(q_norm_w), "k_norm_w": f(k_norm_w),
    }
    return shared


def kernel(x, c, ctx, c_ctx, norm_w, w_mod, b_mod, w_in, w_sgu, b_sgu, v_norm_w, q_norm_w, k_norm_w, w_out):
    x = np.asarray(x, dtype=np.float32)
    ctx = np.asarray(ctx, dtype=np.float32)
    c = np.asarray(c, dtype=np.float32)
    c_ctx = np.asarray(c_ctx, dtype=np.float32)
    shared = _common_inputs(c, c_ctx, norm_w, w_mod, b_mod, w_in, w_sgu, b_sgu, v_norm_w, q_norm_w, k_norm_w, w_out)
    H = SEQ // 2
    CH = CTX // 2

    def core_maps(xfull, cfull):
        maps = []
        for core in range(8):
            b, hf = divmod(core, 2)
            o, p = hf, 1 - hf
            xa = np.concatenate([xfull[b, o * H:(o + 1) * H], cfull[b, o * CH:(o + 1) * CH],
                                 xfull[b, p * H:(p + 1) * H], cfull[b, p * CH:(p + 1) * CH]], axis=0)
            pos = np.concatenate([np.arange(o * H, (o + 1) * H), np.arange(p * H, (p + 1) * H)])
            m = dict(shared)
            m["xa"] = np.ascontiguousarray(xa)
            m["rope"] = _rope_tables(pos)
            m["c2"] = np.ascontiguousarray(np.stack([c[b], c_ctx], 0).reshape(32, 128))
            maps.append(m)
        return maps

    all_tiles = list(range(NT_ALL))
    own_lat = list(range(16))
    if MODE == "cc":
        maps = []
        for core in range(8):
            b, hf = divmod(core, 2)
            m = dict(shared)
            m["xa"] = np.ascontiguousarray(np.concatenate([x[b, hf * H:(hf + 1) * H], ctx[b, hf * CH:(hf + 1) * CH]], axis=0))
            m["rope"] = _rope_tables(np.arange(hf * H, (hf + 1) * H))
            m["c2"] = np.ascontiguousarray(np.stack([c[b], c_ctx], 0).reshape(32, 128))
            maps.append(m)
        nc = _get_prog("cc", [0, 1], True, list(range(17)), [list(range(17)), own_lat], 16, True)
        res = run_bass_kernel_spmd(nc, maps, core_ids=list(range(8)))
        outs = [r["out"] for r in res.results]
    elif MODE == "fused":
        nc = _get_prog("fused", [0, 1], True, all_tiles, [all_tiles, own_lat], 16)
        res = run_bass_kernel_spmd(nc, core_maps(x, ctx), core_ids=list(range(8)))
        outs = [r["out"] for r in res.results]
    else:
        ncA = _get_prog("L0", [0], False, all_tiles, [list(range(17))], 17)
        resA = run_bass_kernel_spmd(ncA, core_maps(x, ctx), core_ids=list(range(8)))
        x1 = np.empty_like(x)
        ctx1 = np.empty_like(ctx)
        for core in range(8):
            b, hf = divmod(core, 2)
            xn_ = resA.results[core]["xnext"]
            x1[b, hf * H:(hf + 1) * H] = xn_[0:H]
            ctx1[b, hf * CH:(hf + 1) * CH] = xn_[H:H + CH]
        ncB = _get_prog("L1", [1], True, all_tiles, [own_lat], 16)
        resB = run_bass_kernel_spmd(ncB, core_maps(x1, ctx1), core_ids=list(range(8)))
        outs = [r["out"] for r in resB.results]
    y = np.empty((4, SEQ, D), dtype=np.float32)
    for core in range(8):
        b, hf = divmod(core, 2)
        y[b, hf * H:(hf + 1) * H] = outs[core]
    return y
```

```python
import numpy as np
from contextlib import ExitStack
import ml_dtypes
import concourse.bass as bass
import concourse.mybir as mybir
from concourse.bass_utils import run_bass_kernel_spmd

F32 = mybir.dt.float32
BF16 = mybir.dt.bfloat16
AF = mybir.ActivationFunctionType
ALU = mybir.AluOpType
AX = mybir.AxisListType

D = 2048
NKC = 16
DIN = 5632
NCB = 11
SEQ = 4096
CTX = 256
GRID_W = 64
EPS = 1e-6
TG = 4
NT_ALL = 34
DEBUG = False
MODE = "fused"


class Tok:
    __slots__ = ("w", "r")

    def __init__(self):
        self.w = None
        self.r = {}


class Sched:
    EPOCH = 20000

    def __init__(self, nc, es):
        self.nc = nc
        self.es = es
        self.names = ["pe", "act", "dve", "pool", "sp"]
        self.q = {k: [] for k in self.names}
        self.cnt = {k: 0 for k in self.names}
        self.esems = {k: [] for k in self.names}
        self.dsems = {}
        self.dcnt = {}
        self.waited = {k: {} for k in self.names}

    def _newsem(self, name):
        return self.es.enter_context(self.nc.semaphore(name))

    def _eng_event(self, eng):
        c = self.cnt[eng]
        ep, v = divmod(c, self.EPOCH)
        while len(self.esems[eng]) <= ep:
            self.esems[eng].append(self._newsem(f"e_{eng}_{len(self.esems[eng])}"))
        self.cnt[eng] = c + 1
        return (self.esems[eng][ep], v + 1, eng)

    def _collect(self, eng, reads, writes):
        evs = []
        for t in reads:
            if t.w is not None:
                evs.append(t.w)
        for t in writes:
            if t.w is not None:
                evs.append(t.w)
            evs.extend(t.r.values())
        waits = {}
        for (sem, val, e) in evs:
            if e is not None and e == eng and eng == "pe":
                continue
            key = id(sem)
            if self.waited[eng].get(key, 0) >= val:
                continue
            if key not in waits or waits[key][1] < val:
                waits[key] = (sem, val)
        for key, (sem, val) in waits.items():
            self.waited[eng][key] = val
        return list(waits.values())

    def _mark(self, ev, reads, writes):
        k = id(ev[0])
        for t in reads:
            old = t.r.get(k)
            if old is None or old[1] < ev[1]:
                t.r[k] = ev
        for t in writes:
            t.w = ev
            t.r = {}

    def op(self, eng, fn, reads=(), writes=()):
        waits = self._collect(eng, reads, writes)
        ev = self._eng_event(eng)
        self.q[eng].append((waits, fn, ev[0], 1))
        self._mark(ev, reads, writes)

    def dma(self, eng, fn, key, reads=(), writes=(), n=1, inc=16):
        waits = self._collect(eng, reads, writes)
        if key not in self.dsems:
            self.dsems[key] = self._newsem(f"d_{key}")
            self.dcnt[key] = 0
        self.dcnt[key] += inc * n
        ev = (self.dsems[key], self.dcnt[key], None)
        self.q[eng].append((waits, fn, self.dsems[key], inc))
        self._mark(ev, reads, writes)

    def final_wait(self, eng, toks):
        waits = self._collect(eng, toks, toks)
        self.q[eng].append((waits, None, None, 0))

    def emit(self, block):
        def mk(name):
            def body(e):
                for (waits, fn, sem, inc) in self.q[name]:
                    for (s, v) in waits:
                        e.wait_ge(s, v)
                    if fn is None:
                        continue
                    r = fn(e)
                    if isinstance(r, (list, tuple)):
                        for ins in r:
                            ins.then_inc(sem, inc)
                    else:
                        r.then_inc(sem, inc)
            return body
        block.tensor(mk("pe"))
        block.scalar(mk("act"))
        block.vector(mk("dve"))
        block.gpsimd(mk("pool"))
        block.sync(mk("sp"))


class Ring:
    def __init__(self, items):
        self.items = items
        self.i = 0

    def next(self):
        it = self.items[self.i % len(self.items)]
        self.i += 1
        return it


def is_ctx(t):
    return t % 17 == 16


def lat_index(t):
    return (t // 17) * 16 + (t % 17)


PAIRS = [[0, 1], [2, 3], [4, 5], [6, 7]]
KVW = 2 * 17 * 128 + 17 * 260


def build(layers, final, p1_tiles, p2_tiles_per_layer, n_out_tiles, cc=False):
    nc = bass.Bass("TRN2", target_bir_lowering=False)
    NL = len(layers)

    def din(name, shape, dt=F32):
        return nc.dram_tensor(name, shape, dt, kind="ExternalInput").ap()

    NX = 17 if cc else NT_ALL
    xa = din("xa", [NX * 128, D])
    rope = din("rope", [(16 if cc else 32) * 128, 256])
    c2 = din("c2", [32, 128])
    identb_in = din("identb", [128, 128], BF16)
    identf_in = din("identf", [128, 128])
    w_mod = din("w_mod", [2, D, 3 * D])
    b_mod = din("b_mod", [2, 48, 128])
    norm_w = din("norm_w", [2, 16, 128])
    w_in = din("w_in", [2, D, DIN])
    w_out = din("w_out", [2, D, D])
    w_sgu = din("w_sgu", [2, 8, 128, 128])
    b_sgu = din("b_sgu", [2, 8, 128])
    v_norm_w = din("v_norm_w", [2, 1024])
    q_norm_w = din("q_norm_w", [2, 128])
    k_norm_w = din("k_norm_w", [2, 128])
    if final:
        out = nc.dram_tensor("out", [16 * 128, D], F32, kind="ExternalOutput").ap()
    else:
        out = nc.dram_tensor("xnext", [n_out_tiles * 128, D], F32, kind="ExternalOutput").ap()
    xs = nc.dram_tensor("xs", [NX * 128, D], F32).ap() if NL > 1 else None
    if cc:
        kv_send = [nc.dram_tensor(f"kv_send{i}", [128, KVW], BF16).ap() for i in range(NL)]
        kv_recv = [nc.dram_tensor(f"kv_recv{i}", [256, KVW], BF16).ap() for i in range(NL)]
    wbi = [nc.dram_tensor(f"wbi{l}", [NCB, 128, NKC * 512], BF16).ap() for l in layers]
    wbo = [nc.dram_tensor(f"wbo{l}", [4, 128, NKC * 512], BF16).ap() for l in layers]

    with ExitStack() as es:
        def sb(name, shape, dt):
            return es.enter_context(nc.sbuf_tensor(name, shape, dt))

        S = Sched(nc, es)
        identb = sb("identb_sb", [128, 128], BF16); t_identb = Tok()
        identf = sb("identf_sb", [128, 128], F32); t_identf = Tok()
        onesf = sb("onesf", [128, 128], F32); t_onesf = Tok()
        KT = sb("KT", [128, 2, NT_ALL * 128], BF16)
        VA = sb("VA", [128, NT_ALL, 2, 130], BF16)
        if cc:
            tkb = [Tok(), Tok()]
            tvb = [Tok(), Tok()]
            t_K = [tkb[kt // 17] for kt in range(NT_ALL)]
            t_V = [tvb[kt // 17] for kt in range(NT_ALL)]
            kst = [sb(f"kst{i}", [128, 2, 128], BF16) for i in range(2)]
            kstring = Ring([(kst[i], Tok(), f"kst{i}") for i in range(2)])
            vst = [sb(f"vst{i}", [128, 2, 130], BF16) for i in range(2)]
            vstring = Ring([(vst[i], Tok(), f"vst{i}") for i in range(2)])
        else:
            t_K = [Tok() for _ in range(NT_ALL)]
            t_V = [Tok() for _ in range(NT_ALL)]
        t_Vones = Tok()
        wbuf = [sb(f"wbuf{i}", [128, NKC * 512], BF16) for i in range(2)]
        t_wbuf = [Tok() for _ in range(2)]
        wring = Ring(list(zip(wbuf, t_wbuf, ["wbuf0", "wbuf1"])))
        xbuf = [sb(f"xbuf{i}", [128, D], F32) for i in range(2)]
        xring = Ring([(xbuf[i], Tok(), f"xbuf{i}") for i in range(2)])
        xn = [sb(f"xn{i}", [128, D], BF16) for i in range(2)]
        xnring = Ring([(xn[i], Tok()) for i in range(2)])
        hT = [sb(f"hT{i}", [128, NKC, 128], BF16) for i in range(TG)]
        t_hT = [(Tok(), Tok()) for _ in range(TG)]
        h1ring = Ring([(hT[i], t_hT[i]) for i in range(2)])
        s_sb = [sb(f"s_sb{i}", [128, 512], F32) for i in range(TG)]
        t_s = [Tok() for _ in range(TG)]
        gated = [sb(f"gated{i}", [128, D], BF16) for i in range(TG)]
        t_gated = [Tok() for _ in range(TG)]
        qz = [sb(f"qz{i}", [128, D], BF16) for i in range(TG)]
        qT = [qz[i][:, 0:1024].rearrange("p (h q) -> p h q", h=8) for i in range(TG)]
        t_qT = [Tok() for _ in range(TG)]
        szb = [qz[i][:, 1024:2048] for i in range(TG)]
        t_szb = [Tok() for _ in range(TG)]
        gTv = [qz[i][:].rearrange("p (k c) -> p k c", c=128) for i in range(TG)]
        ropeT = [sb(f"ropeT{i}", [128, 256], F32) for i in range(TG)]
        t_rope = [Tok() for _ in range(TG)]
        r1ring = Ring([(ropeT[i], t_rope[i], f"ropeT{i}") for i in range(2)])
        tmps = [sb(f"tmp{i}", [128, 512], F32) for i in range(6)]
        tring = Ring([(tmps[i], Tok()) for i in range(6)])
        vnb = [sb(f"vnb{i}", [128, 512], BF16) for i in range(3)]
        vnring = Ring([(vnb[i], Tok()) for i in range(3)])
        qbf = [sb(f"qbf{i}", [128, 512], BF16) for i in range(3)]
        qbring = Ring([(qbf[i], Tok()) for i in range(3)])
        PT = [sb(f"PT{i}", [128, 512], BF16) for i in range(4)]
        ptring = Ring([(PT[i], Tok()) for i in range(4)])
        xp = [sb(f"xp{i}", [128, 512], F32) for i in range(4)]
        xpring = Ring([(xp[i], Tok(), f"xp{i}") for i in range(4)])
        gate_b = sb("gate_b", [128, D], F32)
        t_gate = Tok()
        small = sb("small", [128, 64], F32)
        smring = Ring([(small[:, 8 * i:8 * i + 8], Tok()) for i in range(8)])
        cT = sb("cT", [128, 32], F32); t_cT = Tok()
        modT = [sb(f"modT{i}", [128, 48, 2], F32) for i in range(NL)]
        t_mod = [Tok() for _ in range(NL)]
        gT = [sb(f"gT{i}", [128, NKC, 2], F32) for i in range(NL)]
        t_gT = [Tok() for _ in range(NL)]
        nwT = sb("nwT", [128, 16], F32); t_nwT = Tok()
        bmT = sb("bmT", [128, 48], F32); t_bmT = Tok()
        rows = sb("rows", [48, 128], F32); t_rows = Tok()
        wsg_f = sb("wsg_f", [128, 8, 128], F32); t_wsgf = Tok()
        wsg_b = sb("wsg_b", [128, 8, 128], BF16); t_wsgb = Tok()
        wsguT = sb("wsguT", [128, 8, 128], BF16); t_wsguT = Tok()
        bsguT = sb("bsguT", [128, 8], F32); t_bsguT = Tok()
        vnw_b = sb("vnw_b", [128, 1024], F32); t_vnw = Tok()
        qnw_b = sb("qnw_b", [128, 128], F32); t_qnw = Tok()
        knw_b = sb("knw_b", [128, 128], F32); t_knw = Tok()
        negC = sb("negC", [128, 2], F32); t_negC = Tok()
        diag = [sb(f"diag{i}", [128, 128], F32) for i in range(2)]
        dring = Ring([(diag[i], Tok()) for i in range(2)])
        bank = [es.enter_context(nc.psum_tensor(f"bank{i}", [128, 512], F32)) for i in range(8)]
        t_bank = [Tok() for _ in range(8)]
        aring = Ring([(bank[i], t_bank[i]) for i in range(3)])
        Sb, t_Sb = bank[3], t_bank[3]
        Tb = [(bank[4], t_bank[4]), (bank[5], t_bank[5])]
        Ob = [(bank[6], t_bank[6]), (bank[7], t_bank[7])]

        block = es.enter_context(nc.Block())
        dbg_toks = []

        def dbg(name, ap, tok):
            if not DEBUG:
                return
            d = nc.dram_tensor("dbg_" + name, list(ap.shape), ap.dtype, kind="ExternalOutput").ap()
            tk = Tok()
            S.dma("sp", lambda e: e.dma_start(out=d, in_=ap), "dbg_" + name, reads=[tok], writes=[tk])
            dbg_toks.append(tk)

        S.dma("sp", lambda e: e.dma_start(out=identb[:], in_=identb_in), "c_identb", writes=[t_identb])
        S.dma("sp", lambda e: e.dma_start(out=identf[:], in_=identf_in), "c_identf", writes=[t_identf])
        S.op("dve", lambda e: e.memset(onesf[:], 1.0), writes=[t_onesf])
        if cc:
            def ones_v(e):
                e.memset(vst[0][:, :, 128:130], 1.0)
                return e.memset(vst[1][:, :, 128:130], 1.0)
            S.op("dve", ones_v, writes=[t_Vones])
        else:
            S.op("dve", lambda e: e.memset(VA[:, :, :, 128:130], 1.0), writes=[t_Vones])

        t_wci = [Tok() for _ in range(NL)]
        t_wco = [Tok() for _ in range(NL)]
        def emit_casts(li):
            l = layers[li]

            def cast_in(e):
                res = []
                for kc in range(NKC):
                    for c0 in (0, 6):
                        ncb = 6 if c0 == 0 else 5
                        dst = wbi[li][c0:c0 + ncb, :, kc * 512:(kc + 1) * 512]
                        src = w_in[l, kc * 128:(kc + 1) * 128, c0 * 512:(c0 + ncb) * 512].rearrange("p (cb c) -> cb p c", c=512)
                        res.append(e.dma_start(out=dst, in_=src))
                return res
            S.dma("pool", cast_in, f"cast_in{li}", writes=[t_wci[li]], n=2 * NKC)

            def cast_out(e):
                res = []
                for kc in range(NKC):
                    dst = wbo[li][:, :, kc * 512:(kc + 1) * 512]
                    src = w_out[l, kc * 128:(kc + 1) * 128, :].rearrange("p (cb c) -> cb p c", c=512)
                    res.append(e.dma_start(out=dst, in_=src))
                return res
            S.dma("pool", cast_out, f"cast_out{li}", writes=[t_wco[li]], n=NKC)
        emit_casts(0)

        def small_T(src_ap, n, dst, t_dst, extra_reads=()):
            S.dma("sp", lambda e: e.dma_start(out=rows[0:n, :], in_=src_ap), "rows", writes=[t_rows])
            S.op("pe", lambda e: e.transpose(out=Sb[:, 0:n], in_=rows[0:n, :], identity=identf[0:n, 0:n]),
                 reads=[t_rows, t_identf], writes=[t_Sb])
            S.op("dve", lambda e: e.tensor_copy(out=dst, in_=Sb[:, 0:n]), reads=[t_Sb], writes=[t_dst])

        S.dma("sp", lambda e: e.dma_start(out=rows[0:32, :], in_=c2), "rows", writes=[t_rows])
        S.op("act", lambda e: e.activation(out=rows[0:32, :], in_=rows[0:32, :], func=AF.Silu), reads=[t_rows], writes=[t_rows])
        S.op("pe", lambda e: e.transpose(out=Sb[:, 0:32], in_=rows[0:32, :], identity=identf[0:32, 0:32]),
             reads=[t_rows, t_identf], writes=[t_Sb])
        S.op("dve", lambda e: e.tensor_copy(out=cT[:], in_=Sb[:, 0:32]), reads=[t_Sb], writes=[t_cT])
        cTv = cT[:].rearrange("p (r k) -> p r k", r=2)

        for li, l in enumerate(layers):
            small_T(b_mod[l], 48, bmT[:], t_bmT)
            small_T(norm_w[l], 16, nwT[:], t_nwT)
            for jb in range(24):
                wb, t_wb, wkey = wring.next()
                wv = wb[:].bitcast(F32).rearrange("p (k c) -> p k c", c=256)
                S.dma("sp", lambda e, wv=wv, l=l, jb=jb: e.dma_start(
                    out=wv, in_=w_mod[l, :, jb * 256:(jb + 1) * 256].rearrange("(k p) c -> p k c", p=128)),
                    wkey, writes=[t_wb])

                def mm_mod(e, wv=wv, jb=jb):
                    r = None
                    for jj in range(2):
                        j = jb * 2 + jj
                        for kc in range(NKC):
                            r = e.matmul(Sb[:, 2 * j:2 * j + 2], lhsT=wv[:, kc, jj * 128:(jj + 1) * 128], rhs=cTv[:, :, kc],
                                         start=(kc == 0), stop=(kc == NKC - 1))
                    return r
                S.op("pe", mm_mod, reads=[t_wb, t_cT], writes=[t_Sb])
            modv = modT[li]
            S.op("dve", lambda e, modv=modv: e.tensor_tensor(
                out=modv[:], in0=Sb[:, 0:96].rearrange("p (j r) -> p j r", r=2),
                in1=bmT[:].unsqueeze(2).broadcast_to([128, 48, 2]), op=ALU.add),
                reads=[t_Sb, t_bmT], writes=[t_mod[li]])
            S.op("dve", lambda e, modv=modv, li=li: e.tensor_scalar(
                out=gT[li][:], in0=modv[:, 16:32, :], scalar1=1.0, scalar2=None, op0=ALU.add),
                reads=[t_mod[li]], writes=[t_gT[li]])
            S.op("dve", lambda e, li=li: e.tensor_tensor(
                out=gT[li][:], in0=gT[li][:], in1=nwT[:].unsqueeze(2).broadcast_to([128, 16, 2]), op=ALU.mult),
                reads=[t_nwT], writes=[t_gT[li]])

        for li in range(NL):
            dbg(f"modT{li}", modT[li][:], t_mod[li])
            dbg(f"gT{li}", gT[li][:], t_gT[li])
        dbg("cT", cT[:], t_cT)

        def rstd_small(ss_ap, t_ss, n, inv_n):
            sd, t_sd = smring.next()
            S.op("act", lambda e: e.activation(out=sd[:, 0:n], in_=ss_ap, func=AF.Sqrt, bias=EPS, scale=inv_n),
                 reads=[t_ss], writes=[t_sd])
            rs, t_rs = smring.next()
            S.op("dve", lambda e: e.reciprocal(out=rs[:, 0:n], in_=sd[:, 0:n]), reads=[t_sd], writes=[t_rs])
            return rs[:, 0:n], t_rs

        def make_hT_a0(src_ap, t_src, t):
            xb, t_xb, xkey = xring.next()
            S.dma("sp", lambda e: e.dma_start(out=xb[:], in_=src_ap[t * 128:(t + 1) * 128, :]), xkey, reads=[t_src[t]], writes=[t_xb])
            return xb, t_xb

        def make_hT_a1(st0):
            xb, t_xb = st0
            xnb, t_xn = xnring.next()
            ss, t_ss = smring.next()
            S.op("act", lambda e: e.activation(out=xnb[:], in_=xb[:], func=AF.Square, accum_out=ss[:, 0:1]),
                 reads=[t_xb], writes=[t_xn, t_ss])
            rs, t_rs = rstd_small(ss[:, 0:1], t_ss, 1, 1.0 / D)
            S.op("act", lambda e: e.activation(out=xnb[:], in_=xb[:], func=AF.Copy, scale=rs[:, 0:1]),
                 reads=[t_xb, t_rs], writes=[t_xn])
            return xnb, t_xn

        def make_hT_a(src_ap, t_src, t):
            return make_hT_a1(make_hT_a0(src_ap, t_src, t))

        def make_hT_b(st, t, li, dst, t_dst, n_act=4):
            xnb, t_xn = st
            r = 1 if is_ctx(t) else 0
            for half in range(2):
                tb, t_tb = Tb[half]
                tbv = tb[:].bitcast(BF16).rearrange("p (k c) -> p k c", c=128)

                def tr(e, half=half, tbv=tbv):
                    rr = None
                    for k in range(8):
                        kc = half * 8 + k
                        rr = e.transpose(out=tbv[:, k, :], in_=xnb[:, kc * 128:(kc + 1) * 128], identity=identb[:])
                    return rr
                S.op("pe", tr, reads=[t_xn, t_identb], writes=[t_tb])
                na = 8 if half == 1 else 0

                def ev(e, half=half, tbv=tbv, na=na):
                    rr = None
                    for k in range(8 - na):
                        kc = half * 8 + k
                        rr = e.tensor_scalar(out=dst[:, kc, :], in0=tbv[:, k, :], scalar1=gT[li][:, kc, r:r + 1],
                                             scalar2=modT[li][:, kc, r:r + 1], op0=ALU.mult, op1=ALU.add)
                    return rr

                def ev_act(e, half=half, tbv=tbv, na=na):
                    rr = None
                    for k in range(8 - na, 8):
                        kc = half * 8 + k
                        rr = e.activation(out=dst[:, kc, :], in_=tbv[:, k, :], func=AF.Identity,
                                          scale=gT[li][:, kc, r:r + 1], bias=modT[li][:, kc, r:r + 1])
                    return rr
                if na < 8:
                    S.op("dve", ev, reads=[t_tb, t_gT[li], t_mod[li]], writes=[t_dst[0]])
                if na:
                    S.op("act", ev_act, reads=[t_tb, t_gT[li], t_mod[li]], writes=[t_dst[1]])

        def make_hT(src_ap, t_src, t, li, dst, t_dst):
            make_hT_b(make_hT_a(src_ap, t_src, t), t, li, dst, t_dst)

        def proj(lhs, t_lhs, wb, t_wb):
            ab, t_ab = aring.next()
            wv = wb[:].rearrange("p (k c) -> p k c", c=512)

            def mm(e):
                rr = None
                for kc in range(NKC):
                    rr = e.matmul(ab[:], lhsT=lhs[:, kc, :], rhs=wv[:, kc, :], start=(kc == 0), stop=(kc == NKC - 1))
                return rr
            S.op("pe", mm, reads=[*t_lhs, t_wb], writes=[t_ab])
            return ab, t_ab

        def head_rstd(src_ap, t_src, nh):
            sq, t_sq = tring.next()
            S.op("act", lambda e: e.activation(out=sq[:, 0:nh * 128], in_=src_ap, func=AF.Square), reads=[t_src], writes=[t_sq])
            ss, t_ss = smring.next()
            S.op("dve", lambda e: e.tensor_reduce(out=ss[:, 0:nh], in_=sq[:, 0:nh * 128].rearrange("p (h d) -> p h d", d=128),
                                                  axis=AX.X, op=ALU.add), reads=[t_sq], writes=[t_ss])
            return rstd_small(ss[:, 0:nh], t_ss, nh, 1.0 / 128)

        def apply_rope(src, t_src, nh, rt, t_rt, dst, t_dst):
            n = nh * 128
            t1, t_t1 = tring.next()
            cosb = rt[:, 0:128].unsqueeze(1).broadcast_to([128, nh, 128])
            S.op("dve", lambda e: e.tensor_tensor(out=t1[:, 0:n].rearrange("p (h d) -> p h d", d=128),
                                                  in0=src[:, 0:n].rearrange("p (h d) -> p h d", d=128), in1=cosb, op=ALU.mult),
                 reads=[t_src, t_rt], writes=[t_t1])
            rot, t_rot = tring.next()
            sv = src[:, 0:n].rearrange("p (h b t d) -> p h b t d", b=2, t=2, d=32)
            rv = rot[:, 0:n].rearrange("p (h b t d) -> p h b t d", b=2, t=2, d=32)
            t1v = t1[:, 0:n].rearrange("p (h b t d) -> p h b t d", b=2, t=2, d=32)
            dv = dst[:, 0:n].rearrange("p (h b t d) -> p h b t d", b=2, t=2, d=32)
            sinv = rt[:, 128:256].rearrange("p (b t d) -> p b t d", b=2, t=2)

            def rotf(e):
                e.tensor_tensor(out=rv[:, :, :, 0, :], in0=sv[:, :, :, 1, :],
                                in1=sinv[:, :, 0, :].unsqueeze(1).broadcast_to([128, nh, 2, 32]), op=ALU.mult)
                return e.tensor_tensor(out=rv[:, :, :, 1, :], in0=sv[:, :, :, 0, :],
                                       in1=sinv[:, :, 1, :].unsqueeze(1).broadcast_to([128, nh, 2, 32]), op=ALU.mult)
            S.op("dve", rotf, reads=[t_src, t_rt], writes=[t_rot])

            def fin(e):
                e.tensor_tensor(out=dv[:, :, :, 0, :], in0=t1v[:, :, :, 0, :], in1=rv[:, :, :, 0, :], op=ALU.subtract)
                return e.tensor_tensor(out=dv[:, :, :, 1, :], in0=t1v[:, :, :, 1, :], in1=rv[:, :, :, 1, :], op=ALU.add)
            S.op("dve", fin, reads=[t_t1, t_rot], writes=[t_dst])

        out_toks = []

        def run_layer(li, l, src_ap, dst_ap, t_srcx, p2_tiles, last_in_prog):

            S.dma("sp", lambda e, l=l: e.dma_start(out=wsg_f[:], in_=w_sgu[l].rearrange("g p q -> p g q")), "wsgf", writes=[t_wsgf])
            S.op("dve", lambda e: e.tensor_copy(out=wsg_b[:], in_=wsg_f[:]), reads=[t_wsgf], writes=[t_wsgb])
            tb, t_tb = Tb[0]
            tbv0 = tb[:].bitcast(BF16).rearrange("p (k c) -> p k c", c=128)

            def trw(e, tbv0=tbv0):
                rr = None
                for g in range(8):
                    rr = e.transpose(out=tbv0[:, g, :], in_=wsg_b[:, g, :], identity=identb[:])
                return rr
            S.op("pe", trw, reads=[t_wsgb, t_identb], writes=[t_tb])
            S.op("dve", lambda e, tbv0=tbv0: e.tensor_copy(out=wsguT[:], in_=tbv0[:]), reads=[t_tb], writes=[t_wsguT])
            small_T(b_sgu[l], 8, bsguT[:], t_bsguT)
            S.dma("sp", lambda e, l=l: e.dma_start(out=vnw_b[:], in_=v_norm_w[l].partition_broadcast(128)), "vnw", writes=[t_vnw])
            S.dma("sp", lambda e, l=l: e.dma_start(out=qnw_b[:], in_=q_norm_w[l].partition_broadcast(128)), "qnw", writes=[t_qnw])
            S.dma("sp", lambda e, l=l: e.dma_start(out=knw_b[:], in_=k_norm_w[l].partition_broadcast(128)), "knw", writes=[t_knw])
            mq, t_mq = smring.next()
            S.op("dve", lambda e, mq=mq: e.tensor_reduce(out=mq[:, 0:1], in_=qnw_b[:], axis=AX.X, op=ALU.max, apply_absolute_value=True),
                 reads=[t_qnw], writes=[t_mq])
            S.op("dve", lambda e, mq=mq: e.tensor_reduce(out=mq[:, 1:2], in_=knw_b[:], axis=AX.X, op=ALU.max, apply_absolute_value=True),
                 reads=[t_knw], writes=[t_mq])
            S.op("dve", lambda e, mq=mq: e.tensor_tensor(out=negC[:, 0:1], in0=mq[:, 0:1], in1=mq[:, 1:2], op=ALU.mult),
                 reads=[t_mq], writes=[t_negC])
            S.op("dve", lambda e: e.tensor_scalar(out=negC[:, 0:1], in0=negC[:, 0:1], scalar1=-float(np.sqrt(128.0)), scalar2=None, op0=ALU.mult),
                 writes=[t_negC])
            def build_gate(r, li=li):
                for q4 in range(4):
                    for k in range(4):
                        kc = q4 * 4 + k
                        dg, t_dg = dring.next()
                        S.op("dve", lambda e, dg=dg, kc=kc, r=r: e.tensor_scalar(
                            out=dg[:], in0=identf[:], scalar1=modT[li][:, 32 + kc, r:r + 1], scalar2=None, op0=ALU.mult),
                            reads=[t_identf, t_mod[li]], writes=[t_dg])
                        S.op("pe", lambda e, dg=dg, k=k: e.matmul(Sb[:, k * 128:(k + 1) * 128], lhsT=onesf[:], rhs=dg[:], start=True, stop=True),
                             reads=[t_dg, t_onesf], writes=[t_Sb])
                    S.op("dve", lambda e, q4=q4: e.tensor_copy(out=gate_b[:, q4 * 512:(q4 + 1) * 512], in_=Sb[:]),
                         reads=[t_Sb], writes=[t_gate])
            build_gate(0)

            wkv, t_wkv, wkey = wring.next()
            S.dma("sp", lambda e, wkv=wkv: e.dma_start(out=wkv[:], in_=wbi[li][8]), wkey, reads=[t_wci[li]], writes=[t_wkv])
            kv_toks = []
            P1 = list(p1_tiles)
            NP1 = len(P1)
            hbs = {}
            st0 = {}
            st1 = {}
            p1_tails = []
            for j in range(min(2, NP1)):
                st0[j] = make_hT_a0(src_ap, t_srcx, P1[j])
            st1[0] = make_hT_a1(st0.pop(0))
            hbs[0] = h1ring.next()
            make_hT_b(st1.pop(0), P1[0], li, hbs[0][0], hbs[0][1])
            if NP1 > 1:
                st1[1] = make_hT_a1(st0.pop(1))
            if NP1 > 2:
                st0[2] = make_hT_a0(src_ap, t_srcx, P1[2])
            for n_, t in enumerate(P1):
                if n_ + 3 < NP1:
                    st0[n_ + 3] = make_hT_a0(src_ap, t_srcx, P1[n_ + 3])
                if n_ + 2 < NP1:
                    st1[n_ + 2] = make_hT_a1(st0.pop(n_ + 2))
                hb, t_hb = hbs.pop(n_)
                if n_ == 0 and li == 0:
                    dbg("hT_p1", hb[:], t_hb[1])
                ab, t_ab = proj(hb, t_hb, wkv, t_wkv)
                if n_ + 1 < NP1:
                    hbs[n_ + 1] = h1ring.next()
                    make_hT_b(st1.pop(n_ + 1), P1[n_ + 1], li, hbs[n_ + 1][0], hbs[n_ + 1][1])
                if p1_tails:
                    p1_tails.pop(0)()
                rk, t_rk = head_rstd(ab[:, 0:256], t_ab, 2)
                kn, t_kn = tring.next()

                def knf(e, ab=ab, rk=rk, kn=kn):
                    rr = None
                    for h in range(2):
                        rr = e.scalar_tensor_tensor(out=kn[:, h * 128:(h + 1) * 128], in0=ab[:, h * 128:(h + 1) * 128],
                                                    scalar=rk[:, h:h + 1], in1=knw_b[:], op0=ALU.mult, op1=ALU.mult)
                    return rr
                S.op("dve", knf, reads=[t_ab, t_rk, t_knw], writes=[t_kn])
                kb, t_kb = qbring.next()
                if is_ctx(t):
                    S.op("dve", lambda e, kb=kb, kn=kn: e.tensor_copy(out=kb[:, 0:256], in_=kn[:, 0:256]), reads=[t_kn], writes=[t_kb])
                else:
                    rt, t_rt, rkey = r1ring.next()
                    lt = lat_index(t)
                    S.dma("sp", lambda e, rt=rt, lt=lt: e.dma_start(out=rt[:], in_=rope[lt * 128:(lt + 1) * 128, :]), rkey, writes=[t_rt])
                    apply_rope(kn, t_kn, 2, rt, t_rt, kb, t_kb)
                def tail(kb=kb, t_kb=t_kb, ab=ab, t_ab=t_ab, t=t):
                    tb, t_tb = Tb[0]
                    tbv = tb[:].bitcast(BF16).rearrange("p (k c) -> p k c", c=128)

                    def trk(e, kb=kb, tbv=tbv):
                        e.transpose(out=tbv[:, 0, :], in_=kb[:, 0:128], identity=identb[:])
                        return e.transpose(out=tbv[:, 1, :], in_=kb[:, 128:256], identity=identb[:])
                    S.op("pe", trk, reads=[t_kb, t_identb], writes=[t_tb])
                    if cc:
                        ks, t_ks, kkey = kstring.next()
                        S.op("act", lambda e, tbv=tbv, ks=ks: e.copy(out=ks[:], in_=tbv[:, 0:2, :]), reads=[t_tb], writes=[t_ks])
                        tk_ = Tok()
                        S.dma("sp", lambda e, ks=ks, t=t: e.dma_start(
                            out=kv_send[li][:, 0:4352].rearrange("p (h n) -> p h n", h=2)[:, :, t * 128:(t + 1) * 128], in_=ks[:]),
                            kkey, reads=[t_ks], writes=[tk_])
                        vs, t_vs, vkey = vstring.next()
                        S.op("act", lambda e, ab=ab, vs=vs: e.copy(out=vs[:, :, 0:128], in_=ab[:, 256:512].rearrange("p (h d) -> p h d", d=128)),
                             reads=[t_ab, t_Vones], writes=[t_vs])
                        tv_ = Tok()
                        S.dma("sp", lambda e, vs=vs, t=t: e.dma_start(
                            out=kv_send[li][:, 4352 + t * 260:4352 + (t + 1) * 260], in_=vs[:].rearrange("p g c -> p (g c)")),
                            vkey, reads=[t_vs], writes=[tv_])
                        kv_toks.extend([tk_, tv_])
                    else:
                        S.op("act", lambda e, tbv=tbv, t=t: e.copy(out=KT[:, :, t * 128:(t + 1) * 128], in_=tbv[:, 0:2, :]),
                             reads=[t_tb], writes=[t_K[t]])
                        S.op("act", lambda e, ab=ab, t=t: e.copy(out=VA[:, t, :, 0:128], in_=ab[:, 256:512].rearrange("p (h d) -> p h d", d=128)),
                             reads=[t_ab, t_Vones], writes=[t_V[t]])
                p1_tails.append(tail)
            while p1_tails:
                p1_tails.pop(0)()
            if cc:
                t_recv = Tok()
                S.dma("pool", lambda e: e.collective_compute("AllGather", ALU.bypass, replica_groups=PAIRS,
                                                             ins=[kv_send[li]], outs=[kv_recv[li]]),
                      f"cc{li}", reads=kv_toks, writes=[t_recv], inc=1)
                for blk in range(2):
                    S.dma("sp", lambda e, blk=blk: e.dma_start(
                        out=KT[:, :, blk * 2176:(blk + 1) * 2176],
                        in_=kv_recv[li][blk * 128:(blk + 1) * 128, 0:4352].rearrange("p (h n) -> p h n", h=2)),
                        f"KTl{blk}", reads=[t_recv], writes=[tkb[blk]])
                    S.dma("sp", lambda e, blk=blk: e.dma_start(
                        out=VA[:, blk * 17:(blk + 1) * 17, :, :],
                        in_=kv_recv[li][blk * 128:(blk + 1) * 128, 4352:KVW].rearrange("p (j g c) -> p j g c", g=2, c=130)),
                        f"VAl{blk}", reads=[t_recv], writes=[tvb[blk]])

            if li == 0:
                t0_ = p1_tiles[0]
                dbg("KT0", KT[:, :, t0_ * 128:(t0_ + 1) * 128], t_K[t0_])
                dbg("VA0", VA[:, t0_, :, :], t_V[t0_])
                if cc:
                    dbg("KT33", KT[:, :, 33 * 128:34 * 128], t_K[33])
                dbg("negC", negC[:], t_negC)
                dbg("gate_b", gate_b[:], t_gate)
                dbg("wsguT", wsguT[:], t_wsguT)
            if li + 1 < NL:
                emit_casts(li + 1)
            lat_tiles = [t for t in p2_tiles if not is_ctx(t)]
            ctx_tiles = [t for t in p2_tiles if is_ctx(t)]
            groups = [lat_tiles[i:i + TG] for i in range(0, len(lat_tiles), TG)]
            if ctx_tiles:
                groups.append(ctx_tiles)
            t_xs_next = [Tok() for _ in range(NT_ALL)]
            def load_rope(grp_):
                if is_ctx(grp_[0]):
                    return
                for i, t in enumerate(grp_):
                    lt = lat_index(t)
                    S.dma("sp", lambda e, i=i, lt=lt: e.dma_start(out=ropeT[i][:], in_=rope[lt * 128:(lt + 1) * 128, :]),
                          f"ropeT{i}", writes=[t_rope[i]])

            prepared = False
            for gi, grp in enumerate(groups):
                ng = len(grp)
                nxt = groups[gi + 1] if gi + 1 < len(groups) else None
                rflag = 1 if is_ctx(grp[0]) else 0
                if rflag:
                    build_gate(1)
                kall = list(range(NT_ALL)) if cc else list(p1_tiles)
                ktiles = [t for t in kall if is_ctx(t)] if rflag else kall
                if not prepared:
                    for i, t in enumerate(grp):
                        make_hT(src_ap, t_srcx, t, li, hT[i], t_hT[i])
                    load_rope(grp)
                order = [(2, "v", 0), (0, "u", 0), (4, "za", 0), (3, "v", 1), (1, "u", 1), (5, "za", 1),
                         (6, "q", 0), (7, "q", 1), (9, "zb", 0), (10, "zb", 1)]
                pending_backs = []
                for (cb, kind, hh) in order:
                    wb, t_wb, wkey = wring.next()
                    S.dma("sp", lambda e, wb=wb, cb=cb: e.dma_start(out=wb[:], in_=wbi[li][cb]), wkey, reads=[t_wci[li]], writes=[t_wb])
                    for i, t in enumerate(grp):
                        ab, t_ab = proj(hT[i], t_hT[i], wb, t_wb)
                        if i == 0:
                            while pending_backs:
                                pending_backs.pop(0)()
                        elif len(pending_backs) >= 2:
                            pending_backs.pop(0)()
                        back = None
                        if kind == "v":
                            gv, t_gv = tring.next()
                            S.op("act", lambda e, gv=gv, ab=ab: e.activation(out=gv[:], in_=ab[:], func=AF.Gelu_apprx_tanh),
                                 reads=[t_ab], writes=[t_gv])
                            rv, t_rv = head_rstd(gv[:], t_gv, 4)
                            v1, t_v1 = tring.next()
                            S.op("dve", lambda e, v1=v1, gv=gv, rv=rv: e.tensor_tensor(
                                out=v1[:].rearrange("p (g d) -> p g d", d=128), in0=gv[:].rearrange("p (g d) -> p g d", d=128),
                                in1=rv.unsqueeze(2).broadcast_to([128, 4, 128]), op=ALU.mult), reads=[t_gv, t_rv], writes=[t_v1])
                            vb, t_vb = vnring.next()
                            S.op("dve", lambda e, vb=vb, v1=v1, hh=hh: e.tensor_tensor(
                                out=vb[:], in0=v1[:], in1=vnw_b[:, hh * 512:(hh + 1) * 512], op=ALU.mult),
                                reads=[t_v1, t_vnw], writes=[t_vb])

                            def back(vb=vb, t_vb=t_vb, hh=hh, i=i):
                                def sgu(e):
                                    rr = None
                                    for g in range(4):
                                        rr = e.matmul(Sb[:, g * 128:(g + 1) * 128], lhsT=wsguT[:, 4 * hh + g, :],
                                                      rhs=vb[:, g * 128:(g + 1) * 128], start=True, stop=True)
                                    return rr
                                S.op("pe", sgu, reads=[t_vb, t_wsguT], writes=[t_Sb])
                                S.op("dve", lambda e: e.tensor_tensor(
                                    out=s_sb[i][:].rearrange("p (g d) -> p g d", d=128), in0=Sb[:].rearrange("p (g d) -> p g d", d=128),
                                    in1=bsguT[:, 4 * hh:4 * hh + 4].unsqueeze(2).broadcast_to([128, 4, 128]), op=ALU.add),
                                    reads=[t_Sb, t_bsguT], writes=[t_s[i]])
                        elif kind == "u":
                            gu, t_gu = tring.next()
                            S.op("act", lambda e, gu=gu, ab=ab: e.activation(out=gu[:], in_=ab[:], func=AF.Gelu_apprx_tanh),
                                 reads=[t_ab], writes=[t_gu])
                            S.op("dve", lambda e, gu=gu, i=i: e.tensor_tensor(out=s_sb[i][:], in0=gu[:], in1=s_sb[i][:], op=ALU.mult),
                                 reads=[t_gu], writes=[t_s[i]])
                        elif kind == "za":
                            sz, t_sz = tring.next()
                            S.op("act", lambda e, sz=sz, ab=ab: e.activation(out=sz[:], in_=ab[:], func=AF.Silu), reads=[t_ab], writes=[t_sz])
                            S.op("dve", lambda e, sz=sz, i=i, hh=hh: e.tensor_tensor(
                                out=gated[i][:, hh * 512:(hh + 1) * 512], in0=sz[:], in1=s_sb[i][:], op=ALU.mult),
                                reads=[t_sz, t_s[i]], writes=[t_gated[i]])
                        elif kind == "q":
                            rq, t_rq = head_rstd(ab[:], t_ab, 4)
                            q1, t_q1 = tring.next()
                            S.op("dve", lambda e, q1=q1, ab=ab, rq=rq: e.tensor_tensor(
                                out=q1[:].rearrange("p (g d) -> p g d", d=128), in0=ab[:].rearrange("p (g d) -> p g d", d=128),
                                in1=rq.unsqueeze(2).broadcast_to([128, 4, 128]), op=ALU.mult), reads=[t_ab, t_rq], writes=[t_q1])
                            S.op("dve", lambda e, q1=q1: e.tensor_tensor(
                                out=q1[:].rearrange("p (g d) -> p g d", d=128), in0=q1[:].rearrange("p (g d) -> p g d", d=128),
                                in1=qnw_b[:].unsqueeze(1).broadcast_to([128, 4, 128]), op=ALU.mult), reads=[t_qnw], writes=[t_q1])
                            qb, t_qb = qbring.next()
                            if rflag:
                                S.op("dve", lambda e, qb=qb, q1=q1: e.tensor_copy(out=qb[:], in_=q1[:]), reads=[t_q1], writes=[t_qb])
                            else:
                                apply_rope(q1, t_q1, 4, ropeT[i], t_rope[i], qb, t_qb)

                            def back(qb=qb, t_qb=t_qb, hh=hh, i=i):
                                tb, t_tb = Tb[(i + hh) % 2]
                                tbv = tb[:].bitcast(BF16).rearrange("p (k c) -> p k c", c=128)

                                def trq(e):
                                    rr = None
                                    for h in range(4):
                                        rr = e.transpose(out=tbv[:, h, :], in_=qb[:, h * 128:(h + 1) * 128], identity=identb[:])
                                    return rr
                                S.op("pe", trq, reads=[t_qb, t_identb], writes=[t_tb])
                                S.op("act", lambda e: e.copy(out=qT[i][:, 4 * hh:4 * hh + 4, :], in_=tbv[:, 0:4, :]),
                                     reads=[t_tb], writes=[t_qT[i]])
                        else:
                            S.op("act", lambda e, ab=ab, i=i, hh=hh: e.activation(out=szb[i][:, hh * 512:(hh + 1) * 512], in_=ab[:], func=AF.Silu),
                                 reads=[t_ab], writes=[t_szb[i]])
                        if back is not None:
                            pending_backs.append(back)
                while pending_backs:
                    pending_backs.pop(0)()

                if li == 0 and grp is groups[0]:
                    dbg("gatedA", gated[0][:, 0:1024], t_gated[0])
                    dbg("qT", qT[0][:], t_qT[0])
                    dbg("szb", szb[0][:], t_szb[0])
                inv_sqrt = float(128.0 ** -0.5)
                nk = len(ktiles)
                units = [(i, g, ki, kt) for i in range(ng) for g in range(2) for ki, kt in enumerate(ktiles)]
                LAG = 2
                (o0, t_o0), (o1, t_o1) = Ob

                def attn_front(u):
                    i, g, ki, kt = u
                    sbk, t_sbk = aring.next()
                    S.op("pe", lambda e: e.matmul(
                        sbk[:], lhsT=KT[:, g, kt * 128:(kt + 1) * 128], rhs=qT[i][:, 4 * g:4 * g + 4, :], start=True, stop=True),
                        reads=[t_K[kt], t_qT[i]], writes=[t_sbk])
                    pt, t_pt = ptring.next()
                    S.op("act", lambda e: e.activation(out=pt[:], in_=sbk[:], func=AF.Exp, bias=negC[:, 0:1], scale=inv_sqrt),
                         reads=[t_sbk, t_negC], writes=[t_pt])
                    return pt, t_pt

                def attn_back(u, pt, t_pt):
                    i, g, ki, kt = u

                    def pv(e):
                        rr = None
                        for hq in range(4):
                            if hq < 3:
                                oap = o0[:, hq * 129:hq * 129 + 129]
                                st = (ki == 0 and hq == 0)
                            else:
                                oap = o1[:, 0:129]
                                st = (ki == 0)
                            rr = e.matmul(oap, lhsT=pt[:, hq * 128:(hq + 1) * 128], rhs=VA[:, kt, g, 0:129],
                                          start=st, stop=(ki == nk - 1), skip_group_check=True)
                        return rr
                    S.op("pe", pv, reads=[t_pt, t_V[kt]], writes=[t_o0, t_o1])
                    if ki != nk - 1:
                        return
                    rd, t_rd = smring.next()

                    def rden(e):
                        e.reciprocal(out=rd[:, 0:3], in_=o0[:, 0:387].rearrange("p (h c) -> p h c", c=129)[:, :, 128])
                        return e.reciprocal(out=rd[:, 3:4], in_=o1[:, 128:129])
                    S.op("dve", rden, reads=[t_o0, t_o1], writes=[t_rd])

                    def onorm(e):
                        rr = None
                        for hq in range(4):
                            h = 4 * g + hq
                            oap = o0[:, hq * 129:hq * 129 + 128] if hq < 3 else o1[:, 0:128]
                            rr = e.scalar_tensor_tensor(out=gated[i][:, 1024 + h * 128:1024 + (h + 1) * 128], in0=oap,
                                                        scalar=rd[:, hq:hq + 1], in1=szb[i][:, h * 128:(h + 1) * 128],
                                                        op0=ALU.mult, op1=ALU.mult)
                        return rr
                    S.op("dve", onorm, reads=[t_o0, t_o1, t_rd, t_szb[i]], writes=[t_gated[i]])

                pend = []
                for idx in range(len(units) + LAG):
                    if idx < len(units):
                        pend.append(attn_front(units[idx]))
                    if idx >= LAG:
                        attn_back(units[idx - LAG], *pend[idx - LAG])

                if li == 0 and grp is groups[0]:
                    dbg("gated", gated[0][:], t_gated[0])
                nst = {}
                if nxt is not None:
                    for j in range(min(2, len(nxt))):
                        nst[j] = make_hT_a(src_ap, t_srcx, nxt[j])
                    load_rope(nxt)
                for i, t in enumerate(grp):
                    for half in range(2):
                        tb, t_tb = Tb[half]
                        tbv = tb[:].bitcast(BF16).rearrange("p (k c) -> p k c", c=128)

                        def trg(e, half=half, tbv=tbv, i=i):
                            rr = None
                            for k in range(8):
                                kc = half * 8 + k
                                rr = e.transpose(out=tbv[:, k, :], in_=gated[i][:, kc * 128:(kc + 1) * 128], identity=identb[:])
                            return rr
                        S.op("pe", trg, reads=[t_gated[i], t_identb], writes=[t_tb])
                        S.op("act", lambda e, half=half, tbv=tbv, i=i: e.copy(out=gTv[i][:, half * 8:(half + 1) * 8, :], in_=tbv[:]),
                             reads=[t_tb], writes=[t_qT[i], t_szb[i]])

                for cb in range(4):
                    wb, t_wb, wkey = wring.next()
                    S.dma("sp", lambda e, wb=wb, cb=cb: e.dma_start(out=wb[:], in_=wbo[li][cb]), wkey, reads=[t_wco[li]], writes=[t_wb])
                    for i, t in enumerate(grp):
                        xpb, t_xp, xkey = xpring.next()
                        S.dma("sp", lambda e, xpb=xpb, t=t, cb=cb: e.dma_start(
                            out=xpb[:], in_=src_ap[t * 128:(t + 1) * 128, cb * 512:(cb + 1) * 512]), xkey, reads=[t_srcx[t]], writes=[t_xp])
                        ab, t_ab = proj(gTv[i], [t_qT[i], t_szb[i]], wb, t_wb)
                        yg, t_yg = tring.next()
                        S.op("dve", lambda e, yg=yg, ab=ab, cb=cb: e.tensor_tensor(
                            out=yg[:], in0=ab[:], in1=gate_b[:, cb * 512:(cb + 1) * 512], op=ALU.mult),
                            reads=[t_ab, t_gate], writes=[t_yg])
                        S.op("pool", lambda e, yg=yg, xpb=xpb: e.tensor_tensor(out=xpb[:], in0=xpb[:], in1=yg[:], op=ALU.add),
                             reads=[t_yg], writes=[t_xp])
                        if last_in_prog:
                            if final:
                                drow = t
                            else:
                                drow = t
                        else:
                            drow = t
                        S.dma("pool", lambda e, xpb=xpb, drow=drow, cb=cb: e.dma_start(
                            out=dst_ap[drow * 128:(drow + 1) * 128, cb * 512:(cb + 1) * 512], in_=xpb[:]), xkey,
                            reads=[t_xp], writes=[t_xs_next[t]])
                        if last_in_prog:
                            out_toks.append(t_xp)
                    if nxt is not None and cb < len(nxt):
                        make_hT_b(nst.pop(cb), nxt[cb], li, hT[cb], t_hT[cb])
                        if cb + 2 < len(nxt):
                            nst[cb + 2] = make_hT_a(src_ap, t_srcx, nxt[cb + 2])
                prepared = nxt is not None
            return t_xs_next

        t_xs_all = [Tok() for _ in range(NT_ALL)]
        for li_, l_ in enumerate(layers):
            last_ = (li_ == NL - 1)
            t_xs_all = run_layer(li_, l_, xa if li_ == 0 else xs, out if last_ else xs, t_xs_all,
                                 p2_tiles_per_layer[li_], last_)

        S.final_wait("pool", list({id(t): t for t in out_toks}.values()) + dbg_toks)
        S.emit(block)
    return nc


def _rope_tables(pos):
    rows = (pos // GRID_W).astype(np.float32)
    cols = (pos % GRID_W).astype(np.float32)
    inv_freq = (np.float32(10000.0) ** (-np.arange(0, 64, 2, dtype=np.float32) / np.float32(64))).astype(np.float32)
    ang_r = rows[:, None] * inv_freq[None, :]
    ang_c = cols[:, None] * inv_freq[None, :]
    ang = np.concatenate([ang_r, ang_r, ang_c, ang_c], axis=-1).astype(np.float32)
    return np.concatenate([np.cos(ang), np.sin(ang)], axis=-1).astype(np.float32)


_PROG_CACHE = {}


def _get_prog(key, *args):
    if key not in _PROG_CACHE:
        _PROG_CACHE[key] = build(*args)
    return _PROG_CACHE[key]


def _common_inputs(c, c_ctx, norm_w, w_mod, b_mod, w_in, w_sgu, b_sgu, v_norm_w, q_norm_w, k_norm_w, w_out):
    f = lambda a: np.ascontiguousarray(np.asarray(a, dtype=np.float32))
    shared = {
        "identb": np.eye(128, dtype=np.float32).astype(ml_dtypes.bfloat16),
        "identf": np.eye(128, dtype=np.float32),
        "w_mod": f(w_mod), "b_mod": f(b_mod).reshape(2, 48, 128), "norm_w": f(norm_w).reshape(2, 16, 128),
        "w_in": f(w_in), "w_out": f(w_out), "w_sgu": f(w_sgu), "b_sgu": f(b_sgu),
        "v_norm_w": f(v_norm_w).reshape(2, 1024), "q_norm_w": f(q_norm_w), "k_norm_w": f(k_norm_w),
    }
    return shared


def kernel(x, c, ctx, c_ctx, norm_w, w_mod, b_mod, w_in, w_sgu, b_sgu, v_norm_w, q_norm_w, k_norm_w, w_out):
    x = np.asarray(x, dtype=np.float32)
    ctx = np.asarray(ctx, dtype=np.float32)
    c = np.asarray(c, dtype=np.float32)
    c_ctx = np.asarray(c_ctx, dtype=np.float32)
    shared = _common_inputs(c, c_ctx, norm_w, w_mod, b_mod, w_in, w_sgu, b_sgu, v_norm_w, q_norm_w, k_norm_w, w_out)
    H = SEQ // 2
    CH = CTX // 2

    def core_maps(xfull, cfull):
        maps = []
        for core in range(8):
            b, hf = divmod(core, 2)
            o, p = hf, 1 - hf
            xa = np.concatenate([xfull[b, o * H:(o + 1) * H], cfull[b, o * CH:(o + 1) * CH],
                                 xfull[b, p * H:(p + 1) * H], cfull[b, p * CH:(p + 1) * CH]], axis=0)
            pos = np.concatenate([np.arange(o * H, (o + 1) * H), np.arange(p * H, (p + 1) * H)])
            m = dict(shared)
            m["xa"] = np.ascontiguousarray(xa)
            m["rope"] = _rope_tables(pos)
            m["c2"] = np.ascontiguousarray(np.stack([c[b], c_ctx], 0).reshape(32, 128))
            maps.append(m)
        return maps

    all_tiles = list(range(NT_ALL))
    own_lat = list(range(16))
    if MODE == "cc":
        maps = []
        for core in range(8):
            b, hf = divmod(core, 2)
            m = dict(shared)
            m["xa"] = np.ascontiguousarray(np.concatenate([x[b, hf * H:(hf + 1) * H], ctx[b, hf * CH:(hf + 1) * CH]], axis=0))
            m["rope"] = _rope_tables(np.arange(hf * H, (hf + 1) * H))
            m["c2"] = np.ascontiguousarray(np.stack([c[b], c_ctx], 0).reshape(32, 128))
            maps.append(m)
        nc = _get_prog("cc", [0, 1], True, list(range(17)), [list(range(17)), own_lat], 16, True)
        res = run_bass_kernel_spmd(nc, maps, core_ids=list(range(8)))
        outs = [r["out"] for r in res.results]
    elif MODE == "fused":
        nc = _get_prog("fused", [0, 1], True, all_tiles, [all_tiles, own_lat], 16)
        res = run_bass_kernel_spmd(nc, core_maps(x, ctx), core_ids=list(range(8)))
        outs = [r["out"] for r in res.results]
    else:
        ncA = _get_prog("L0", [0], False, all_tiles, [list(range(17))], 17)
        resA = run_bass_kernel_spmd(ncA, core_maps(x, ctx), core_ids=list(range(8)))
        x1 = np.empty_like(x)
        ctx1 = np.empty_like(ctx)
        for core in range(8):
            b, hf = divmod(core, 2)
            xn_ = resA.results[core]["xnext"]
            x1[b, hf * H:(hf + 1) * H] = xn_[0:H]
            ctx1[b, hf * CH:(hf + 1) * CH] = xn_[H:H + CH]
        ncB = _get_prog("L1", [1], True, all_tiles, [own_lat], 16)
        resB = run_bass_kernel_spmd(ncB, core_maps(x1, ctx1), core_ids=list(range(8)))
        outs = [r["out"] for r in resB.results]
    y = np.empty((4, SEQ, D), dtype=np.float32)
    for core in range(8):
        b, hf = divmod(core, 2)
        y[b, hf * H:(hf + 1) * H] = outs[core]
    return y
```

```python
import numpy as np
from contextlib import ExitStack
import ml_dtypes
import concourse.bass as bass
import concourse.mybir as mybir
from concourse.bass_utils import run_bass_kernel_spmd

F32 = mybir.dt.float32
BF16 = mybir.dt.bfloat16
AF = mybir.ActivationFunctionType
ALU = mybir.AluOpType
AX = mybir.AxisListType

D = 2048
NKC = 16
DIN = 5632
NCB = 11
SEQ = 4096
CTX = 256
GRID_W = 64
EPS = 1e-6
TG = 4
NT_ALL = 34
DEBUG = False
MODE = "fused"


class Tok:
    __slots__ = ("w", "r")

    def __init__(self):
        self.w = None
        self.r = {}


class Sched:
    EPOCH = 20000

    def __init__(self, nc, es):
        self.nc = nc
        self.es = es
        self.names = ["pe", "act", "dve", "pool", "sp"]
        self.q = {k: [] for k in self.names}
        self.cnt = {k: 0 for k in self.names}
        self.esems = {k: [] for k in self.names}
        self.dsems = {}
        self.dcnt = {}
        self.waited = {k: {} for k in self.names}

    def _newsem(self, name):
        return self.es.enter_context(self.nc.semaphore(name))

    def _eng_event(self, eng):
        c = self.cnt[eng]
        ep, v = divmod(c, self.EPOCH)
        while len(self.esems[eng]) <= ep:
            self.esems[eng].append(self._newsem(f"e_{eng}_{len(self.esems[eng])}"))
        self.cnt[eng] = c + 1
        return (self.esems[eng][ep], v + 1, eng)

    def _collect(self, eng, reads, writes):
        evs = []
        for t in reads:
            if t.w is not None:
                evs.append(t.w)
        for t in writes:
            if t.w is not None:
                evs.append(t.w)
            evs.extend(t.r.values())
        waits = {}
        for (sem, val, e) in evs:
            if e is not None and e == eng and eng == "pe":
                continue
            key = id(sem)
            if self.waited[eng].get(key, 0) >= val:
                continue
            if key not in waits or waits[key][1] < val:
                waits[key] = (sem, val)
        for key, (sem, val) in waits.items():
            self.waited[eng][key] = val
        return list(waits.values())

    def _mark(self, ev, reads, writes):
        k = id(ev[0])
        for t in reads:
            old = t.r.get(k)
            if old is None or old[1] < ev[1]:
                t.r[k] = ev
        for t in writes:
            t.w = ev
            t.r = {}

    def op(self, eng, fn, reads=(), writes=()):
        waits = self._collect(eng, reads, writes)
        ev = self._eng_event(eng)
        self.q[eng].append((waits, fn, ev[0], 1))
        self._mark(ev, reads, writes)

    def dma(self, eng, fn, key, reads=(), writes=(), n=1, inc=16):
        waits = self._collect(eng, reads, writes)
        if key not in self.dsems:
            self.dsems[key] = self._newsem(f"d_{key}")
            self.dcnt[key] = 0
        self.dcnt[key] += inc * n
        ev = (self.dsems[key], self.dcnt[key], None)
        self.q[eng].append((waits, fn, self.dsems[key], inc))
        self._mark(ev, reads, writes)

    def final_wait(self, eng, toks):
        waits = self._collect(eng, toks, toks)
        self.q[eng].append((waits, None, None, 0))

    def emit(self, block):
        def mk(name):
            def body(e):
                for (waits, fn, sem, inc) in self.q[name]:
                    for (s, v) in waits:
                        e.wait_ge(s, v)
                    if fn is None:
                        continue
                    r = fn(e)
                    if isinstance(r, (list, tuple)):
                        for ins in r:
                            ins.then_inc(sem, inc)
                    else:
                        r.then_inc(sem, inc)
            return body
        block.tensor(mk("pe"))
        block.scalar(mk("act"))
        block.vector(mk("dve"))
        block.gpsimd(mk("pool"))
        block.sync(mk("sp"))


class Ring:
    def __init__(self, items):
        self.items = items
        self.i = 0

    def next(self):
        it = self.items[self.i % len(self.items)]
        self.i += 1
        return it


def is_ctx(t):
    return t % 17 == 16


def lat_index(t):
    return (t // 17) * 16 + (t % 17)


PAIRS = [[0, 1], [2, 3], [4, 5], [6, 7]]
KVW = 2 * 17 * 128 + 17 * 260


def build(layers, final, p1_tiles, p2_tiles_per_layer, n_out_tiles, cc=False):
    nc = bass.Bass("TRN2", target_bir_lowering=False)
    NL = len(layers)

    def din(name, shape, dt=F32):
        return nc.dram_tensor(name, shape, dt, kind="ExternalInput").ap()

    NX = 17 if cc else NT_ALL
    xa = din("xa", [NX * 128, D])
    rope = din("rope", [(16 if cc else 32) * 128, 256])
    c2 = din("c2", [32, 128])
    identb_in = din("identb", [128, 128], BF16)
    identf_in = din("identf", [128, 128])
    w_mod = din("w_mod", [2, D, 3 * D])
    b_mod = din("b_mod", [2, 48, 128])
    norm_w = din("norm_w", [2, 16, 128])
    w_in = din("w_in", [2, D, DIN])
    w_out = din("w_out", [2, D, D])
    w_sgu = din("w_sgu", [2, 8, 128, 128])
    b_sgu = din("b_sgu", [2, 8, 128])
    v_norm_w = din("v_norm_w", [2, 1024])
    q_norm_w = din("q_norm_w", [2, 128])
    k_norm_w = din("k_norm_w", [2, 128])
    if final:
        out = nc.dram_tensor("out", [16 * 128, D], F32, kind="ExternalOutput").ap()
    else:
        out = nc.dram_tensor("xnext", [n_out_tiles * 128, D], F32, kind="ExternalOutput").ap()
    xs = nc.dram_tensor("xs", [NX * 128, D], F32).ap() if NL > 1 else None
    if cc:
        kv_send = [nc.dram_tensor(f"kv_send{i}", [128, KVW], BF16).ap() for i in range(NL)]
        kv_recv = [nc.dram_tensor(f"kv_recv{i}", [256, KVW], BF16).ap() for i in range(NL)]
    wbi = [nc.dram_tensor(f"wbi{l}", [NCB, 128, NKC * 512], BF16).ap() for l in layers]
    wbo = [nc.dram_tensor(f"wbo{l}", [4, 128, NKC * 512], BF16).ap() for l in layers]

    with ExitStack() as es:
        def sb(name, shape, dt):
            return es.enter_context(nc.sbuf_tensor(name, shape, dt))

        S = Sched(nc, es)
        identb = sb("identb_sb", [128, 128], BF16); t_identb = Tok()
        identf = sb("identf_sb", [128, 128], F32); t_identf = Tok()
        onesf = sb("onesf", [128, 128], F32); t_onesf = Tok()
        KT = sb("KT", [128, 2, NT_ALL * 128], BF16)
        VA = sb("VA", [128, NT_ALL, 2, 130], BF16)
        if cc:
            tkb = [Tok(), Tok()]
            tvb = [Tok(), Tok()]
            t_K = [tkb[kt // 17] for kt in range(NT_ALL)]
            t_V = [tvb[kt // 17] for kt in range(NT_ALL)]
            kst = [sb(f"kst{i}", [128, 2, 128], BF16) for i in range(2)]
            kstring = Ring([(kst[i], Tok(), f"kst{i}") for i in range(2)])
            vst = [sb(f"vst{i}", [128, 2, 130], BF16) for i in range(2)]
            vstring = Ring([(vst[i], Tok(), f"vst{i}") for i in range(2)])
        else:
            t_K = [Tok() for _ in range(NT_ALL)]
            t_V = [Tok() for _ in range(NT_ALL)]
        t_Vones = Tok()
        wbuf = [sb(f"wbuf{i}", [128, NKC * 512], BF16) for i in range(2)]
        t_wbuf = [Tok() for _ in range(2)]
        wring = Ring(list(zip(wbuf, t_wbuf, ["wbuf0", "wbuf1"])))
        xbuf = [sb(f"xbuf{i}", [128, D], F32) for i in range(2)]
        xring = Ring([(xbuf[i], Tok(), f"xbuf{i}") for i in range(2)])
        xn = [sb(f"xn{i}", [128, D], BF16) for i in range(2)]
        xnring = Ring([(xn[i], Tok()) for i in range(2)])
        hT = [sb(f"hT{i}", [128, NKC, 128], BF16) for i in range(TG)]
        t_hT = [(Tok(), Tok()) for _ in range(TG)]
        h1ring = Ring([(hT[i], t_hT[i]) for i in range(2)])
        s_sb = [sb(f"s_sb{i}", [128, 512], F32) for i in range(TG)]
        t_s = [Tok() for _ in range(TG)]
        gated = [sb(f"gated{i}", [128, D], BF16) for i in range(TG)]
        t_gated = [Tok() for _ in range(TG)]
        qz = [sb(f"qz{i}", [128, D], BF16) for i in range(TG)]
        qT = [qz[i][:, 0:1024].rearrange("p (h q) -> p h q", h=8) for i in range(TG)]
        t_qT = [Tok() for _ in range(TG)]
        szb = [qz[i][:, 1024:2048] for i in range(TG)]
        t_szb = [Tok() for _ in range(TG)]
        gTv = [qz[i][:].rearrange("p (k c) -> p k c", c=128) for i in range(TG)]
        ropeT = [sb(f"ropeT{i}", [128, 256], F32) for i in range(TG)]
        t_rope = [Tok() for _ in range(TG)]
        r1ring = Ring([(ropeT[i], t_rope[i], f"ropeT{i}") for i in range(2)])
        tmps = [sb(f"tmp{i}", [128, 512], F32) for i in range(6)]
        tring = Ring([(tmps[i], Tok()) for i in range(6)])
        vnb = [sb(f"vnb{i}", [128, 512], BF16) for i in range(3)]
        vnring = Ring([(vnb[i], Tok()) for i in range(3)])
        qbf = [sb(f"qbf{i}", [128, 512], BF16) for i in range(3)]
        qbring = Ring([(qbf[i], Tok()) for i in range(3)])
        PT = [sb(f"PT{i}", [128, 512], BF16) for i in range(4)]
        ptring = Ring([(PT[i], Tok()) for i in range(4)])
        xp = [sb(f"xp{i}", [128, 512], F32) for i in range(4)]
        xpring = Ring([(xp[i], Tok(), f"xp{i}") for i in range(4)])
        gate_b = sb("gate_b", [128, D], F32)
        t_gate = Tok()
        small = sb("small", [128, 64], F32)
        smring = Ring([(small[:, 8 * i:8 * i + 8], Tok()) for i in range(8)])
        cT = sb("cT", [128, 32], F32); t_cT = Tok()
        modT = [sb(f"modT{i}", [128, 48, 2], F32) for i in range(NL)]
        t_mod = [Tok() for _ in range(NL)]
        gT = [sb(f"gT{i}", [128, NKC, 2], F32) for i in range(NL)]
        t_gT = [Tok() for _ in range(NL)]
        nwT = sb("nwT", [128, 16], F32); t_nwT = Tok()
        bmT = sb("bmT", [128, 48], F32); t_bmT = Tok()
        rows = sb("rows", [48, 128], F32); t_rows = Tok()
        wsg_f = sb("wsg_f", [128, 8, 128], F32); t_wsgf = Tok()
        wsg_b = sb("wsg_b", [128, 8, 128], BF16); t_wsgb = Tok()
        wsguT = sb("wsguT", [128, 8, 128], BF16); t_wsguT = Tok()
        bsguT = sb("bsguT", [128, 8], F32); t_bsguT = Tok()
        vnw_b = sb("vnw_b", [128, 1024], F32); t_vnw = Tok()
        qnw_b = sb("qnw_b", [128, 128], F32); t_qnw = Tok()
        knw_b = sb("knw_b", [128, 128], F32); t_knw = Tok()
        negC = sb("negC", [128, 2], F32); t_negC = Tok()
        diag = [sb(f"diag{i}", [128, 128], F32) for i in range(2)]
        dring = Ring([(diag[i], Tok()) for i in range(2)])
        bank = [es.enter_context(nc.psum_tensor(f"bank{i}", [128, 512], F32)) for i in range(8)]
        t_bank = [Tok() for _ in range(8)]
        aring = Ring([(bank[i], t_bank[i]) for i in range(3)])
        Sb, t_Sb = bank[3], t_bank[3]
        Tb = [(bank[4], t_bank[4]), (bank[5], t_bank[5])]
        Ob = [(bank[6], t_bank[6]), (bank[7], t_bank[7])]

        block = es.enter_context(nc.Block())
        dbg_toks = []

        def dbg(name, ap, tok):
            if not DEBUG:
                return
            d = nc.dram_tensor("dbg_" + name, list(ap.shape), ap.dtype, kind="ExternalOutput").ap()
            tk = Tok()
            S.dma("sp", lambda e: e.dma_start(out=d, in_=ap), "dbg_" + name, reads=[tok], writes=[tk])
            dbg_toks.append(tk)

        S.dma("sp", lambda e: e.dma_start(out=identb[:], in_=identb_in), "c_identb", writes=[t_identb])
        S.dma("sp", lambda e: e.dma_start(out=identf[:], in_=identf_in), "c_identf", writes=[t_identf])
        S.op("dve", lambda e: e.memset(onesf[:], 1.0), writes=[t_onesf])
        if cc:
            def ones_v(e):
                e.memset(vst[0][:, :, 128:130], 1.0)
                return e.memset(vst[1][:, :, 128:130], 1.0)
            S.op("dve", ones_v, writes=[t_Vones])
        else:
            S.op("dve", lambda e: e.memset(VA[:, :, :, 128:130], 1.0), writes=[t_Vones])

        t_wci = [Tok() for _ in range(NL)]
        t_wco = [Tok() for _ in range(NL)]
        def emit_casts(li):
            l = layers[li]

            def cast_in(e):
                res = []
                for kc in range(NKC):
                    for c0 in (0, 6):
                        ncb = 6 if c0 == 0 else 5
                        dst = wbi[li][c0:c0 + ncb, :, kc * 512:(kc + 1) * 512]
                        src = w_in[l, kc * 128:(kc + 1) * 128, c0 * 512:(c0 + ncb) * 512].rearrange("p (cb c) -> cb p c", c=512)
                        res.append(e.dma_start(out=dst, in_=src))
                return res
            S.dma("pool", cast_in, f"cast_in{li}", writes=[t_wci[li]], n=2 * NKC)

            def cast_out(e):
                res = []
                for kc in range(NKC):
                    dst = wbo[li][:, :, kc * 512:(kc + 1) * 512]
                    src = w_out[l, kc * 128:(kc + 1) * 128, :].rearrange("p (cb c) -> cb p c", c=512)
                    res.append(e.dma_start(out=dst, in_=src))
                return res
            S.dma("pool", cast_out, f"cast_out{li}", writes=[t_wco[li]], n=NKC)
        emit_casts(0)

        def small_T(src_ap, n, dst, t_dst, extra_reads=()):
            S.dma("sp", lambda e: e.dma_start(out=rows[0:n, :], in_=src_ap), "rows", writes=[t_rows])
            S.op("pe", lambda e: e.transpose(out=Sb[:, 0:n], in_=rows[0:n, :], identity=identf[0:n, 0:n]),
                 reads=[t_rows, t_identf], writes=[t_Sb])
            S.op("dve", lambda e: e.tensor_copy(out=dst, in_=Sb[:, 0:n]), reads=[t_Sb], writes=[t_dst])

        S.dma("sp", lambda e: e.dma_start(out=rows[0:32, :], in_=c2), "rows", writes=[t_rows])
        S.op("act", lambda e: e.activation(out=rows[0:32, :], in_=rows[0:32, :], func=AF.Silu), reads=[t_rows], writes=[t_rows])
        S.op("pe", lambda e: e.transpose(out=Sb[:, 0:32], in_=rows[0:32, :], identity=identf[0:32, 0:32]),
             reads=[t_rows, t_identf], writes=[t_Sb])
        S.op("dve", lambda e: e.tensor_copy(out=cT[:], in_=Sb[:, 0:32]), reads=[t_Sb], writes=[t_cT])
        cTv = cT[:].rearrange("p (r k) -> p r k", r=2)

        for li, l in enumerate(layers):
            small_T(b_mod[l], 48, bmT[:], t_bmT)
            small_T(norm_w[l], 16, nwT[:], t_nwT)
            for jb in range(24):
                wb, t_wb, wkey = wring.next()
                wv = wb[:].bitcast(F32).rearrange("p (k c) -> p k c", c=256)
                S.dma("sp", lambda e, wv=wv, l=l, jb=jb: e.dma_start(
                    out=wv, in_=w_mod[l, :, jb * 256:(jb + 1) * 256].rearrange("(k p) c -> p k c", p=128)),
                    wkey, writes=[t_wb])

                def mm_mod(e, wv=wv, jb=jb):
                    r = None
                    for jj in range(2):
                        j = jb * 2 + jj
                        for kc in range(NKC):
                            r = e.matmul(Sb[:, 2 * j:2 * j + 2], lhsT=wv[:, kc, jj * 128:(jj + 1) * 128], rhs=cTv[:, :, kc],
                                         start=(kc == 0), stop=(kc == NKC - 1))
                    return r
                S.op("pe", mm_mod, reads=[t_wb, t_cT], writes=[t_Sb])
            modv = modT[li]
            S.op("dve", lambda e, modv=modv: e.tensor_tensor(
                out=modv[:], in0=Sb[:, 0:96].rearrange("p (j r) -> p j r", r=2),
                in1=bmT[:].unsqueeze(2).broadcast_to([128, 48, 2]), op=ALU.add),
                reads=[t_Sb, t_bmT], writes=[t_mod[li]])
            S.op("dve", lambda e, modv=modv, li=li: e.tensor_scalar(
                out=gT[li][:], in0=modv[:, 16:32, :], scalar1=1.0, scalar2=None, op0=ALU.add),
                reads=[t_mod[li]], writes=[t_gT[li]])
            S.op("dve", lambda e, li=li: e.tensor_tensor(
                out=gT[li][:], in0=gT[li][:], in1=nwT[:].unsqueeze(2).broadcast_to([128, 16, 2]), op=ALU.mult),
                reads=[t_nwT], writes=[t_gT[li]])

        for li in range(NL):
            dbg(f"modT{li}", modT[li][:], t_mod[li])
            dbg(f"gT{li}", gT[li][:], t_gT[li])
        dbg("cT", cT[:], t_cT)

        def rstd_small(ss_ap, t_ss, n, inv_n):
            sd, t_sd = smring.next()
            S.op("act", lambda e: e.activation(out=sd[:, 0:n], in_=ss_ap, func=AF.Sqrt, bias=EPS, scale=inv_n),
                 reads=[t_ss], writes=[t_sd])
            rs, t_rs = smring.next()
            S.op("dve", lambda e: e.reciprocal(out=rs[:, 0:n], in_=sd[:, 0:n]), reads=[t_sd], writes=[t_rs])
            return rs[:, 0:n], t_rs

        def make_hT_a0(src_ap, t_src, t):
            xb, t_xb, xkey = xring.next()
            S.dma("sp", lambda e: e.dma_start(out=xb[:], in_=src_ap[t * 128:(t + 1) * 128, :]), xkey, reads=[t_src[t]], writes=[t_xb])
            return xb, t_xb

        def make_hT_a1(st0):
            xb, t_xb = st0
            xnb, t_xn = xnring.next()
            ss, t_ss = smring.next()
            S.op("act", lambda e: e.activation(out=xnb[:], in_=xb[:], func=AF.Square, accum_out=ss[:, 0:1]),
                 reads=[t_xb], writes=[t_xn, t_ss])
            rs, t_rs = rstd_small(ss[:, 0:1], t_ss, 1, 1.0 / D)
            S.op("act", lambda e: e.activation(out=xnb[:], in_=xb[:], func=AF.Copy, scale=rs[:, 0:1]),
                 reads=[t_xb, t_rs], writes=[t_xn])
            return xnb, t_xn

        def make_hT_a(src_ap, t_src, t):
            return make_hT_a1(make_hT_a0(src_ap, t_src, t))

        def make_hT_b(st, t, li, dst, t_dst, n_act=4):
            xnb, t_xn = st
            r = 1 if is_ctx(t) else 0
            for half in range(2):
                tb, t_tb = Tb[half]
                tbv = tb[:].bitcast(BF16).rearrange("p (k c) -> p k c", c=128)

                def tr(e, half=half, tbv=tbv):
                    rr = None
                    for k in range(8):
                        kc = half * 8 + k
                        rr = e.transpose(out=tbv[:, k, :], in_=xnb[:, kc * 128:(kc + 1) * 128], identity=identb[:])
                    return rr
                S.op("pe", tr, reads=[t_xn, t_identb], writes=[t_tb])
                na = 8 if half == 1 else 0

                def ev(e, half=half, tbv=tbv, na=na):
                    rr = None
                    for k in range(8 - na):
                        kc = half * 8 + k
                        rr = e.tensor_scalar(out=dst[:, kc, :], in0=tbv[:, k, :], scalar1=gT[li][:, kc, r:r + 1],
                                             scalar2=modT[li][:, kc, r:r + 1], op0=ALU.mult, op1=ALU.add)
                    return rr

                def ev_act(e, half=half, tbv=tbv, na=na):
                    rr = None
                    for k in range(8 - na, 8):
                        kc = half * 8 + k
                        rr = e.activation(out=dst[:, kc, :], in_=tbv[:, k, :], func=AF.Identity,
                                          scale=gT[li][:, kc, r:r + 1], bias=modT[li][:, kc, r:r + 1])
                    return rr
                if na < 8:
                    S.op("dve", ev, reads=[t_tb, t_gT[li], t_mod[li]], writes=[t_dst[0]])
                if na:
                    S.op("act", ev_act, reads=[t_tb, t_gT[li], t_mod[li]], writes=[t_dst[1]])

        def make_hT(src_ap, t_src, t, li, dst, t_dst):
            make_hT_b(make_hT_a(src_ap, t_src, t), t, li, dst, t_dst)

        def proj(lhs, t_lhs, wb, t_wb):
            ab, t_ab = aring.next()
            wv = wb[:].rearrange("p (k c) -> p k c", c=512)

            def mm(e):
                rr = None
                for kc in range(NKC):
                    rr = e.matmul(ab[:], lhsT=lhs[:, kc, :], rhs=wv[:, kc, :], start=(kc == 0), stop=(kc == NKC - 1))
                return rr
            S.op("pe", mm, reads=[*t_lhs, t_wb], writes=[t_ab])
            return ab, t_ab

        def head_rstd(src_ap, t_src, nh):
            sq, t_sq = tring.next()
            S.op("act", lambda e: e.activation(out=sq[:, 0:nh * 128], in_=src_ap, func=AF.Square), reads=[t_src], writes=[t_sq])
            ss, t_ss = smring.next()
            S.op("dve", lambda e: e.tensor_reduce(out=ss[:, 0:nh], in_=sq[:, 0:nh * 128].rearrange("p (h d) -> p h d", d=128),
                                                  axis=AX.X, op=ALU.add), reads=[t_sq], writes=[t_ss])
            return rstd_small(ss[:, 0:nh], t_ss, nh, 1.0 / 128)

        def apply_rope(src, t_src, nh, rt, t_rt, dst, t_dst):
            n = nh * 128
            t1, t_t1 = tring.next()
            cosb = rt[:, 0:128].unsqueeze(1).broadcast_to([128, nh, 128])
            S.op("dve", lambda e: e.tensor_tensor(out=t1[:, 0:n].rearrange("p (h d) -> p h d", d=128),
                                                  in0=src[:, 0:n].rearrange("p (h d) -> p h d", d=128), in1=cosb, op=ALU.mult),
                 reads=[t_src, t_rt], writes=[t_t1])
            rot, t_rot = tring.next()
            sv = src[:, 0:n].rearrange("p (h b t d) -> p h b t d", b=2, t=2, d=32)
            rv = rot[:, 0:n].rearrange("p (h b t d) -> p h b t d", b=2, t=2, d=32)
            t1v = t1[:, 0:n].rearrange("p (h b t d) -> p h b t d", b=2, t=2, d=32)
            dv = dst[:, 0:n].rearrange("p (h b t d) -> p h b t d", b=2, t=2, d=32)
            sinv = rt[:, 128:256].rearrange("p (b t d) -> p b t d", b=2, t=2)

            def rotf(e):
                e.tensor_tensor(out=rv[:, :, :, 0, :], in0=sv[:, :, :, 1, :],
                                in1=sinv[:, :, 0, :].unsqueeze(1).broadcast_to([128, nh, 2, 32]), op=ALU.mult)
                return e.tensor_tensor(out=rv[:, :, :, 1, :], in0=sv[:, :, :, 0, :],
                                       in1=sinv[:, :, 1, :].unsqueeze(1).broadcast_to([128, nh, 2, 32]), op=ALU.mult)
            S.op("dve", rotf, reads=[t_src, t_rt], writes=[t_rot])

            def fin(e):
                e.tensor_tensor(out=dv[:, :, :, 0, :], in0=t1v[:, :, :, 0, :], in1=rv[:, :, :, 0, :], op=ALU.subtract)
                return e.tensor_tensor(out=dv[:, :, :, 1, :], in0=t1v[:, :, :, 1, :], in1=rv[:, :, :, 1, :], op=ALU.add)
            S.op("dve", fin, reads=[t_t1, t_rot], writes=[t_dst])

        out_toks = []

        def run_layer(li, l, src_ap, dst_ap, t_srcx, p2_tiles, last_in_prog):

            S.dma("sp", lambda e, l=l: e.dma_start(out=wsg_f[:], in_=w_sgu[l].rearrange("g p q -> p g q")), "wsgf", writes=[t_wsgf])
            S.op("dve", lambda e: e.tensor_copy(out=wsg_b[:], in_=wsg_f[:]), reads=[t_wsgf], writes=[t_wsgb])
            tb, t_tb = Tb[0]
            tbv0 = tb[:].bitcast(BF16).rearrange("p (k c) -> p k c", c=128)

            def trw(e, tbv0=tbv0):
                rr = None
                for g in range(8):
                    rr = e.transpose(out=tbv0[:, g, :], in_=wsg_b[:, g, :], identity=identb[:])
                return rr
            S.op("pe", trw, reads=[t_wsgb, t_identb], writes=[t_tb])
            S.op("dve", lambda e, tbv0=tbv0: e.tensor_copy(out=wsguT[:], in_=tbv0[:]), reads=[t_tb], writes=[t_wsguT])
            small_T(b_sgu[l], 8, bsguT[:], t_bsguT)
            S.dma("sp", lambda e, l=l: e.dma_start(out=vnw_b[:], in_=v_norm_w[l].partition_broadcast(128)), "vnw", writes=[t_vnw])
            S.dma("sp", lambda e, l=l: e.dma_start(out=qnw_b[:], in_=q_norm_w[l].partition_broadcast(128)), "qnw", writes=[t_qnw])
            S.dma("sp", lambda e, l=l: e.dma_start(out=knw_b[:], in_=k_norm_w[l].partition_broadcast(128)), "knw", writes=[t_knw])
            mq, t_mq = smring.next()
            S.op("dve", lambda e, mq=mq: e.tensor_reduce(out=mq[:, 0:1], in_=qnw_b[:], axis=AX.X, op=ALU.max, apply_absolute_value=True),
                 reads=[t_qnw], writes=[t_mq])
            S.op("dve", lambda e, mq=mq: e.tensor_reduce(out=mq[:, 1:2], in_=knw_b[:], axis=AX.X, op=ALU.max, apply_absolute_value=True),
                 reads=[t_knw], writes=[t_mq])
            S.op("dve", lambda e, mq=mq: e.tensor_tensor(out=negC[:, 0:1], in0=mq[:, 0:1], in1=mq[:, 1:2], op=ALU.mult),
                 reads=[t_mq], writes=[t_negC])
            S.op("dve", lambda e: e.tensor_scalar(out=negC[:, 0:1], in0=negC[:, 0:1], scalar1=-float(np.sqrt(128.0)), scalar2=None, op0=ALU.mult),
                 writes=[t_negC])
            def build_gate(r, li=li):
                for q4 in range(4):
                    for k in range(4):
                        kc = q4 * 4 + k
                        dg, t_dg = dring.next()
                        S.op("dve", lambda e, dg=dg, kc=kc, r=r: e.tensor_scalar(
                            out=dg[:], in0=identf[:], scalar1=modT[li][:, 32 + kc, r:r + 1], scalar2=None, op0=ALU.mult),
                            reads=[t_identf, t_mod[li]], writes=[t_dg])
                        S.op("pe", lambda e, dg=dg, k=k: e.matmul(Sb[:, k * 128:(k + 1) * 128], lhsT=onesf[:], rhs=dg[:], start=True, stop=True),
                             reads=[t_dg, t_onesf], writes=[t_Sb])
                    S.op("dve", lambda e, q4=q4: e.tensor_copy(out=gate_b[:, q4 * 512:(q4 + 1) * 512], in_=Sb[:]),
                         reads=[t_Sb], writes=[t_gate])
            build_gate(0)

            wkv, t_wkv, wkey = wring.next()
            S.dma("sp", lambda e, wkv=wkv: e.dma_start(out=wkv[:], in_=wbi[li][8]), wkey, reads=[t_wci[li]], writes=[t_wkv])
            kv_toks = []
            P1 = list(p1_tiles)
            NP1 = len(P1)
            hbs = {}
            st0 = {}
            st1 = {}
            p1_tails = []
            for j in range(min(2, NP1)):
                st0[j] = make_hT_a0(src_ap, t_srcx, P1[j])
            st1[0] = make_hT_a1(st0.pop(0))
            hbs[0] = h1ring.next()
            make_hT_b(st1.pop(0), P1[0], li, hbs[0][0], hbs[0][1])
            if NP1 > 1:
                st1[1] = make_hT_a1(st0.pop(1))
            if NP1 > 2:
                st0[2] = make_hT_a0(src_ap, t_srcx, P1[2])
            for n_, t in enumerate(P1):
                if n_ + 3 < NP1:
                    st0[n_ + 3] = make_hT_a0(src_ap, t_srcx, P1[n_ + 3])
                if n_ + 2 < NP1:
                    st1[n_ + 2] = make_hT_a1(st0.pop(n_ + 2))
                hb, t_hb = hbs.pop(n_)
                if n_ == 0 and li == 0:
                    dbg("hT_p1", hb[:], t_hb[1])
                ab, t_ab = proj(hb, t_hb, wkv, t_wkv)
                if n_ + 1 < NP1:
                    hbs[n_ + 1] = h1ring.next()
                    make_hT_b(st1.pop(n_ + 1), P1[n_ + 1], li, hbs[n_ + 1][0], hbs[n_ + 1][1])
                if p1_tails:
                    p1_tails.pop(0)()
                rk, t_rk = head_rstd(ab[:, 0:256], t_ab, 2)
                kn, t_kn = tring.next()

                def knf(e, ab=ab, rk=rk, kn=kn):
                    rr = None
                    for h in range(2):
                        rr = e.scalar_tensor_tensor(out=kn[:, h * 128:(h + 1) * 128], in0=ab[:, h * 128:(h + 1) * 128],
                                                    scalar=rk[:, h:h + 1], in1=knw_b[:], op0=ALU.mult, op1=ALU.mult)
                    return rr
                S.op("dve", knf, reads=[t_ab, t_rk, t_knw], writes=[t_kn])
                kb, t_kb = qbring.next()
                if is_ctx(t):
                    S.op("dve", lambda e, kb=kb, kn=kn: e.tensor_copy(out=kb[:, 0:256], in_=kn[:, 0:256]), reads=[t_kn], writes=[t_kb])
                else:
                    rt, t_rt, rkey = r1ring.next()
                    lt = lat_index(t)
                    S.dma("sp", lambda e, rt=rt, lt=lt: e.dma_start(out=rt[:], in_=rope[lt * 128:(lt + 1) * 128, :]), rkey, writes=[t_rt])
                    apply_rope(kn, t_kn, 2, rt, t_rt, kb, t_kb)
                def tail(kb=kb, t_kb=t_kb, ab=ab, t_ab=t_ab, t=t):
                    tb, t_tb = Tb[0]
                    tbv = tb[:].bitcast(BF16).rearrange("p (k c) -> p k c", c=128)

                    def trk(e, kb=kb, tbv=tbv):
                        e.transpose(out=tbv[:, 0, :], in_=kb[:, 0:128], identity=identb[:])
                        return e.transpose(out=tbv[:, 1, :], in_=kb[:, 128:256], identity=identb[:])
                    S.op("pe", trk, reads=[t_kb, t_identb], writes=[t_tb])
                    if cc:
                        ks, t_ks, kkey = kstring.next()
                        S.op("act", lambda e, tbv=tbv, ks=ks: e.copy(out=ks[:], in_=tbv[:, 0:2, :]), reads=[t_tb], writes=[t_ks])
                        tk_ = Tok()
                        S.dma("sp", lambda e, ks=ks, t=t: e.dma_start(
                            out=kv_send[li][:, 0:4352].rearrange("p (h n) -> p h n", h=2)[:, :, t * 128:(t + 1) * 128], in_=ks[:]),
                            kkey, reads=[t_ks], writes=[tk_])
                        vs, t_vs, vkey = vstring.next()
                        S.op("act", lambda e, ab=ab, vs=vs: e.copy(out=vs[:, :, 0:128], in_=ab[:, 256:512].rearrange("p (h d) -> p h d", d=128)),
                             reads=[t_ab, t_Vones], writes=[t_vs])
                        tv_ = Tok()
                        S.dma("sp", lambda e, vs=vs, t=t: e.dma_start(
                            out=kv_send[li][:, 4352 + t * 260:4352 + (t + 1) * 260], in_=vs[:].rearrange("p g c -> p (g c)")),
                            vkey, reads=[t_vs], writes=[tv_])
                        kv_toks.extend([tk_, tv_])
                    else:
                        S.op("act", lambda e, tbv=tbv, t=t: e.copy(out=KT[:, :, t * 128:(t + 1) * 128], in_=tbv[:, 0:2, :]),
                             reads=[t_tb], writes=[t_K[t]])
                        S.op("act", lambda e, ab=ab, t=t: e.copy(out=VA[:, t, :, 0:128], in_=ab[:, 256:512].rearrange("p (h d) -> p h d", d=128)),
                             reads=[t_ab, t_Vones], writes=[t_V[t]])
                p1_tails.append(tail)
            while p1_tails:
                p1_tails.pop(0)()
            if cc:
                t_recv = Tok()
                S.dma("pool", lambda e: e.collective_compute("AllGather", ALU.bypass, replica_groups=PAIRS,
                                                             ins=[kv_send[li]], outs=[kv_recv[li]]),
                      f"cc{li}", reads=kv_toks, writes=[t_recv], inc=1)
                for blk in range(2):
                    S.dma("sp", lambda e, blk=blk: e.dma_start(
                        out=KT[:, :, blk * 2176:(blk + 1) * 2176],
                        in_=kv_recv[li][blk * 128:(blk + 1) * 128, 0:4352].rearrange("p (h n) -> p h n", h=2)),
                        f"KTl{blk}", reads=[t_recv], writes=[tkb[blk]])
                    S.dma("sp", lambda e, blk=blk: e.dma_start(
                        out=VA[:, blk * 17:(blk + 1) * 17, :, :],
                        in_=kv_recv[li][blk * 128:(blk + 1) * 128, 4352:KVW].rearrange("p (j g c) -> p j g c", g=2, c=130)),
                        f"VAl{blk}", reads=[t_recv], writes=[tvb[blk]])

            if li == 0:
                t0_ = p1_tiles[0]
                dbg("KT0", KT[:, :, t0_ * 128:(t0_ + 1) * 128], t_K[t0_])
                dbg("VA0", VA[:, t0_, :, :], t_V[t0_])
                if cc:
                    dbg("KT33", KT[:, :, 33 * 128:34 * 128], t_K[33])
                dbg("negC", negC[:], t_negC)
                dbg("gate_b", gate_b[:], t_gate)
                dbg("wsguT", wsguT[:], t_wsguT)
            if li + 1 < NL:
                emit_casts(li + 1)
            lat_tiles = [t for t in p2_tiles if not is_ctx(t)]
            ctx_tiles = [t for t in p2_tiles if is_ctx(t)]
            groups = [lat_tiles[i:i + TG] for i in range(0, len(lat_tiles), TG)]
            if ctx_tiles:
                groups.append(ctx_tiles)
            t_xs_next = [Tok() for _ in range(NT_ALL)]
            def load_rope(grp_):
                if is_ctx(grp_[0]):
                    return
                for i, t in enumerate(grp_):
                    lt = lat_index(t)
                    S.dma("sp", lambda e, i=i, lt=lt: e.dma_start(out=ropeT[i][:], in_=rope[lt * 128:(lt + 1) * 128, :]),
                          f"ropeT{i}", writes=[t_rope[i]])

            prepared = False
            for gi, grp in enumerate(groups):
                ng = len(grp)
                nxt = groups[gi + 1] if gi + 1 < len(groups) else None
                rflag = 1 if is_ctx(grp[0]) else 0
                if rflag:
                    build_gate(1)
                kall = list(range(NT_ALL)) if cc else list(p1_tiles)
                ktiles = [t for t in kall if is_ctx(t)] if rflag else kall
                if not prepared:
                    for i, t in enumerate(grp):
                        make_hT(src_ap, t_srcx, t, li, hT[i], t_hT[i])
                    load_rope(grp)
                order = [(2, "v", 0), (0, "u", 0), (4, "za", 0), (3, "v", 1), (1, "u", 1), (5, "za", 1),
                         (6, "q", 0), (7, "q", 1), (9, "zb", 0), (10, "zb", 1)]
                pending_backs = []
                for (cb, kind, hh) in order:
                    wb, t_wb, wkey = wring.next()
                    S.dma("sp", lambda e, wb=wb, cb=cb: e.dma_start(out=wb[:], in_=wbi[li][cb]), wkey, reads=[t_wci[li]], writes=[t_wb])
                    for i, t in enumerate(grp):
                        ab, t_ab = proj(hT[i], t_hT[i], wb, t_wb)
                        if i == 0:
                            while pending_backs:
                                pending_backs.pop(0)()
                        elif len(pending_backs) >= 2:
                            pending_backs.pop(0)()
                        back = None
                        if kind == "v":
                            gv, t_gv = tring.next()
                            S.op("act", lambda e, gv=gv, ab=ab: e.activation(out=gv[:], in_=ab[:], func=AF.Gelu_apprx_tanh),
                                 reads=[t_ab], writes=[t_gv])
                            rv, t_rv = head_rstd(gv[:], t_gv, 4)
                            v1, t_v1 = tring.next()
                            S.op("dve", lambda e, v1=v1, gv=gv, rv=rv: e.tensor_tensor(
                                out=v1[:].rearrange("p (g d) -> p g d", d=128), in0=gv[:].rearrange("p (g d) -> p g d", d=128),
                                in1=rv.unsqueeze(2).broadcast_to([128, 4, 128]), op=ALU.mult), reads=[t_gv, t_rv], writes=[t_v1])
                            vb, t_vb = vnring.next()
                            S.op("dve", lambda e, vb=vb, v1=v1, hh=hh: e.tensor_tensor(
                                out=vb[:], in0=v1[:], in1=vnw_b[:, hh * 512:(hh + 1) * 512], op=ALU.mult),
                                reads=[t_v1, t_vnw], writes=[t_vb])

                            def back(vb=vb, t_vb=t_vb, hh=hh, i=i):
                                def sgu(e):
                                    rr = None
                                    for g in range(4):
                                        rr = e.matmul(Sb[:, g * 128:(g + 1) * 128], lhsT=wsguT[:, 4 * hh + g, :],
                                                      rhs=vb[:, g * 128:(g + 1) * 128], start=True, stop=True)
                                    return rr
                                S.op("pe", sgu, reads=[t_vb, t_wsguT], writes=[t_Sb])
                                S.op("dve", lambda e: e.tensor_tensor(
                                    out=s_sb[i][:].rearrange("p (g d) -> p g d", d=128), in0=Sb[:].rearrange("p (g d) -> p g d", d=128),
                                    in1=bsguT[:, 4 * hh:4 * hh + 4].unsqueeze(2).broadcast_to([128, 4, 128]), op=ALU.add),
                                    reads=[t_Sb, t_bsguT], writes=[t_s[i]])
                        elif kind == "u":
                            gu, t_gu = tring.next()
                            S.op("act", lambda e, gu=gu, ab=ab: e.activation(out=gu[:], in_=ab[:], func=AF.Gelu_apprx_tanh),
                                 reads=[t_ab], writes=[t_gu])
                            S.op("dve", lambda e, gu=gu, i=i: e.tensor_tensor(out=s_sb[i][:], in0=gu[:], in1=s_sb[i][:], op=ALU.mult),
                                 reads=[t_gu], writes=[t_s[i]])
                        elif kind == "za":
                            sz, t_sz = tring.next()
                            S.op("act", lambda e, sz=sz, ab=ab: e.activation(out=sz[:], in_=ab[:], func=AF.Silu), reads=[t_ab], writes=[t_sz])
                            S.op("dve", lambda e, sz=sz, i=i, hh=hh: e.tensor_tensor(
                                out=gated[i][:, hh * 512:(hh + 1) * 512], in0=sz[:], in1=s_sb[i][:], op=ALU.mult),
                                reads=[t_sz, t_s[i]], writes=[t_gated[i]])
                        elif kind == "q":
                            rq, t_rq = head_rstd(ab[:], t_ab, 4)
                            q1, t_q1 = tring.next()
                            S.op("dve", lambda e, q1=q1, ab=ab, rq=rq: e.tensor_tensor(
                                out=q1[:].rearrange("p (g d) -> p g d", d=128), in0=ab[:].rearrange("p (g d) -> p g d", d=128),
                                in1=rq.unsqueeze(2).broadcast_to([128, 4, 128]), op=ALU.mult), reads=[t_ab, t_rq], writes=[t_q1])
                            S.op("dve", lambda e, q1=q1: e.tensor_tensor(
                                out=q1[:].rearrange("p (g d) -> p g d", d=128), in0=q1[:].rearrange("p (g d) -> p g d", d=128),
                                in1=qnw_b[:].unsqueeze(1).broadcast_to([128, 4, 128]), op=ALU.mult), reads=[t_qnw], writes=[t_q1])
                            qb, t_qb = qbring.next()
                            if rflag:
                                S.op("dve", lambda e, qb=qb, q1=q1: e.tensor_copy(out=qb[:], in_=q1[:]), reads=[t_q1], writes=[t_qb])
                            else:
                                apply_rope(q1, t_q1, 4, ropeT[i], t_rope[i], qb, t_qb)

                            def back(qb=qb, t_qb=t_qb, hh=hh, i=i):
                                tb, t_tb = Tb[(i + hh) % 2]
                                tbv = tb[:].bitcast(BF16).rearrange("p (k c) -> p k c", c=128)

                                def trq(e):
                                    rr = None
                                    for h in range(4):
                                        rr = e.transpose(out=tbv[:, h, :], in_=qb[:, h * 128:(h + 1) * 128], identity=identb[:])
                                    return rr
                                S.op("pe", trq, reads=[t_qb, t_identb], writes=[t_tb])
                                S.op("act", lambda e: e.copy(out=qT[i][:, 4 * hh:4 * hh + 4, :], in_=tbv[:, 0:4, :]),
                                     reads=[t_tb], writes=[t_qT[i]])
                        else:
                            S.op("act", lambda e, ab=ab, i=i, hh=hh: e.activation(out=szb[i][:, hh * 512:(hh + 1) * 512], in_=ab[:], func=AF.Silu),
                                 reads=[t_ab], writes=[t_szb[i]])
                        if back is not None:
                            pending_backs.append(back)
                while pending_backs:
                    pending_backs.pop(0)()

                if li == 0 and grp is groups[0]:
                    dbg("gatedA", gated[0][:, 0:1024], t_gated[0])
                    dbg("qT", qT[0][:], t_qT[0])
                    dbg("szb", szb[0][:], t_szb[0])
                inv_sqrt = float(128.0 ** -0.5)
                nk = len(ktiles)
                units = [(i, g, ki, kt) for i in range(ng) for g in range(2) for ki, kt in enumerate(ktiles)]
                LAG = 2
                Opairs = [Ob, Tb]

                def attn_front(u):
                    i, g, ki, kt = u
                    sbk, t_sbk = aring.next()
                    S.op("pe", lambda e: e.matmul(
                        sbk[:], lhsT=KT[:, g, kt * 128:(kt + 1) * 128], rhs=qT[i][:, 4 * g:4 * g + 4, :], start=True, stop=True),
                        reads=[t_K[kt], t_qT[i]], writes=[t_sbk])
                    pt, t_pt = ptring.next()
                    S.op("act", lambda e: e.activation(out=pt[:], in_=sbk[:], func=AF.Exp, bias=negC[:, 0:1], scale=inv_sqrt),
                         reads=[t_sbk, t_negC], writes=[t_pt])
                    return pt, t_pt

                def attn_back(u, pt, t_pt):
                    i, g, ki, kt = u
                    (o0, t_o0), (o1, t_o1) = Opairs[(2 * i + g) % 2]

                    def pv(e):
                        rr = None
                        for hq in range(4):
                            if hq < 3:
                                oap = o0[:, hq * 129:hq * 129 + 129]
                                st = (ki == 0 and hq == 0)
                            else:
                                oap = o1[:, 0:129]
                                st = (ki == 0)
                            rr = e.matmul(oap, lhsT=pt[:, hq * 128:(hq + 1) * 128], rhs=VA[:, kt, g, 0:129],
                                          start=st, stop=(ki == nk - 1), skip_group_check=True)
                        return rr
                    S.op("pe", pv, reads=[t_pt, t_V[kt]], writes=[t_o0, t_o1])
                    if ki != nk - 1:
                        return
                    rd, t_rd = smring.next()

                    def rden(e):
                        e.reciprocal(out=rd[:, 0:3], in_=o0[:, 0:387].rearrange("p (h c) -> p h c", c=129)[:, :, 128])
                        return e.reciprocal(out=rd[:, 3:4], in_=o1[:, 128:129])
                    S.op("dve", rden, reads=[t_o0, t_o1], writes=[t_rd])

                    def onorm(e):
                        rr = None
                        for hq in range(4):
                            h = 4 * g + hq
                            oap = o0[:, hq * 129:hq * 129 + 128] if hq < 3 else o1[:, 0:128]
                            rr = e.scalar_tensor_tensor(out=gated[i][:, 1024 + h * 128:1024 + (h + 1) * 128], in0=oap,
                                                        scalar=rd[:, hq:hq + 1], in1=szb[i][:, h * 128:(h + 1) * 128],
                                                        op0=ALU.mult, op1=ALU.mult)
                        return rr
                    S.op("dve", onorm, reads=[t_o0, t_o1, t_rd, t_szb[i]], writes=[t_gated[i]])

                pend = []
                for idx in range(len(units) + LAG):
                    if idx < len(units):
                        pend.append(attn_front(units[idx]))
                    if idx >= LAG:
                        attn_back(units[idx - LAG], *pend[idx - LAG])

                if li == 0 and grp is groups[0]:
                    dbg("gated", gated[0][:], t_gated[0])
                nst = {}
                if nxt is not None:
                    for j in range(min(2, len(nxt))):
                        nst[j] = make_hT_a(src_ap, t_srcx, nxt[j])
                    load_rope(nxt)
                for i, t in enumerate(grp):
                    for half in range(2):
                        tb, t_tb = Tb[half]
                        tbv = tb[:].bitcast(BF16).rearrange("p (k c) -> p k c", c=128)

                        def trg(e, half=half, tbv=tbv, i=i):
                            rr = None
                            for k in range(8):
                                kc = half * 8 + k
                                rr = e.transpose(out=tbv[:, k, :], in_=gated[i][:, kc * 128:(kc + 1) * 128], identity=identb[:])
                            return rr
                        S.op("pe", trg, reads=[t_gated[i], t_identb], writes=[t_tb])
                        S.op("act", lambda e, half=half, tbv=tbv, i=i: e.copy(out=gTv[i][:, half * 8:(half + 1) * 8, :], in_=tbv[:]),
                             reads=[t_tb], writes=[t_qT[i], t_szb[i]])

                for cb in range(4):
                    wb, t_wb, wkey = wring.next()
                    S.dma("sp", lambda e, wb=wb, cb=cb: e.dma_start(out=wb[:], in_=wbo[li][cb]), wkey, reads=[t_wco[li]], writes=[t_wb])
                    for i, t in enumerate(grp):
                        xpb, t_xp, xkey = xpring.next()
                        S.dma("sp", lambda e, xpb=xpb, t=t, cb=cb: e.dma_start(
                            out=xpb[:], in_=src_ap[t * 128:(t + 1) * 128, cb * 512:(cb + 1) * 512]), xkey, reads=[t_srcx[t]], writes=[t_xp])
                        ab, t_ab = proj(gTv[i], [t_qT[i], t_szb[i]], wb, t_wb)
                        yg, t_yg = tring.next()
                        S.op("dve", lambda e, yg=yg, ab=ab, cb=cb: e.tensor_tensor(
                            out=yg[:], in0=ab[:], in1=gate_b[:, cb * 512:(cb + 1) * 512], op=ALU.mult),
                            reads=[t_ab, t_gate], writes=[t_yg])
                        S.op("pool", lambda e, yg=yg, xpb=xpb: e.tensor_tensor(out=xpb[:], in0=xpb[:], in1=yg[:], op=ALU.add),
                             reads=[t_yg], writes=[t_xp])
                        if last_in_prog:
                            if final:
                                drow = t
                            else:
                                drow = t
                        else:
                            drow = t
                        S.dma("pool", lambda e, xpb=xpb, drow=drow, cb=cb: e.dma_start(
                            out=dst_ap[drow * 128:(drow + 1) * 128, cb * 512:(cb + 1) * 512], in_=xpb[:]), xkey,
                            reads=[t_xp], writes=[t_xs_next[t]])
                        if last_in_prog:
                            out_toks.append(t_xp)
                    if nxt is not None and cb < len(nxt):
                        make_hT_b(nst.pop(cb), nxt[cb], li, hT[cb], t_hT[cb])
                        if cb + 2 < len(nxt):
                            nst[cb + 2] = make_hT_a(src_ap, t_srcx, nxt[cb + 2])
                prepared = nxt is not None
            return t_xs_next

        t_xs_all = [Tok() for _ in range(NT_ALL)]
        for li_, l_ in enumerate(layers):
            last_ = (li_ == NL - 1)
            t_xs_all = run_layer(li_, l_, xa if li_ == 0 else xs, out if last_ else xs, t_xs_all,
                                 p2_tiles_per_layer[li_], last_)

        S.final_wait("pool", list({id(t): t for t in out_toks}.values()) + dbg_toks)
        S.emit(block)
    return nc


def _rope_tables(pos):
    rows = (pos // GRID_W).astype(np.float32)
    cols = (pos % GRID_W).astype(np.float32)
    inv_freq = (np.float32(10000.0) ** (-np.arange(0, 64, 2, dtype=np.float32) / np.float32(64))).astype(np.float32)
    ang_r = rows[:, None] * inv_freq[None, :]
    ang_c = cols[:, None] * inv_freq[None, :]
    ang = np.concatenate([ang_r, ang_r, ang_c, ang_c], axis=-1).astype(np.float32)
    return np.concatenate([np.cos(ang), np.sin(ang)], axis=-1).astype(np.float32)


_PROG_CACHE = {}


def _get_prog(key, *args):
    if key not in _PROG_CACHE:
        _PROG_CACHE[key] = build(*args)
    return _PROG_CACHE[key]


def _common_inputs(c, c_ctx, norm_w, w_mod, b_mod, w_in, w_sgu, b_sgu, v_norm_w, q_norm_w, k_norm_w, w_out):
    f = lambda a: np.ascontiguousarray(np.asarray(a, dtype=np.float32))
    shared = {
        "identb": np.eye(128, dtype=np.float32).astype(ml_dtypes.bfloat16),
        "identf": np.eye(128, dtype=np.float32),
        "w_mod": f(w_mod), "b_mod": f(b_mod).reshape(2, 48, 128), "norm_w": f(norm_w).reshape(2, 16, 128),
        "w_in": f(w_in), "w_out": f(w_out), "w_sgu": f(w_sgu), "b_sgu": f(b_sgu),
        "v_norm_w": f(v_norm_w).reshape(2, 1024), "q_norm_w": f(q_norm_w), "k_norm_w": f(k_norm_w),
    }
    return shared


def kernel(x, c, ctx, c_ctx, norm_w, w_mod, b_mod, w_in, w_sgu, b_sgu, v_norm_w, q_norm_w, k_norm_w, w_out):
    x = np.asarray(x, dtype=np.float32)
    ctx = np.asarray(ctx, dtype=np.float32)
    c = np.asarray(c, dtype=np.float32)
    c_ctx = np.asarray(c_ctx, dtype=np.float32)
    shared = _common_inputs(c, c_ctx, norm_w, w_mod, b_mod, w_in, w_sgu, b_sgu, v_norm_w, q_norm_w, k_norm_w, w_out)
    H = SEQ // 2
    CH = CTX // 2

    def core_maps(xfull, cfull):
        maps = []
        for core in range(8):
            b, hf = divmod(core, 2)
            o, p = hf, 1 - hf
            xa = np.concatenate([xfull[b, o * H:(o + 1) * H], cfull[b, o * CH:(o + 1) * CH],
                                 xfull[b, p * H:(p + 1) * H], cfull[b, p * CH:(p + 1) * CH]], axis=0)
            pos = np.concatenate([np.arange(o * H, (o + 1) * H), np.arange(p * H, (p + 1) * H)])
            m = dict(shared)
            m["xa"] = np.ascontiguousarray(xa)
            m["rope"] = _rope_tables(pos)
            m["c2"] = np.ascontiguousarray(np.stack([c[b], c_ctx], 0).reshape(32, 128))
            maps.append(m)
        return maps

    all_tiles = list(range(NT_ALL))
    own_lat = list(range(16))
    if MODE == "cc":
        maps = []
        for core in range(8):
            b, hf = divmod(core, 2)
            m = dict(shared)
            m["xa"] = np.ascontiguousarray(np.concatenate([x[b, hf * H:(hf + 1) * H], ctx[b, hf * CH:(hf + 1) * CH]], axis=0))
            m["rope"] = _rope_tables(np.arange(hf * H, (hf + 1) * H))
            m["c2"] = np.ascontiguousarray(np.stack([c[b], c_ctx], 0).reshape(32, 128))
            maps.append(m)
        nc = _get_prog("cc", [0, 1], True, list(range(17)), [list(range(17)), own_lat], 16, True)
        res = run_bass_kernel_spmd(nc, maps, core_ids=list(range(8)))
        outs = [r["out"] for r in res.results]
    elif MODE == "fused":
        nc = _get_prog("fused", [0, 1], True, all_tiles, [all_tiles, own_lat], 16)
        res = run_bass_kernel_spmd(nc, core_maps(x, ctx), core_ids=list(range(8)))
        outs = [r["out"] for r in res.results]
    else:
        ncA = _get_prog("L0", [0], False, all_tiles, [list(range(17))], 17)
        resA = run_bass_kernel_spmd(ncA, core_maps(x, ctx), core_ids=list(range(8)))
        x1 = np.empty_like(x)
        ctx1 = np.empty_like(ctx)
        for core in range(8):
            b, hf = divmod(core, 2)
            xn_ = resA.results[core]["xnext"]
            x1[b, hf * H:(hf + 1) * H] = xn_[0:H]
            ctx1[b, hf * CH:(hf + 1) * CH] = xn_[H:H + CH]
        ncB = _get_prog("L1", [1], True, all_tiles, [own_lat], 16)
        resB = run_bass_kernel_spmd(ncB, core_maps(x1, ctx1), core_ids=list(range(8)))
        outs = [r["out"] for r in resB.results]
    y = np.empty((4, SEQ, D), dtype=np.float32)
    for core in range(8):
        b, hf = divmod(core, 2)
        y[b, hf * H:(hf + 1) * H] = outs[core]
    return y
```
